# Optimizing a Trainium2 kernel written in Bass

```python
import jax, jax.numpy as jnp
from jax import lax
import numpy as np

D_MODEL = 1024
BATCH = 4
SEQ = 8192
DEPTH = 1

D_RNN = 5 * D_MODEL // 4
RG_BLOCKS = 16
RG_BLOCK_W = D_RNN // RG_BLOCKS
RG_CONV = 4
RG_C = 8.0
DN_QK_HEADS = D_MODEL // 128
DN_V_HEADS = 2 * DN_QK_HEADS
DN_DK = 128
DN_DV = 128
DN_QK = DN_QK_HEADS * DN_DK
DN_V = DN_V_HEADS * DN_DV
DN_CONV = 4
DN_CHUNK = 64
D_FF = 11 * D_MODEL // 4
FFN_CONV = 3
LN_EPS = 1e-5
RMS_EPS = 1e-6
L2_EPS = 1e-6
DEEPNORM_ALPHA = (2 * DEPTH) ** 0.25
DEEPNORM_BETA = (8 * DEPTH) ** -0.25
IN_SPLITS = (D_RNN, D_RNN, DN_QK, DN_QK, DN_V, DN_V, DN_V_HEADS, DN_V_HEADS, D_MODEL, D_MODEL)
D_IN = sum(IN_SPLITS)

kernel_name = "hybrid_rglru_gdn_convffn_deepnorm_adaln"

F32 = jnp.float32


def split_cols(t, sizes):
    idx = np.cumsum(sizes)[:-1].tolist()
    return jnp.split(t, idx, axis=-1)


def causal_dwconv(x, w):
    k, ch = w.shape
    return lax.conv_general_dilated(
        x, w[:, None, :].astype(x.dtype), window_strides=(1,), padding=[(k - 1, 0)],
        dimension_numbers=('NWC', 'WIO', 'NWC'), feature_group_count=ch)


def layer_norm(x, g, b):
    xf = x.astype(F32)
    mu = jnp.mean(xf, axis=-1, keepdims=True)
    xc = xf - mu
    var = jnp.mean(xc * xc, axis=-1, keepdims=True)
    return (xc * lax.rsqrt(var + LN_EPS) * g.astype(F32) + b.astype(F32)).astype(x.dtype)


def l2_normalize(t):
    return t * lax.rsqrt(jnp.sum(t * t, axis=-1, keepdims=True) + L2_EPS)


def rg_lru(xr, w_a, b_a, w_x, b_x, lam):
    bsz, s, _ = xr.shape
    xf = xr.astype(F32)
    xb = xf.reshape(bsz, s, RG_BLOCKS, RG_BLOCK_W)
    gate_r = jax.nn.sigmoid(jnp.einsum('bsni,nij->bsnj', xb, w_a.astype(F32)).reshape(bsz, s, D_RNN) + b_a.astype(F32))
    gate_i = jax.nn.sigmoid(jnp.einsum('bsni,nij->bsnj', xb, w_x.astype(F32)).reshape(bsz, s, D_RNN) + b_x.astype(F32))
    log_a = -RG_C * gate_r * jax.nn.softplus(-lam.astype(F32))
    a = jnp.exp(log_a)
    mult = jnp.sqrt(-jnp.expm1(2.0 * log_a))
    u = mult * gate_i * xf

    def combine(left, right):
        a_l, h_l = left
        a_r, h_r = right
        return a_l * a_r, a_r * h_l + h_r

    _, h = lax.associative_scan(combine, (a, u), axis=1)
    return h


def chunk_gated_delta_rule(q, k, v, g, beta):
    bsz, nh, s, dk = q.shape
    dv = v.shape[-1]
    n = s // DN_CHUNK
    c = DN_CHUNK
    rs = lambda t: t.reshape((bsz, nh, n, c) + t.shape[3:])
    q, k, v, g, beta = rs(q), rs(k), rs(v), rs(g), rs(beta)
    G = jnp.cumsum(g, axis=-1)
    causal = jnp.tril(jnp.ones((c, c), dtype=bool))
    strict = jnp.tril(jnp.ones((c, c), dtype=bool), k=-1)
    diff = G[..., :, None] - G[..., None, :]
    decay = jnp.exp(jnp.where(causal, diff, -jnp.inf))
    kb = k * beta[..., None]
    A = jnp.where(strict, jnp.einsum('bhnid,bhnjd->bhnij', kb, k) * decay, 0.0)
    T = A + jnp.eye(c, dtype=F32)
    u = lax.linalg.triangular_solve(T, v * beta[..., None], left_side=True, lower=True)
    w = lax.linalg.triangular_solve(T, kb * jnp.exp(G)[..., None], left_side=True, lower=True)
    qk = jnp.einsum('bhnid,bhnjd->bhnij', q, k) * decay
    q_dec = q * jnp.exp(G)[..., None]
    k_dec = k * jnp.exp(G[..., -1:] - G)[..., None]
    g_last = jnp.exp(G[..., -1])
    xs = tuple(jnp.moveaxis(t, 2, 0) for t in (qk, q_dec, k_dec, u, w, g_last))

    def step(state, inp):
        qk_n, qd_n, kd_n, u_n, w_n, gl_n = inp
        v_new = u_n - jnp.einsum('bhcd,bhde->bhce', w_n, state)
        o = jnp.einsum('bhcd,bhde->bhce', qd_n, state) + jnp.einsum('bhij,bhje->bhie', qk_n, v_new)
        state = gl_n[..., None, None] * state + jnp.einsum('bhcd,bhce->bhde', kd_n, v_new)
        return state, o

    _, o = lax.scan(step, jnp.zeros((bsz, nh, dk, dv), F32), xs)
    return jnp.moveaxis(o, 0, 2).reshape(bsz, nh, s, dv)


def gated_deltanet(q, k, v, z, a_in, b_in, conv_w, a_log, dt_bias, norm_w):
    bsz, s, _ = q.shape
    qkv = jax.nn.silu(causal_dwconv(jnp.concatenate([q, k, v], axis=-1).astype(F32), conv_w.astype(F32)))
    q, k, v = split_cols(qkv, (DN_QK, DN_QK, DN_V))
    rep = DN_V_HEADS // DN_QK_HEADS
    q = jnp.repeat(l2_normalize(q.reshape(bsz, s, DN_QK_HEADS, DN_DK)), rep, axis=2) * (DN_DK ** -0.5)
    k = jnp.repeat(l2_normalize(k.reshape(bsz, s, DN_QK_HEADS, DN_DK)), rep, axis=2)
    v = v.reshape(bsz, s, DN_V_HEADS, DN_DV)
    beta = jax.nn.sigmoid(b_in.astype(F32))
    g = -jnp.exp(a_log.astype(F32)) * jax.nn.softplus(a_in.astype(F32) + dt_bias.astype(F32))
    o = chunk_gated_delta_rule(jnp.swapaxes(q, 1, 2), jnp.swapaxes(k, 1, 2), jnp.swapaxes(v, 1, 2),
                               jnp.swapaxes(g, 1, 2), jnp.swapaxes(beta, 1, 2))
    o = jnp.swapaxes(o, 1, 2)
    o = o * lax.rsqrt(jnp.mean(o * o, axis=-1, keepdims=True) + RMS_EPS) * norm_w.astype(F32)
    o = o * jax.nn.silu(z.astype(F32).reshape(bsz, s, DN_V_HEADS, DN_DV))
    return o.reshape(bsz, s, DN_V)


def token_mixer(h, w_in, rg_conv_w, rg_conv_b, rg_w_a, rg_b_a, rg_w_x, rg_b_x, rg_lambda,
                dn_conv_w, dn_a_log, dn_dt_bias, dn_norm_w, w_proj_a, w_proj_b, w_out):
    proj = h @ w_in
    xr, gr, q, k, v, z, a_in, b_in, g_a, g_b = split_cols(proj, IN_SPLITS)
    xr = causal_dwconv(xr, rg_conv_w) + rg_conv_b
    rec = rg_lru(xr, rg_w_a, rg_b_a, rg_w_x, rg_b_x, rg_lambda) * jax.nn.gelu(gr.astype(F32))
    y_a = rec.astype(h.dtype) @ w_proj_a
    dn = gated_deltanet(q, k, v, z, a_in, b_in, dn_conv_w, dn_a_log, dn_dt_bias, dn_norm_w)
    y_b = dn.astype(h.dtype) @ w_proj_b
    merged = jax.nn.sigmoid(g_a) * y_a + jax.nn.sigmoid(g_b) * y_b
    return merged @ w_out


def conv_ffn(h, w_gate, w_up, conv_w, conv_b, w_down):
    gate = causal_dwconv(h @ w_gate, conv_w) + conv_b
    return (jax.nn.gelu(gate) * (h @ w_up)) @ w_down


def setup_inputs(seed: int = 0) -> dict:
    key = jax.random.key(seed)
    ks = jax.random.split(key, 32)
    L, D = DEPTH, D_MODEL
    nrm = lambda kk, shape, scale: jax.random.normal(kk, shape, F32) * scale
    u_a = jax.random.uniform(ks[9], (L, D_RNN), F32, 0.9, 0.999)
    s_a = u_a ** (1.0 / RG_C)
    rg_lambda = jnp.log(s_a) - jnp.log1p(-s_a)
    dt = jnp.exp(jax.random.uniform(ks[12], (L, DN_V_HEADS), F32, np.log(1e-3), np.log(1e-1)))
    dt = jnp.maximum(dt, 1e-4)
    return {
        "x": nrm(ks[0], (BATCH, SEQ, D), 1.0),
        "c": nrm(ks[1], (BATCH, D), 1.0),
        "w_ada": nrm(ks[2], (L, D, 6 * D), 0.1 * D ** -0.5),
        "b_ada": nrm(ks[3], (L, 6 * D), 0.01),
        "w_in": nrm(ks[4], (L, D, D_IN), D ** -0.5),
        "rg_conv_w": nrm(ks[5], (L, RG_CONV, D_RNN), RG_CONV ** -0.5),
        "rg_conv_b": nrm(ks[6], (L, D_RNN), 0.01),
        "rg_w_a": nrm(ks[7], (L, RG_BLOCKS, RG_BLOCK_W, RG_BLOCK_W), RG_BLOCK_W ** -0.5),
        "rg_b_a": nrm(ks[8], (L, D_RNN), 0.01),
        "rg_w_x": nrm(ks[10], (L, RG_BLOCKS, RG_BLOCK_W, RG_BLOCK_W), RG_BLOCK_W ** -0.5),
        "rg_b_x": nrm(ks[11], (L, D_RNN), 0.01),
        "rg_lambda": rg_lambda,
        "dn_conv_w": nrm(ks[13], (L, DN_CONV, 2 * DN_QK + DN_V), DN_CONV ** -0.5),
        "dn_a_log": jnp.log(jax.random.uniform(ks[14], (L, DN_V_HEADS), F32, 1.0, 16.0)),
        "dn_dt_bias": dt + jnp.log(-jnp.expm1(-dt)),
        "dn_norm_w": 1.0 + nrm(ks[15], (L, DN_DV), 0.02),
        "w_proj_a": nrm(ks[16], (L, D_RNN, D), D_RNN ** -0.5),
        "w_proj_b": nrm(ks[17], (L, DN_V, D), DN_V ** -0.5),
        "w_out": nrm(ks[18], (L, D, D), DEEPNORM_BETA * D ** -0.5),
        "ln1_g": 1.0 + nrm(ks[19], (L, D), 0.02),
        "ln1_b": nrm(ks[20], (L, D), 0.01),
        "ffn_w_gate": nrm(ks[21], (L, D, D_FF), D ** -0.5),
        "ffn_w_up": nrm(ks[22], (L, D, D_FF), D ** -0.5),
        "ffn_conv_w": nrm(ks[23], (L, FFN_CONV, D_FF), FFN_CONV ** -0.5),
        "ffn_conv_b": nrm(ks[24], (L, D_FF), 0.01),
        "ffn_w_down": nrm(ks[25], (L, D_FF, D), DEEPNORM_BETA * D_FF ** -0.5),
        "ln2_g": 1.0 + nrm(ks[26], (L, D), 0.02),
        "ln2_b": nrm(ks[27], (L, D), 0.01),
    }


def reference(x, c, w_ada, b_ada, w_in, rg_conv_w, rg_conv_b, rg_w_a, rg_b_a, rg_w_x, rg_b_x,
              rg_lambda, dn_conv_w, dn_a_log, dn_dt_bias, dn_norm_w, w_proj_a, w_proj_b, w_out,
              ln1_g, ln1_b, ffn_w_gate, ffn_w_up, ffn_conv_w, ffn_conv_b, ffn_w_down, ln2_g, ln2_b):
    for l in range(DEPTH):
        ada = jax.nn.silu(c) @ w_ada[l] + b_ada[l]
        sh1, sc1, gt1, sh2, sc2, gt2 = [t[:, None, :] for t in jnp.split(ada, 6, axis=-1)]
        h = x * (1.0 + sc1) + sh1
        mix = token_mixer(h, w_in[l], rg_conv_w[l], rg_conv_b[l], rg_w_a[l], rg_b_a[l], rg_w_x[l],
                          rg_b_x[l], rg_lambda[l], dn_conv_w[l], dn_a_log[l], dn_dt_bias[l],
                          dn_norm_w[l], w_proj_a[l], w_proj_b[l], w_out[l])
        x = layer_norm(DEEPNORM_ALPHA * x + (1.0 + gt1) * mix, ln1_g[l], ln1_b[l])
        h = x * (1.0 + sc2) + sh2
        ff = conv_ffn(h, ffn_w_gate[l], ffn_w_up[l], ffn_conv_w[l], ffn_conv_b[l], ffn_w_down[l])
        x = layer_norm(DEEPNORM_ALPHA * x + (1.0 + gt2) * ff, ln2_g[l], ln2_b[l])
    return x
```

```python
import contextlib
import numpy as np
import concourse.bass as bass
import concourse.mybir as mybir
from concourse.bass_utils import run_bass_kernel_spmd

F32 = mybir.dt.float32
BF16 = mybir.dt.bfloat16
AF = mybir.ActivationFunctionType
ALU = mybir.AluOpType

D = 1024
DRNN = 1280
DFF = 2816
DIN = 10784
NHV = 16
C_XR, C_GR, C_Q, C_K, C_V, C_Z, C_A, C_B, C_GA, C_GB = 0, 1280, 2560, 3584, 4608, 6656, 8704, 8720, 8736, 9760
ALPHA = 2.0 ** 0.25
BIG = 32768.0
GELU_C = 1.5957691216057308

_P = {}
_off = 0
for _n, _w in [("b_ada", 48), ("c", 8), ("rg_cw", 40), ("rg_cb", 10), ("rg_ba", 10), ("rg_bx", 10), ("rg_lam", 10),
               ("dn_cw", 128), ("ffn_cw", 66), ("ffn_cb", 22)]:
    _P[_n] = (_off, _w)
    _off += _w
NP = _off
_R = {}
_off = 0
for _n, _w in [("b_gt1", 1024), ("b_gt2", 1024), ("ln1_g", 1024), ("ln1_b", 1024), ("ln2_g", 1024), ("ln2_b", 1024),
               ("nw", 2048)]:
    _R[_n] = (_off, _w)
    _off += _w
NR = _off
CM_IDENT, CM_L, CM_U, CM_MS, CM_MIT, CM_BD, CM_OFF, CM_ONES = range(8)
NCM = 8


class Buf:
    def __init__(self, kb, name, t):
        self.kb = kb
        self.name = name
        self.t = t
        self.w = None
        self.r = []
        self.lsem = None
        self.ssem = None
        self.excl = False

    def add_reader(self, tok):
        for i, (s, v) in enumerate(self.r):
            if s == tok[0]:
                if v < tok[1]:
                    self.r[i] = tok
                return
        self.r.append(tok)


class EngW:
    LIMIT = 24000

    def __init__(self, kb, eng, name):
        self.kb = kb
        self.eng = eng
        self.name = name
        self.sem = None
        self.cnt = 0
        self.seen = {}
        self.pending = []

    def wait(self, toks):
        best = {}
        for s, v in toks:
            if best.get(s, 0) < v:
                best[s] = v
        for s, v in best.items():
            if self.seen.get(s, 0) >= v:
                continue
            self.eng.wait_ge(self.kb.sems[s], v)
            self.seen[s] = v

    def bump(self, inst):
        if self.sem is None or self.cnt >= self.LIMIT:
            self.sem = self.kb.new_sem(f"e_{self.name}_{len(self.kb.sems)}")
            self.cnt = 0
        inst.then_inc(self.kb.sems[self.sem], 1)
        self.cnt += 1
        return (self.sem, self.cnt)

    def cur(self):
        return None if self.sem is None else (self.sem, self.cnt)


class KB:
    def __init__(self, nc, es):
        self.nc = nc
        self.es = es
        self.sems = []
        self.semcnt = []
        self.free_dma_sems = []
        self.pe = EngW(self, nc.tensor, "pe")
        self.act = EngW(self, nc.scalar, "act")
        self.dve = EngW(self, nc.vector, "dve")
        self.pool = EngW(self, nc.gpsimd, "pool")
        self.sp = EngW(self, nc.sync, "sp")
        self.engs = [self.pe, self.act, self.dve, self.pool, self.sp]
        self.dma_toks = {}
        self.phase_bufs = []
        self.n_inst = 0

    def new_sem(self, name):
        h = self.es.enter_context(self.nc.semaphore(name))
        self.sems.append(h)
        self.semcnt.append(0)
        return len(self.sems) - 1

    def dma_sem(self):
        if self.free_dma_sems:
            return self.free_dma_sems.pop()
        return self.new_sem(f"d_{len(self.sems)}")

    def sb(self, pes, name, shape, dt):
        self.n_alloc = getattr(self, "n_alloc", 0) + 1
        t = pes.enter_context(self.nc.sbuf_tensor(f"sb{self.n_alloc}_{name}", list(shape), dt))
        b = Buf(self, name, t)
        self.phase_bufs.append(b)
        return b

    def ps(self, pes, name, shape, dt):
        self.n_alloc = getattr(self, "n_alloc", 0) + 1
        t = pes.enter_context(self.nc.psum_tensor(f"ps{self.n_alloc}_{name}", list(shape), dt))
        b = Buf(self, name, t)
        b.excl = True
        self.phase_bufs.append(b)
        return b

    def op(self, E, fn, reads=(), writes=(), inc=True):
        ex = [b for b in reads if b.excl]
        if ex:
            reads = [b for b in reads if not b.excl]
            writes = list(writes) + [b for b in ex if b not in writes]
        toks = []
        for b in reads:
            if b.w is not None:
                toks.append(b.w)
        for b in writes:
            if b.w is not None:
                toks.append(b.w)
            toks.extend(b.r)
        E.wait(toks)
        inst = fn()
        self.n_inst += 1
        if inc:
            tok = E.bump(inst)
            sets = E.pending + [(reads, writes)]
            E.pending = []
            for rd, wr in sets:
                for b in rd:
                    b.add_reader(tok)
                for b in wr:
                    b.w = tok
                    b.r = []
        else:
            E.pending.append((reads, writes))
        return inst

    def dma(self, out_ap, in_ap, reads=(), writes=(), Q=None):
        Q = Q or self.sp
        toks = []
        for b in reads:
            if b.w is not None:
                toks.append(b.w)
        for b in writes:
            if b.w is not None:
                toks.append(b.w)
            toks.extend(b.r)
        Q.wait(toks)
        inst = Q.eng.dma_start(out=out_ap, in_=in_ap)
        self.n_inst += 1
        if writes:
            tb = writes[0]
            if tb.lsem is None:
                tb.lsem = self.dma_sem()
            s = tb.lsem
        else:
            tb = reads[0]
            if tb.ssem is None:
                tb.ssem = self.dma_sem()
            s = tb.ssem
        self.semcnt[s] += 16
        inst.then_inc(self.sems[s], 16)
        tok = (s, self.semcnt[s])
        self.dma_toks[s] = tok
        for b in writes:
            b.w = tok
            b.r = []
        for b in reads:
            b.add_reader(tok)
        return tok

    def barrier(self, keep=()):
        toks = list(self.dma_toks.values())
        for E in self.engs:
            assert not E.pending
            c = E.cur()
            if c is not None:
                toks.append(c)
        for E in self.engs:
            E.wait(toks)
        for b in self.phase_bufs:
            if b.lsem is not None:
                self.free_dma_sems.append(b.lsem)
                b.lsem = None
            if b.ssem is not None:
                self.free_dma_sems.append(b.ssem)
                b.ssem = None
        self.dma_toks = {}

    def final_wait(self):
        toks = list(self.dma_toks.values())
        self.sp.wait(toks)


def _cm(cm, idx, dt=None):
    return cm.t[:, idx * 128:(idx + 1) * 128]


def build_program(S, debug=False):
    assert S % 512 == 0
    NT = S // 512
    NCH = S // 128
    nc = bass.Bass("TRN2", target_bir_lowering=False)
    okind = "ExternalOutput" if debug else "Internal"

    def din(name, shape, dt=F32):
        return nc.dram_tensor(name, list(shape), dt, kind="ExternalInput").ap()

    xT_d = din("xT", [8, 128, S])
    x_d = din("x", [S, D])
    w_ada_d = din("w_ada", [D, 6 * D])
    w_in_d = din("w_in", [D, DIN])
    gate_a_d = din("gate_a", [128, 10, 3, 128])
    gate_x_d = din("gate_x", [128, 10, 3, 128])
    w_pa_d = din("w_proj_a", [DRNN, D])
    w_pb_d = din("w_proj_b", [2048, D])
    w_out_d = din("w_out", [D, D])
    w_fg_d = din("ffn_w_gate", [D, DFF])
    w_fu_d = din("ffn_w_up", [D, DFF])
    w_fd_d = din("ffn_w_down", [DFF, D])
    params_d = din("params", [128, NP])
    rows_d = din("rows", [128, NR])
    cmat_d = din("cmat", [128, NCM * 128])
    hrep_d = din("hrep", [128, 2, NCH * 16])
    out_d = nc.dram_tensor("out", [S, D], F32, kind="ExternalOutput").ap()
    hT_d = nc.dram_tensor("hT_s", [8, 128, S], BF16, kind=okind).ap()
    recT_d = nc.dram_tensor("recT_s", [10, 128, S], BF16, kind=okind).ap()
    dnT_d = nc.dram_tensor("dnT_s", [16, 128, S], BF16, kind=okind).ap()
    x1_d = nc.dram_tensor("x1_s", [S, D], F32, kind=okind).ap()

    with contextlib.ExitStack() as es:
        kb = KB(nc, es)
        PE, ACT, DVE, POOL = kb.pe, kb.act, kb.dve, kb.pool
        params = kb.sb(es, "params", [128, NP], F32)
        cm = kb.sb(es, "cm", [128, NCM * 128], F32)
        cmb = kb.sb(es, "cmb", [128, NCM * 128], BF16)
        ada = kb.sb(es, "ada", [128, 48], F32)
        gt1r = kb.sb(es, "gt1r", [128, 1024], F32)
        gt2r = kb.sb(es, "gt2r", [128, 1024], F32)
        kb.dma(params.t[:], params_d[:, :], writes=[params])
        kb.dma(cm.t[:], cmat_d[:, :], writes=[cm])
        kb.op(ACT, lambda: nc.scalar.copy(out=cmb.t[:], in_=cm.t[:]), reads=[cm], writes=[cmb])

        def pcol(name, j=0, n=1):
            o, w = _P[name]
            return params.t[:, o + j:o + j + n]

        ident_f = cm.t[:, CM_IDENT * 128:(CM_IDENT + 1) * 128]
        ident_b = cmb.t[:, CM_IDENT * 128:(CM_IDENT + 1) * 128]
        ones_f = cm.t[:, CM_ONES * 128:(CM_ONES + 1) * 128]
        ones_b = cmb.t[:, CM_ONES * 128:(CM_ONES + 1) * 128]

        def load_w_bf16(pes, dst, dram, r0, nk, c0, ncols, stg, dst_c0=0):
            PW = stg[0].t.shape[-1]
            i = load_w_bf16.cnt
            for cc in range(0, ncols, PW):
                w = min(PW, ncols - cc)
                st = stg[i % len(stg)]
                src = dram[r0:r0 + nk * 128, c0 + cc:c0 + cc + w].rearrange("(k p) c -> p k c", p=128)
                kb.dma(st.t[:, 0:nk, 0:w], src, writes=[st])
                o = dst.t[:, 0:nk, dst_c0 + cc:dst_c0 + cc + w]
                if i % 2 == 0:
                    kb.op(ACT, lambda o=o, st=st, w=w: nc.scalar.copy(out=o, in_=st.t[:, 0:nk, 0:w]), reads=[st], writes=[dst])
                else:
                    kb.op(DVE, lambda o=o, st=st, w=w: nc.vector.tensor_copy(out=o, in_=st.t[:, 0:nk, 0:w]), reads=[st], writes=[dst])
                i += 1
            load_w_bf16.cnt = i
        load_w_bf16.cnt = 0

        with contextlib.ExitStack() as pes:
            scf = kb.sb(pes, "scf", [128, 8], F32)
            screp = kb.sb(pes, "screp", [128, 8, 128], F32)
            wst = [kb.sb(pes, f"wst{i}", [128, 8, 1024], F32) for i in range(2)]
            rows01 = kb.sb(pes, "rows01", [128, 2048], F32)
            pa = kb.ps(pes, "pa", [128, 512], F32)
            pb = [kb.ps(pes, f"pb{i}", [128, 512], F32) for i in range(2)]
            kb.dma(rows01.t[:], rows_d[:, 0:2048], writes=[rows01])
            kb.op(ACT, lambda: nc.scalar.activation(out=scf.t[:], in_=pcol("c", 0, 8), func=AF.Silu), reads=[params], writes=[scf])
            for k in range(8):
                kb.op(DVE, lambda k=k: nc.vector.tensor_copy(out=screp.t[:, k, :], in_=scf.t[:, k:k + 1].to_broadcast([128, 128])),
                      reads=[scf], writes=[screp])
            for jb in range(6):
                st = wst[jb % 2]
                kb.dma(st.t[:], w_ada_d[:, jb * 1024:(jb + 1) * 1024].rearrange("(k p) c -> p k c", p=128), writes=[st])
                if jb in (2, 5):
                    dst = gt1r if jb == 2 else gt2r
                    for cb in range(2):
                        for k in range(8):
                            kb.op(PE, lambda k=k, cb=cb, st=st: nc.tensor.matmul(pb[cb].t[:], lhsT=screp.t[:, k, :], rhs=st.t[:, k, cb * 512:(cb + 1) * 512],
                                                                               start=(k == 0), stop=(k == 7)),
                                  reads=[screp, st], writes=[pb[cb]], inc=(k == 7))
                        ro = (0 if jb == 2 else 1024) + cb * 512
                        kb.op(DVE, lambda cb=cb, ro=ro, dst=dst: nc.vector.scalar_tensor_tensor(
                            out=dst.t[:, cb * 512:(cb + 1) * 512], in0=pb[cb].t[:], scalar=1.0, in1=rows01.t[:, ro:ro + 512],
                            op0=ALU.add, op1=ALU.add), reads=[pb[cb], rows01], writes=[dst])
                for j in range(8):
                    col = jb * 8 + j
                    for k in range(8):
                        kb.op(PE, lambda k=k, j=j, col=col, st=st: nc.tensor.matmul(pa.t[:, col:col + 1], lhsT=st.t[:, k, j * 128:(j + 1) * 128],
                                                                                 rhs=scf.t[:, k:k + 1], start=(k == 0), stop=(k == 7)),
                              reads=[scf, st], writes=[pa], inc=(k == 7))
            kb.op(DVE, lambda: nc.vector.tensor_tensor(out=ada.t[:], in0=pa.t[:, 0:48], in1=pcol("b_ada", 0, 48), op=ALU.add),
                  reads=[pa, params], writes=[ada])
            for o in (8, 32):
                kb.op(DVE, lambda o=o: nc.vector.tensor_scalar(out=ada.t[:, o:o + 8], in0=ada.t[:, o:o + 8], scalar1=1.0, scalar2=None, op0=ALU.add),
                      reads=[ada], writes=[ada])
            xin = [kb.sb(pes, f"xin{i}", [128, 8, 512], F32) for i in range(2)]
            hto = [kb.sb(pes, f"hto{i}", [128, 8, 512], BF16) for i in range(2)]
            for t in range(NT):
                xi, ho = xin[t % 2], hto[t % 2]
                kb.dma(xi.t[:], xT_d[:, :, t * 512:(t + 1) * 512].rearrange("c p t -> p c t"), writes=[xi])
                for c in range(8):
                    if c % 2 == 0:
                        kb.op(ACT, lambda c=c, xi=xi, ho=ho: nc.scalar.activation(out=ho.t[:, c, :], in_=xi.t[:, c, :], func=AF.Identity,
                                                                              scale=ada.t[:, 8 + c:9 + c], bias=ada.t[:, c:c + 1]),
                              reads=[xi, ada], writes=[ho])
                    else:
                        kb.op(DVE, lambda c=c, xi=xi, ho=ho: nc.vector.tensor_scalar(out=ho.t[:, c, :], in0=xi.t[:, c, :], scalar1=ada.t[:, 8 + c:9 + c],
                                                                                 scalar2=ada.t[:, c:c + 1], op0=ALU.mult, op1=ALU.add),
                              reads=[xi, ada], writes=[ho])
                kb.dma(hT_d[:, :, t * 512:(t + 1) * 512].rearrange("c p t -> p c t"), ho.t[:], reads=[ho])
            kb.barrier()

        PHASES = build_program.phases
        eps_ln = kb.sb(es, "eps_ln", [128, 2], F32)
        kb.eps_ln = eps_ln
        kb.op(DVE, lambda: nc.vector.memset(eps_ln.t[:, 0:1], 1e-5), writes=[eps_ln])
        kb.op(DVE, lambda: nc.vector.memset(eps_ln.t[:, 1:2], 1e-6), writes=[eps_ln])
        if "rg" in PHASES:
            phase_rg(nc, kb, S, params, pcol, cm, cmb, load_w_bf16, w_in_d, gate_a_d, gate_x_d, hT_d, recT_d)
        if "gdn" in PHASES:
            phase_gdn(nc, kb, S, params, pcol, cm, cmb, load_w_bf16, w_in_d, rows_d, hrep_d, hT_d, dnT_d)
        if "mix" in PHASES:
            phase_mix(nc, kb, S, load_w_bf16, gt1r, w_in_d, w_pa_d, w_pb_d, w_out_d, rows_d, x_d, hT_d, recT_d, dnT_d, x1_d)
        if "ffn" in PHASES:
            phase_ffn(nc, kb, S, params, pcol, cm, load_w_bf16, ada, gt2r, w_fg_d, w_fu_d, w_fd_d, rows_d, x1_d, out_d)
        kb.final_wait()
    return nc


build_program.phases = ("rg", "gdn", "mix", "ffn")


def gelu_tanh(nc, kb, src_ap, src_bufs, out_ap, out_bufs, tmp, tmp2, mul_ap=None, mul_bufs=()):
    ACT, DVE = kb.act, kb.dve
    shp = tuple(slice(None) for _ in range(len(tmp.t.shape)))
    kb.op(ACT, lambda: nc.scalar.activation(out=tmp.t[:], in_=src_ap, func=AF.Square), reads=src_bufs, writes=[tmp])
    kb.op(DVE, lambda: nc.vector.tensor_scalar(out=tmp.t[:], in0=tmp.t[:], scalar1=0.044715 * GELU_C, scalar2=GELU_C, op0=ALU.mult, op1=ALU.add),
          reads=[tmp], writes=[tmp])
    kb.op(DVE, lambda: nc.vector.tensor_tensor(out=tmp.t[:], in0=tmp.t[:], in1=src_ap, op=ALU.mult), reads=[tmp] + list(src_bufs), writes=[tmp])
    kb.op(ACT, lambda: nc.scalar.activation(out=tmp.t[:], in_=tmp.t[:], func=AF.Sigmoid), reads=[tmp], writes=[tmp])
    if mul_ap is None:
        kb.op(DVE, lambda: nc.vector.tensor_tensor(out=out_ap, in0=tmp.t[:], in1=src_ap, op=ALU.mult), reads=[tmp] + list(src_bufs), writes=out_bufs)
    else:
        kb.op(DVE, lambda: nc.vector.tensor_tensor(out=tmp2.t[:], in0=tmp.t[:], in1=src_ap, op=ALU.mult), reads=[tmp] + list(src_bufs), writes=[tmp2])
        kb.op(DVE, lambda: nc.vector.tensor_tensor(out=out_ap, in0=tmp2.t[:], in1=mul_ap, op=ALU.mult), reads=[tmp2] + list(mul_bufs), writes=out_bufs)


def phase_rg(nc, kb, S, params, pcol, cm, cmb, load_w_bf16, w_in_d, gate_a_d, gate_x_d, hT_d, recT_d):
    PE, ACT, DVE, POOL = kb.pe, kb.act, kb.dve, kb.pool
    TT = 256
    NT = S // TT
    with contextlib.ExitStack() as pes:
        wx = kb.sb(pes, "wx", [128, 8, 2560], BF16)
        wga = kb.sb(pes, "wga", [128, 10, 3, 128], BF16)
        wgx = kb.sb(pes, "wgx", [128, 10, 3, 128], BF16)
        cl = kb.sb(pes, "cl", [128, 20], F32)
        with contextlib.ExitStack() as ses:
            stg = [kb.sb(ses, f"stg{i}", [128, 8, 512], F32) for i in range(2)]
            gst = kb.sb(ses, "gst", [128, 10, 3, 128], F32)
            load_w_bf16(ses, wx, w_in_d, 0, 8, 0, 2560, stg)
            kb.dma(gst.t[:], gate_a_d[:, :, :, :], writes=[gst])
            kb.op(ACT, lambda: nc.scalar.copy(out=wga.t[:], in_=gst.t[:]), reads=[gst], writes=[wga])
            kb.dma(gst.t[:], gate_x_d[:, :, :, :], writes=[gst])
            kb.op(ACT, lambda: nc.scalar.copy(out=wgx.t[:], in_=gst.t[:]), reads=[gst], writes=[wgx])
            kb.barrier(keep=[wx, wga, wgx, cl])
        kb.op(ACT, lambda: nc.scalar.activation(out=cl.t[:, 0:10], in_=pcol("rg_lam", 0, 10), func=AF.Exp, scale=-1.0), reads=[params], writes=[cl])
        kb.op(ACT, lambda: nc.scalar.activation(out=cl.t[:, 0:10], in_=cl.t[:, 0:10], func=AF.Ln, bias=1.0), reads=[cl], writes=[cl])
        kb.op(DVE, lambda: nc.vector.tensor_scalar(out=cl.t[:, 10:20], in0=cl.t[:, 0:10], scalar1=-16.0, scalar2=None, op0=ALU.mult), reads=[cl], writes=[cl])
        kb.op(DVE, lambda: nc.vector.tensor_scalar(out=cl.t[:, 0:10], in0=cl.t[:, 0:10], scalar1=-8.0, scalar2=None, op0=ALU.mult), reads=[cl], writes=[cl])

        hT = [kb.sb(pes, f"hT{i}", [128, 8, TT], BF16) for i in range(2)]
        raw = [kb.sb(pes, f"raw{i}", [128, TT + 3], F32) for i in range(3)]
        halo = kb.sb(pes, "halo", [128, 10, 3], F32)
        hst = kb.sb(pes, "hst", [128, 10], F32)
        xr = kb.sb(pes, "xr", [128, 10, TT], F32)
        xrb = kb.sb(pes, "xrb", [128, 10, TT], BF16)
        ga = kb.sb(pes, "ga", [128, 10, TT], F32)
        gi = kb.sb(pes, "gi", [128, 10, TT], F32)
        mu = kb.sb(pes, "mu", [128, 10, TT], F32)
        hh = kb.sb(pes, "hh", [128, 10, TT], F32)
        tmp = [kb.sb(pes, f"gtmp{i}", [128, TT], F32) for i in range(2)]
        rec = [kb.sb(pes, f"rec{i}", [128, 10, TT], BF16) for i in range(2)]
        pp = [kb.ps(pes, f"pp{i}", [128, 512], F32) for i in range(6)]
        kb.op(DVE, lambda: nc.vector.memset(halo.t[:], 0.0), writes=[halo])
        kb.op(DVE, lambda: nc.vector.memset(hst.t[:], 0.0), writes=[hst])
        cwo = _P["rg_cw"][0]

        def load_h(t):
            kb.dma(hT[t % 2].t[:], hT_d[:, :, t * TT:(t + 1) * TT].rearrange("c p t -> p c t"), writes=[hT[t % 2]])

        load_h(0)
        pi = 0
        for t in range(NT):
            if t + 1 < NT:
                load_h(t + 1)
            h = hT[t % 2]
            for j in range(10):
                p = pp[pi % 6]; pi += 1
                for k in range(8):
                    kb.op(PE, lambda: nc.tensor.matmul(p.t[:, 0:TT], lhsT=wx.t[:, k, j * 128:(j + 1) * 128], rhs=h.t[:, k, :],
                                                       start=(k == 0), stop=(k == 7)), reads=[wx, h], writes=[p], inc=(k == 7))
                rw = raw[j % 3]
                kb.op(ACT, lambda: nc.scalar.copy(out=rw.t[:, 0:3], in_=halo.t[:, j, :]), reads=[halo], writes=[rw])
                kb.op(ACT, lambda: nc.scalar.copy(out=rw.t[:, 3:TT + 3], in_=p.t[:, 0:TT]), reads=[p], writes=[rw])
                kb.op(ACT, lambda: nc.scalar.copy(out=halo.t[:, j, :], in_=rw.t[:, TT:TT + 3]), reads=[rw], writes=[halo])
                kb.op(DVE, lambda: nc.vector.tensor_scalar(out=xr.t[:, j, :], in0=rw.t[:, 3:TT + 3], scalar1=params.t[:, cwo + j * 4 + 3:cwo + j * 4 + 4],
                                                           scalar2=pcol("rg_cb", j), op0=ALU.mult, op1=ALU.add), reads=[rw, params], writes=[xr])
                for kk in range(3):
                    kb.op(DVE, lambda: nc.vector.scalar_tensor_tensor(
                        out=xr.t[:, j, :], in0=rw.t[:, kk:kk + TT], scalar=params.t[:, cwo + j * 4 + kk:cwo + j * 4 + kk + 1], in1=xr.t[:, j, :],
                        op0=ALU.mult, op1=ALU.add), reads=[rw, params, xr], writes=[xr])
                kb.op(ACT, lambda: nc.scalar.copy(out=xrb.t[:, j, :], in_=xr.t[:, j, :]), reads=[xr], writes=[xrb])
            for (wg, dst, bname) in ((wga, ga, "rg_ba"), (wgx, gi, "rg_bx")):
                for m in range(10):
                    p = pp[pi % 6]; pi += 1
                    ks = [k for k in (m - 1, m, m + 1) if 0 <= k < 10]
                    for n_, k in enumerate(ks):
                        kb.op(PE, lambda: nc.tensor.matmul(
                            p.t[:, 0:TT], lhsT=wg.t[:, m, k - m + 1, :], rhs=xrb.t[:, k, :], start=(n_ == 0), stop=(n_ == len(ks) - 1)),
                            reads=[wg, xrb], writes=[p], inc=(n_ == len(ks) - 1))
                    kb.op(ACT, lambda: nc.scalar.activation(out=dst.t[:, m, :], in_=p.t[:, 0:TT], func=AF.Sigmoid, bias=pcol(bname, m)),
                          reads=[p, params], writes=[dst])
            for m in range(10):
                kb.op(ACT, lambda: nc.scalar.activation(out=mu.t[:, m, :], in_=ga.t[:, m, :], func=AF.Exp, scale=cl.t[:, 10 + m:11 + m]), reads=[ga, cl], writes=[mu])
                kb.op(ACT, lambda: nc.scalar.activation(out=ga.t[:, m, :], in_=ga.t[:, m, :], func=AF.Exp, scale=cl.t[:, m:m + 1]), reads=[ga, cl], writes=[ga])
            for m in range(10):
                kb.op(ACT, lambda: nc.scalar.activation(out=mu.t[:, m, :], in_=mu.t[:, m, :], func=AF.Sqrt, scale=-1.0, bias=1.0), reads=[mu], writes=[mu])
            for m in range(10):
                kb.op(POOL, lambda: nc.gpsimd.tensor_tensor(out=gi.t[:, m, :], in0=gi.t[:, m, :], in1=xr.t[:, m, :], op=ALU.mult), reads=[gi, xr], writes=[gi])
                kb.op(DVE, lambda: nc.vector.tensor_tensor(out=gi.t[:, m, :], in0=gi.t[:, m, :], in1=mu.t[:, m, :], op=ALU.mult), reads=[gi, mu], writes=[gi])
                kb.op(DVE, lambda: nc.vector.tensor_tensor_scan(out=hh.t[:, m, :], data0=ga.t[:, m, :], data1=gi.t[:, m, :], initial=hst.t[:, m:m + 1],
                                                                op0=ALU.mult, op1=ALU.add), reads=[ga, gi, hst], writes=[hh])
                kb.op(ACT, lambda: nc.scalar.copy(out=hst.t[:, m:m + 1], in_=hh.t[:, m, TT - 1:TT]), reads=[hh], writes=[hst])
            rc = rec[t % 2]
            for j in range(10):
                p = pp[pi % 6]; pi += 1
                for k in range(8):
                    kb.op(PE, lambda: nc.tensor.matmul(p.t[:, 0:TT], lhsT=wx.t[:, k, 1280 + j * 128:1280 + (j + 1) * 128], rhs=h.t[:, k, :],
                                                       start=(k == 0), stop=(k == 7)), reads=[wx, h], writes=[p], inc=(k == 7))
                gelu_tanh(nc, kb, p.t[:, 0:TT], [p], rc.t[:, j, :], [rc], tmp[0], tmp[1], mul_ap=hh.t[:, j, :], mul_bufs=[hh])
            kb.dma(recT_d[:, :, t * TT:(t + 1) * TT].rearrange("c p t -> p c t"), rc.t[:], reads=[rc])
        kb.barrier()


def layer_norm_rows(nc, kb, y, nblk, g_ap, b_ap, rowbuf, stats, mv, sd, out_buf):
    ACT, DVE, POOL = kb.act, kb.dve, kb.pool
    for blk in range(nblk):
        for hf in range(2):
            kb.op(DVE, lambda: nc.vector.bn_stats(out=stats.t[:, blk, hf, :], in_=y.t[:, blk, hf * 512:(hf + 1) * 512]), reads=[y], writes=[stats])
        kb.op(DVE, lambda: nc.vector.bn_aggr(out=mv.t[:, blk, :], in_=stats.t[:, blk, :, :].rearrange("p a b -> p (a b)")), reads=[stats], writes=[mv])
        kb.op(ACT, lambda: nc.scalar.activation(out=sd.t[:, blk:blk + 1], in_=mv.t[:, blk, 1:2], func=AF.Ln, bias=kb.eps_ln.t[:, 0:1]), reads=[mv, kb.eps_ln], writes=[sd])
        kb.op(ACT, lambda: nc.scalar.activation(out=sd.t[:, blk:blk + 1], in_=sd.t[:, blk:blk + 1], func=AF.Exp, scale=-0.5), reads=[sd], writes=[sd])
        kb.op(DVE, lambda: nc.vector.tensor_scalar(out=y.t[:, blk, :], in0=y.t[:, blk, :], scalar1=mv.t[:, blk, 0:1], scalar2=sd.t[:, blk:blk + 1],
                                                   op0=ALU.subtract, op1=ALU.mult), reads=[y, mv, sd], writes=[y])
        kb.op(POOL, lambda: nc.gpsimd.tensor_tensor(out=y.t[:, blk, :], in0=y.t[:, blk, :], in1=g_ap, op=ALU.mult), reads=[y, rowbuf], writes=[y])
        kb.op(DVE, lambda: nc.vector.tensor_tensor(out=out_buf.t[:, blk, :], in0=y.t[:, blk, :], in1=b_ap, op=ALU.add), reads=[y, rowbuf], writes=[out_buf])


def phase_mix(nc, kb, S, load_w_bf16, gt1r, w_in_d, w_pa_d, w_pb_d, w_out_d, rows_d, x_d, hT_d, recT_d, dnT_d, x1_d):
    PE, ACT, DVE, POOL = kb.pe, kb.act, kb.dve, kb.pool
    TT = 256
    NT = S // TT
    NB = TT // 128
    with contextlib.ExitStack() as pes:
        wg = kb.sb(pes, "wg", [128, 8, 2048], BF16)
        wpa = kb.sb(pes, "wpa", [128, 10, 1024], BF16)
        wpb = kb.sb(pes, "wpb", [128, 16, 1024], BF16)
        wo = kb.sb(pes, "wo", [128, 8, 1024], BF16)
        lnr = kb.sb(pes, "lnr", [128, 2048], F32)
        with contextlib.ExitStack() as ses:
            stg = [kb.sb(ses, f"stg{i}", [128, 16, 256], F32) for i in range(2)]
            load_w_bf16(ses, wg, w_in_d, 0, 8, C_GA, 2048, stg)
            load_w_bf16(ses, wpa, w_pa_d, 0, 10, 0, 1024, stg)
            load_w_bf16(ses, wpb, w_pb_d, 0, 16, 0, 1024, stg)
            load_w_bf16(ses, wo, w_out_d, 0, 8, 0, 1024, stg)
            kb.dma(lnr.t[:], rows_d[:, _R["ln1_g"][0]:_R["ln1_g"][0] + 2048], writes=[lnr])
            kb.barrier(keep=[wg, wpa, wpb, wo, lnr])
        hT = [kb.sb(pes, f"hT{i}", [128, 8, TT], BF16) for i in range(2)]
        rcT = [kb.sb(pes, f"rcT{i}", [128, 10, TT], BF16) for i in range(2)]
        dnT = [kb.sb(pes, f"dnT{i}", [128, 16, TT], BF16) for i in range(2)]
        xt = [kb.sb(pes, f"xt{i}", [128, NB, 1024], F32) for i in range(2)]
        mg = kb.sb(pes, "mg", [128, 8, TT], BF16)
        sg = [kb.sb(pes, f"sg{i}", [128, TT], F32) for i in range(2)]
        t1 = [kb.sb(pes, f"t1{i}", [128, TT], F32) for i in range(2)]
        y = kb.sb(pes, "y", [128, NB, 1024], F32)
        xo = [y, y]
        stats = kb.sb(pes, "stats", [128, NB, 2, 6], F32)
        mv = kb.sb(pes, "mv", [128, NB, 2], F32)
        sd = kb.sb(pes, "sd", [128, NB], F32)
        pp = [kb.ps(pes, f"pp{i}", [128, 512], F32) for i in range(8)]

        def load(t):
            i = t % 2
            sl = slice(t * TT, (t + 1) * TT)
            kb.dma(hT[i].t[:], hT_d[:, :, sl].rearrange("c p t -> p c t"), writes=[hT[i]])
            kb.dma(rcT[i].t[:], recT_d[:, :, sl].rearrange("c p t -> p c t"), writes=[rcT[i]])
            kb.dma(dnT[i].t[:], dnT_d[:, :, sl].rearrange("c p t -> p c t"), writes=[dnT[i]])
            kb.dma(xt[i].t[:], x_d[sl, :].rearrange("(b p) f -> p b f", p=128), writes=[xt[i]])

        load(0)
        pi = 0
        for t in range(NT):
            if t + 1 < NT:
                load(t + 1)
            i = t % 2
            h, rc, dn, xx = hT[i], rcT[i], dnT[i], xt[i]
            for m in range(8):
                ms = slice(m * 128, (m + 1) * 128)
                pga = pp[pi % 8]; pya = pp[(pi + 1) % 8]; pgb = pp[(pi + 2) % 8]; pyb = pp[(pi + 3) % 8]; pi += 4
                for k in range(8):
                    kb.op(PE, lambda: nc.tensor.matmul(pga.t[:, 0:TT], lhsT=wg.t[:, k, m * 128:(m + 1) * 128], rhs=h.t[:, k, :], start=(k == 0), stop=(k == 7)),
                          reads=[wg, h], writes=[pga], inc=(k == 7))
                for k in range(10):
                    kb.op(PE, lambda: nc.tensor.matmul(pya.t[:, 0:TT], lhsT=wpa.t[:, k, ms], rhs=rc.t[:, k, :], start=(k == 0), stop=(k == 9)),
                          reads=[wpa, rc], writes=[pya], inc=(k == 9))
                for k in range(8):
                    kb.op(PE, lambda: nc.tensor.matmul(pgb.t[:, 0:TT], lhsT=wg.t[:, k, 1024 + m * 128:1024 + (m + 1) * 128], rhs=h.t[:, k, :], start=(k == 0), stop=(k == 7)),
                          reads=[wg, h], writes=[pgb], inc=(k == 7))
                for k in range(16):
                    kb.op(PE, lambda: nc.tensor.matmul(pyb.t[:, 0:TT], lhsT=wpb.t[:, k, ms], rhs=dn.t[:, k, :], start=(k == 0), stop=(k == 15)),
                          reads=[wpb, dn], writes=[pyb], inc=(k == 15))
                s0, s1, ta, tb = sg[0], sg[1], t1[0], t1[1]
                kb.op(ACT, lambda: nc.scalar.activation(out=s0.t[:], in_=pga.t[:, 0:TT], func=AF.Sigmoid), reads=[pga], writes=[s0])
                kb.op(ACT, lambda: nc.scalar.activation(out=s1.t[:], in_=pgb.t[:, 0:TT], func=AF.Sigmoid), reads=[pgb], writes=[s1])
                kb.op(DVE, lambda: nc.vector.tensor_tensor(out=ta.t[:], in0=s0.t[:], in1=pya.t[:, 0:TT], op=ALU.mult), reads=[s0, pya], writes=[ta])
                kb.op(DVE, lambda: nc.vector.tensor_tensor(out=tb.t[:], in0=s1.t[:], in1=pyb.t[:, 0:TT], op=ALU.mult), reads=[s1, pyb], writes=[tb])
                kb.op(POOL, lambda: nc.gpsimd.tensor_tensor(out=mg.t[:, m, :], in0=ta.t[:], in1=tb.t[:], op=ALU.add), reads=[ta, tb], writes=[mg])
            for blk in range(NB):
                for cb in range(2):
                    p = pp[pi % 8]; pi += 1
                    cs = slice(cb * 512, (cb + 1) * 512)
                    for k in range(8):
                        kb.op(PE, lambda: nc.tensor.matmul(p.t[:], lhsT=mg.t[:, k, blk * 128:(blk + 1) * 128], rhs=wo.t[:, k, cs], start=(k == 0), stop=(k == 7)),
                              reads=[mg, wo], writes=[p], inc=(k == 7))
                    kb.op(DVE, lambda: nc.vector.tensor_tensor(out=y.t[:, blk, cs], in0=p.t[:], in1=gt1r.t[:, cs], op=ALU.mult), reads=[p, gt1r], writes=[y])
                    kb.op(DVE, lambda: nc.vector.scalar_tensor_tensor(out=y.t[:, blk, cs], in0=xx.t[:, blk, cs], scalar=ALPHA, in1=y.t[:, blk, cs],
                                                                      op0=ALU.mult, op1=ALU.add), reads=[xx, y], writes=[y])
            o = xo[i]
            layer_norm_rows(nc, kb, y, NB, lnr.t[:, 0:1024], lnr.t[:, 1024:2048], lnr, stats, mv, sd, o)
            kb.dma(x1_d[t * TT:(t + 1) * TT, :].rearrange("(b p) f -> p b f", p=128), o.t[:], reads=[o])
        kb.barrier()


def phase_ffn(nc, kb, S, params, pcol, cm, load_w_bf16, ada, gt2r, w_fg_d, w_fu_d, w_fd_d, rows_d, x1_d, out_d):
    PE, ACT, DVE, POOL = kb.pe, kb.act, kb.dve, kb.pool
    TT = 256
    NT = S // TT
    NB = TT // 128
    ident_f = cm.t[:, CM_IDENT * 128:(CM_IDENT + 1) * 128]
    with contextlib.ExitStack() as pes:
        wfg = kb.sb(pes, "wfg", [128, 8, DFF], BF16)
        wfu = kb.sb(pes, "wfu", [128, 8, DFF], BF16)
        wfd = kb.sb(pes, "wfd", [128, 22, 1024], BF16)
        lnr = kb.sb(pes, "lnr", [128, 2048], F32)
        with contextlib.ExitStack() as ses:
            stg = [kb.sb(ses, f"stg{i}", [128, 22, 128], F32) for i in range(2)]
            load_w_bf16(ses, wfg, w_fg_d, 0, 8, 0, DFF, stg)
            load_w_bf16(ses, wfu, w_fu_d, 0, 8, 0, DFF, stg)
            load_w_bf16(ses, wfd, w_fd_d, 0, 22, 0, 1024, stg)
            kb.dma(lnr.t[:], rows_d[:, _R["ln2_g"][0]:_R["ln2_g"][0] + 2048], writes=[lnr])
            kb.barrier(keep=[wfg, wfu, wfd, lnr])
        xt = [kb.sb(pes, f"xt{i}", [128, NB, 1024], F32) for i in range(2)]
        h2 = kb.sb(pes, "h2", [128, 8, TT], BF16)
        act = kb.sb(pes, "act", [128, 22, TT], BF16)
        raw = [kb.sb(pes, f"raw{i}", [128, TT + 2], F32) for i in range(3)]
        cv = [kb.sb(pes, f"cv{i}", [128, TT], F32) for i in range(2)]
        tmp = [kb.sb(pes, f"tmp{i}", [128, TT], F32) for i in range(2)]
        halo = kb.sb(pes, "halo", [128, 22, 2], F32)
        y = kb.sb(pes, "y", [128, NB, 1024], F32)
        xo = [y, y]
        stats = kb.sb(pes, "stats", [128, NB, 2, 6], F32)
        mv = kb.sb(pes, "mv", [128, NB, 2], F32)
        sd = kb.sb(pes, "sd", [128, NB], F32)
        pp = [kb.ps(pes, f"pp{i}", [128, 512], F32) for i in range(8)]
        kb.op(DVE, lambda: nc.vector.memset(halo.t[:], 0.0), writes=[halo])
        cwo = _P["ffn_cw"][0]

        def load(t):
            kb.dma(xt[t % 2].t[:], x1_d[t * TT:(t + 1) * TT, :].rearrange("(b p) f -> p b f", p=128), writes=[xt[t % 2]])

        load(0)
        pi = 0
        for t in range(NT):
            if t + 1 < NT:
                load(t + 1)
            xx = xt[t % 2]
            for blk in range(NB):
                for c in range(8):
                    p = pp[pi % 8]; pi += 1
                    kb.op(PE, lambda: nc.tensor.matmul(p.t[:, 0:128], lhsT=xx.t[:, blk, c * 128:(c + 1) * 128], rhs=ident_f, start=True, stop=True), reads=[xx, cm], writes=[p])
                    kb.op(ACT, lambda: nc.scalar.activation(out=h2.t[:, c, blk * 128:(blk + 1) * 128], in_=p.t[:, 0:128], func=AF.Identity,
                                                            scale=ada.t[:, 32 + c:33 + c], bias=ada.t[:, 24 + c:25 + c]), reads=[p, ada], writes=[h2])
            for m in range(22):
                ms = slice(m * 128, (m + 1) * 128)
                pg = pp[pi % 8]; pu = pp[(pi + 1) % 8]; pi += 2
                for k in range(8):
                    kb.op(PE, lambda: nc.tensor.matmul(pg.t[:, 0:TT], lhsT=wfg.t[:, k, ms], rhs=h2.t[:, k, :], start=(k == 0), stop=(k == 7)),
                          reads=[wfg, h2], writes=[pg], inc=(k == 7))
                for k in range(8):
                    kb.op(PE, lambda: nc.tensor.matmul(pu.t[:, 0:TT], lhsT=wfu.t[:, k, ms], rhs=h2.t[:, k, :], start=(k == 0), stop=(k == 7)),
                          reads=[wfu, h2], writes=[pu], inc=(k == 7))
                rw = raw[m % 3]
                c_ = cv[m % 2]
                kb.op(ACT, lambda: nc.scalar.copy(out=rw.t[:, 0:2], in_=halo.t[:, m, :]), reads=[halo], writes=[rw])
                kb.op(ACT, lambda: nc.scalar.copy(out=rw.t[:, 2:TT + 2], in_=pg.t[:, 0:TT]), reads=[pg], writes=[rw])
                kb.op(ACT, lambda: nc.scalar.copy(out=halo.t[:, m, :], in_=rw.t[:, TT:TT + 2]), reads=[rw], writes=[halo])
                kb.op(DVE, lambda: nc.vector.tensor_scalar(out=c_.t[:], in0=rw.t[:, 2:TT + 2], scalar1=params.t[:, cwo + m * 3 + 2:cwo + m * 3 + 3],
                                                           scalar2=pcol("ffn_cb", m), op0=ALU.mult, op1=ALU.add), reads=[rw, params], writes=[c_])
                for kk in range(2):
                    kb.op(DVE, lambda: nc.vector.scalar_tensor_tensor(out=c_.t[:], in0=rw.t[:, kk:kk + TT], scalar=params.t[:, cwo + m * 3 + kk:cwo + m * 3 + kk + 1],
                                                                      in1=c_.t[:], op0=ALU.mult, op1=ALU.add), reads=[rw, params, c_], writes=[c_])
                gelu_tanh(nc, kb, c_.t[:], [c_], act.t[:, m, :], [act], tmp[0], tmp[1], mul_ap=pu.t[:, 0:TT], mul_bufs=[pu])
            for blk in range(NB):
                for cb in range(2):
                    p = pp[pi % 8]; pi += 1
                    cs = slice(cb * 512, (cb + 1) * 512)
                    for k in range(22):
                        kb.op(PE, lambda: nc.tensor.matmul(p.t[:], lhsT=act.t[:, k, blk * 128:(blk + 1) * 128], rhs=wfd.t[:, k, cs], start=(k == 0), stop=(k == 21)),
                              reads=[act, wfd], writes=[p], inc=(k == 21))
                    kb.op(DVE, lambda: nc.vector.tensor_tensor(out=y.t[:, blk, cs], in0=p.t[:], in1=gt2r.t[:, cs], op=ALU.mult), reads=[p, gt2r], writes=[y])
                    kb.op(DVE, lambda: nc.vector.scalar_tensor_tensor(out=y.t[:, blk, cs], in0=xx.t[:, blk, cs], scalar=ALPHA, in1=y.t[:, blk, cs],
                                                                      op0=ALU.mult, op1=ALU.add), reads=[xx, y], writes=[y])
            o = xo[t % 2]
            layer_norm_rows(nc, kb, y, NB, lnr.t[:, 0:1024], lnr.t[:, 1024:2048], lnr, stats, mv, sd, o)
            kb.dma(out_d[t * TT:(t + 1) * TT, :].rearrange("(b p) f -> p b f", p=128), o.t[:], reads=[o])
        kb.barrier()


class PSlot:
    def __init__(self, kb, bank, i, name):
        self.b = bank
        self.ap = bank.t[:, i * 128:(i + 1) * 128]
        self.apb = self.ap.bitcast(BF16)[:, 0:128]


def phase_gdn(nc, kb, S, params, pcol, cm, cmb, load_w_bf16, w_in_d, rows_d, hrep_d, hT_d, dnT_d):
    PE, ACT, DVE, POOL = kb.pe, kb.act, kb.dve, kb.pool
    NCH = S // 128
    NT = S // 512
    HVG, HQG = 4, 2
    NG = 16 // HVG
    NCOL = NCH * 16
    cf = lambda i: cm.t[:, i * 128:(i + 1) * 128]
    cb = lambda i: cmb.t[:, i * 128:(i + 1) * 128]
    ident_f, ident_b, L_f, U_f, ones_f, ones_b = cf(CM_IDENT), cb(CM_IDENT), cf(CM_L), cf(CM_U), cf(CM_ONES), cb(CM_ONES)
    mS_b, mIT_b, BD_f, OFF_f = cb(CM_MS), cb(CM_MIT), cf(CM_BD), cf(CM_OFF)
    eps6 = kb.eps_ln.t[:, 1:2]
    with contextlib.ExitStack() as pes:
        tm = {n: kb.sb(pes, "tm_" + n, [128, NCH, 16], F32) for n in ("beta", "negg", "eG", "beG", "kds", "gl")}
        flat = lambda bf: bf.t[:].rearrange("p n h -> p (n h)")
        with contextlib.ExitStack() as ses:
            wab = kb.sb(ses, "wab", [128, 8, 32], BF16)
            stg = [kb.sb(ses, "stg0", [128, 8, 32], F32)]
            hT = [kb.sb(ses, f"hT{i}", [128, 8, 512], BF16) for i in range(2)]
            ab = kb.sb(ses, "ab", [128, NCH, 32], F32)
            hrep = kb.sb(ses, "hrep", [128, 2, NCOL], F32)
            X = kb.sb(ses, "X", [128, NCH, 16], F32)
            Y = kb.sb(ses, "Y", [128, NCH, 16], F32)
            Z = kb.sb(ses, "Z", [128, NCH, 16], F32)
            pp = [kb.ps(ses, f"pp{i}", [128, 512], F32) for i in range(4)]
            load_w_bf16(ses, wab, w_in_d, 0, 8, C_A, 32, stg)
            kb.dma(hrep.t[:], hrep_d[:, :, :], writes=[hrep])
            kb.dma(hT[0].t[:], hT_d[:, :, 0:512].rearrange("c p t -> p c t"), writes=[hT[0]])
            for t in range(NT):
                if t + 1 < NT:
                    kb.dma(hT[(t + 1) % 2].t[:], hT_d[:, :, (t + 1) * 512:(t + 2) * 512].rearrange("c p t -> p c t"), writes=[hT[(t + 1) % 2]])
                h = hT[t % 2]
                p = pp[t % 2]
                for blk in range(4):
                    for k in range(8):
                        kb.op(PE, lambda: nc.tensor.matmul(p.t[:, blk * 32:(blk + 1) * 32], lhsT=h.t[:, k, blk * 128:(blk + 1) * 128], rhs=wab.t[:, k, :],
                                                           start=(k == 0), stop=(k == 7)), reads=[h, wab], writes=[p], inc=(k == 7))
                kb.op(ACT, lambda: nc.scalar.copy(out=ab.t[:, t * 4:(t + 1) * 4, :], in_=p.t[:, 0:128].rearrange("p (b c) -> p b c", c=32)), reads=[p], writes=[ab])
            a_v, b_v = ab.t[:, :, 0:16], ab.t[:, :, 16:32]
            dtb = hrep.t[:, 0, :].rearrange("p (n h) -> p n h", h=16)
            alog = hrep.t[:, 1, :].rearrange("p (n h) -> p n h", h=16)
            kb.op(ACT, lambda: nc.scalar.activation(out=tm["beta"].t[:], in_=b_v, func=AF.Sigmoid), reads=[ab], writes=[tm["beta"]])
            kb.op(DVE, lambda: nc.vector.tensor_tensor(out=X.t[:], in0=a_v, in1=dtb, op=ALU.add), reads=[ab, hrep], writes=[X])
            kb.op(DVE, lambda: nc.vector.tensor_scalar(out=Y.t[:], in0=X.t[:], scalar1=-1.0, scalar2=None, op0=ALU.mult), reads=[X], writes=[Y])
            kb.op(DVE, lambda: nc.vector.tensor_tensor(out=Y.t[:], in0=Y.t[:], in1=X.t[:], op=ALU.max), reads=[X, Y], writes=[Y])
            kb.op(ACT, lambda: nc.scalar.activation(out=Y.t[:], in_=Y.t[:], func=AF.Exp, scale=-1.0), reads=[Y], writes=[Y])
            kb.op(ACT, lambda: nc.scalar.activation(out=Y.t[:], in_=Y.t[:], func=AF.Ln, bias=1.0), reads=[Y], writes=[Y])
            kb.op(DVE, lambda: nc.vector.tensor_scalar(out=X.t[:], in0=X.t[:], scalar1=0.0, scalar2=None, op0=ALU.max), reads=[X], writes=[X])
            kb.op(DVE, lambda: nc.vector.tensor_tensor(out=X.t[:], in0=X.t[:], in1=Y.t[:], op=ALU.add), reads=[X, Y], writes=[X])
            kb.op(ACT, lambda: nc.scalar.activation(out=Z.t[:], in_=alog, func=AF.Exp), reads=[hrep], writes=[Z])
            kb.op(DVE, lambda: nc.vector.tensor_tensor(out=tm["negg"].t[:], in0=X.t[:], in1=Z.t[:], op=ALU.mult), reads=[X, Z], writes=[tm["negg"]])
            ng_f = flat(tm["negg"])
            for c0 in range(0, NCOL, 512):
                w = min(512, NCOL - c0)
                p1, p2 = pp[2], pp[3]
                kb.op(PE, lambda: nc.tensor.matmul(p1.t[:, 0:w], lhsT=L_f, rhs=ng_f[:, c0:c0 + w], start=True, stop=True), reads=[cm, tm["negg"]], writes=[p1])
                kb.op(PE, lambda: nc.tensor.matmul(p2.t[:, 0:w], lhsT=ones_f, rhs=ng_f[:, c0:c0 + w], start=True, stop=True), reads=[cm, tm["negg"]], writes=[p2])
                Xf, Yf = flat(X), flat(Y)
                kb.op(ACT, lambda: nc.scalar.copy(out=Xf[:, c0:c0 + w], in_=p1.t[:, 0:w]), reads=[p1], writes=[X])
                kb.op(ACT, lambda: nc.scalar.activation(out=flat(tm["eG"])[:, c0:c0 + w], in_=p1.t[:, 0:w], func=AF.Exp, scale=-1.0), reads=[p1], writes=[tm["eG"]])
                kb.op(ACT, lambda: nc.scalar.activation(out=flat(tm["gl"])[:, c0:c0 + w], in_=p2.t[:, 0:w], func=AF.Exp, scale=-1.0), reads=[p2], writes=[tm["gl"]])
                kb.op(DVE, lambda: nc.vector.tensor_tensor(out=Yf[:, c0:c0 + w], in0=p2.t[:, 0:w], in1=Xf[:, c0:c0 + w], op=ALU.subtract), reads=[p2, X], writes=[Y])
                kb.op(ACT, lambda: nc.scalar.activation(out=flat(tm["kds"])[:, c0:c0 + w], in_=Yf[:, c0:c0 + w], func=AF.Exp, scale=-1.0), reads=[Y], writes=[tm["kds"]])
            kb.op(DVE, lambda: nc.vector.tensor_tensor(out=tm["beG"].t[:], in0=tm["beta"].t[:], in1=tm["eG"].t[:], op=ALU.mult), reads=[tm["beta"], tm["eG"]], writes=[tm["beG"]])
            kb.barrier()

        STOP = getattr(build_program, "gdn_stop", 0)
        for gi in range(NG if STOP != 1 else 0):
            hv0, hq0 = gi * HVG, gi * HQG
            with contextlib.ExitStack() as ges:
                wq = kb.sb(ges, "wq", [128, 8, HQG * 128], BF16)
                wk = kb.sb(ges, "wk", [128, 8, HQG * 128], BF16)
                wv = kb.sb(ges, "wv", [128, 8, HVG * 128], BF16)
                wz = kb.sb(ges, "wz", [128, 8, HVG * 128], BF16)
                nwr = kb.sb(ges, "nwr", [128, HVG * 128], F32)
                with contextlib.ExitStack() as ses:
                    stg = [kb.sb(ses, f"stg{i}", [128, 8, 512], F32) for i in range(2)]
                    load_w_bf16(ses, wq, w_in_d, 0, 8, C_Q + hq0 * 128, HQG * 128, stg)
                    load_w_bf16(ses, wk, w_in_d, 0, 8, C_K + hq0 * 128, HQG * 128, stg)
                    load_w_bf16(ses, wv, w_in_d, 0, 8, C_V + hv0 * 128, HVG * 128, stg)
                    load_w_bf16(ses, wz, w_in_d, 0, 8, C_Z + hv0 * 128, HVG * 128, stg)
                    kb.dma(nwr.t[:], rows_d[:, _R["nw"][0]:_R["nw"][0] + HVG * 128], writes=[nwr])
                    kb.barrier()
                hT = [kb.sb(ges, f"hT{i}", [128, 8, 512], BF16) for i in range(2)]
                raw = [kb.sb(ges, f"raw{i}", [128, 515], F32) for i in range(3)]
                cv = [kb.sb(ges, f"cv{i}", [128, 512], F32) for i in range(2)]
                NCK = 2 * HQG + HVG
                halo = kb.sb(ges, "halo", [128, NCK, 3], F32)
                qkf = [kb.sb(ges, f"qkf{i}", [128, 512], F32) for i in range(2)]
                sqb = [kb.sb(ges, f"sqb{i}", [128, 512], BF16) for i in range(2)]
                rn = [kb.sb(ges, f"rn{i}", [128, 512], F32) for i in range(2)]
                qT2 = [kb.sb(ges, f"qT{i}", [128, HQG, 512], BF16) for i in range(2)]
                kT2 = [kb.sb(ges, f"kT{i}", [128, HQG, 512], BF16) for i in range(2)]
                vT2 = [kb.sb(ges, f"vT{i}", [128, HVG, 512], BF16) for i in range(2)]
                zs = [kb.sb(ges, f"zs{i}", [128, 512], F32) for i in range(2)]
                zg2 = [kb.sb(ges, f"zg{i}", [128, 4, HVG * 128], F32) for i in range(2)]
                dno = [kb.sb(ges, f"dno{i}", [128, HVG, 512], BF16) for i in range(2)]
                Sf = [kb.sb(ges, f"Sf{i}", [128, 128], F32) for i in range(HVG)]
                Sb = [kb.sb(ges, f"Sb{i}", [128, 128], BF16) for i in range(HVG)]
                ob = kb.sb(ges, "ob", [128, HVG, 128], F32)
                junk = kb.sb(ges, "junk", [128, 128], F32)
                ss = kb.sb(ges, "ss", [128, HVG], F32)
                rstd = kb.sb(ges, "rstd", [128, HVG], F32)
                ktm = [kb.sb(ges, f"ktm{i}", [128, 128], BF16) for i in range(HQG)]
                KKbd = [kb.sb(ges, f"KKbd{i}", [128, 128], F32) for i in range(HQG)]
                KKoff = [kb.sb(ges, f"KKoff{i}", [128, 128], F32) for i in range(HQG)]
                QKT = [kb.sb(ges, f"QKT{i}", [128, 128], F32) for i in range(HQG)]

                class PB:
                    pass
                pb = []
                for hi in range(HVG):
                    B = PB()
                    for n_ in ("rh", "Ds", "DT", "T1", "Mf", "o1"):
                        setattr(B, n_, kb.sb(ges, f"{n_}_{hi}", [128, 128], F32))
                    for n_ in ("Abd", "Aoff", "P0", "P1", "Q0", "Q1", "M0", "M1", "Y1", "Tb", "bek", "vn", "dntm"):
                        setattr(B, n_, kb.sb(ges, f"{n_}_{hi}", [128, 128], BF16))
                    for n_ in ("QKd", "M128", "bv", "kdec", "nwT"):
                        setattr(B, n_, [kb.sb(ges, f"{n_}{i}_{hi}", [128, 128], BF16) for i in range(2)])
                    B.P = [B.P0, B.P1]; B.Q = [B.Q0, B.Q1]; B.M = [B.M0, B.M1]
                    pb.append(B)
                banks = [kb.ps(ges, f"bk{i}", [128, 512], F32) for i in range(8)]
                slots = [[PSlot(kb, banks[i], j, f"sl{i}_{j}") for j in range(4)] for i in range(8)]
                hsA = [slots[2 * hi] for hi in range(HVG)]
                hsB = [slots[2 * hi + 1] for hi in range(HVG)]

                def bank_bufs(i):
                    return [banks[i]]

                for b_ in Sf + [halo]:
                    kb.op(DVE, lambda: nc.vector.memset(b_.t[:], 0.0), writes=[b_])
                for b_ in Sb:
                    kb.op(DVE, lambda: nc.vector.memset(b_.t[:], 0.0), writes=[b_])
                cwo = _P["dn_cw"][0]
                chunks = [("q", wq, c, hq0 + c) for c in range(HQG)] + [("k", wk, c, 8 + hq0 + c) for c in range(HQG)] + \
                         [("v", wv, c, 16 + hv0 + c) for c in range(HVG)]

                def load_h(t):
                    kb.dma(hT[t % 2].t[:], hT_d[:, :, t * 512:(t + 1) * 512].rearrange("c p t -> p c t"), writes=[hT[t % 2]])

                bi_ = [0]

                def stageA(t):
                    h = hT[t % 2]
                    qT, kT, vT, zg = qT2[t % 2], kT2[t % 2], vT2[t % 2], zg2[t % 2]
                    for ci, (kind, w, c, gch) in enumerate(chunks):
                        bk = bi_[0] % 8; bi_[0] += 1
                        pt, pbufs = banks[bk].t, bank_bufs(bk)
                        for k in range(8):
                            kb.op(PE, lambda: nc.tensor.matmul(pt[:], lhsT=w.t[:, k, c * 128:(c + 1) * 128], rhs=h.t[:, k, :], start=(k == 0), stop=(k == 7)),
                                  reads=[w, h], writes=pbufs, inc=(k == 7))
                        rw, cvb = raw[ci % 3], cv[ci % 2]
                        kb.op(ACT, lambda: nc.scalar.copy(out=rw.t[:, 0:3], in_=halo.t[:, ci, :]), reads=[halo], writes=[rw])
                        kb.op(ACT, lambda: nc.scalar.copy(out=rw.t[:, 3:515], in_=pt[:]), reads=pbufs, writes=[rw])
                        kb.op(ACT, lambda: nc.scalar.copy(out=halo.t[:, ci, :], in_=rw.t[:, 512:515]), reads=[rw], writes=[halo])
                        kb.op(DVE, lambda: nc.vector.tensor_scalar(out=cvb.t[:], in0=rw.t[:, 3:515], scalar1=params.t[:, cwo + gch * 4 + 3:cwo + gch * 4 + 4],
                                                                   scalar2=None, op0=ALU.mult), reads=[rw, params], writes=[cvb])
                        for kk in range(3):
                            kb.op(DVE, lambda: nc.vector.scalar_tensor_tensor(out=cvb.t[:], in0=rw.t[:, kk:kk + 512], scalar=params.t[:, cwo + gch * 4 + kk:cwo + gch * 4 + kk + 1],
                                                                              in1=cvb.t[:], op0=ALU.mult, op1=ALU.add), reads=[rw, params, cvb], writes=[cvb])
                        if kind == "v":
                            kb.op(ACT, lambda: nc.scalar.activation(out=vT.t[:, c, :], in_=cvb.t[:], func=AF.Silu), reads=[cvb], writes=[vT])
                        else:
                            f, sq, r_ = qkf[ci % 2], sqb[ci % 2], rn[ci % 2]
                            dst = qT if kind == "q" else kT
                            kb.op(ACT, lambda: nc.scalar.activation(out=f.t[:], in_=cvb.t[:], func=AF.Silu), reads=[cvb], writes=[f])
                            kb.op(POOL, lambda: nc.gpsimd.tensor_tensor(out=sq.t[:], in0=f.t[:], in1=f.t[:], op=ALU.mult), reads=[f], writes=[sq])
                            bk2 = bi_[0] % 8; bi_[0] += 1
                            pt2, pbufs2 = banks[bk2].t, bank_bufs(bk2)
                            kb.op(PE, lambda: nc.tensor.matmul(pt2[:], lhsT=ones_b, rhs=sq.t[:], start=True, stop=True), reads=[cmb, sq], writes=pbufs2)
                            kb.op(ACT, lambda: nc.scalar.activation(out=r_.t[:], in_=pt2[:], func=AF.Ln, bias=eps6), reads=pbufs2 + [kb.eps_ln], writes=[r_])
                            kb.op(ACT, lambda: nc.scalar.activation(out=r_.t[:], in_=r_.t[:], func=AF.Exp, scale=-0.5), reads=[r_], writes=[r_])
                            kb.op(DVE, lambda: nc.vector.scalar_tensor_tensor(out=dst.t[:, c, :], in0=f.t[:], scalar=(128.0 ** -0.5 if kind == "q" else 1.0), in1=r_.t[:],
                                                                              op0=ALU.mult, op1=ALU.mult), reads=[f, r_], writes=[dst])
                    for blk in range(4):
                        bk = bi_[0] % 8; bi_[0] += 1
                        pt, pbufs = banks[bk].t, bank_bufs(bk)
                        for k in range(8):
                            kb.op(PE, lambda: nc.tensor.matmul(pt[:, 0:HVG * 128], lhsT=h.t[:, k, blk * 128:(blk + 1) * 128], rhs=wz.t[:, k, :], start=(k == 0), stop=(k == 7)),
                                  reads=[h, wz], writes=pbufs, inc=(k == 7))
                        z_ = zs[blk % 2]
                        kb.op(ACT, lambda: nc.scalar.activation(out=z_.t[:, 0:HVG * 128], in_=pt[:, 0:HVG * 128], func=AF.Silu), reads=pbufs, writes=[z_])
                        kb.op(POOL, lambda: nc.gpsimd.tensor_tensor(out=zg.t[:, blk, :], in0=z_.t[:, 0:HVG * 128], in1=nwr.t[:], op=ALU.mult), reads=[z_, nwr], writes=[zg])

                def hq_ops(n):
                    t, blk = n // 4, n % 4
                    cs = slice(blk * 128, (blk + 1) * 128)
                    qT, kT = qT2[t % 2], kT2[t % 2]
                    for hq in range(HQG):
                        kc, qc = kT.t[:, hq, cs], qT.t[:, hq, cs]
                        s0, s1, s2 = hsA[2 * hq][0], hsA[2 * hq][1], hsA[2 * hq][2]
                        kb.op(PE, lambda: nc.tensor.transpose(s0.apb, kc, ident_b), reads=[kT, cmb], writes=[s0.b])
                        kb.op(PE, lambda: nc.tensor.matmul(s1.ap, lhsT=kc, rhs=kc, start=True, stop=True), reads=[kT], writes=[s1.b])
                        kb.op(PE, lambda: nc.tensor.matmul(s2.ap, lhsT=kc, rhs=qc, start=True, stop=True), reads=[kT, qT], writes=[s2.b])
                        kb.op(ACT, lambda: nc.scalar.copy(out=ktm[hq].t[:], in_=s0.apb), reads=[s0.b], writes=[ktm[hq]])
                        kb.op(DVE, lambda: nc.vector.tensor_tensor(out=KKbd[hq].t[:], in0=s1.ap, in1=BD_f, op=ALU.mult), reads=[s1.b, cm], writes=[KKbd[hq]])
                        kb.op(DVE, lambda: nc.vector.tensor_tensor(out=KKoff[hq].t[:], in0=s1.ap, in1=OFF_f, op=ALU.mult), reads=[s1.b, cm], writes=[KKoff[hq]])
                        kb.op(ACT, lambda: nc.scalar.copy(out=QKT[hq].t[:], in_=s2.ap), reads=[s2.b], writes=[QKT[hq]])

                def pre_seq(hi, n):
                    t, blk = n // 4, n % 4
                    cs = slice(blk * 128, (blk + 1) * 128)
                    vT = vT2[t % 2]
                    pr = n % 2
                    hh_ = hv0 + hi
                    hq = hi // 2
                    B, sl = pb[hi], hsA[hi]
                    sc = lambda nm: tm[nm].t[:, n, hh_:hh_ + 1]
                    QKd, M128, bv, kdec, nwT = B.QKd[pr], B.M128[pr], B.bv[pr], B.kdec[pr], B.nwT[pr]
                    kb.op(ACT, lambda: nc.scalar.activation(out=B.rh.t[:], in_=U_f, func=AF.Identity, scale=sc("negg")), reads=[cm, tm["negg"]], writes=[B.rh])
                    kb.op(ACT, lambda: nc.scalar.activation(out=B.bek.t[:], in_=ktm[hq].t[:], func=AF.Identity, scale=sc("beG")), reads=[ktm[hq], tm["beG"]], writes=[B.bek])
                    kb.op(ACT, lambda: nc.scalar.activation(out=kdec.t[:], in_=ktm[hq].t[:], func=AF.Identity, scale=sc("kds")), reads=[ktm[hq], tm["kds"]], writes=[kdec])
                    yield
                    kb.op(PE, lambda: nc.tensor.matmul(sl[0].ap, lhsT=L_f, rhs=B.rh.t[:], start=True, stop=False), reads=[cm, B.rh], writes=[sl[0].b], inc=False)
                    kb.op(PE, lambda: nc.tensor.matmul(sl[0].ap, lhsT=ident_b, rhs=mS_b, start=False, stop=True), reads=[cmb], writes=[sl[0].b], inc=False)
                    kb.op(PE, lambda: nc.tensor.matmul(sl[1].ap, lhsT=B.rh.t[:], rhs=L_f, start=True, stop=False), reads=[cm, B.rh], writes=[sl[1].b], inc=False)
                    kb.op(PE, lambda: nc.tensor.matmul(sl[1].ap, lhsT=ident_b, rhs=mIT_b, start=False, stop=True), reads=[cmb], writes=[sl[1].b], inc=False)
                    kb.op(PE, lambda: nc.tensor.transpose(sl[3].apb, vT.t[:, hi, cs], ident_b), reads=[vT, cmb], writes=[sl[3].b])
                    kb.op(ACT, lambda: nc.scalar.activation(out=B.Ds.t[:], in_=sl[0].ap, func=AF.Exp, scale=-1.0), reads=[sl[0].b], writes=[B.Ds])
                    kb.op(ACT, lambda: nc.scalar.activation(out=B.DT.t[:], in_=sl[1].ap, func=AF.Exp, scale=-1.0), reads=[sl[1].b], writes=[B.DT])
                    kb.op(DVE, lambda: nc.vector.tensor_scalar(out=bv.t[:], in0=sl[3].apb, scalar1=sc("beta"), scalar2=None, op0=ALU.mult), reads=[sl[3].b, tm["beta"]], writes=[bv])
                    yield
                    kb.op(DVE, lambda: nc.vector.tensor_scalar(out=B.T1.t[:], in0=B.Ds.t[:], scalar1=sc("beta"), scalar2=None, op0=ALU.mult), reads=[B.Ds, tm["beta"]], writes=[B.T1])
                    kb.op(DVE, lambda: nc.vector.tensor_tensor(out=B.Abd.t[:], in0=B.T1.t[:], in1=KKbd[hq].t[:], op=ALU.mult), reads=[B.T1, KKbd[hq]], writes=[B.Abd])
                    kb.op(DVE, lambda: nc.vector.tensor_tensor(out=B.Aoff.t[:], in0=B.T1.t[:], in1=KKoff[hq].t[:], op=ALU.mult), reads=[B.T1, KKoff[hq]], writes=[B.Aoff])
                    kb.op(DVE, lambda: nc.vector.tensor_tensor(out=QKd.t[:], in0=QKT[hq].t[:], in1=B.DT.t[:], op=ALU.mult), reads=[QKT[hq], B.DT], writes=[QKd])
                    yield
                    kb.op(PE, lambda: nc.tensor.transpose(sl[2].apb, B.Abd.t[:], ident_b), reads=[B.Abd, cmb], writes=[sl[2].b])
                    kb.op(ACT, lambda: nc.scalar.copy(out=B.Q0.t[:], in_=sl[2].apb), reads=[sl[2].b], writes=[B.Q0])
                    kb.op(DVE, lambda: nc.vector.tensor_tensor(out=B.M0.t[:], in0=ident_f, in1=sl[2].apb, op=ALU.subtract), reads=[cm, sl[2].b], writes=[B.M0])
                    kb.op(DVE, lambda: nc.vector.tensor_tensor(out=B.Mf.t[:], in0=ident_f, in1=sl[2].apb, op=ALU.subtract), reads=[cm, sl[2].b], writes=[B.Mf])
                    yield
                    P, Q, M = B.Abd, B.Q0, B.M0
                    for lv in range(1, 6):
                        Pn, Qn, Mn = B.P[lv % 2], B.Q[lv % 2], B.M[lv % 2]
                        kb.op(PE, lambda: nc.tensor.matmul(sl[2].ap, lhsT=Q.t[:], rhs=P.t[:], start=True, stop=True), reads=[P, Q], writes=[sl[2].b])
                        if lv < 5:
                            kb.op(PE, lambda: nc.tensor.matmul(sl[3].ap, lhsT=P.t[:], rhs=Q.t[:], start=True, stop=True), reads=[P, Q], writes=[sl[3].b])
                        kb.op(ACT, lambda: nc.scalar.copy(out=Pn.t[:], in_=sl[2].ap), reads=[sl[2].b], writes=[Pn])
                        if lv < 5:
                            kb.op(DVE, lambda: nc.vector.tensor_copy(out=Qn.t[:], in_=sl[3].ap), reads=[sl[3].b], writes=[Qn])
                        yield
                        kb.op(PE, lambda: nc.tensor.matmul(sl[0].ap, lhsT=Pn.t[:], rhs=M.t[:], start=True, stop=True), reads=[Pn, M], writes=[sl[0].b])
                        kb.op(DVE, lambda: nc.vector.tensor_tensor(out=Mn.t[:], in0=M.t[:], in1=sl[0].ap, op=ALU.add), reads=[M, sl[0].b], writes=[Mn])
                        kb.op(DVE, lambda: nc.vector.tensor_tensor(out=B.Mf.t[:], in0=B.Mf.t[:], in1=sl[0].ap, op=ALU.add), reads=[B.Mf, sl[0].b], writes=[B.Mf])
                        yield
                        P, Q, M = Pn, Qn, Mn
                    kb.op(PE, lambda: nc.tensor.matmul(sl[2].ap, lhsT=B.Aoff.t[:], rhs=M.t[:], start=True, stop=True), reads=[B.Aoff, M], writes=[sl[2].b])
                    kb.op(PE, lambda: nc.tensor.transpose(sl[3].apb, M.t[:], ident_b), reads=[M, cmb], writes=[sl[3].b])
                    kb.op(ACT, lambda: nc.scalar.copy(out=B.Y1.t[:], in_=sl[2].ap), reads=[sl[2].b], writes=[B.Y1])
                    kb.op(DVE, lambda: nc.vector.tensor_copy(out=B.Tb.t[:], in_=sl[3].apb), reads=[sl[3].b], writes=[B.Tb])
                    yield
                    kb.op(PE, lambda: nc.tensor.matmul(sl[0].ap, lhsT=B.Tb.t[:], rhs=B.Y1.t[:], start=True, stop=True), reads=[B.Tb, B.Y1], writes=[sl[0].b])
                    kb.op(DVE, lambda: nc.vector.tensor_tensor(out=M128.t[:], in0=B.Mf.t[:], in1=sl[0].ap, op=ALU.subtract), reads=[B.Mf, sl[0].b], writes=[M128])
                    yield
                    kb.op(PE, lambda: nc.tensor.matmul(sl[1].ap, lhsT=B.bek.t[:], rhs=M128.t[:], start=True, stop=True), reads=[B.bek, M128], writes=[sl[1].b])
                    kb.op(ACT, lambda: nc.scalar.activation(out=nwT.t[:], in_=sl[1].ap, func=AF.Identity, scale=-1.0), reads=[sl[1].b], writes=[nwT])
                    yield

                def post_seq(hi, n):
                    t, blk = n // 4, n % 4
                    cs = slice(blk * 128, (blk + 1) * 128)
                    qT = qT2[t % 2]
                    pr = n % 2
                    hh_ = hv0 + hi
                    hq = hi // 2
                    B, sl = pb[hi], hsB[hi]
                    sc = lambda nm: tm[nm].t[:, n, hh_:hh_ + 1]
                    QKd, M128, bv, kdec, nwT = B.QKd[pr], B.M128[pr], B.bv[pr], B.kdec[pr], B.nwT[pr]
                    kb.op(PE, lambda: nc.tensor.matmul(sl[0].ap, lhsT=M128.t[:], rhs=bv.t[:], start=True, stop=False), reads=[M128, bv], writes=[sl[0].b], inc=False)
                    kb.op(PE, lambda: nc.tensor.matmul(sl[0].ap, lhsT=nwT.t[:], rhs=Sb[hi].t[:], start=False, stop=True), reads=[nwT, Sb[hi]], writes=[sl[0].b], inc=False)
                    kb.op(PE, lambda: nc.tensor.matmul(sl[1].ap, lhsT=qT.t[:, hq, cs], rhs=Sb[hi].t[:], start=True, stop=True), reads=[qT, Sb[hi]], writes=[sl[1].b])
                    kb.op(ACT, lambda: nc.scalar.copy(out=B.vn.t[:], in_=sl[0].ap), reads=[sl[0].b], writes=[B.vn])
                    kb.op(ACT, lambda: nc.scalar.activation(out=B.o1.t[:], in_=sl[1].ap, func=AF.Identity, scale=sc("eG")), reads=[sl[1].b, tm["eG"]], writes=[B.o1])
                    yield
                    kb.op(PE, lambda: nc.tensor.matmul(sl[2].ap, lhsT=kdec.t[:], rhs=B.vn.t[:], start=True, stop=True), reads=[kdec, B.vn], writes=[sl[2].b], inc=False)
                    kb.op(PE, lambda: nc.tensor.matmul(sl[3].ap, lhsT=QKd.t[:], rhs=B.vn.t[:], start=True, stop=True), reads=[QKd, B.vn], writes=[sl[3].b])
                    kb.op(DVE, lambda: nc.vector.scalar_tensor_tensor(out=Sf[hi].t[:], in0=Sf[hi].t[:], scalar=sc("gl"), in1=sl[2].ap, op0=ALU.mult, op1=ALU.add),
                          reads=[Sf[hi], tm["gl"], sl[2].b], writes=[Sf[hi]])
                    kb.op(DVE, lambda: nc.vector.tensor_tensor(out=ob.t[:, hi, :], in0=B.o1.t[:], in1=sl[3].ap, op=ALU.add), reads=[B.o1, sl[3].b], writes=[ob])
                    yield
                    kb.op(ACT, lambda: nc.scalar.copy(out=Sb[hi].t[:], in_=Sf[hi].t[:]), reads=[Sf[hi]], writes=[Sb[hi]])
                    yield

                def norm_store(n):
                    t, blk = n // 4, n % 4
                    cs = slice(blk * 128, (blk + 1) * 128)
                    zg, dn_o = zg2[t % 2], dno[t % 2]
                    for hi in range(HVG):
                        kb.op(ACT, lambda: nc.scalar.activation(out=junk.t[:], in_=ob.t[:, hi, :], func=AF.Square, accum_out=ss.t[:, hi:hi + 1]), reads=[ob], writes=[junk, ss])
                    kb.op(ACT, lambda: nc.scalar.activation(out=rstd.t[:], in_=ss.t[:], func=AF.Ln, scale=1.0 / 128.0, bias=eps6), reads=[ss, kb.eps_ln], writes=[rstd])
                    kb.op(ACT, lambda: nc.scalar.activation(out=rstd.t[:], in_=rstd.t[:], func=AF.Exp, scale=-0.5), reads=[rstd], writes=[rstd])
                    for hi in range(HVG):
                        B, sl = pb[hi], hsB[hi]
                        kb.op(DVE, lambda: nc.vector.scalar_tensor_tensor(out=B.dntm.t[:], in0=ob.t[:, hi, :], scalar=rstd.t[:, hi:hi + 1], in1=zg.t[:, blk, hi * 128:(hi + 1) * 128],
                                                                          op0=ALU.mult, op1=ALU.mult), reads=[ob, rstd, zg], writes=[B.dntm])
                        kb.op(PE, lambda: nc.tensor.transpose(sl[0].apb, B.dntm.t[:], ident_b), reads=[B.dntm, cmb], writes=[sl[0].b])
                        kb.op(ACT, lambda: nc.scalar.copy(out=dn_o.t[:, hi, cs], in_=sl[0].apb), reads=[sl[0].b], writes=[dn_o])
                    if blk == 3:
                        kb.dma(dnT_d[hv0:hv0 + HVG, :, t * 512:(t + 1) * 512].rearrange("c p t -> p c t"), dn_o.t[:], reads=[dn_o])

                def run(gens):
                    while gens:
                        for g in list(gens):
                            try:
                                next(g)
                            except StopIteration:
                                gens.remove(g)

                load_h(0)
                for t in range(NT):
                    if t + 1 < NT:
                        load_h(t + 1)
                    stageA(t)
                    for blk in range(4):
                        n = t * 4 + blk
                        hq_ops(n)
                        run([pre_seq(hi, n) for hi in range(HVG)])
                        run([post_seq(hi, n) for hi in range(HVG)])
                        norm_store(n)
                kb.barrier()


def _consts():
    i = np.arange(128)
    ident = np.eye(128, dtype=np.float32)
    L = (i[:, None] <= i[None, :]).astype(np.float32)
    U = (i[:, None] > i[None, :]).astype(np.float32)
    mS = np.where(i[:, None] > i[None, :], 0.0, BIG).astype(np.float32)
    mIT = np.where(i[None, :] >= i[:, None], 0.0, BIG).astype(np.float32)
    blk = i // 64
    BD = (blk[:, None] == blk[None, :]).astype(np.float32)
    OFF = ((blk[:, None] == 1) & (blk[None, :] == 0)).astype(np.float32)
    ones = np.ones((128, 128), np.float32)
    return np.concatenate([ident, L, U, mS, mIT, BD, OFF, ones], axis=1)


def _fm(v, n):
    return np.ascontiguousarray(np.asarray(v, np.float32).reshape(n, 128).T)


def _band(w):
    full = np.zeros((DRNN, DRNN), np.float32)
    for b in range(16):
        full[b * 80:(b + 1) * 80, b * 80:(b + 1) * 80] = w[b]
    out = np.zeros((128, 10, 3, 128), np.float32)
    for m in range(10):
        for d in range(3):
            k = m + d - 1
            if 0 <= k < 10:
                out[:, m, d, :] = full[k * 128:(k + 1) * 128, m * 128:(m + 1) * 128]
    return out


def make_in_maps(inp, S, batches):
    l = 0
    f = lambda a: np.ascontiguousarray(np.asarray(a, np.float32))
    NCH = S // 128
    rows = np.zeros((128, NR), np.float32)
    b_ada = f(inp["b_ada"][l])
    def setr(name, v):
        o, w = _R[name]
        rows[:, o:o + w] = np.asarray(v, np.float32)[None, :]
    setr("b_gt1", b_ada[2048:3072]); setr("b_gt2", b_ada[5120:6144])
    setr("ln1_g", inp["ln1_g"][l]); setr("ln1_b", inp["ln1_b"][l]); setr("ln2_g", inp["ln2_g"][l]); setr("ln2_b", inp["ln2_b"][l])
    setr("nw", np.tile(f(inp["dn_norm_w"][l]), 16))
    hrep = np.zeros((128, 2, NCH * 16), np.float32)
    hrep[:, 0, :] = np.tile(f(inp["dn_dt_bias"][l]), NCH)[None, :]
    hrep[:, 1, :] = np.tile(f(inp["dn_a_log"][l]), NCH)[None, :]
    cmat = _consts()
    shared = dict(
        w_ada=f(inp["w_ada"][l]), w_in=f(inp["w_in"][l]), gate_a=_band(f(inp["rg_w_a"][l])), gate_x=_band(f(inp["rg_w_x"][l])),
        w_proj_a=f(inp["w_proj_a"][l]), w_proj_b=f(inp["w_proj_b"][l]), w_out=f(inp["w_out"][l]),
        ffn_w_gate=f(inp["ffn_w_gate"][l]), ffn_w_up=f(inp["ffn_w_up"][l]), ffn_w_down=f(inp["ffn_w_down"][l]),
        rows=rows, cmat=cmat, hrep=hrep)
    maps = []
    for b in batches:
        params = np.zeros((128, NP), np.float32)
        def setp(name, v):
            o, w = _P[name]
            params[:, o:o + w] = v
        setp("b_ada", _fm(b_ada, 48))
        setp("c", _fm(inp["c"][b], 8))
        setp("rg_cw", np.stack([_fm(inp["rg_conv_w"][l][k], 10) for k in range(4)], axis=2).reshape(128, 40))
        setp("rg_cb", _fm(inp["rg_conv_b"][l], 10)); setp("rg_ba", _fm(inp["rg_b_a"][l], 10)); setp("rg_bx", _fm(inp["rg_b_x"][l], 10))
        setp("rg_lam", _fm(inp["rg_lambda"][l], 10))
        setp("dn_cw", np.stack([_fm(inp["dn_conv_w"][l][k], 32) for k in range(4)], axis=2).reshape(128, 128))
        setp("ffn_cw", np.stack([_fm(inp["ffn_conv_w"][l][k], 22) for k in range(3)], axis=2).reshape(128, 66))
        setp("ffn_cb", _fm(inp["ffn_conv_b"][l], 22))
        xb = f(inp["x"][b])
        m = dict(shared)
        m["x"] = xb
        m["xT"] = np.ascontiguousarray(xb.T.reshape(8, 128, S))
        m["params"] = params
        maps.append(m)
    return maps


_NC_CACHE = {}


def kernel(**inputs):
    S = inputs["x"].shape[1]
    B = inputs["x"].shape[0]
    if S not in _NC_CACHE:
        _NC_CACHE[S] = build_program(S)
    nc = _NC_CACHE[S]
    maps = make_in_maps(inputs, S, list(range(B)))
    res = run_bass_kernel_spmd(nc, maps, core_ids=list(range(B)))
    return np.stack([np.asarray(r["out"], np.float32) for r in res.results], axis=0)
```

```python
import contextlib
import numpy as np
import concourse.bass as bass
import concourse.mybir as mybir
from concourse.bass_utils import run_bass_kernel_spmd

F32 = mybir.dt.float32
BF16 = mybir.dt.bfloat16
AF = mybir.ActivationFunctionType
ALU = mybir.AluOpType

D = 1024
DRNN = 1280
DFF = 2816
DIN = 10784
NHV = 16
C_XR, C_GR, C_Q, C_K, C_V, C_Z, C_A, C_B, C_GA, C_GB = 0, 1280, 2560, 3584, 4608, 6656, 8704, 8720, 8736, 9760
ALPHA = 2.0 ** 0.25
BIG = 32768.0
GELU_C = 1.5957691216057308

_P = {}
_off = 0
for _n, _w in [("b_ada", 48), ("c", 8), ("rg_cw", 40), ("rg_cb", 10), ("rg_ba", 10), ("rg_bx", 10), ("rg_lam", 10),
               ("dn_cw", 128), ("ffn_cw", 66), ("ffn_cb", 22)]:
    _P[_n] = (_off, _w)
    _off += _w
NP = _off
_R = {}
_off = 0
for _n, _w in [("b_gt1", 1024), ("b_gt2", 1024), ("ln1_g", 1024), ("ln1_b", 1024), ("ln2_g", 1024), ("ln2_b", 1024),
               ("nw", 2048)]:
    _R[_n] = (_off, _w)
    _off += _w
NR = _off
CM_IDENT, CM_L, CM_U, CM_MS, CM_MIT, CM_BD, CM_OFF, CM_ONES = range(8)
NCM = 8


class Buf:
    def __init__(self, kb, name, t):
        self.kb = kb
        self.name = name
        self.t = t
        self.w = None
        self.r = []
        self.lsem = None
        self.ssem = None
        self.excl = False

    def add_reader(self, tok):
        for i, (s, v) in enumerate(self.r):
            if s == tok[0]:
                if v < tok[1]:
                    self.r[i] = tok
                return
        self.r.append(tok)


class EngW:
    LIMIT = 24000

    def __init__(self, kb, eng, name):
        self.kb = kb
        self.eng = eng
        self.name = name
        self.sem = None
        self.cnt = 0
        self.seen = {}
        self.pending = []

    def wait(self, toks):
        best = {}
        for s, v in toks:
            if best.get(s, 0) < v:
                best[s] = v
        for s, v in best.items():
            if self.seen.get(s, 0) >= v:
                continue
            self.eng.wait_ge(self.kb.sems[s], v)
            self.seen[s] = v

    def bump(self, inst):
        if self.sem is None or self.cnt >= self.LIMIT:
            self.sem = self.kb.new_sem(f"e_{self.name}_{len(self.kb.sems)}")
            self.cnt = 0
        inst.then_inc(self.kb.sems[self.sem], 1)
        self.cnt += 1
        return (self.sem, self.cnt)

    def cur(self):
        return None if self.sem is None else (self.sem, self.cnt)


class KB:
    def __init__(self, nc, es):
        self.nc = nc
        self.es = es
        self.sems = []
        self.semcnt = []
        self.free_dma_sems = []
        self.pe = EngW(self, nc.tensor, "pe")
        self.act = EngW(self, nc.scalar, "act")
        self.dve = EngW(self, nc.vector, "dve")
        self.pool = EngW(self, nc.gpsimd, "pool")
        self.sp = EngW(self, nc.sync, "sp")
        self.engs = [self.pe, self.act, self.dve, self.pool, self.sp]
        self.dma_toks = {}
        self.phase_bufs = []
        self.n_inst = 0

    def new_sem(self, name):
        h = self.es.enter_context(self.nc.semaphore(name))
        self.sems.append(h)
        self.semcnt.append(0)
        return len(self.sems) - 1

    def dma_sem(self):
        if self.free_dma_sems:
            return self.free_dma_sems.pop()
        return self.new_sem(f"d_{len(self.sems)}")

    def sb(self, pes, name, shape, dt):
        self.n_alloc = getattr(self, "n_alloc", 0) + 1
        t = pes.enter_context(self.nc.sbuf_tensor(f"sb{self.n_alloc}_{name}", list(shape), dt))
        b = Buf(self, name, t)
        self.phase_bufs.append(b)
        return b

    def sbs(self, pes, name, shape, dt):
        b = self.sb(pes, name, shape, dt)
        b.sub = [Buf(self, f"{name}[{i}]", b.t) for i in range(shape[1])]
        self.phase_bufs.extend(b.sub)
        return b

    def ps(self, pes, name, shape, dt):
        self.n_alloc = getattr(self, "n_alloc", 0) + 1
        t = pes.enter_context(self.nc.psum_tensor(f"ps{self.n_alloc}_{name}", list(shape), dt))
        b = Buf(self, name, t)
        b.excl = True
        self.phase_bufs.append(b)
        return b

    def op(self, E, fn, reads=(), writes=(), inc=True):
        ex = [b for b in reads if b.excl]
        if ex:
            reads = [b for b in reads if not b.excl]
            writes = list(writes) + [b for b in ex if b not in writes]
        toks = []
        for b in reads:
            if b.w is not None:
                toks.append(b.w)
        for b in writes:
            if b.w is not None:
                toks.append(b.w)
            toks.extend(b.r)
        E.wait(toks)
        inst = fn()
        self.n_inst += 1
        if inc:
            tok = E.bump(inst)
            sets = E.pending + [(reads, writes)]
            E.pending = []
            for rd, wr in sets:
                for b in rd:
                    b.add_reader(tok)
                for b in wr:
                    b.w = tok
                    b.r = []
        else:
            E.pending.append((reads, writes))
        return inst

    def dma(self, out_ap, in_ap, reads=(), writes=(), Q=None):
        Q = Q or self.sp
        toks = []
        for b in reads:
            if b.w is not None:
                toks.append(b.w)
        for b in writes:
            if b.w is not None:
                toks.append(b.w)
            toks.extend(b.r)
        Q.wait(toks)
        inst = Q.eng.dma_start(out=out_ap, in_=in_ap)
        self.n_inst += 1
        if writes:
            tb = writes[0]
            if tb.lsem is None:
                tb.lsem = self.dma_sem()
            s = tb.lsem
        else:
            tb = reads[0]
            if tb.ssem is None:
                tb.ssem = self.dma_sem()
            s = tb.ssem
        self.semcnt[s] += 16
        inst.then_inc(self.sems[s], 16)
        tok = (s, self.semcnt[s])
        self.dma_toks[s] = tok
        for b in writes:
            b.w = tok
            b.r = []
        for b in reads:
            b.add_reader(tok)
        return tok

    def barrier(self, keep=()):
        toks = list(self.dma_toks.values())
        for E in self.engs:
            assert not E.pending
            c = E.cur()
            if c is not None:
                toks.append(c)
        for E in self.engs:
            E.wait(toks)
        for b in self.phase_bufs:
            if b.lsem is not None:
                self.free_dma_sems.append(b.lsem)
                b.lsem = None
            if b.ssem is not None:
                self.free_dma_sems.append(b.ssem)
                b.ssem = None
        self.dma_toks = {}

    def final_wait(self):
        toks = list(self.dma_toks.values())
        self.sp.wait(toks)


def _cm(cm, idx, dt=None):
    return cm.t[:, idx * 128:(idx + 1) * 128]


def build_program(S, debug=False):
    assert S % 512 == 0
    NT = S // 512
    NCH = S // 128
    nc = bass.Bass("TRN2", target_bir_lowering=False)
    okind = "ExternalOutput" if debug else "Internal"

    def din(name, shape, dt=F32):
        return nc.dram_tensor(name, list(shape), dt, kind="ExternalInput").ap()

    xT_d = din("xT", [8, 128, S])
    x_d = din("x", [S, D])
    w_ada_d = din("w_ada", [D, 6 * D])
    w_in_d = din("w_in", [D, DIN])
    gate_a_d = din("gate_a", [128, 10, 3, 128])
    gate_x_d = din("gate_x", [128, 10, 3, 128])
    w_pa_d = din("w_proj_a", [DRNN, D])
    w_pb_d = din("w_proj_b", [2048, D])
    w_out_d = din("w_out", [D, D])
    w_fg_d = din("ffn_w_gate", [D, DFF])
    w_fu_d = din("ffn_w_up", [D, DFF])
    w_fd_d = din("ffn_w_down", [DFF, D])
    params_d = din("params", [128, NP])
    rows_d = din("rows", [128, NR])
    cmat_d = din("cmat", [128, NCM * 128])
    hrep_d = din("hrep", [128, 2, NCH * 16])
    out_d = nc.dram_tensor("out", [S, D], F32, kind="ExternalOutput").ap()
    hT_d = nc.dram_tensor("hT_s", [8, 128, S], BF16, kind=okind).ap()
    recT_d = nc.dram_tensor("recT_s", [10, 128, S], BF16, kind=okind).ap()
    dnT_d = nc.dram_tensor("dnT_s", [16, 128, S], BF16, kind=okind).ap()
    x1_d = nc.dram_tensor("x1_s", [S, D], F32, kind=okind).ap()

    with contextlib.ExitStack() as es:
        kb = KB(nc, es)
        PE, ACT, DVE, POOL = kb.pe, kb.act, kb.dve, kb.pool
        params = kb.sb(es, "params", [128, NP], F32)
        cm = kb.sb(es, "cm", [128, NCM * 128], F32)
        cmb = kb.sb(es, "cmb", [128, NCM * 128], BF16)
        ada = kb.sb(es, "ada", [128, 48], F32)
        gt1r = kb.sb(es, "gt1r", [128, 1024], F32)
        gt2r = kb.sb(es, "gt2r", [128, 1024], F32)
        kb.dma(params.t[:], params_d[:, :], writes=[params])
        kb.dma(cm.t[:], cmat_d[:, :], writes=[cm])
        kb.op(ACT, lambda: nc.scalar.copy(out=cmb.t[:], in_=cm.t[:]), reads=[cm], writes=[cmb])

        def pcol(name, j=0, n=1):
            o, w = _P[name]
            return params.t[:, o + j:o + j + n]

        ident_f = cm.t[:, CM_IDENT * 128:(CM_IDENT + 1) * 128]
        ident_b = cmb.t[:, CM_IDENT * 128:(CM_IDENT + 1) * 128]
        ones_f = cm.t[:, CM_ONES * 128:(CM_ONES + 1) * 128]
        ones_b = cmb.t[:, CM_ONES * 128:(CM_ONES + 1) * 128]

        def load_w_bf16(pes, dst, dram, r0, nk, c0, ncols, stg, dst_c0=0):
            PW = stg[0].t.shape[-1]
            i = load_w_bf16.cnt
            for cc in range(0, ncols, PW):
                w = min(PW, ncols - cc)
                st = stg[i % len(stg)]
                src = dram[r0:r0 + nk * 128, c0 + cc:c0 + cc + w].rearrange("(k p) c -> p k c", p=128)
                kb.dma(st.t[:, 0:nk, 0:w], src, writes=[st])
                o = dst.t[:, 0:nk, dst_c0 + cc:dst_c0 + cc + w]
                if i % 2 == 0:
                    kb.op(ACT, lambda o=o, st=st, w=w: nc.scalar.copy(out=o, in_=st.t[:, 0:nk, 0:w]), reads=[st], writes=[dst])
                else:
                    kb.op(DVE, lambda o=o, st=st, w=w: nc.vector.tensor_copy(out=o, in_=st.t[:, 0:nk, 0:w]), reads=[st], writes=[dst])
                i += 1
            load_w_bf16.cnt = i
        load_w_bf16.cnt = 0

        with contextlib.ExitStack() as pes:
            scf = kb.sb(pes, "scf", [128, 8], F32)
            screp = kb.sb(pes, "screp", [128, 8, 128], F32)
            wst = [kb.sb(pes, f"wst{i}", [128, 8, 1024], F32) for i in range(2)]
            rows01 = kb.sb(pes, "rows01", [128, 2048], F32)
            pa = kb.ps(pes, "pa", [128, 512], F32)
            pb = [kb.ps(pes, f"pb{i}", [128, 512], F32) for i in range(2)]
            kb.dma(rows01.t[:], rows_d[:, 0:2048], writes=[rows01])
            kb.op(ACT, lambda: nc.scalar.activation(out=scf.t[:], in_=pcol("c", 0, 8), func=AF.Silu), reads=[params], writes=[scf])
            for k in range(8):
                kb.op(DVE, lambda k=k: nc.vector.tensor_copy(out=screp.t[:, k, :], in_=scf.t[:, k:k + 1].to_broadcast([128, 128])),
                      reads=[scf], writes=[screp])
            for jb in range(6):
                st = wst[jb % 2]
                kb.dma(st.t[:], w_ada_d[:, jb * 1024:(jb + 1) * 1024].rearrange("(k p) c -> p k c", p=128), writes=[st])
                if jb in (2, 5):
                    dst = gt1r if jb == 2 else gt2r
                    for cb in range(2):
                        for k in range(8):
                            kb.op(PE, lambda k=k, cb=cb, st=st: nc.tensor.matmul(pb[cb].t[:], lhsT=screp.t[:, k, :], rhs=st.t[:, k, cb * 512:(cb + 1) * 512],
                                                                               start=(k == 0), stop=(k == 7)),
                                  reads=[screp, st], writes=[pb[cb]], inc=(k == 7))
                        ro = (0 if jb == 2 else 1024) + cb * 512
                        kb.op(DVE, lambda cb=cb, ro=ro, dst=dst: nc.vector.scalar_tensor_tensor(
                            out=dst.t[:, cb * 512:(cb + 1) * 512], in0=pb[cb].t[:], scalar=1.0, in1=rows01.t[:, ro:ro + 512],
                            op0=ALU.add, op1=ALU.add), reads=[pb[cb], rows01], writes=[dst])
                for j in range(8):
                    col = jb * 8 + j
                    for k in range(8):
                        kb.op(PE, lambda k=k, j=j, col=col, st=st: nc.tensor.matmul(pa.t[:, col:col + 1], lhsT=st.t[:, k, j * 128:(j + 1) * 128],
                                                                                 rhs=scf.t[:, k:k + 1], start=(k == 0), stop=(k == 7)),
                              reads=[scf, st], writes=[pa], inc=(k == 7))
            kb.op(DVE, lambda: nc.vector.tensor_tensor(out=ada.t[:], in0=pa.t[:, 0:48], in1=pcol("b_ada", 0, 48), op=ALU.add),
                  reads=[pa, params], writes=[ada])
            for o in (8, 32):
                kb.op(DVE, lambda o=o: nc.vector.tensor_scalar(out=ada.t[:, o:o + 8], in0=ada.t[:, o:o + 8], scalar1=1.0, scalar2=None, op0=ALU.add),
                      reads=[ada], writes=[ada])
            xin = [kb.sb(pes, f"xin{i}", [128, 8, 512], F32) for i in range(2)]
            hto = [kb.sb(pes, f"hto{i}", [128, 8, 512], BF16) for i in range(2)]
            for t in range(NT):
                xi, ho = xin[t % 2], hto[t % 2]
                kb.dma(xi.t[:], xT_d[:, :, t * 512:(t + 1) * 512].rearrange("c p t -> p c t"), writes=[xi])
                for c in range(8):
                    if c % 2 == 0:
                        kb.op(ACT, lambda c=c, xi=xi, ho=ho: nc.scalar.activation(out=ho.t[:, c, :], in_=xi.t[:, c, :], func=AF.Identity,
                                                                              scale=ada.t[:, 8 + c:9 + c], bias=ada.t[:, c:c + 1]),
                              reads=[xi, ada], writes=[ho])
                    else:
                        kb.op(DVE, lambda c=c, xi=xi, ho=ho: nc.vector.tensor_scalar(out=ho.t[:, c, :], in0=xi.t[:, c, :], scalar1=ada.t[:, 8 + c:9 + c],
                                                                                 scalar2=ada.t[:, c:c + 1], op0=ALU.mult, op1=ALU.add),
                              reads=[xi, ada], writes=[ho])
                kb.dma(hT_d[:, :, t * 512:(t + 1) * 512].rearrange("c p t -> p c t"), ho.t[:], reads=[ho])
            kb.barrier()

        PHASES = build_program.phases
        eps_ln = kb.sb(es, "eps_ln", [128, 2], F32)
        kb.eps_ln = eps_ln
        kb.op(DVE, lambda: nc.vector.memset(eps_ln.t[:, 0:1], 1e-5), writes=[eps_ln])
        kb.op(DVE, lambda: nc.vector.memset(eps_ln.t[:, 1:2], 1e-6), writes=[eps_ln])
        if "rg" in PHASES:
            phase_rg(nc, kb, S, params, pcol, cm, cmb, load_w_bf16, w_in_d, gate_a_d, gate_x_d, hT_d, recT_d)
        if "gdn" in PHASES:
            phase_gdn(nc, kb, S, params, pcol, cm, cmb, load_w_bf16, w_in_d, rows_d, hrep_d, hT_d, dnT_d)
        if "mix" in PHASES:
            phase_mix(nc, kb, S, load_w_bf16, gt1r, w_in_d, w_pa_d, w_pb_d, w_out_d, rows_d, x_d, hT_d, recT_d, dnT_d, x1_d)
        if "ffn" in PHASES:
            phase_ffn(nc, kb, S, params, pcol, cm, load_w_bf16, ada, gt2r, w_fg_d, w_fu_d, w_fd_d, rows_d, x1_d, out_d)
        kb.final_wait()
    return nc


build_program.phases = ("rg", "gdn", "mix", "ffn")


def gelu_tanh(nc, kb, src_ap, src_bufs, out_ap, out_bufs, tmp, tmp2=None, mul_ap=None, mul_bufs=()):
    ACT, DVE = kb.act, kb.dve
    if mul_ap is None:
        kb.op(ACT, lambda: nc.scalar.activation(out=out_ap, in_=src_ap, func=AF.Gelu_apprx_tanh), reads=src_bufs, writes=out_bufs)
    else:
        kb.op(ACT, lambda: nc.scalar.activation(out=tmp.t[:], in_=src_ap, func=AF.Gelu_apprx_tanh), reads=src_bufs, writes=[tmp])
        kb.op(DVE, lambda: nc.vector.tensor_tensor(out=out_ap, in0=tmp.t[:], in1=mul_ap, op=ALU.mult), reads=[tmp] + list(mul_bufs), writes=out_bufs)


def phase_rg(nc, kb, S, params, pcol, cm, cmb, load_w_bf16, w_in_d, gate_a_d, gate_x_d, hT_d, recT_d):
    PE, ACT, DVE, POOL = kb.pe, kb.act, kb.dve, kb.pool
    TT = 256
    NT = S // TT
    with contextlib.ExitStack() as pes:
        wx = kb.sb(pes, "wx", [128, 8, 2560], BF16)
        wga = kb.sb(pes, "wga", [128, 10, 3, 128], BF16)
        wgx = kb.sb(pes, "wgx", [128, 10, 3, 128], BF16)
        cl = kb.sb(pes, "cl", [128, 20], F32)
        with contextlib.ExitStack() as ses:
            stg = [kb.sb(ses, f"stg{i}", [128, 8, 512], F32) for i in range(2)]
            gst = kb.sb(ses, "gst", [128, 10, 3, 128], F32)
            load_w_bf16(ses, wx, w_in_d, 0, 8, 0, 2560, stg)
            kb.dma(gst.t[:], gate_a_d[:, :, :, :], writes=[gst])
            kb.op(ACT, lambda: nc.scalar.copy(out=wga.t[:], in_=gst.t[:]), reads=[gst], writes=[wga])
            kb.dma(gst.t[:], gate_x_d[:, :, :, :], writes=[gst])
            kb.op(ACT, lambda: nc.scalar.copy(out=wgx.t[:], in_=gst.t[:]), reads=[gst], writes=[wgx])
            kb.barrier(keep=[wx, wga, wgx, cl])
        kb.op(ACT, lambda: nc.scalar.activation(out=cl.t[:, 0:10], in_=pcol("rg_lam", 0, 10), func=AF.Exp, scale=-1.0), reads=[params], writes=[cl])
        kb.op(ACT, lambda: nc.scalar.activation(out=cl.t[:, 0:10], in_=cl.t[:, 0:10], func=AF.Ln, bias=1.0), reads=[cl], writes=[cl])
        kb.op(DVE, lambda: nc.vector.tensor_scalar(out=cl.t[:, 10:20], in0=cl.t[:, 0:10], scalar1=-16.0, scalar2=None, op0=ALU.mult), reads=[cl], writes=[cl])
        kb.op(DVE, lambda: nc.vector.tensor_scalar(out=cl.t[:, 0:10], in0=cl.t[:, 0:10], scalar1=-8.0, scalar2=None, op0=ALU.mult), reads=[cl], writes=[cl])

        hT = [kb.sb(pes, f"hT{i}", [128, 8, TT], BF16) for i in range(2)]
        raw = [kb.sb(pes, f"raw{i}", [128, TT + 3], F32) for i in range(3)]
        halo = kb.sbs(pes, "halo", [128, 10, 3], F32)
        hst = kb.sbs(pes, "hst", [128, 10, 1], F32)
        xr = kb.sbs(pes, "xr", [128, 10, TT], F32)
        xrb = kb.sbs(pes, "xrb", [128, 10, TT], BF16)
        ga = kb.sbs(pes, "ga", [128, 10, TT], F32)
        gi = kb.sbs(pes, "gi", [128, 10, TT], F32)
        mu = kb.sbs(pes, "mu", [128, 10, TT], F32)
        hh = kb.sbs(pes, "hh", [128, 10, TT], F32)
        tmp = [kb.sb(pes, f"gtmp{i}", [128, TT], F32) for i in range(3)]
        rec = [kb.sbs(pes, f"rec{i}", [128, 10, TT], BF16) for i in range(2)]
        pp = [kb.ps(pes, f"pp{i}", [128, 512], F32) for i in range(6)]
        kb.op(DVE, lambda: nc.vector.memset(halo.t[:], 0.0), writes=halo.sub)
        kb.op(DVE, lambda: nc.vector.memset(hst.t[:], 0.0), writes=hst.sub)
        cwo = _P["rg_cw"][0]

        def load_h(t):
            kb.dma(hT[t % 2].t[:], hT_d[:, :, t * TT:(t + 1) * TT].rearrange("c p t -> p c t"), writes=[hT[t % 2]])

        load_h(0)
        pi = 0
        for t in range(NT):
            if t + 1 < NT:
                load_h(t + 1)
            h = hT[t % 2]
            for j in range(10):
                p = pp[pi % 6]; pi += 1
                for k in range(8):
                    kb.op(PE, lambda: nc.tensor.matmul(p.t[:, 0:TT], lhsT=wx.t[:, k, j * 128:(j + 1) * 128], rhs=h.t[:, k, :],
                                                       start=(k == 0), stop=(k == 7)), reads=[wx, h], writes=[p], inc=(k == 7))
                rw = raw[j % 3]
                kb.op(ACT, lambda: nc.scalar.copy(out=rw.t[:, 0:3], in_=halo.t[:, j, :]), reads=[halo.sub[j]], writes=[rw])
                kb.op(ACT, lambda: nc.scalar.copy(out=rw.t[:, 3:TT + 3], in_=p.t[:, 0:TT]), reads=[p], writes=[rw])
                kb.op(ACT, lambda: nc.scalar.copy(out=halo.t[:, j, :], in_=rw.t[:, TT:TT + 3]), reads=[rw], writes=[halo.sub[j]])
                kb.op(DVE, lambda: nc.vector.tensor_scalar(out=xr.t[:, j, :], in0=rw.t[:, 3:TT + 3], scalar1=params.t[:, cwo + j * 4 + 3:cwo + j * 4 + 4],
                                                           scalar2=pcol("rg_cb", j), op0=ALU.mult, op1=ALU.add), reads=[rw, params], writes=[xr.sub[j]])
                for kk in range(3):
                    kb.op(DVE, lambda: nc.vector.scalar_tensor_tensor(
                        out=xr.t[:, j, :], in0=rw.t[:, kk:kk + TT], scalar=params.t[:, cwo + j * 4 + kk:cwo + j * 4 + kk + 1], in1=xr.t[:, j, :],
                        op0=ALU.mult, op1=ALU.add), reads=[rw, params, xr.sub[j]], writes=[xr.sub[j]])
                kb.op(ACT, lambda: nc.scalar.copy(out=xrb.t[:, j, :], in_=xr.t[:, j, :]), reads=[xr.sub[j]], writes=[xrb.sub[j]])
            for (wg, dst, bname) in ((wga, ga, "rg_ba"), (wgx, gi, "rg_bx")):
                for m in range(10):
                    p = pp[pi % 6]; pi += 1
                    ks = [k for k in (m - 1, m, m + 1) if 0 <= k < 10]
                    for n_, k in enumerate(ks):
                        kb.op(PE, lambda: nc.tensor.matmul(
                            p.t[:, 0:TT], lhsT=wg.t[:, m, k - m + 1, :], rhs=xrb.t[:, k, :], start=(n_ == 0), stop=(n_ == len(ks) - 1)),
                            reads=[wg, xrb.sub[k]], writes=[p], inc=(n_ == len(ks) - 1))
                    kb.op(ACT, lambda: nc.scalar.activation(out=dst.t[:, m, :], in_=p.t[:, 0:TT], func=AF.Sigmoid, bias=pcol(bname, m)),
                          reads=[p, params], writes=[dst.sub[m]])
            for m in range(10):
                kb.op(ACT, lambda: nc.scalar.activation(out=mu.t[:, m, :], in_=ga.t[:, m, :], func=AF.Exp, scale=cl.t[:, 10 + m:11 + m]), reads=[ga.sub[m], cl], writes=[mu.sub[m]])
                kb.op(ACT, lambda: nc.scalar.activation(out=ga.t[:, m, :], in_=ga.t[:, m, :], func=AF.Exp, scale=cl.t[:, m:m + 1]), reads=[ga.sub[m], cl], writes=[ga.sub[m]])
            for m in range(10):
                kb.op(ACT, lambda: nc.scalar.activation(out=mu.t[:, m, :], in_=mu.t[:, m, :], func=AF.Sqrt, scale=-1.0, bias=1.0), reads=[mu.sub[m]], writes=[mu.sub[m]])
            for m in range(10):
                kb.op(POOL, lambda: nc.gpsimd.tensor_tensor(out=gi.t[:, m, :], in0=gi.t[:, m, :], in1=xr.t[:, m, :], op=ALU.mult), reads=[gi.sub[m], xr.sub[m]], writes=[gi.sub[m]])
                kb.op(DVE, lambda: nc.vector.tensor_tensor(out=gi.t[:, m, :], in0=gi.t[:, m, :], in1=mu.t[:, m, :], op=ALU.mult), reads=[gi.sub[m], mu.sub[m]], writes=[gi.sub[m]])
                kb.op(DVE, lambda: nc.vector.tensor_tensor_scan(out=hh.t[:, m, :], data0=ga.t[:, m, :], data1=gi.t[:, m, :], initial=hst.t[:, m, :],
                                                                op0=ALU.mult, op1=ALU.add), reads=[ga.sub[m], gi.sub[m], hst.sub[m]], writes=[hh.sub[m]])
                kb.op(ACT, lambda: nc.scalar.copy(out=hst.t[:, m, :], in_=hh.t[:, m, TT - 1:TT]), reads=[hh.sub[m]], writes=[hst.sub[m]])
            rc = rec[t % 2]
            for j in range(10):
                p = pp[pi % 6]; pi += 1
                for k in range(8):
                    kb.op(PE, lambda: nc.tensor.matmul(p.t[:, 0:TT], lhsT=wx.t[:, k, 1280 + j * 128:1280 + (j + 1) * 128], rhs=h.t[:, k, :],
                                                       start=(k == 0), stop=(k == 7)), reads=[wx, h], writes=[p], inc=(k == 7))
                gelu_tanh(nc, kb, p.t[:, 0:TT], [p], rc.t[:, j, :], [rc.sub[j]], tmp[j % 3], mul_ap=hh.t[:, j, :], mul_bufs=[hh.sub[j]])
            kb.dma(recT_d[:, :, t * TT:(t + 1) * TT].rearrange("c p t -> p c t"), rc.t[:], reads=rc.sub)
        kb.barrier()


def layer_norm_rows(nc, kb, y, nblk, g_ap, b_ap, rowbuf, stats, mv, sd, out_buf):
    ACT, DVE, POOL = kb.act, kb.dve, kb.pool
    for blk in range(nblk):
        for hf in range(2):
            kb.op(DVE, lambda: nc.vector.bn_stats(out=stats.t[:, blk, hf, :], in_=y.t[:, blk, hf * 512:(hf + 1) * 512]), reads=[y], writes=[stats])
        kb.op(DVE, lambda: nc.vector.bn_aggr(out=mv.t[:, blk, :], in_=stats.t[:, blk, :, :].rearrange("p a b -> p (a b)")), reads=[stats], writes=[mv])
        kb.op(ACT, lambda: nc.scalar.activation(out=sd.t[:, blk:blk + 1], in_=mv.t[:, blk, 1:2], func=AF.Ln, bias=kb.eps_ln.t[:, 0:1]), reads=[mv, kb.eps_ln], writes=[sd])
        kb.op(ACT, lambda: nc.scalar.activation(out=sd.t[:, blk:blk + 1], in_=sd.t[:, blk:blk + 1], func=AF.Exp, scale=-0.5), reads=[sd], writes=[sd])
        kb.op(DVE, lambda: nc.vector.tensor_scalar(out=y.t[:, blk, :], in0=y.t[:, blk, :], scalar1=mv.t[:, blk, 0:1], scalar2=sd.t[:, blk:blk + 1],
                                                   op0=ALU.subtract, op1=ALU.mult), reads=[y, mv, sd], writes=[y])
        kb.op(POOL, lambda: nc.gpsimd.tensor_tensor(out=y.t[:, blk, :], in0=y.t[:, blk, :], in1=g_ap, op=ALU.mult), reads=[y, rowbuf], writes=[y])
        kb.op(DVE, lambda: nc.vector.tensor_tensor(out=out_buf.t[:, blk, :], in0=y.t[:, blk, :], in1=b_ap, op=ALU.add), reads=[y, rowbuf], writes=[out_buf])


def phase_mix(nc, kb, S, load_w_bf16, gt1r, w_in_d, w_pa_d, w_pb_d, w_out_d, rows_d, x_d, hT_d, recT_d, dnT_d, x1_d):
    PE, ACT, DVE, POOL = kb.pe, kb.act, kb.dve, kb.pool
    TT = 256
    NT = S // TT
    NB = TT // 128
    with contextlib.ExitStack() as pes:
        wg = kb.sb(pes, "wg", [128, 8, 2048], BF16)
        wpa = kb.sb(pes, "wpa", [128, 10, 1024], BF16)
        wpb = kb.sb(pes, "wpb", [128, 16, 1024], BF16)
        wo = kb.sb(pes, "wo", [128, 8, 1024], BF16)
        lnr = kb.sb(pes, "lnr", [128, 2048], F32)
        with contextlib.ExitStack() as ses:
            stg = [kb.sb(ses, f"stg{i}", [128, 16, 256], F32) for i in range(2)]
            load_w_bf16(ses, wg, w_in_d, 0, 8, C_GA, 2048, stg)
            load_w_bf16(ses, wpa, w_pa_d, 0, 10, 0, 1024, stg)
            load_w_bf16(ses, wpb, w_pb_d, 0, 16, 0, 1024, stg)
            load_w_bf16(ses, wo, w_out_d, 0, 8, 0, 1024, stg)
            kb.dma(lnr.t[:], rows_d[:, _R["ln1_g"][0]:_R["ln1_g"][0] + 2048], writes=[lnr])
            kb.barrier(keep=[wg, wpa, wpb, wo, lnr])
        hT = [kb.sb(pes, f"hT{i}", [128, 8, TT], BF16) for i in range(2)]
        rcT = [kb.sb(pes, f"rcT{i}", [128, 10, TT], BF16) for i in range(2)]
        dnT = [kb.sb(pes, f"dnT{i}", [128, 16, TT], BF16) for i in range(2)]
        xt = [kb.sb(pes, f"xt{i}", [128, NB, 1024], F32) for i in range(2)]
        mg = kb.sbs(pes, "mg", [128, 8, TT], BF16)
        sg = [kb.sb(pes, f"sg{i}", [128, TT], F32) for i in range(4)]
        t1 = [kb.sb(pes, f"t1{i}", [128, TT], F32) for i in range(4)]
        y = kb.sb(pes, "y", [128, NB, 1024], F32)
        xo = [y, y]
        stats = kb.sb(pes, "stats", [128, NB, 2, 6], F32)
        mv = kb.sb(pes, "mv", [128, NB, 2], F32)
        sd = kb.sb(pes, "sd", [128, NB], F32)
        pp = [kb.ps(pes, f"pp{i}", [128, 512], F32) for i in range(8)]

        def load(t):
            i = t % 2
            sl = slice(t * TT, (t + 1) * TT)
            kb.dma(hT[i].t[:], hT_d[:, :, sl].rearrange("c p t -> p c t"), writes=[hT[i]])
            kb.dma(rcT[i].t[:], recT_d[:, :, sl].rearrange("c p t -> p c t"), writes=[rcT[i]])
            kb.dma(dnT[i].t[:], dnT_d[:, :, sl].rearrange("c p t -> p c t"), writes=[dnT[i]])
            kb.dma(xt[i].t[:], x_d[sl, :].rearrange("(b p) f -> p b f", p=128), writes=[xt[i]])

        load(0)
        pi = 0
        for t in range(NT):
            if t + 1 < NT:
                load(t + 1)
            i = t % 2
            h, rc, dn, xx = hT[i], rcT[i], dnT[i], xt[i]
            for m in range(8):
                ms = slice(m * 128, (m + 1) * 128)
                pga = pp[pi % 8]; pya = pp[(pi + 1) % 8]; pgb = pp[(pi + 2) % 8]; pyb = pp[(pi + 3) % 8]; pi += 4
                for k in range(8):
                    kb.op(PE, lambda: nc.tensor.matmul(pga.t[:, 0:TT], lhsT=wg.t[:, k, m * 128:(m + 1) * 128], rhs=h.t[:, k, :], start=(k == 0), stop=(k == 7)),
                          reads=[wg, h], writes=[pga], inc=(k == 7))
                for k in range(10):
                    kb.op(PE, lambda: nc.tensor.matmul(pya.t[:, 0:TT], lhsT=wpa.t[:, k, ms], rhs=rc.t[:, k, :], start=(k == 0), stop=(k == 9)),
                          reads=[wpa, rc], writes=[pya], inc=(k == 9))
                for k in range(8):
                    kb.op(PE, lambda: nc.tensor.matmul(pgb.t[:, 0:TT], lhsT=wg.t[:, k, 1024 + m * 128:1024 + (m + 1) * 128], rhs=h.t[:, k, :], start=(k == 0), stop=(k == 7)),
                          reads=[wg, h], writes=[pgb], inc=(k == 7))
                for k in range(16):
                    kb.op(PE, lambda: nc.tensor.matmul(pyb.t[:, 0:TT], lhsT=wpb.t[:, k, ms], rhs=dn.t[:, k, :], start=(k == 0), stop=(k == 15)),
                          reads=[wpb, dn], writes=[pyb], inc=(k == 15))
                s0, s1, ta, tb = sg[2 * (m % 2)], sg[2 * (m % 2) + 1], t1[2 * (m % 2)], t1[2 * (m % 2) + 1]
                kb.op(ACT, lambda: nc.scalar.activation(out=s0.t[:], in_=pga.t[:, 0:TT], func=AF.Sigmoid), reads=[pga], writes=[s0])
                kb.op(ACT, lambda: nc.scalar.activation(out=s1.t[:], in_=pgb.t[:, 0:TT], func=AF.Sigmoid), reads=[pgb], writes=[s1])
                kb.op(DVE, lambda: nc.vector.tensor_tensor(out=ta.t[:], in0=s0.t[:], in1=pya.t[:, 0:TT], op=ALU.mult), reads=[s0, pya], writes=[ta])
                kb.op(DVE, lambda: nc.vector.tensor_tensor(out=tb.t[:], in0=s1.t[:], in1=pyb.t[:, 0:TT], op=ALU.mult), reads=[s1, pyb], writes=[tb])
                kb.op(POOL, lambda: nc.gpsimd.tensor_tensor(out=mg.t[:, m, :], in0=ta.t[:], in1=tb.t[:], op=ALU.add), reads=[ta, tb], writes=[mg.sub[m]])
            for blk in range(NB):
                for cb in range(2):
                    p = pp[pi % 8]; pi += 1
                    cs = slice(cb * 512, (cb + 1) * 512)
                    for k in range(8):
                        kb.op(PE, lambda: nc.tensor.matmul(p.t[:], lhsT=mg.t[:, k, blk * 128:(blk + 1) * 128], rhs=wo.t[:, k, cs], start=(k == 0), stop=(k == 7)),
                              reads=[mg.sub[k], wo], writes=[p], inc=(k == 7))
                    kb.op(DVE, lambda: nc.vector.tensor_tensor(out=y.t[:, blk, cs], in0=p.t[:], in1=gt1r.t[:, cs], op=ALU.mult), reads=[p, gt1r], writes=[y])
                    kb.op(DVE, lambda: nc.vector.scalar_tensor_tensor(out=y.t[:, blk, cs], in0=xx.t[:, blk, cs], scalar=ALPHA, in1=y.t[:, blk, cs],
                                                                      op0=ALU.mult, op1=ALU.add), reads=[xx, y], writes=[y])
            o = xo[i]
            layer_norm_rows(nc, kb, y, NB, lnr.t[:, 0:1024], lnr.t[:, 1024:2048], lnr, stats, mv, sd, o)
            kb.dma(x1_d[t * TT:(t + 1) * TT, :].rearrange("(b p) f -> p b f", p=128), o.t[:], reads=[o])
        kb.barrier()


def phase_ffn(nc, kb, S, params, pcol, cm, load_w_bf16, ada, gt2r, w_fg_d, w_fu_d, w_fd_d, rows_d, x1_d, out_d):
    PE, ACT, DVE, POOL = kb.pe, kb.act, kb.dve, kb.pool
    TT = 256
    NT = S // TT
    NB = TT // 128
    ident_f = cm.t[:, CM_IDENT * 128:(CM_IDENT + 1) * 128]
    with contextlib.ExitStack() as pes:
        wfg = kb.sb(pes, "wfg", [128, 8, DFF], BF16)
        wfu = kb.sb(pes, "wfu", [128, 8, DFF], BF16)
        wfd = kb.sb(pes, "wfd", [128, 22, 1024], BF16)
        lnr = kb.sb(pes, "lnr", [128, 2048], F32)
        with contextlib.ExitStack() as ses:
            stg = [kb.sb(ses, f"stg{i}", [128, 22, 128], F32) for i in range(2)]
            load_w_bf16(ses, wfg, w_fg_d, 0, 8, 0, DFF, stg)
            load_w_bf16(ses, wfu, w_fu_d, 0, 8, 0, DFF, stg)
            load_w_bf16(ses, wfd, w_fd_d, 0, 22, 0, 1024, stg)
            kb.dma(lnr.t[:], rows_d[:, _R["ln2_g"][0]:_R["ln2_g"][0] + 2048], writes=[lnr])
            kb.barrier(keep=[wfg, wfu, wfd, lnr])
        xt = [kb.sb(pes, f"xt{i}", [128, NB, 1024], F32) for i in range(2)]
        h2 = kb.sbs(pes, "h2", [128, 8, TT], BF16)
        act = kb.sbs(pes, "act", [128, 22, TT], BF16)
        raw = [kb.sb(pes, f"raw{i}", [128, TT + 2], F32) for i in range(3)]
        cv = [kb.sb(pes, f"cv{i}", [128, TT], F32) for i in range(3)]
        tmp = [kb.sb(pes, f"tmp{i}", [128, TT], F32) for i in range(3)]
        halo = kb.sbs(pes, "halo", [128, 22, 2], F32)
        y = kb.sb(pes, "y", [128, NB, 1024], F32)
        xo = [y, y]
        stats = kb.sb(pes, "stats", [128, NB, 2, 6], F32)
        mv = kb.sb(pes, "mv", [128, NB, 2], F32)
        sd = kb.sb(pes, "sd", [128, NB], F32)
        pp = [kb.ps(pes, f"pp{i}", [128, 512], F32) for i in range(8)]
        kb.op(DVE, lambda: nc.vector.memset(halo.t[:], 0.0), writes=halo.sub)
        cwo = _P["ffn_cw"][0]

        def load(t):
            kb.dma(xt[t % 2].t[:], x1_d[t * TT:(t + 1) * TT, :].rearrange("(b p) f -> p b f", p=128), writes=[xt[t % 2]])

        load(0)
        pi = 0
        for t in range(NT):
            if t + 1 < NT:
                load(t + 1)
            xx = xt[t % 2]
            for blk in range(NB):
                for c in range(8):
                    p = pp[pi % 8]; pi += 1
                    kb.op(PE, lambda: nc.tensor.matmul(p.t[:, 0:128], lhsT=xx.t[:, blk, c * 128:(c + 1) * 128], rhs=ident_f, start=True, stop=True), reads=[xx, cm], writes=[p])
                    kb.op(ACT, lambda: nc.scalar.activation(out=h2.t[:, c, blk * 128:(blk + 1) * 128], in_=p.t[:, 0:128], func=AF.Identity,
                                                            scale=ada.t[:, 32 + c:33 + c], bias=ada.t[:, 24 + c:25 + c]), reads=[p, ada], writes=[h2.sub[c]])
            for m in range(22):
                ms = slice(m * 128, (m + 1) * 128)
                pg = pp[pi % 8]; pu = pp[(pi + 1) % 8]; pi += 2
                for k in range(8):
                    kb.op(PE, lambda: nc.tensor.matmul(pg.t[:, 0:TT], lhsT=wfg.t[:, k, ms], rhs=h2.t[:, k, :], start=(k == 0), stop=(k == 7)),
                          reads=[wfg, h2.sub[k]], writes=[pg], inc=(k == 7))
                for k in range(8):
                    kb.op(PE, lambda: nc.tensor.matmul(pu.t[:, 0:TT], lhsT=wfu.t[:, k, ms], rhs=h2.t[:, k, :], start=(k == 0), stop=(k == 7)),
                          reads=[wfu, h2.sub[k]], writes=[pu], inc=(k == 7))
                rw = raw[m % 3]
                c_ = cv[m % 3]
                kb.op(ACT, lambda: nc.scalar.copy(out=rw.t[:, 0:2], in_=halo.t[:, m, :]), reads=[halo.sub[m]], writes=[rw])
                kb.op(ACT, lambda: nc.scalar.copy(out=rw.t[:, 2:TT + 2], in_=pg.t[:, 0:TT]), reads=[pg], writes=[rw])
                kb.op(ACT, lambda: nc.scalar.copy(out=halo.t[:, m, :], in_=rw.t[:, TT:TT + 2]), reads=[rw], writes=[halo.sub[m]])
                kb.op(DVE, lambda: nc.vector.tensor_scalar(out=c_.t[:], in0=rw.t[:, 2:TT + 2], scalar1=params.t[:, cwo + m * 3 + 2:cwo + m * 3 + 3],
                                                           scalar2=pcol("ffn_cb", m), op0=ALU.mult, op1=ALU.add), reads=[rw, params], writes=[c_])
                for kk in range(2):
                    kb.op(DVE, lambda: nc.vector.scalar_tensor_tensor(out=c_.t[:], in0=rw.t[:, kk:kk + TT], scalar=params.t[:, cwo + m * 3 + kk:cwo + m * 3 + kk + 1],
                                                                      in1=c_.t[:], op0=ALU.mult, op1=ALU.add), reads=[rw, params, c_], writes=[c_])
                gelu_tanh(nc, kb, c_.t[:], [c_], act.t[:, m, :], [act.sub[m]], tmp[m % 3], mul_ap=pu.t[:, 0:TT], mul_bufs=[pu])
            for blk in range(NB):
                for cb in range(2):
                    p = pp[pi % 8]; pi += 1
                    cs = slice(cb * 512, (cb + 1) * 512)
                    for k in range(22):
                        kb.op(PE, lambda: nc.tensor.matmul(p.t[:], lhsT=act.t[:, k, blk * 128:(blk + 1) * 128], rhs=wfd.t[:, k, cs], start=(k == 0), stop=(k == 21)),
                              reads=[act.sub[k], wfd], writes=[p], inc=(k == 21))
                    kb.op(DVE, lambda: nc.vector.tensor_tensor(out=y.t[:, blk, cs], in0=p.t[:], in1=gt2r.t[:, cs], op=ALU.mult), reads=[p, gt2r], writes=[y])
                    kb.op(DVE, lambda: nc.vector.scalar_tensor_tensor(out=y.t[:, blk, cs], in0=xx.t[:, blk, cs], scalar=ALPHA, in1=y.t[:, blk, cs],
                                                                      op0=ALU.mult, op1=ALU.add), reads=[xx, y], writes=[y])
            o = xo[t % 2]
            layer_norm_rows(nc, kb, y, NB, lnr.t[:, 0:1024], lnr.t[:, 1024:2048], lnr, stats, mv, sd, o)
            kb.dma(out_d[t * TT:(t + 1) * TT, :].rearrange("(b p) f -> p b f", p=128), o.t[:], reads=[o])
        kb.barrier()


class PSlot:
    def __init__(self, kb, bank, i, name):
        self.b = bank
        self.ap = bank.t[:, i * 128:(i + 1) * 128]
        self.apb = self.ap.bitcast(BF16)[:, 0:128]


def phase_gdn(nc, kb, S, params, pcol, cm, cmb, load_w_bf16, w_in_d, rows_d, hrep_d, hT_d, dnT_d):
    PE, ACT, DVE, POOL = kb.pe, kb.act, kb.dve, kb.pool
    NCH = S // 128
    NT = S // 512
    HVG, HQG = 4, 2
    NG = 16 // HVG
    NCOL = NCH * 16
    cf = lambda i: cm.t[:, i * 128:(i + 1) * 128]
    cb = lambda i: cmb.t[:, i * 128:(i + 1) * 128]
    ident_f, ident_b, L_f, U_f, ones_f, ones_b = cf(CM_IDENT), cb(CM_IDENT), cf(CM_L), cf(CM_U), cf(CM_ONES), cb(CM_ONES)
    mS_b, mIT_b, BD_f, OFF_f = cb(CM_MS), cb(CM_MIT), cf(CM_BD), cf(CM_OFF)
    eps6 = kb.eps_ln.t[:, 1:2]
    with contextlib.ExitStack() as pes:
        tm = {n: kb.sb(pes, "tm_" + n, [128, NCH, 16], F32) for n in ("beta", "negg", "eG", "beG", "kds", "gl")}
        flat = lambda bf: bf.t[:].rearrange("p n h -> p (n h)")
        with contextlib.ExitStack() as ses:
            wab = kb.sb(ses, "wab", [128, 8, 32], BF16)
            stg = [kb.sb(ses, "stg0", [128, 8, 32], F32)]
            hT = [kb.sb(ses, f"hT{i}", [128, 8, 512], BF16) for i in range(2)]
            ab = kb.sb(ses, "ab", [128, NCH, 32], F32)
            hrep = kb.sb(ses, "hrep", [128, 2, NCOL], F32)
            X = kb.sb(ses, "X", [128, NCH, 16], F32)
            Y = kb.sb(ses, "Y", [128, NCH, 16], F32)
            Z = kb.sb(ses, "Z", [128, NCH, 16], F32)
            pp = [kb.ps(ses, f"pp{i}", [128, 512], F32) for i in range(4)]
            load_w_bf16(ses, wab, w_in_d, 0, 8, C_A, 32, stg)
            kb.dma(hrep.t[:], hrep_d[:, :, :], writes=[hrep])
            kb.dma(hT[0].t[:], hT_d[:, :, 0:512].rearrange("c p t -> p c t"), writes=[hT[0]])
            for t in range(NT):
                if t + 1 < NT:
                    kb.dma(hT[(t + 1) % 2].t[:], hT_d[:, :, (t + 1) * 512:(t + 2) * 512].rearrange("c p t -> p c t"), writes=[hT[(t + 1) % 2]])
                h = hT[t % 2]
                p = pp[t % 2]
                for blk in range(4):
                    for k in range(8):
                        kb.op(PE, lambda: nc.tensor.matmul(p.t[:, blk * 32:(blk + 1) * 32], lhsT=h.t[:, k, blk * 128:(blk + 1) * 128], rhs=wab.t[:, k, :],
                                                           start=(k == 0), stop=(k == 7)), reads=[h, wab], writes=[p], inc=(k == 7))
                kb.op(ACT, lambda: nc.scalar.copy(out=ab.t[:, t * 4:(t + 1) * 4, :], in_=p.t[:, 0:128].rearrange("p (b c) -> p b c", c=32)), reads=[p], writes=[ab])
            a_v, b_v = ab.t[:, :, 0:16], ab.t[:, :, 16:32]
            dtb = hrep.t[:, 0, :].rearrange("p (n h) -> p n h", h=16)
            alog = hrep.t[:, 1, :].rearrange("p (n h) -> p n h", h=16)
            kb.op(ACT, lambda: nc.scalar.activation(out=tm["beta"].t[:], in_=b_v, func=AF.Sigmoid), reads=[ab], writes=[tm["beta"]])
            kb.op(DVE, lambda: nc.vector.tensor_tensor(out=X.t[:], in0=a_v, in1=dtb, op=ALU.add), reads=[ab, hrep], writes=[X])
            kb.op(DVE, lambda: nc.vector.tensor_scalar(out=Y.t[:], in0=X.t[:], scalar1=-1.0, scalar2=None, op0=ALU.mult), reads=[X], writes=[Y])
            kb.op(DVE, lambda: nc.vector.tensor_tensor(out=Y.t[:], in0=Y.t[:], in1=X.t[:], op=ALU.max), reads=[X, Y], writes=[Y])
            kb.op(ACT, lambda: nc.scalar.activation(out=Y.t[:], in_=Y.t[:], func=AF.Exp, scale=-1.0), reads=[Y], writes=[Y])
            kb.op(ACT, lambda: nc.scalar.activation(out=Y.t[:], in_=Y.t[:], func=AF.Ln, bias=1.0), reads=[Y], writes=[Y])
            kb.op(DVE, lambda: nc.vector.tensor_scalar(out=X.t[:], in0=X.t[:], scalar1=0.0, scalar2=None, op0=ALU.max), reads=[X], writes=[X])
            kb.op(DVE, lambda: nc.vector.tensor_tensor(out=X.t[:], in0=X.t[:], in1=Y.t[:], op=ALU.add), reads=[X, Y], writes=[X])
            kb.op(ACT, lambda: nc.scalar.activation(out=Z.t[:], in_=alog, func=AF.Exp), reads=[hrep], writes=[Z])
            kb.op(DVE, lambda: nc.vector.tensor_tensor(out=tm["negg"].t[:], in0=X.t[:], in1=Z.t[:], op=ALU.mult), reads=[X, Z], writes=[tm["negg"]])
            ng_f = flat(tm["negg"])
            for c0 in range(0, NCOL, 512):
                w = min(512, NCOL - c0)
                p1, p2 = pp[2], pp[3]
                kb.op(PE, lambda: nc.tensor.matmul(p1.t[:, 0:w], lhsT=L_f, rhs=ng_f[:, c0:c0 + w], start=True, stop=True), reads=[cm, tm["negg"]], writes=[p1])
                kb.op(PE, lambda: nc.tensor.matmul(p2.t[:, 0:w], lhsT=ones_f, rhs=ng_f[:, c0:c0 + w], start=True, stop=True), reads=[cm, tm["negg"]], writes=[p2])
                Xf, Yf = flat(X), flat(Y)
                kb.op(ACT, lambda: nc.scalar.copy(out=Xf[:, c0:c0 + w], in_=p1.t[:, 0:w]), reads=[p1], writes=[X])
                kb.op(ACT, lambda: nc.scalar.activation(out=flat(tm["eG"])[:, c0:c0 + w], in_=p1.t[:, 0:w], func=AF.Exp, scale=-1.0), reads=[p1], writes=[tm["eG"]])
                kb.op(ACT, lambda: nc.scalar.activation(out=flat(tm["gl"])[:, c0:c0 + w], in_=p2.t[:, 0:w], func=AF.Exp, scale=-1.0), reads=[p2], writes=[tm["gl"]])
                kb.op(DVE, lambda: nc.vector.tensor_tensor(out=Yf[:, c0:c0 + w], in0=p2.t[:, 0:w], in1=Xf[:, c0:c0 + w], op=ALU.subtract), reads=[p2, X], writes=[Y])
                kb.op(ACT, lambda: nc.scalar.activation(out=flat(tm["kds"])[:, c0:c0 + w], in_=Yf[:, c0:c0 + w], func=AF.Exp, scale=-1.0), reads=[Y], writes=[tm["kds"]])
            kb.op(DVE, lambda: nc.vector.tensor_tensor(out=tm["beG"].t[:], in0=tm["beta"].t[:], in1=tm["eG"].t[:], op=ALU.mult), reads=[tm["beta"], tm["eG"]], writes=[tm["beG"]])
            kb.barrier()

        STOP = getattr(build_program, "gdn_stop", 0)
        for gi in range(NG if STOP != 1 else 0):
            hv0, hq0 = gi * HVG, gi * HQG
            with contextlib.ExitStack() as ges:
                wq = kb.sb(ges, "wq", [128, 8, HQG * 128], BF16)
                wk = kb.sb(ges, "wk", [128, 8, HQG * 128], BF16)
                wv = kb.sb(ges, "wv", [128, 8, HVG * 128], BF16)
                wz = kb.sb(ges, "wz", [128, 8, HVG * 128], BF16)
                nwr = kb.sb(ges, "nwr", [128, HVG * 128], F32)
                with contextlib.ExitStack() as ses:
                    stg = [kb.sb(ses, f"stg{i}", [128, 8, 512], F32) for i in range(2)]
                    load_w_bf16(ses, wq, w_in_d, 0, 8, C_Q + hq0 * 128, HQG * 128, stg)
                    load_w_bf16(ses, wk, w_in_d, 0, 8, C_K + hq0 * 128, HQG * 128, stg)
                    load_w_bf16(ses, wv, w_in_d, 0, 8, C_V + hv0 * 128, HVG * 128, stg)
                    load_w_bf16(ses, wz, w_in_d, 0, 8, C_Z + hv0 * 128, HVG * 128, stg)
                    kb.dma(nwr.t[:], rows_d[:, _R["nw"][0]:_R["nw"][0] + HVG * 128], writes=[nwr])
                    kb.barrier()
                hT = [kb.sb(ges, f"hT{i}", [128, 8, 512], BF16) for i in range(2)]
                raw = [kb.sb(ges, f"raw{i}", [128, 515], F32) for i in range(3)]
                cv = [kb.sb(ges, f"cv{i}", [128, 512], F32) for i in range(3)]
                NCK = 2 * HQG + HVG
                halo = kb.sb(ges, "halo", [128, NCK, 3], F32)
                qkf = [kb.sb(ges, f"qkf{i}", [128, 512], F32) for i in range(2 * HQG)]
                sqb = [kb.sb(ges, f"sqb{i}", [128, 512], BF16) for i in range(2 * HQG)]
                rn = [kb.sb(ges, f"rn{i}", [128, 512], F32) for i in range(2)]
                qT2 = [kb.sb(ges, f"qT{i}", [128, HQG, 512], BF16) for i in range(2)]
                kT2 = [kb.sb(ges, f"kT{i}", [128, HQG, 512], BF16) for i in range(2)]
                vT2 = [kb.sb(ges, f"vT{i}", [128, HVG, 512], BF16) for i in range(2)]
                zs = [kb.sb(ges, f"zs{i}", [128, 512], F32) for i in range(2)]
                zg2 = [kb.sb(ges, f"zg{i}", [128, 4, HVG * 128], F32) for i in range(2)]
                dno = [kb.sb(ges, f"dno{i}", [128, HVG, 512], BF16) for i in range(2)]
                Sf = [kb.sb(ges, f"Sf{i}", [128, 128], F32) for i in range(HVG)]
                Sb = [kb.sb(ges, f"Sb{i}", [128, 128], BF16) for i in range(HVG)]
                ob = kb.sb(ges, "ob", [128, HVG, 128], F32)
                junk = kb.sb(ges, "junk", [128, 128], F32)
                ss = kb.sb(ges, "ss", [128, HVG], F32)
                rstd = kb.sb(ges, "rstd", [128, HVG], F32)
                ktm = [kb.sb(ges, f"ktm{i}", [128, 128], BF16) for i in range(HQG)]
                KKbd = [kb.sb(ges, f"KKbd{i}", [128, 128], F32) for i in range(HQG)]
                KKoff = [kb.sb(ges, f"KKoff{i}", [128, 128], F32) for i in range(HQG)]
                QKT = [kb.sb(ges, f"QKT{i}", [128, 128], F32) for i in range(HQG)]

                class PB:
                    pass
                pb = []
                for hi in range(HVG):
                    B = PB()
                    for n_ in ("rh", "Ds", "DT", "T1", "o1"):
                        setattr(B, n_, kb.sb(ges, f"{n_}_{hi}", [128, 128], F32))
                    for n_ in ("Abd", "Aoff", "P0", "P1", "M0", "Y1", "Tb", "bek", "vn", "dntm"):
                        setattr(B, n_, kb.sb(ges, f"{n_}_{hi}", [128, 128], BF16))
                    for n_ in ("QKd", "M128", "bv", "kdec", "nwT"):
                        setattr(B, n_, [kb.sb(ges, f"{n_}{i}_{hi}", [128, 128], BF16) for i in range(2)])
                    B.P = [B.P0, B.P1]
                    B.QM = [kb.sb(ges, f"QM{i}_{hi}", [128, 256], BF16) for i in range(2)]
                    pb.append(B)
                banks = [kb.ps(ges, f"bk{i}", [128, 512], F32) for i in range(8)]
                slots = [[PSlot(kb, banks[i], j, f"sl{i}_{j}") for j in range(4)] for i in range(8)]
                hsA = [slots[2 * hi] for hi in range(HVG)]
                hsB = [slots[2 * hi + 1] for hi in range(HVG)]

                def bank_bufs(i):
                    return [banks[i]]

                for b_ in Sf + [halo]:
                    kb.op(DVE, lambda: nc.vector.memset(b_.t[:], 0.0), writes=[b_])
                for b_ in Sb:
                    kb.op(DVE, lambda: nc.vector.memset(b_.t[:], 0.0), writes=[b_])
                cwo = _P["dn_cw"][0]
                chunks = [("q", wq, c, hq0 + c) for c in range(HQG)] + [("k", wk, c, 8 + hq0 + c) for c in range(HQG)] + \
                         [("v", wv, c, 16 + hv0 + c) for c in range(HVG)]

                def load_h(t):
                    kb.dma(hT[t % 2].t[:], hT_d[:, :, t * 512:(t + 1) * 512].rearrange("c p t -> p c t"), writes=[hT[t % 2]])

                bi_ = [0]

                def stageA(t):
                    h = hT[t % 2]
                    qT, kT, vT, zg = qT2[t % 2], kT2[t % 2], vT2[t % 2], zg2[t % 2]
                    for ci, (kind, w, c, gch) in enumerate(chunks):
                        bk = bi_[0] % 8; bi_[0] += 1
                        pt, pbufs = banks[bk].t, bank_bufs(bk)
                        for k in range(8):
                            kb.op(PE, lambda: nc.tensor.matmul(pt[:], lhsT=w.t[:, k, c * 128:(c + 1) * 128], rhs=h.t[:, k, :], start=(k == 0), stop=(k == 7)),
                                  reads=[w, h], writes=pbufs, inc=(k == 7))
                        rw, cvb = raw[ci % 3], cv[ci % 3]
                        kb.op(ACT, lambda: nc.scalar.copy(out=rw.t[:, 0:3], in_=halo.t[:, ci, :]), reads=[halo], writes=[rw])
                        kb.op(ACT, lambda: nc.scalar.copy(out=rw.t[:, 3:515], in_=pt[:]), reads=pbufs, writes=[rw])
                        kb.op(ACT, lambda: nc.scalar.copy(out=halo.t[:, ci, :], in_=rw.t[:, 512:515]), reads=[rw], writes=[halo])
                        kb.op(DVE, lambda: nc.vector.tensor_scalar(out=cvb.t[:], in0=rw.t[:, 3:515], scalar1=params.t[:, cwo + gch * 4 + 3:cwo + gch * 4 + 4],
                                                                   scalar2=None, op0=ALU.mult), reads=[rw, params], writes=[cvb])
                        for kk in range(3):
                            kb.op(DVE, lambda: nc.vector.scalar_tensor_tensor(out=cvb.t[:], in0=rw.t[:, kk:kk + 512], scalar=params.t[:, cwo + gch * 4 + kk:cwo + gch * 4 + kk + 1],
                                                                              in1=cvb.t[:], op0=ALU.mult, op1=ALU.add), reads=[rw, params, cvb], writes=[cvb])
                        if kind == "v":
                            kb.op(ACT, lambda: nc.scalar.activation(out=vT.t[:, c, :], in_=cvb.t[:], func=AF.Silu), reads=[cvb], writes=[vT])
                        else:
                            f = qkf[ci]
                            kb.op(ACT, lambda: nc.scalar.activation(out=f.t[:], in_=cvb.t[:], func=AF.Silu), reads=[cvb], writes=[f])
                            kb.op(POOL, lambda: nc.gpsimd.tensor_tensor(out=sqb[ci].t[:], in0=f.t[:], in1=f.t[:], op=ALU.mult), reads=[f], writes=[sqb[ci]])
                    for blk in range(4):
                        bk = bi_[0] % 8; bi_[0] += 1
                        pt, pbufs = banks[bk].t, bank_bufs(bk)
                        for k in range(8):
                            kb.op(PE, lambda: nc.tensor.matmul(pt[:, 0:HVG * 128], lhsT=h.t[:, k, blk * 128:(blk + 1) * 128], rhs=wz.t[:, k, :], start=(k == 0), stop=(k == 7)),
                                  reads=[h, wz], writes=pbufs, inc=(k == 7))
                        z_ = zs[blk % 2]
                        kb.op(ACT, lambda: nc.scalar.activation(out=z_.t[:, 0:HVG * 128], in_=pt[:, 0:HVG * 128], func=AF.Silu), reads=pbufs, writes=[z_])
                        kb.op(POOL, lambda: nc.gpsimd.tensor_tensor(out=zg.t[:, blk, :], in0=z_.t[:, 0:HVG * 128], in1=nwr.t[:], op=ALU.mult), reads=[z_, nwr], writes=[zg])

                    for ci, (kind, w, c, gch) in enumerate(chunks):
                        if kind == "v":
                            continue
                        f, sq, r_ = qkf[ci], sqb[ci], rn[ci % 2]
                        dst = qT if kind == "q" else kT
                        bk2 = bi_[0] % 8; bi_[0] += 1
                        pt2, pbufs2 = banks[bk2].t, bank_bufs(bk2)
                        kb.op(PE, lambda: nc.tensor.matmul(pt2[:], lhsT=ones_b, rhs=sq.t[:], start=True, stop=True), reads=[cmb, sq], writes=pbufs2)
                        kb.op(ACT, lambda: nc.scalar.activation(out=r_.t[:], in_=pt2[:], func=AF.Ln, bias=eps6), reads=pbufs2 + [kb.eps_ln], writes=[r_])
                        kb.op(ACT, lambda: nc.scalar.activation(out=r_.t[:], in_=r_.t[:], func=AF.Exp, scale=-0.5), reads=[r_], writes=[r_])
                        kb.op(DVE, lambda: nc.vector.scalar_tensor_tensor(out=dst.t[:, c, :], in0=f.t[:], scalar=(128.0 ** -0.5 if kind == "q" else 1.0), in1=r_.t[:],
                                                                          op0=ALU.mult, op1=ALU.mult), reads=[f, r_], writes=[dst])

                def hq_ops(n):
                    t, blk = n // 4, n % 4
                    cs = slice(blk * 128, (blk + 1) * 128)
                    qT, kT = qT2[t % 2], kT2[t % 2]
                    for hq in range(HQG):
                        kc, qc = kT.t[:, hq, cs], qT.t[:, hq, cs]
                        s0, s1, s2 = hsA[2 * hq][0], hsA[2 * hq][1], hsA[2 * hq][2]
                        kb.op(PE, lambda: nc.tensor.transpose(s0.apb, kc, ident_b), reads=[kT, cmb], writes=[s0.b])
                        kb.op(PE, lambda: nc.tensor.matmul(s1.ap, lhsT=kc, rhs=kc, start=True, stop=True), reads=[kT], writes=[s1.b])
                        kb.op(PE, lambda: nc.tensor.matmul(s2.ap, lhsT=kc, rhs=qc, start=True, stop=True), reads=[kT, qT], writes=[s2.b])
                        kb.op(ACT, lambda: nc.scalar.copy(out=ktm[hq].t[:], in_=s0.apb), reads=[s0.b], writes=[ktm[hq]])
                        kb.op(DVE, lambda: nc.vector.tensor_tensor(out=KKbd[hq].t[:], in0=s1.ap, in1=BD_f, op=ALU.mult), reads=[s1.b, cm], writes=[KKbd[hq]])
                        kb.op(DVE, lambda: nc.vector.tensor_tensor(out=KKoff[hq].t[:], in0=s1.ap, in1=OFF_f, op=ALU.mult), reads=[s1.b, cm], writes=[KKoff[hq]])
                        kb.op(ACT, lambda: nc.scalar.copy(out=QKT[hq].t[:], in_=s2.ap), reads=[s2.b], writes=[QKT[hq]])

                def pre_seq(hi, n):
                    t, blk = n // 4, n % 4
                    cs = slice(blk * 128, (blk + 1) * 128)
                    vT = vT2[t % 2]
                    pr = n % 2
                    hh_ = hv0 + hi
                    hq = hi // 2
                    B, sl = pb[hi], hsA[hi]
                    sc = lambda nm: tm[nm].t[:, n, hh_:hh_ + 1]
                    QKd, M128, bv, kdec, nwT = B.QKd[pr], B.M128[pr], B.bv[pr], B.kdec[pr], B.nwT[pr]
                    kb.op(ACT, lambda: nc.scalar.activation(out=B.rh.t[:], in_=U_f, func=AF.Identity, scale=sc("negg")), reads=[cm, tm["negg"]], writes=[B.rh])
                    kb.op(ACT, lambda: nc.scalar.activation(out=B.bek.t[:], in_=ktm[hq].t[:], func=AF.Identity, scale=sc("beG")), reads=[ktm[hq], tm["beG"]], writes=[B.bek])
                    kb.op(ACT, lambda: nc.scalar.activation(out=kdec.t[:], in_=ktm[hq].t[:], func=AF.Identity, scale=sc("kds")), reads=[ktm[hq], tm["kds"]], writes=[kdec])
                    yield
                    kb.op(PE, lambda: nc.tensor.matmul(sl[0].ap, lhsT=L_f, rhs=B.rh.t[:], start=True, stop=False), reads=[cm, B.rh], writes=[sl[0].b], inc=False)
                    kb.op(PE, lambda: nc.tensor.matmul(sl[0].ap, lhsT=ident_b, rhs=mS_b, start=False, stop=True), reads=[cmb], writes=[sl[0].b], inc=False)
                    kb.op(PE, lambda: nc.tensor.matmul(sl[1].ap, lhsT=B.rh.t[:], rhs=L_f, start=True, stop=False), reads=[cm, B.rh], writes=[sl[1].b], inc=False)
                    kb.op(PE, lambda: nc.tensor.matmul(sl[1].ap, lhsT=ident_b, rhs=mIT_b, start=False, stop=True), reads=[cmb], writes=[sl[1].b], inc=False)
                    kb.op(PE, lambda: nc.tensor.transpose(sl[3].apb, vT.t[:, hi, cs], ident_b), reads=[vT, cmb], writes=[sl[3].b])
                    kb.op(ACT, lambda: nc.scalar.activation(out=B.Ds.t[:], in_=sl[0].ap, func=AF.Exp, scale=-1.0), reads=[sl[0].b], writes=[B.Ds])
                    kb.op(ACT, lambda: nc.scalar.activation(out=B.DT.t[:], in_=sl[1].ap, func=AF.Exp, scale=-1.0), reads=[sl[1].b], writes=[B.DT])
                    kb.op(DVE, lambda: nc.vector.tensor_scalar(out=bv.t[:], in0=sl[3].apb, scalar1=sc("beta"), scalar2=None, op0=ALU.mult), reads=[sl[3].b, tm["beta"]], writes=[bv])
                    yield
                    kb.op(DVE, lambda: nc.vector.tensor_scalar(out=B.T1.t[:], in0=B.Ds.t[:], scalar1=sc("beta"), scalar2=None, op0=ALU.mult), reads=[B.Ds, tm["beta"]], writes=[B.T1])
                    kb.op(DVE, lambda: nc.vector.tensor_tensor(out=B.Abd.t[:], in0=B.T1.t[:], in1=KKbd[hq].t[:], op=ALU.mult), reads=[B.T1, KKbd[hq]], writes=[B.Abd])
                    kb.op(DVE, lambda: nc.vector.tensor_tensor(out=B.Aoff.t[:], in0=B.T1.t[:], in1=KKoff[hq].t[:], op=ALU.mult), reads=[B.T1, KKoff[hq]], writes=[B.Aoff])
                    kb.op(DVE, lambda: nc.vector.tensor_tensor(out=QKd.t[:], in0=QKT[hq].t[:], in1=B.DT.t[:], op=ALU.mult), reads=[QKT[hq], B.DT], writes=[QKd])
                    yield
                    bankA = banks[2 * hi]
                    kb.op(PE, lambda: nc.tensor.transpose(sl[2].apb, B.Abd.t[:], ident_b), reads=[B.Abd, cmb], writes=[sl[2].b])
                    kb.op(ACT, lambda: nc.scalar.copy(out=B.QM[0].t[:, 0:128], in_=sl[2].apb), reads=[sl[2].b], writes=[B.QM[0]])
                    kb.op(DVE, lambda: nc.vector.tensor_tensor(out=B.QM[0].t[:, 128:256], in0=ident_f, in1=sl[2].apb, op=ALU.subtract), reads=[cm, sl[2].b], writes=[B.QM[0]])
                    kb.op(DVE, lambda: nc.vector.tensor_tensor(out=B.QM[1].t[:, 128:256], in0=ident_f, in1=sl[2].apb, op=ALU.subtract), reads=[cm, sl[2].b], writes=[B.QM[1]])
                    yield
                    P = B.Abd
                    for lv in range(1, 6):
                        cur, nxt, Pn = B.QM[(lv - 1) % 2], B.QM[lv % 2], B.P[lv % 2]
                        lo, hi_ = (0, 128) if lv == 1 else ((0, 256) if lv < 5 else (128, 256))
                        kb.op(PE, lambda: nc.tensor.matmul(sl[2].ap, lhsT=cur.t[:, 0:128], rhs=P.t[:], start=True, stop=True), reads=[P, cur], writes=[sl[2].b], inc=False)
                        kb.op(PE, lambda: nc.tensor.matmul(bankA.t[:, lo:hi_], lhsT=P.t[:], rhs=cur.t[:, lo:hi_], start=True, stop=True), reads=[P, cur], writes=[bankA])
                        kb.op(ACT, lambda: nc.scalar.copy(out=Pn.t[:], in_=sl[2].ap), reads=[bankA], writes=[Pn])
                        if lv < 5:
                            kb.op(ACT, lambda: nc.scalar.copy(out=nxt.t[:, 0:128], in_=bankA.t[:, 0:128]), reads=[bankA], writes=[nxt])
                        if lv >= 2:
                            kb.op(DVE, lambda: nc.vector.tensor_tensor(out=nxt.t[:, 128:256], in0=cur.t[:, 128:256], in1=bankA.t[:, 128:256], op=ALU.add), reads=[cur, bankA], writes=[nxt])
                        yield
                        P = Pn
                    M4 = B.QM[1]
                    kb.op(PE, lambda: nc.tensor.matmul(sl[0].ap, lhsT=P.t[:], rhs=M4.t[:, 128:256], start=True, stop=True), reads=[P, M4], writes=[bankA])
                    kb.op(DVE, lambda: nc.vector.tensor_tensor(out=B.M0.t[:], in0=M4.t[:, 128:256], in1=sl[0].ap, op=ALU.add), reads=[M4, bankA], writes=[B.M0])
                    yield
                    M = B.M0
                    kb.op(PE, lambda: nc.tensor.matmul(sl[2].ap, lhsT=B.Aoff.t[:], rhs=M.t[:], start=True, stop=True), reads=[B.Aoff, M], writes=[bankA], inc=False)
                    kb.op(PE, lambda: nc.tensor.transpose(sl[3].apb, M.t[:], ident_b), reads=[M, cmb], writes=[bankA])
                    kb.op(ACT, lambda: nc.scalar.copy(out=B.Y1.t[:], in_=sl[2].ap), reads=[bankA], writes=[B.Y1])
                    kb.op(DVE, lambda: nc.vector.tensor_copy(out=B.Tb.t[:], in_=sl[3].apb), reads=[bankA], writes=[B.Tb])
                    yield
                    kb.op(PE, lambda: nc.tensor.matmul(sl[0].ap, lhsT=B.Tb.t[:], rhs=B.Y1.t[:], start=True, stop=True), reads=[B.Tb, B.Y1], writes=[bankA])
                    kb.op(DVE, lambda: nc.vector.tensor_tensor(out=M128.t[:], in0=M.t[:], in1=sl[0].ap, op=ALU.subtract), reads=[M, bankA], writes=[M128])
                    yield
                    kb.op(PE, lambda: nc.tensor.matmul(sl[1].ap, lhsT=B.bek.t[:], rhs=M128.t[:], start=True, stop=True), reads=[B.bek, M128], writes=[sl[1].b])
                    kb.op(ACT, lambda: nc.scalar.activation(out=nwT.t[:], in_=sl[1].ap, func=AF.Identity, scale=-1.0), reads=[sl[1].b], writes=[nwT])
                    yield

                def post_seq(hi, n):
                    t, blk = n // 4, n % 4
                    cs = slice(blk * 128, (blk + 1) * 128)
                    qT = qT2[t % 2]
                    pr = n % 2
                    hh_ = hv0 + hi
                    hq = hi // 2
                    B, sl = pb[hi], hsB[hi]
                    sc = lambda nm: tm[nm].t[:, n, hh_:hh_ + 1]
                    QKd, M128, bv, kdec, nwT = B.QKd[pr], B.M128[pr], B.bv[pr], B.kdec[pr], B.nwT[pr]
                    kb.op(PE, lambda: nc.tensor.matmul(sl[0].ap, lhsT=M128.t[:], rhs=bv.t[:], start=True, stop=False), reads=[M128, bv], writes=[sl[0].b], inc=False)
                    kb.op(PE, lambda: nc.tensor.matmul(sl[0].ap, lhsT=nwT.t[:], rhs=Sb[hi].t[:], start=False, stop=True), reads=[nwT, Sb[hi]], writes=[sl[0].b], inc=False)
                    kb.op(PE, lambda: nc.tensor.matmul(sl[1].ap, lhsT=qT.t[:, hq, cs], rhs=Sb[hi].t[:], start=True, stop=True), reads=[qT, Sb[hi]], writes=[sl[1].b])
                    kb.op(ACT, lambda: nc.scalar.copy(out=B.vn.t[:], in_=sl[0].ap), reads=[sl[0].b], writes=[B.vn])
                    kb.op(ACT, lambda: nc.scalar.activation(out=B.o1.t[:], in_=sl[1].ap, func=AF.Identity, scale=sc("eG")), reads=[sl[1].b, tm["eG"]], writes=[B.o1])
                    yield
                    kb.op(PE, lambda: nc.tensor.matmul(sl[2].ap, lhsT=kdec.t[:], rhs=B.vn.t[:], start=True, stop=True), reads=[kdec, B.vn], writes=[sl[2].b], inc=False)
                    kb.op(PE, lambda: nc.tensor.matmul(sl[3].ap, lhsT=QKd.t[:], rhs=B.vn.t[:], start=True, stop=True), reads=[QKd, B.vn], writes=[sl[3].b])
                    kb.op(DVE, lambda: nc.vector.scalar_tensor_tensor(out=Sf[hi].t[:], in0=Sf[hi].t[:], scalar=sc("gl"), in1=sl[2].ap, op0=ALU.mult, op1=ALU.add),
                          reads=[Sf[hi], tm["gl"], sl[2].b], writes=[Sf[hi]])
                    kb.op(DVE, lambda: nc.vector.tensor_tensor(out=ob.t[:, hi, :], in0=B.o1.t[:], in1=sl[3].ap, op=ALU.add), reads=[B.o1, sl[3].b], writes=[ob])
                    yield
                    kb.op(ACT, lambda: nc.scalar.copy(out=Sb[hi].t[:], in_=Sf[hi].t[:]), reads=[Sf[hi]], writes=[Sb[hi]])
                    yield

                def norm_store(n):
                    t, blk = n // 4, n % 4
                    cs = slice(blk * 128, (blk + 1) * 128)
                    zg, dn_o = zg2[t % 2], dno[t % 2]
                    for hi in range(HVG):
                        kb.op(ACT, lambda: nc.scalar.activation(out=junk.t[:], in_=ob.t[:, hi, :], func=AF.Square, accum_out=ss.t[:, hi:hi + 1]), reads=[ob], writes=[junk, ss])
                    kb.op(ACT, lambda: nc.scalar.activation(out=rstd.t[:], in_=ss.t[:], func=AF.Ln, scale=1.0 / 128.0, bias=eps6), reads=[ss, kb.eps_ln], writes=[rstd])
                    kb.op(ACT, lambda: nc.scalar.activation(out=rstd.t[:], in_=rstd.t[:], func=AF.Exp, scale=-0.5), reads=[rstd], writes=[rstd])
                    for hi in range(HVG):
                        B, sl = pb[hi], hsB[hi]
                        kb.op(DVE, lambda: nc.vector.scalar_tensor_tensor(out=B.dntm.t[:], in0=ob.t[:, hi, :], scalar=rstd.t[:, hi:hi + 1], in1=zg.t[:, blk, hi * 128:(hi + 1) * 128],
                                                                          op0=ALU.mult, op1=ALU.mult), reads=[ob, rstd, zg], writes=[B.dntm])
                        kb.op(PE, lambda: nc.tensor.transpose(sl[0].apb, B.dntm.t[:], ident_b), reads=[B.dntm, cmb], writes=[sl[0].b])
                        kb.op(ACT, lambda: nc.scalar.copy(out=dn_o.t[:, hi, cs], in_=sl[0].apb), reads=[sl[0].b], writes=[dn_o])
                    if blk == 3:
                        kb.dma(dnT_d[hv0:hv0 + HVG, :, t * 512:(t + 1) * 512].rearrange("c p t -> p c t"), dn_o.t[:], reads=[dn_o])

                def run(gens):
                    while gens:
                        for g in list(gens):
                            try:
                                next(g)
                            except StopIteration:
                                gens.remove(g)

                load_h(0)
                for t in range(NT):
                    if t + 1 < NT:
                        load_h(t + 1)
                    stageA(t)
                    for blk in range(4):
                        n = t * 4 + blk
                        hq_ops(n)
                        run([pre_seq(hi, n) for hi in range(HVG)])
                        run([post_seq(hi, n) for hi in range(HVG)])
                        norm_store(n)
                kb.barrier()


def _consts():
    i = np.arange(128)
    ident = np.eye(128, dtype=np.float32)
    L = (i[:, None] <= i[None, :]).astype(np.float32)
    U = (i[:, None] > i[None, :]).astype(np.float32)
    mS = np.where(i[:, None] > i[None, :], 0.0, BIG).astype(np.float32)
    mIT = np.where(i[None, :] >= i[:, None], 0.0, BIG).astype(np.float32)
    blk = i // 64
    BD = (blk[:, None] == blk[None, :]).astype(np.float32)
    OFF = ((blk[:, None] == 1) & (blk[None, :] == 0)).astype(np.float32)
    ones = np.ones((128, 128), np.float32)
    return np.concatenate([ident, L, U, mS, mIT, BD, OFF, ones], axis=1)


def _fm(v, n):
    return np.ascontiguousarray(np.asarray(v, np.float32).reshape(n, 128).T)


def _band(w):
    full = np.zeros((DRNN, DRNN), np.float32)
    for b in range(16):
        full[b * 80:(b + 1) * 80, b * 80:(b + 1) * 80] = w[b]
    out = np.zeros((128, 10, 3, 128), np.float32)
    for m in range(10):
        for d in range(3):
            k = m + d - 1
            if 0 <= k < 10:
                out[:, m, d, :] = full[k * 128:(k + 1) * 128, m * 128:(m + 1) * 128]
    return out


def make_in_maps(inp, S, batches):
    l = 0
    f = lambda a: np.ascontiguousarray(np.asarray(a, np.float32))
    NCH = S // 128
    rows = np.zeros((128, NR), np.float32)
    b_ada = f(inp["b_ada"][l])
    def setr(name, v):
        o, w = _R[name]
        rows[:, o:o + w] = np.asarray(v, np.float32)[None, :]
    setr("b_gt1", b_ada[2048:3072]); setr("b_gt2", b_ada[5120:6144])
    setr("ln1_g", inp["ln1_g"][l]); setr("ln1_b", inp["ln1_b"][l]); setr("ln2_g", inp["ln2_g"][l]); setr("ln2_b", inp["ln2_b"][l])
    setr("nw", np.tile(f(inp["dn_norm_w"][l]), 16))
    hrep = np.zeros((128, 2, NCH * 16), np.float32)
    hrep[:, 0, :] = np.tile(f(inp["dn_dt_bias"][l]), NCH)[None, :]
    hrep[:, 1, :] = np.tile(f(inp["dn_a_log"][l]), NCH)[None, :]
    cmat = _consts()
    shared = dict(
        w_ada=f(inp["w_ada"][l]), w_in=f(inp["w_in"][l]), gate_a=_band(f(inp["rg_w_a"][l])), gate_x=_band(f(inp["rg_w_x"][l])),
        w_proj_a=f(inp["w_proj_a"][l]), w_proj_b=f(inp["w_proj_b"][l]), w_out=f(inp["w_out"][l]),
        ffn_w_gate=f(inp["ffn_w_gate"][l]), ffn_w_up=f(inp["ffn_w_up"][l]), ffn_w_down=f(inp["ffn_w_down"][l]),
        rows=rows, cmat=cmat, hrep=hrep)
    maps = []
    for b in batches:
        params = np.zeros((128, NP), np.float32)
        def setp(name, v):
            o, w = _P[name]
            params[:, o:o + w] = v
        setp("b_ada", _fm(b_ada, 48))
        setp("c", _fm(inp["c"][b], 8))
        setp("rg_cw", np.stack([_fm(inp["rg_conv_w"][l][k], 10) for k in range(4)], axis=2).reshape(128, 40))
        setp("rg_cb", _fm(inp["rg_conv_b"][l], 10)); setp("rg_ba", _fm(inp["rg_b_a"][l], 10)); setp("rg_bx", _fm(inp["rg_b_x"][l], 10))
        setp("rg_lam", _fm(inp["rg_lambda"][l], 10))
        setp("dn_cw", np.stack([_fm(inp["dn_conv_w"][l][k], 32) for k in range(4)], axis=2).reshape(128, 128))
        setp("ffn_cw", np.stack([_fm(inp["ffn_conv_w"][l][k], 22) for k in range(3)], axis=2).reshape(128, 66))
        setp("ffn_cb", _fm(inp["ffn_conv_b"][l], 22))
        xb = f(inp["x"][b])
        m = dict(shared)
        m["x"] = xb
        m["xT"] = np.ascontiguousarray(xb.T.reshape(8, 128, S))
        m["params"] = params
        maps.append(m)
    return maps


_NC_CACHE = {}


def kernel(**inputs):
    S = inputs["x"].shape[1]
    B = inputs["x"].shape[0]
    if S not in _NC_CACHE:
        _NC_CACHE[S] = build_program(S)
    nc = _NC_CACHE[S]
    maps = make_in_maps(inputs, S, list(range(B)))
    res = run_bass_kernel_spmd(nc, maps, core_ids=list(range(B)))
    return np.stack([np.asarray(r["out"], np.float32) for r in res.results], axis=0)
```

```python
import contextlib
import numpy as np
import concourse.bass as bass
import concourse.mybir as mybir
from concourse.bass_utils import run_bass_kernel_spmd

F32 = mybir.dt.float32
BF16 = mybir.dt.bfloat16
AF = mybir.ActivationFunctionType
ALU = mybir.AluOpType

D = 1024
DRNN = 1280
DFF = 2816
DIN = 10784
NHV = 16
C_XR, C_GR, C_Q, C_K, C_V, C_Z, C_A, C_B, C_GA, C_GB = 0, 1280, 2560, 3584, 4608, 6656, 8704, 8720, 8736, 9760
ALPHA = 2.0 ** 0.25
BIG = 32768.0
GELU_C = 1.5957691216057308

_P = {}
_off = 0
for _n, _w in [("b_ada", 48), ("c", 8), ("rg_cw", 40), ("rg_cb", 10), ("rg_ba", 10), ("rg_bx", 10), ("rg_lam", 10),
               ("dn_cw", 128), ("ffn_cw", 66), ("ffn_cb", 22)]:
    _P[_n] = (_off, _w)
    _off += _w
NP = _off
_R = {}
_off = 0
for _n, _w in [("b_gt1", 1024), ("b_gt2", 1024), ("ln1_g", 1024), ("ln1_b", 1024), ("ln2_g", 1024), ("ln2_b", 1024),
               ("nw", 2048)]:
    _R[_n] = (_off, _w)
    _off += _w
NR = _off
CM_IDENT, CM_L, CM_U, CM_MS, CM_MIT, CM_BD, CM_OFF, CM_ONES = range(8)
NCM = 8


class Buf:
    def __init__(self, kb, name, t):
        self.kb = kb
        self.name = name
        self.t = t
        self.w = None
        self.r = []
        self.lsem = None
        self.ssem = None
        self.excl = False

    def add_reader(self, tok):
        for i, (s, v) in enumerate(self.r):
            if s == tok[0]:
                if v < tok[1]:
                    self.r[i] = tok
                return
        self.r.append(tok)


class EngW:
    LIMIT = 24000

    def __init__(self, kb, eng, name):
        self.kb = kb
        self.eng = eng
        self.name = name
        self.sem = None
        self.cnt = 0
        self.seen = {}
        self.pending = []

    def wait(self, toks):
        best = {}
        for s, v in toks:
            if best.get(s, 0) < v:
                best[s] = v
        for s, v in best.items():
            if self.seen.get(s, 0) >= v:
                continue
            self.eng.wait_ge(self.kb.sems[s], v)
            self.seen[s] = v

    def bump(self, inst):
        if self.sem is None or self.cnt >= self.LIMIT:
            self.sem = self.kb.new_sem(f"e_{self.name}_{len(self.kb.sems)}")
            self.cnt = 0
        inst.then_inc(self.kb.sems[self.sem], 1)
        self.cnt += 1
        return (self.sem, self.cnt)

    def cur(self):
        return None if self.sem is None else (self.sem, self.cnt)


class KB:
    def __init__(self, nc, es):
        self.nc = nc
        self.es = es
        self.sems = []
        self.semcnt = []
        self.free_dma_sems = []
        self.pe = EngW(self, nc.tensor, "pe")
        self.act = EngW(self, nc.scalar, "act")
        self.dve = EngW(self, nc.vector, "dve")
        self.pool = EngW(self, nc.gpsimd, "pool")
        self.sp = EngW(self, nc.sync, "sp")
        self.engs = [self.pe, self.act, self.dve, self.pool, self.sp]
        self.dma_toks = {}
        self.phase_bufs = []
        self.n_inst = 0

    def new_sem(self, name):
        h = self.es.enter_context(self.nc.semaphore(name))
        self.sems.append(h)
        self.semcnt.append(0)
        return len(self.sems) - 1

    def dma_sem(self):
        if self.free_dma_sems:
            return self.free_dma_sems.pop()
        return self.new_sem(f"d_{len(self.sems)}")

    def sb(self, pes, name, shape, dt):
        self.n_alloc = getattr(self, "n_alloc", 0) + 1
        t = pes.enter_context(self.nc.sbuf_tensor(f"sb{self.n_alloc}_{name}", list(shape), dt))
        b = Buf(self, name, t)
        self.phase_bufs.append(b)
        return b

    def sbs(self, pes, name, shape, dt):
        b = self.sb(pes, name, shape, dt)
        b.sub = [Buf(self, f"{name}[{i}]", b.t) for i in range(shape[1])]
        self.phase_bufs.extend(b.sub)
        return b

    def ps(self, pes, name, shape, dt):
        self.n_alloc = getattr(self, "n_alloc", 0) + 1
        t = pes.enter_context(self.nc.psum_tensor(f"ps{self.n_alloc}_{name}", list(shape), dt))
        b = Buf(self, name, t)
        b.excl = True
        self.phase_bufs.append(b)
        return b

    def op(self, E, fn, reads=(), writes=(), inc=True):
        ex = [b for b in reads if b.excl]
        if ex:
            reads = [b for b in reads if not b.excl]
            writes = list(writes) + [b for b in ex if b not in writes]
        toks = []
        for b in reads:
            if b.w is not None:
                toks.append(b.w)
        for b in writes:
            if b.w is not None:
                toks.append(b.w)
            toks.extend(b.r)
        E.wait(toks)
        inst = fn()
        self.n_inst += 1
        if inc:
            tok = E.bump(inst)
            sets = E.pending + [(reads, writes)]
            E.pending = []
            for rd, wr in sets:
                for b in rd:
                    b.add_reader(tok)
                for b in wr:
                    b.w = tok
                    b.r = []
        else:
            E.pending.append((reads, writes))
        return inst

    def dma(self, out_ap, in_ap, reads=(), writes=(), Q=None):
        Q = Q or self.sp
        toks = []
        for b in reads:
            if b.w is not None:
                toks.append(b.w)
        for b in writes:
            if b.w is not None:
                toks.append(b.w)
            toks.extend(b.r)
        Q.wait(toks)
        inst = Q.eng.dma_start(out=out_ap, in_=in_ap)
        self.n_inst += 1
        if writes:
            tb = writes[0]
            if tb.lsem is None:
                tb.lsem = self.dma_sem()
            s = tb.lsem
        else:
            tb = reads[0]
            if tb.ssem is None:
                tb.ssem = self.dma_sem()
            s = tb.ssem
        self.semcnt[s] += 16
        inst.then_inc(self.sems[s], 16)
        tok = (s, self.semcnt[s])
        self.dma_toks[s] = tok
        for b in writes:
            b.w = tok
            b.r = []
        for b in reads:
            b.add_reader(tok)
        return tok

    def barrier(self, keep=()):
        toks = list(self.dma_toks.values())
        for E in self.engs:
            assert not E.pending
            c = E.cur()
            if c is not None:
                toks.append(c)
        for E in self.engs:
            E.wait(toks)
        for b in self.phase_bufs:
            if b.lsem is not None:
                self.free_dma_sems.append(b.lsem)
                b.lsem = None
            if b.ssem is not None:
                self.free_dma_sems.append(b.ssem)
                b.ssem = None
        self.dma_toks = {}

    def final_wait(self):
        toks = list(self.dma_toks.values())
        self.sp.wait(toks)


def _cm(cm, idx, dt=None):
    return cm.t[:, idx * 128:(idx + 1) * 128]


def build_program(S, debug=False):
    assert S % 512 == 0
    NT = S // 512
    NCH = S // 128
    nc = bass.Bass("TRN2", target_bir_lowering=False)
    okind = "ExternalOutput" if debug else "Internal"

    def din(name, shape, dt=F32):
        return nc.dram_tensor(name, list(shape), dt, kind="ExternalInput").ap()

    xT_d = din("xT", [8, 128, S])
    x_d = din("x", [S, D])
    w_ada_d = din("w_ada", [D, 6 * D])
    w_in_d = din("w_in", [D, DIN])
    gate_a_d = din("gate_a", [128, 10, 3, 128])
    gate_x_d = din("gate_x", [128, 10, 3, 128])
    w_pa_d = din("w_proj_a", [DRNN, D])
    w_pb_d = din("w_proj_b", [2048, D])
    w_out_d = din("w_out", [D, D])
    w_fg_d = din("ffn_w_gate", [D, DFF])
    w_fu_d = din("ffn_w_up", [D, DFF])
    w_fd_d = din("ffn_w_down", [DFF, D])
    params_d = din("params", [128, NP])
    rows_d = din("rows", [128, NR])
    cmat_d = din("cmat", [128, NCM * 128])
    hrep_d = din("hrep", [128, 2, NCH * 16])
    out_d = nc.dram_tensor("out", [S, D], F32, kind="ExternalOutput").ap()
    hT_d = nc.dram_tensor("hT_s", [8, 128, S], BF16, kind=okind).ap()
    recT_d = nc.dram_tensor("recT_s", [10, 128, S], BF16, kind=okind).ap()
    dnT_d = nc.dram_tensor("dnT_s", [16, 128, S], BF16, kind=okind).ap()
    x1_d = nc.dram_tensor("x1_s", [S, D], F32, kind=okind).ap()

    with contextlib.ExitStack() as es:
        kb = KB(nc, es)
        PE, ACT, DVE, POOL = kb.pe, kb.act, kb.dve, kb.pool
        params = kb.sb(es, "params", [128, NP], F32)
        cm = kb.sb(es, "cm", [128, NCM * 128], F32)
        cmb = kb.sb(es, "cmb", [128, NCM * 128], BF16)
        ada = kb.sb(es, "ada", [128, 48], F32)
        gt1r = kb.sb(es, "gt1r", [128, 1024], F32)
        gt2r = kb.sb(es, "gt2r", [128, 1024], F32)
        kb.dma(params.t[:], params_d[:, :], writes=[params])
        kb.dma(cm.t[:], cmat_d[:, :], writes=[cm])
        kb.op(ACT, lambda: nc.scalar.copy(out=cmb.t[:], in_=cm.t[:]), reads=[cm], writes=[cmb])

        def pcol(name, j=0, n=1):
            o, w = _P[name]
            return params.t[:, o + j:o + j + n]

        ident_f = cm.t[:, CM_IDENT * 128:(CM_IDENT + 1) * 128]
        ident_b = cmb.t[:, CM_IDENT * 128:(CM_IDENT + 1) * 128]
        ones_f = cm.t[:, CM_ONES * 128:(CM_ONES + 1) * 128]
        ones_b = cmb.t[:, CM_ONES * 128:(CM_ONES + 1) * 128]

        def load_w_bf16(pes, dst, dram, r0, nk, c0, ncols, stg, dst_c0=0):
            PW = stg[0].t.shape[-1]
            i = load_w_bf16.cnt
            for cc in range(0, ncols, PW):
                w = min(PW, ncols - cc)
                st = stg[i % len(stg)]
                src = dram[r0:r0 + nk * 128, c0 + cc:c0 + cc + w].rearrange("(k p) c -> p k c", p=128)
                kb.dma(st.t[:, 0:nk, 0:w], src, writes=[st])
                o = dst.t[:, 0:nk, dst_c0 + cc:dst_c0 + cc + w]
                if i % 2 == 0:
                    kb.op(ACT, lambda o=o, st=st, w=w: nc.scalar.copy(out=o, in_=st.t[:, 0:nk, 0:w]), reads=[st], writes=[dst])
                else:
                    kb.op(DVE, lambda o=o, st=st, w=w: nc.vector.tensor_copy(out=o, in_=st.t[:, 0:nk, 0:w]), reads=[st], writes=[dst])
                i += 1
            load_w_bf16.cnt = i
        load_w_bf16.cnt = 0

        with contextlib.ExitStack() as pes:
            scf = kb.sb(pes, "scf", [128, 8], F32)
            screp = kb.sb(pes, "screp", [128, 8, 128], F32)
            wst = [kb.sb(pes, f"wst{i}", [128, 8, 1024], F32) for i in range(2)]
            rows01 = kb.sb(pes, "rows01", [128, 2048], F32)
            pa = kb.ps(pes, "pa", [128, 512], F32)
            pb = [kb.ps(pes, f"pb{i}", [128, 512], F32) for i in range(2)]
            kb.dma(rows01.t[:], rows_d[:, 0:2048], writes=[rows01])
            kb.op(ACT, lambda: nc.scalar.activation(out=scf.t[:], in_=pcol("c", 0, 8), func=AF.Silu), reads=[params], writes=[scf])
            for k in range(8):
                kb.op(DVE, lambda k=k: nc.vector.tensor_copy(out=screp.t[:, k, :], in_=scf.t[:, k:k + 1].to_broadcast([128, 128])),
                      reads=[scf], writes=[screp])
            for jb in range(6):
                st = wst[jb % 2]
                kb.dma(st.t[:], w_ada_d[:, jb * 1024:(jb + 1) * 1024].rearrange("(k p) c -> p k c", p=128), writes=[st])
                if jb in (2, 5):
                    dst = gt1r if jb == 2 else gt2r
                    for cb in range(2):
                        for k in range(8):
                            kb.op(PE, lambda k=k, cb=cb, st=st: nc.tensor.matmul(pb[cb].t[:], lhsT=screp.t[:, k, :], rhs=st.t[:, k, cb * 512:(cb + 1) * 512],
                                                                               start=(k == 0), stop=(k == 7)),
                                  reads=[screp, st], writes=[pb[cb]], inc=(k == 7))
                        ro = (0 if jb == 2 else 1024) + cb * 512
                        kb.op(DVE, lambda cb=cb, ro=ro, dst=dst: nc.vector.scalar_tensor_tensor(
                            out=dst.t[:, cb * 512:(cb + 1) * 512], in0=pb[cb].t[:], scalar=1.0, in1=rows01.t[:, ro:ro + 512],
                            op0=ALU.add, op1=ALU.add), reads=[pb[cb], rows01], writes=[dst])
                for j in range(8):
                    col = jb * 8 + j
                    for k in range(8):
                        kb.op(PE, lambda k=k, j=j, col=col, st=st: nc.tensor.matmul(pa.t[:, col:col + 1], lhsT=st.t[:, k, j * 128:(j + 1) * 128],
                                                                                 rhs=scf.t[:, k:k + 1], start=(k == 0), stop=(k == 7)),
                              reads=[scf, st], writes=[pa], inc=(k == 7))
            kb.op(DVE, lambda: nc.vector.tensor_tensor(out=ada.t[:], in0=pa.t[:, 0:48], in1=pcol("b_ada", 0, 48), op=ALU.add),
                  reads=[pa, params], writes=[ada])
            for o in (8, 32):
                kb.op(DVE, lambda o=o: nc.vector.tensor_scalar(out=ada.t[:, o:o + 8], in0=ada.t[:, o:o + 8], scalar1=1.0, scalar2=None, op0=ALU.add),
                      reads=[ada], writes=[ada])
            xin = [kb.sb(pes, f"xin{i}", [128, 8, 512], F32) for i in range(2)]
            hto = [kb.sb(pes, f"hto{i}", [128, 8, 512], BF16) for i in range(2)]
            for t in range(NT):
                xi, ho = xin[t % 2], hto[t % 2]
                kb.dma(xi.t[:], xT_d[:, :, t * 512:(t + 1) * 512].rearrange("c p t -> p c t"), writes=[xi])
                for c in range(8):
                    if c % 2 == 0:
                        kb.op(ACT, lambda c=c, xi=xi, ho=ho: nc.scalar.activation(out=ho.t[:, c, :], in_=xi.t[:, c, :], func=AF.Identity,
                                                                              scale=ada.t[:, 8 + c:9 + c], bias=ada.t[:, c:c + 1]),
                              reads=[xi, ada], writes=[ho])
                    else:
                        kb.op(DVE, lambda c=c, xi=xi, ho=ho: nc.vector.tensor_scalar(out=ho.t[:, c, :], in0=xi.t[:, c, :], scalar1=ada.t[:, 8 + c:9 + c],
                                                                                 scalar2=ada.t[:, c:c + 1], op0=ALU.mult, op1=ALU.add),
                              reads=[xi, ada], writes=[ho])
                kb.dma(hT_d[:, :, t * 512:(t + 1) * 512].rearrange("c p t -> p c t"), ho.t[:], reads=[ho])
            kb.barrier()

        PHASES = build_program.phases
        eps_ln = kb.sb(es, "eps_ln", [128, 2], F32)
        kb.eps_ln = eps_ln
        kb.op(DVE, lambda: nc.vector.memset(eps_ln.t[:, 0:1], 1e-5), writes=[eps_ln])
        kb.op(DVE, lambda: nc.vector.memset(eps_ln.t[:, 1:2], 1e-6), writes=[eps_ln])
        if "rg" in PHASES:
            phase_rg(nc, kb, S, params, pcol, cm, cmb, load_w_bf16, w_in_d, gate_a_d, gate_x_d, hT_d, recT_d)
        if "gdn" in PHASES:
            phase_gdn(nc, kb, S, params, pcol, cm, cmb, load_w_bf16, w_in_d, rows_d, hrep_d, hT_d, dnT_d)
        if "mix" in PHASES:
            phase_mix(nc, kb, S, load_w_bf16, gt1r, w_in_d, w_pa_d, w_pb_d, w_out_d, rows_d, x_d, hT_d, recT_d, dnT_d, x1_d)
        if "ffn" in PHASES:
            phase_ffn(nc, kb, S, params, pcol, cm, load_w_bf16, ada, gt2r, w_fg_d, w_fu_d, w_fd_d, rows_d, x1_d, out_d)
        kb.final_wait()
    return nc


build_program.phases = ("rg", "gdn", "mix", "ffn")


def gelu_tanh(nc, kb, src_ap, src_bufs, out_ap, out_bufs, tmp, tmp2=None, mul_ap=None, mul_bufs=()):
    ACT, DVE = kb.act, kb.dve
    if mul_ap is None:
        kb.op(ACT, lambda: nc.scalar.activation(out=out_ap, in_=src_ap, func=AF.Gelu_apprx_tanh), reads=src_bufs, writes=out_bufs)
    else:
        kb.op(ACT, lambda: nc.scalar.activation(out=tmp.t[:], in_=src_ap, func=AF.Gelu_apprx_tanh), reads=src_bufs, writes=[tmp])
        kb.op(DVE, lambda: nc.vector.tensor_tensor(out=out_ap, in0=tmp.t[:], in1=mul_ap, op=ALU.mult), reads=[tmp] + list(mul_bufs), writes=out_bufs)


def phase_rg(nc, kb, S, params, pcol, cm, cmb, load_w_bf16, w_in_d, gate_a_d, gate_x_d, hT_d, recT_d):
    PE, ACT, DVE, POOL = kb.pe, kb.act, kb.dve, kb.pool
    TT = 256
    NT = S // TT
    with contextlib.ExitStack() as pes:
        wx = kb.sb(pes, "wx", [128, 8, 2560], BF16)
        wga = kb.sb(pes, "wga", [128, 10, 3, 128], BF16)
        wgx = kb.sb(pes, "wgx", [128, 10, 3, 128], BF16)
        cl = kb.sb(pes, "cl", [128, 20], F32)
        with contextlib.ExitStack() as ses:
            stg = [kb.sb(ses, f"stg{i}", [128, 8, 512], F32) for i in range(2)]
            gst = kb.sb(ses, "gst", [128, 10, 3, 128], F32)
            load_w_bf16(ses, wx, w_in_d, 0, 8, 0, 2560, stg)
            kb.dma(gst.t[:], gate_a_d[:, :, :, :], writes=[gst])
            kb.op(ACT, lambda: nc.scalar.copy(out=wga.t[:], in_=gst.t[:]), reads=[gst], writes=[wga])
            kb.dma(gst.t[:], gate_x_d[:, :, :, :], writes=[gst])
            kb.op(ACT, lambda: nc.scalar.copy(out=wgx.t[:], in_=gst.t[:]), reads=[gst], writes=[wgx])
            kb.barrier(keep=[wx, wga, wgx, cl])
        kb.op(ACT, lambda: nc.scalar.activation(out=cl.t[:, 0:10], in_=pcol("rg_lam", 0, 10), func=AF.Exp, scale=-1.0), reads=[params], writes=[cl])
        kb.op(ACT, lambda: nc.scalar.activation(out=cl.t[:, 0:10], in_=cl.t[:, 0:10], func=AF.Ln, bias=1.0), reads=[cl], writes=[cl])
        kb.op(DVE, lambda: nc.vector.tensor_scalar(out=cl.t[:, 10:20], in0=cl.t[:, 0:10], scalar1=-16.0, scalar2=None, op0=ALU.mult), reads=[cl], writes=[cl])
        kb.op(DVE, lambda: nc.vector.tensor_scalar(out=cl.t[:, 0:10], in0=cl.t[:, 0:10], scalar1=-8.0, scalar2=None, op0=ALU.mult), reads=[cl], writes=[cl])

        hT = [kb.sb(pes, f"hT{i}", [128, 8, TT], BF16) for i in range(2)]
        raw = [kb.sb(pes, f"raw{i}", [128, TT + 3], F32) for i in range(3)]
        halo = kb.sbs(pes, "halo", [128, 10, 3], F32)
        hst = kb.sbs(pes, "hst", [128, 10, 1], F32)
        xr = kb.sbs(pes, "xr", [128, 10, TT], F32)
        xrb = kb.sbs(pes, "xrb", [128, 10, TT], BF16)
        ga = kb.sbs(pes, "ga", [128, 10, TT], F32)
        gi = kb.sbs(pes, "gi", [128, 10, TT], F32)
        mu = kb.sbs(pes, "mu", [128, 10, TT], F32)
        hh = kb.sbs(pes, "hh", [128, 10, TT], F32)
        tmp = [kb.sb(pes, f"gtmp{i}", [128, TT], F32) for i in range(3)]
        rec = [kb.sbs(pes, f"rec{i}", [128, 10, TT], BF16) for i in range(2)]
        pp = [kb.ps(pes, f"pp{i}", [128, 512], F32) for i in range(6)]
        kb.op(DVE, lambda: nc.vector.memset(halo.t[:], 0.0), writes=halo.sub)
        kb.op(DVE, lambda: nc.vector.memset(hst.t[:], 0.0), writes=hst.sub)
        cwo = _P["rg_cw"][0]

        def load_h(t):
            kb.dma(hT[t % 2].t[:], hT_d[:, :, t * TT:(t + 1) * TT].rearrange("c p t -> p c t"), writes=[hT[t % 2]])

        load_h(0)
        pi = 0
        for t in range(NT):
            if t + 1 < NT:
                load_h(t + 1)
            h = hT[t % 2]
            for j in range(10):
                p = pp[pi % 6]; pi += 1
                for k in range(8):
                    kb.op(PE, lambda: nc.tensor.matmul(p.t[:, 0:TT], lhsT=wx.t[:, k, j * 128:(j + 1) * 128], rhs=h.t[:, k, :],
                                                       start=(k == 0), stop=(k == 7)), reads=[wx, h], writes=[p], inc=(k == 7))
                rw = raw[j % 3]
                kb.op(ACT, lambda: nc.scalar.copy(out=rw.t[:, 0:3], in_=halo.t[:, j, :]), reads=[halo.sub[j]], writes=[rw])
                kb.op(ACT, lambda: nc.scalar.copy(out=rw.t[:, 3:TT + 3], in_=p.t[:, 0:TT]), reads=[p], writes=[rw])
                kb.op(ACT, lambda: nc.scalar.copy(out=halo.t[:, j, :], in_=rw.t[:, TT:TT + 3]), reads=[rw], writes=[halo.sub[j]])
                kb.op(DVE, lambda: nc.vector.tensor_scalar(out=xr.t[:, j, :], in0=rw.t[:, 3:TT + 3], scalar1=params.t[:, cwo + j * 4 + 3:cwo + j * 4 + 4],
                                                           scalar2=pcol("rg_cb", j), op0=ALU.mult, op1=ALU.add), reads=[rw, params], writes=[xr.sub[j]])
                for kk in range(3):
                    kb.op(DVE, lambda: nc.vector.scalar_tensor_tensor(
                        out=xr.t[:, j, :], in0=rw.t[:, kk:kk + TT], scalar=params.t[:, cwo + j * 4 + kk:cwo + j * 4 + kk + 1], in1=xr.t[:, j, :],
                        op0=ALU.mult, op1=ALU.add), reads=[rw, params, xr.sub[j]], writes=[xr.sub[j]])
                kb.op(ACT, lambda: nc.scalar.copy(out=xrb.t[:, j, :], in_=xr.t[:, j, :]), reads=[xr.sub[j]], writes=[xrb.sub[j]])
            for (wg, dst, bname) in ((wga, ga, "rg_ba"), (wgx, gi, "rg_bx")):
                for m in range(10):
                    p = pp[pi % 6]; pi += 1
                    ks = [k for k in (m - 1, m, m + 1) if 0 <= k < 10]
                    for n_, k in enumerate(ks):
                        kb.op(PE, lambda: nc.tensor.matmul(
                            p.t[:, 0:TT], lhsT=wg.t[:, m, k - m + 1, :], rhs=xrb.t[:, k, :], start=(n_ == 0), stop=(n_ == len(ks) - 1)),
                            reads=[wg, xrb.sub[k]], writes=[p], inc=(n_ == len(ks) - 1))
                    kb.op(ACT, lambda: nc.scalar.activation(out=dst.t[:, m, :], in_=p.t[:, 0:TT], func=AF.Sigmoid, bias=pcol(bname, m)),
                          reads=[p, params], writes=[dst.sub[m]])
            for m in range(10):
                kb.op(ACT, lambda: nc.scalar.activation(out=mu.t[:, m, :], in_=ga.t[:, m, :], func=AF.Exp, scale=cl.t[:, 10 + m:11 + m]), reads=[ga.sub[m], cl], writes=[mu.sub[m]])
                kb.op(ACT, lambda: nc.scalar.activation(out=ga.t[:, m, :], in_=ga.t[:, m, :], func=AF.Exp, scale=cl.t[:, m:m + 1]), reads=[ga.sub[m], cl], writes=[ga.sub[m]])
            for m in range(10):
                kb.op(ACT, lambda: nc.scalar.activation(out=mu.t[:, m, :], in_=mu.t[:, m, :], func=AF.Sqrt, scale=-1.0, bias=1.0), reads=[mu.sub[m]], writes=[mu.sub[m]])
            for m in range(10):
                kb.op(POOL, lambda: nc.gpsimd.tensor_tensor(out=gi.t[:, m, :], in0=gi.t[:, m, :], in1=xr.t[:, m, :], op=ALU.mult), reads=[gi.sub[m], xr.sub[m]], writes=[gi.sub[m]])
                kb.op(DVE, lambda: nc.vector.tensor_tensor(out=gi.t[:, m, :], in0=gi.t[:, m, :], in1=mu.t[:, m, :], op=ALU.mult), reads=[gi.sub[m], mu.sub[m]], writes=[gi.sub[m]])
                kb.op(DVE, lambda: nc.vector.tensor_tensor_scan(out=hh.t[:, m, :], data0=ga.t[:, m, :], data1=gi.t[:, m, :], initial=hst.t[:, m, :],
                                                                op0=ALU.mult, op1=ALU.add), reads=[ga.sub[m], gi.sub[m], hst.sub[m]], writes=[hh.sub[m]])
                kb.op(ACT, lambda: nc.scalar.copy(out=hst.t[:, m, :], in_=hh.t[:, m, TT - 1:TT]), reads=[hh.sub[m]], writes=[hst.sub[m]])
            rc = rec[t % 2]
            for j in range(10):
                p = pp[pi % 6]; pi += 1
                for k in range(8):
                    kb.op(PE, lambda: nc.tensor.matmul(p.t[:, 0:TT], lhsT=wx.t[:, k, 1280 + j * 128:1280 + (j + 1) * 128], rhs=h.t[:, k, :],
                                                       start=(k == 0), stop=(k == 7)), reads=[wx, h], writes=[p], inc=(k == 7))
                gelu_tanh(nc, kb, p.t[:, 0:TT], [p], rc.t[:, j, :], [rc.sub[j]], tmp[j % 3], mul_ap=hh.t[:, j, :], mul_bufs=[hh.sub[j]])
            kb.dma(recT_d[:, :, t * TT:(t + 1) * TT].rearrange("c p t -> p c t"), rc.t[:], reads=rc.sub)
        kb.barrier()


def layer_norm_rows(nc, kb, y, nblk, g_ap, b_ap, rowbuf, stats, mv, sd, out_buf):
    ACT, DVE, POOL = kb.act, kb.dve, kb.pool
    for blk in range(nblk):
        for hf in range(2):
            kb.op(DVE, lambda: nc.vector.bn_stats(out=stats.t[:, blk, hf, :], in_=y.t[:, blk, hf * 512:(hf + 1) * 512]), reads=[y], writes=[stats])
        kb.op(DVE, lambda: nc.vector.bn_aggr(out=mv.t[:, blk, :], in_=stats.t[:, blk, :, :].rearrange("p a b -> p (a b)")), reads=[stats], writes=[mv])
        kb.op(ACT, lambda: nc.scalar.activation(out=sd.t[:, blk:blk + 1], in_=mv.t[:, blk, 1:2], func=AF.Ln, bias=kb.eps_ln.t[:, 0:1]), reads=[mv, kb.eps_ln], writes=[sd])
        kb.op(ACT, lambda: nc.scalar.activation(out=sd.t[:, blk:blk + 1], in_=sd.t[:, blk:blk + 1], func=AF.Exp, scale=-0.5), reads=[sd], writes=[sd])
        kb.op(DVE, lambda: nc.vector.tensor_scalar(out=y.t[:, blk, :], in0=y.t[:, blk, :], scalar1=mv.t[:, blk, 0:1], scalar2=sd.t[:, blk:blk + 1],
                                                   op0=ALU.subtract, op1=ALU.mult), reads=[y, mv, sd], writes=[y])
        kb.op(POOL, lambda: nc.gpsimd.tensor_tensor(out=y.t[:, blk, :], in0=y.t[:, blk, :], in1=g_ap, op=ALU.mult), reads=[y, rowbuf], writes=[y])
        kb.op(DVE, lambda: nc.vector.tensor_tensor(out=out_buf.t[:, blk, :], in0=y.t[:, blk, :], in1=b_ap, op=ALU.add), reads=[y, rowbuf], writes=[out_buf])


def phase_mix(nc, kb, S, load_w_bf16, gt1r, w_in_d, w_pa_d, w_pb_d, w_out_d, rows_d, x_d, hT_d, recT_d, dnT_d, x1_d):
    PE, ACT, DVE, POOL = kb.pe, kb.act, kb.dve, kb.pool
    TT = 256
    NT = S // TT
    NB = TT // 128
    with contextlib.ExitStack() as pes:
        wg = kb.sb(pes, "wg", [128, 8, 2048], BF16)
        wpa = kb.sb(pes, "wpa", [128, 10, 1024], BF16)
        wpb = kb.sb(pes, "wpb", [128, 16, 1024], BF16)
        wo = kb.sb(pes, "wo", [128, 8, 1024], BF16)
        lnr = kb.sb(pes, "lnr", [128, 2048], F32)
        with contextlib.ExitStack() as ses:
            stg = [kb.sb(ses, f"stg{i}", [128, 16, 256], F32) for i in range(2)]
            load_w_bf16(ses, wg, w_in_d, 0, 8, C_GA, 2048, stg)
            load_w_bf16(ses, wpa, w_pa_d, 0, 10, 0, 1024, stg)
            load_w_bf16(ses, wpb, w_pb_d, 0, 16, 0, 1024, stg)
            load_w_bf16(ses, wo, w_out_d, 0, 8, 0, 1024, stg)
            kb.dma(lnr.t[:], rows_d[:, _R["ln1_g"][0]:_R["ln1_g"][0] + 2048], writes=[lnr])
            kb.barrier(keep=[wg, wpa, wpb, wo, lnr])
        hT = [kb.sb(pes, f"hT{i}", [128, 8, TT], BF16) for i in range(2)]
        rcT = [kb.sb(pes, f"rcT{i}", [128, 10, TT], BF16) for i in range(2)]
        dnT = [kb.sb(pes, f"dnT{i}", [128, 16, TT], BF16) for i in range(2)]
        xt = [kb.sb(pes, f"xt{i}", [128, NB, 1024], F32) for i in range(2)]
        mg = kb.sbs(pes, "mg", [128, 8, TT], BF16)
        sg = [kb.sb(pes, f"sg{i}", [128, TT], F32) for i in range(4)]
        t1 = [kb.sb(pes, f"t1{i}", [128, TT], F32) for i in range(4)]
        y = kb.sb(pes, "y", [128, NB, 1024], F32)
        xo = [y, y]
        stats = kb.sb(pes, "stats", [128, NB, 2, 6], F32)
        mv = kb.sb(pes, "mv", [128, NB, 2], F32)
        sd = kb.sb(pes, "sd", [128, NB], F32)
        pp = [kb.ps(pes, f"pp{i}", [128, 512], F32) for i in range(8)]

        def load(t):
            i = t % 2
            sl = slice(t * TT, (t + 1) * TT)
            kb.dma(hT[i].t[:], hT_d[:, :, sl].rearrange("c p t -> p c t"), writes=[hT[i]])
            kb.dma(rcT[i].t[:], recT_d[:, :, sl].rearrange("c p t -> p c t"), writes=[rcT[i]])
            kb.dma(dnT[i].t[:], dnT_d[:, :, sl].rearrange("c p t -> p c t"), writes=[dnT[i]])
            kb.dma(xt[i].t[:], x_d[sl, :].rearrange("(b p) f -> p b f", p=128), writes=[xt[i]])

        load(0)
        pi = 0
        for t in range(NT):
            if t + 1 < NT:
                load(t + 1)
            i = t % 2
            h, rc, dn, xx = hT[i], rcT[i], dnT[i], xt[i]
            for m in range(8):
                ms = slice(m * 128, (m + 1) * 128)
                pga = pp[pi % 8]; pya = pp[(pi + 1) % 8]; pgb = pp[(pi + 2) % 8]; pyb = pp[(pi + 3) % 8]; pi += 4
                for k in range(8):
                    kb.op(PE, lambda: nc.tensor.matmul(pga.t[:, 0:TT], lhsT=wg.t[:, k, m * 128:(m + 1) * 128], rhs=h.t[:, k, :], start=(k == 0), stop=(k == 7)),
                          reads=[wg, h], writes=[pga], inc=(k == 7))
                for k in range(10):
                    kb.op(PE, lambda: nc.tensor.matmul(pya.t[:, 0:TT], lhsT=wpa.t[:, k, ms], rhs=rc.t[:, k, :], start=(k == 0), stop=(k == 9)),
                          reads=[wpa, rc], writes=[pya], inc=(k == 9))
                for k in range(8):
                    kb.op(PE, lambda: nc.tensor.matmul(pgb.t[:, 0:TT], lhsT=wg.t[:, k, 1024 + m * 128:1024 + (m + 1) * 128], rhs=h.t[:, k, :], start=(k == 0), stop=(k == 7)),
                          reads=[wg, h], writes=[pgb], inc=(k == 7))
                for k in range(16):
                    kb.op(PE, lambda: nc.tensor.matmul(pyb.t[:, 0:TT], lhsT=wpb.t[:, k, ms], rhs=dn.t[:, k, :], start=(k == 0), stop=(k == 15)),
                          reads=[wpb, dn], writes=[pyb], inc=(k == 15))
                s0, s1, ta, tb = sg[2 * (m % 2)], sg[2 * (m % 2) + 1], t1[2 * (m % 2)], t1[2 * (m % 2) + 1]
                kb.op(ACT, lambda: nc.scalar.activation(out=s0.t[:], in_=pga.t[:, 0:TT], func=AF.Sigmoid), reads=[pga], writes=[s0])
                kb.op(ACT, lambda: nc.scalar.activation(out=s1.t[:], in_=pgb.t[:, 0:TT], func=AF.Sigmoid), reads=[pgb], writes=[s1])
                kb.op(DVE, lambda: nc.vector.tensor_tensor(out=ta.t[:], in0=s0.t[:], in1=pya.t[:, 0:TT], op=ALU.mult), reads=[s0, pya], writes=[ta])
                kb.op(DVE, lambda: nc.vector.tensor_tensor(out=tb.t[:], in0=s1.t[:], in1=pyb.t[:, 0:TT], op=ALU.mult), reads=[s1, pyb], writes=[tb])
                kb.op(POOL, lambda: nc.gpsimd.tensor_tensor(out=mg.t[:, m, :], in0=ta.t[:], in1=tb.t[:], op=ALU.add), reads=[ta, tb], writes=[mg.sub[m]])
            for blk in range(NB):
                for cb in range(2):
                    p = pp[pi % 8]; pi += 1
                    cs = slice(cb * 512, (cb + 1) * 512)
                    for k in range(8):
                        kb.op(PE, lambda: nc.tensor.matmul(p.t[:], lhsT=mg.t[:, k, blk * 128:(blk + 1) * 128], rhs=wo.t[:, k, cs], start=(k == 0), stop=(k == 7)),
                              reads=[mg.sub[k], wo], writes=[p], inc=(k == 7))
                    kb.op(DVE, lambda: nc.vector.tensor_tensor(out=y.t[:, blk, cs], in0=p.t[:], in1=gt1r.t[:, cs], op=ALU.mult), reads=[p, gt1r], writes=[y])
                    kb.op(DVE, lambda: nc.vector.scalar_tensor_tensor(out=y.t[:, blk, cs], in0=xx.t[:, blk, cs], scalar=ALPHA, in1=y.t[:, blk, cs],
                                                                      op0=ALU.mult, op1=ALU.add), reads=[xx, y], writes=[y])
            o = xo[i]
            layer_norm_rows(nc, kb, y, NB, lnr.t[:, 0:1024], lnr.t[:, 1024:2048], lnr, stats, mv, sd, o)
            kb.dma(x1_d[t * TT:(t + 1) * TT, :].rearrange("(b p) f -> p b f", p=128), o.t[:], reads=[o])
        kb.barrier()


def phase_ffn(nc, kb, S, params, pcol, cm, load_w_bf16, ada, gt2r, w_fg_d, w_fu_d, w_fd_d, rows_d, x1_d, out_d):
    PE, ACT, DVE, POOL = kb.pe, kb.act, kb.dve, kb.pool
    TT = 256
    NT = S // TT
    NB = TT // 128
    ident_f = cm.t[:, CM_IDENT * 128:(CM_IDENT + 1) * 128]
    with contextlib.ExitStack() as pes:
        wfg = kb.sb(pes, "wfg", [128, 8, DFF], BF16)
        wfu = kb.sb(pes, "wfu", [128, 8, DFF], BF16)
        wfd = kb.sb(pes, "wfd", [128, 22, 1024], BF16)
        lnr = kb.sb(pes, "lnr", [128, 2048], F32)
        with contextlib.ExitStack() as ses:
            stg = [kb.sb(ses, f"stg{i}", [128, 22, 128], F32) for i in range(2)]
            load_w_bf16(ses, wfg, w_fg_d, 0, 8, 0, DFF, stg)
            load_w_bf16(ses, wfu, w_fu_d, 0, 8, 0, DFF, stg)
            load_w_bf16(ses, wfd, w_fd_d, 0, 22, 0, 1024, stg)
            kb.dma(lnr.t[:], rows_d[:, _R["ln2_g"][0]:_R["ln2_g"][0] + 2048], writes=[lnr])
            kb.barrier(keep=[wfg, wfu, wfd, lnr])
        xt = [kb.sb(pes, f"xt{i}", [128, NB, 1024], F32) for i in range(2)]
        h2 = kb.sbs(pes, "h2", [128, 8, TT], BF16)
        act = kb.sbs(pes, "act", [128, 22, TT], BF16)
        raw = [kb.sb(pes, f"raw{i}", [128, TT + 2], F32) for i in range(3)]
        cv = [kb.sb(pes, f"cv{i}", [128, TT], F32) for i in range(3)]
        tmp = [kb.sb(pes, f"tmp{i}", [128, TT], F32) for i in range(3)]
        halo = kb.sbs(pes, "halo", [128, 22, 2], F32)
        y = kb.sb(pes, "y", [128, NB, 1024], F32)
        xo = [y, y]
        stats = kb.sb(pes, "stats", [128, NB, 2, 6], F32)
        mv = kb.sb(pes, "mv", [128, NB, 2], F32)
        sd = kb.sb(pes, "sd", [128, NB], F32)
        pp = [kb.ps(pes, f"pp{i}", [128, 512], F32) for i in range(8)]
        kb.op(DVE, lambda: nc.vector.memset(halo.t[:], 0.0), writes=halo.sub)
        cwo = _P["ffn_cw"][0]

        def load(t):
            kb.dma(xt[t % 2].t[:], x1_d[t * TT:(t + 1) * TT, :].rearrange("(b p) f -> p b f", p=128), writes=[xt[t % 2]])

        load(0)
        pi = 0
        for t in range(NT):
            if t + 1 < NT:
                load(t + 1)
            xx = xt[t % 2]
            for blk in range(NB):
                for c in range(8):
                    p = pp[pi % 8]; pi += 1
                    kb.op(PE, lambda: nc.tensor.matmul(p.t[:, 0:128], lhsT=xx.t[:, blk, c * 128:(c + 1) * 128], rhs=ident_f, start=True, stop=True), reads=[xx, cm], writes=[p])
                    kb.op(ACT, lambda: nc.scalar.activation(out=h2.t[:, c, blk * 128:(blk + 1) * 128], in_=p.t[:, 0:128], func=AF.Identity,
                                                            scale=ada.t[:, 32 + c:33 + c], bias=ada.t[:, 24 + c:25 + c]), reads=[p, ada], writes=[h2.sub[c]])
            for m in range(22):
                ms = slice(m * 128, (m + 1) * 128)
                pg = pp[pi % 8]; pu = pp[(pi + 1) % 8]; pi += 2
                for k in range(8):
                    kb.op(PE, lambda: nc.tensor.matmul(pg.t[:, 0:TT], lhsT=wfg.t[:, k, ms], rhs=h2.t[:, k, :], start=(k == 0), stop=(k == 7)),
                          reads=[wfg, h2.sub[k]], writes=[pg], inc=(k == 7))
                for k in range(8):
                    kb.op(PE, lambda: nc.tensor.matmul(pu.t[:, 0:TT], lhsT=wfu.t[:, k, ms], rhs=h2.t[:, k, :], start=(k == 0), stop=(k == 7)),
                          reads=[wfu, h2.sub[k]], writes=[pu], inc=(k == 7))
                rw = raw[m % 3]
                c_ = cv[m % 3]
                kb.op(ACT, lambda: nc.scalar.copy(out=rw.t[:, 0:2], in_=halo.t[:, m, :]), reads=[halo.sub[m]], writes=[rw])
                kb.op(ACT, lambda: nc.scalar.copy(out=rw.t[:, 2:TT + 2], in_=pg.t[:, 0:TT]), reads=[pg], writes=[rw])
                kb.op(ACT, lambda: nc.scalar.copy(out=halo.t[:, m, :], in_=rw.t[:, TT:TT + 2]), reads=[rw], writes=[halo.sub[m]])
                kb.op(DVE, lambda: nc.vector.tensor_scalar(out=c_.t[:], in0=rw.t[:, 2:TT + 2], scalar1=params.t[:, cwo + m * 3 + 2:cwo + m * 3 + 3],
                                                           scalar2=pcol("ffn_cb", m), op0=ALU.mult, op1=ALU.add), reads=[rw, params], writes=[c_])
                for kk in range(2):
                    kb.op(DVE, lambda: nc.vector.scalar_tensor_tensor(out=c_.t[:], in0=rw.t[:, kk:kk + TT], scalar=params.t[:, cwo + m * 3 + kk:cwo + m * 3 + kk + 1],
                                                                      in1=c_.t[:], op0=ALU.mult, op1=ALU.add), reads=[rw, params, c_], writes=[c_])
                gelu_tanh(nc, kb, c_.t[:], [c_], act.t[:, m, :], [act.sub[m]], tmp[m % 3], mul_ap=pu.t[:, 0:TT], mul_bufs=[pu])
            for blk in range(NB):
                for cb in range(2):
                    p = pp[pi % 8]; pi += 1
                    cs = slice(cb * 512, (cb + 1) * 512)
                    for k in range(22):
                        kb.op(PE, lambda: nc.tensor.matmul(p.t[:], lhsT=act.t[:, k, blk * 128:(blk + 1) * 128], rhs=wfd.t[:, k, cs], start=(k == 0), stop=(k == 21)),
                              reads=[act.sub[k], wfd], writes=[p], inc=(k == 21))
                    kb.op(DVE, lambda: nc.vector.tensor_tensor(out=y.t[:, blk, cs], in0=p.t[:], in1=gt2r.t[:, cs], op=ALU.mult), reads=[p, gt2r], writes=[y])
                    kb.op(DVE, lambda: nc.vector.scalar_tensor_tensor(out=y.t[:, blk, cs], in0=xx.t[:, blk, cs], scalar=ALPHA, in1=y.t[:, blk, cs],
                                                                      op0=ALU.mult, op1=ALU.add), reads=[xx, y], writes=[y])
            o = xo[t % 2]
            layer_norm_rows(nc, kb, y, NB, lnr.t[:, 0:1024], lnr.t[:, 1024:2048], lnr, stats, mv, sd, o)
            kb.dma(out_d[t * TT:(t + 1) * TT, :].rearrange("(b p) f -> p b f", p=128), o.t[:], reads=[o])
        kb.barrier()


class PSlot:
    def __init__(self, kb, bank, i, name):
        self.b = bank
        self.ap = bank.t[:, i * 128:(i + 1) * 128]
        self.apb = self.ap.bitcast(BF16)[:, 0:128]


def phase_gdn(nc, kb, S, params, pcol, cm, cmb, load_w_bf16, w_in_d, rows_d, hrep_d, hT_d, dnT_d):
    PE, ACT, DVE, POOL = kb.pe, kb.act, kb.dve, kb.pool
    NCH = S // 128
    NT = S // 512
    HVG, HQG = 4, 2
    NG = 16 // HVG
    NCOL = NCH * 16
    cf = lambda i: cm.t[:, i * 128:(i + 1) * 128]
    cb = lambda i: cmb.t[:, i * 128:(i + 1) * 128]
    ident_f, ident_b, L_f, U_f, ones_f, ones_b = cf(CM_IDENT), cb(CM_IDENT), cf(CM_L), cf(CM_U), cf(CM_ONES), cb(CM_ONES)
    mS_b, mIT_b, BD_f, OFF_f = cb(CM_MS), cb(CM_MIT), cf(CM_BD), cf(CM_OFF)
    eps6 = kb.eps_ln.t[:, 1:2]
    with contextlib.ExitStack() as pes:
        tm = {n: kb.sb(pes, "tm_" + n, [128, NCH, 16], F32) for n in ("beta", "negg", "eG", "beG", "kds", "gl")}
        flat = lambda bf: bf.t[:].rearrange("p n h -> p (n h)")
        with contextlib.ExitStack() as ses:
            wab = kb.sb(ses, "wab", [128, 8, 32], BF16)
            stg = [kb.sb(ses, "stg0", [128, 8, 32], F32)]
            hT = [kb.sb(ses, f"hT{i}", [128, 8, 512], BF16) for i in range(2)]
            ab = kb.sb(ses, "ab", [128, NCH, 32], F32)
            hrep = kb.sb(ses, "hrep", [128, 2, NCOL], F32)
            X = kb.sb(ses, "X", [128, NCH, 16], F32)
            Y = kb.sb(ses, "Y", [128, NCH, 16], F32)
            Z = kb.sb(ses, "Z", [128, NCH, 16], F32)
            pp = [kb.ps(ses, f"pp{i}", [128, 512], F32) for i in range(4)]
            load_w_bf16(ses, wab, w_in_d, 0, 8, C_A, 32, stg)
            kb.dma(hrep.t[:], hrep_d[:, :, :], writes=[hrep])
            kb.dma(hT[0].t[:], hT_d[:, :, 0:512].rearrange("c p t -> p c t"), writes=[hT[0]])
            for t in range(NT):
                if t + 1 < NT:
                    kb.dma(hT[(t + 1) % 2].t[:], hT_d[:, :, (t + 1) * 512:(t + 2) * 512].rearrange("c p t -> p c t"), writes=[hT[(t + 1) % 2]])
                h = hT[t % 2]
                p = pp[t % 2]
                for blk in range(4):
                    for k in range(8):
                        kb.op(PE, lambda: nc.tensor.matmul(p.t[:, blk * 32:(blk + 1) * 32], lhsT=h.t[:, k, blk * 128:(blk + 1) * 128], rhs=wab.t[:, k, :],
                                                           start=(k == 0), stop=(k == 7)), reads=[h, wab], writes=[p], inc=(k == 7))
                kb.op(ACT, lambda: nc.scalar.copy(out=ab.t[:, t * 4:(t + 1) * 4, :], in_=p.t[:, 0:128].rearrange("p (b c) -> p b c", c=32)), reads=[p], writes=[ab])
            a_v, b_v = ab.t[:, :, 0:16], ab.t[:, :, 16:32]
            dtb = hrep.t[:, 0, :].rearrange("p (n h) -> p n h", h=16)
            alog = hrep.t[:, 1, :].rearrange("p (n h) -> p n h", h=16)
            kb.op(ACT, lambda: nc.scalar.activation(out=tm["beta"].t[:], in_=b_v, func=AF.Sigmoid), reads=[ab], writes=[tm["beta"]])
            kb.op(DVE, lambda: nc.vector.tensor_tensor(out=X.t[:], in0=a_v, in1=dtb, op=ALU.add), reads=[ab, hrep], writes=[X])
            kb.op(DVE, lambda: nc.vector.tensor_scalar(out=Y.t[:], in0=X.t[:], scalar1=-1.0, scalar2=None, op0=ALU.mult), reads=[X], writes=[Y])
            kb.op(DVE, lambda: nc.vector.tensor_tensor(out=Y.t[:], in0=Y.t[:], in1=X.t[:], op=ALU.max), reads=[X, Y], writes=[Y])
            kb.op(ACT, lambda: nc.scalar.activation(out=Y.t[:], in_=Y.t[:], func=AF.Exp, scale=-1.0), reads=[Y], writes=[Y])
            kb.op(ACT, lambda: nc.scalar.activation(out=Y.t[:], in_=Y.t[:], func=AF.Ln, bias=1.0), reads=[Y], writes=[Y])
            kb.op(DVE, lambda: nc.vector.tensor_scalar(out=X.t[:], in0=X.t[:], scalar1=0.0, scalar2=None, op0=ALU.max), reads=[X], writes=[X])
            kb.op(DVE, lambda: nc.vector.tensor_tensor(out=X.t[:], in0=X.t[:], in1=Y.t[:], op=ALU.add), reads=[X, Y], writes=[X])
            kb.op(ACT, lambda: nc.scalar.activation(out=Z.t[:], in_=alog, func=AF.Exp), reads=[hrep], writes=[Z])
            kb.op(DVE, lambda: nc.vector.tensor_tensor(out=tm["negg"].t[:], in0=X.t[:], in1=Z.t[:], op=ALU.mult), reads=[X, Z], writes=[tm["negg"]])
            ng_f = flat(tm["negg"])
            for c0 in range(0, NCOL, 512):
                w = min(512, NCOL - c0)
                p1, p2 = pp[2], pp[3]
                kb.op(PE, lambda: nc.tensor.matmul(p1.t[:, 0:w], lhsT=L_f, rhs=ng_f[:, c0:c0 + w], start=True, stop=True), reads=[cm, tm["negg"]], writes=[p1])
                kb.op(PE, lambda: nc.tensor.matmul(p2.t[:, 0:w], lhsT=ones_f, rhs=ng_f[:, c0:c0 + w], start=True, stop=True), reads=[cm, tm["negg"]], writes=[p2])
                Xf, Yf = flat(X), flat(Y)
                kb.op(ACT, lambda: nc.scalar.copy(out=Xf[:, c0:c0 + w], in_=p1.t[:, 0:w]), reads=[p1], writes=[X])
                kb.op(ACT, lambda: nc.scalar.activation(out=flat(tm["eG"])[:, c0:c0 + w], in_=p1.t[:, 0:w], func=AF.Exp, scale=-1.0), reads=[p1], writes=[tm["eG"]])
                kb.op(ACT, lambda: nc.scalar.activation(out=flat(tm["gl"])[:, c0:c0 + w], in_=p2.t[:, 0:w], func=AF.Exp, scale=-1.0), reads=[p2], writes=[tm["gl"]])
                kb.op(DVE, lambda: nc.vector.tensor_tensor(out=Yf[:, c0:c0 + w], in0=p2.t[:, 0:w], in1=Xf[:, c0:c0 + w], op=ALU.subtract), reads=[p2, X], writes=[Y])
                kb.op(ACT, lambda: nc.scalar.activation(out=flat(tm["kds"])[:, c0:c0 + w], in_=Yf[:, c0:c0 + w], func=AF.Exp, scale=-1.0), reads=[Y], writes=[tm["kds"]])
            kb.op(DVE, lambda: nc.vector.tensor_tensor(out=tm["beG"].t[:], in0=tm["beta"].t[:], in1=tm["eG"].t[:], op=ALU.mult), reads=[tm["beta"], tm["eG"]], writes=[tm["beG"]])
            kb.barrier()

        STOP = getattr(build_program, "gdn_stop", 0)
        for gi in range(NG if STOP != 1 else 0):
            hv0, hq0 = gi * HVG, gi * HQG
            with contextlib.ExitStack() as ges:
                wq = kb.sb(ges, "wq", [128, 8, HQG * 128], BF16)
                wk = kb.sb(ges, "wk", [128, 8, HQG * 128], BF16)
                wv = kb.sb(ges, "wv", [128, 8, HVG * 128], BF16)
                wz = kb.sb(ges, "wz", [128, 8, HVG * 128], BF16)
                nwr = kb.sb(ges, "nwr", [128, HVG * 128], F32)
                with contextlib.ExitStack() as ses:
                    stg = [kb.sb(ses, f"stg{i}", [128, 8, 512], F32) for i in range(2)]
                    load_w_bf16(ses, wq, w_in_d, 0, 8, C_Q + hq0 * 128, HQG * 128, stg)
                    load_w_bf16(ses, wk, w_in_d, 0, 8, C_K + hq0 * 128, HQG * 128, stg)
                    load_w_bf16(ses, wv, w_in_d, 0, 8, C_V + hv0 * 128, HVG * 128, stg)
                    load_w_bf16(ses, wz, w_in_d, 0, 8, C_Z + hv0 * 128, HVG * 128, stg)
                    kb.dma(nwr.t[:], rows_d[:, _R["nw"][0]:_R["nw"][0] + HVG * 128], writes=[nwr])
                    kb.barrier()
                hT = [kb.sb(ges, f"hT{i}", [128, 8, 512], BF16) for i in range(2)]
                raw = [kb.sb(ges, f"raw{i}", [128, 515], F32) for i in range(3)]
                cv = [kb.sb(ges, f"cv{i}", [128, 512], F32) for i in range(3)]
                NCK = 2 * HQG + HVG
                halo = kb.sb(ges, "halo", [128, NCK, 3], F32)
                qkf = [kb.sb(ges, f"qkf{i}", [128, 512], F32) for i in range(2 * HQG)]
                sqb = [kb.sb(ges, f"sqb{i}", [128, 512], BF16) for i in range(2 * HQG)]
                rn = [kb.sb(ges, f"rn{i}", [128, 512], F32) for i in range(2)]
                qT2 = [kb.sb(ges, f"qT{i}", [128, HQG, 512], BF16) for i in range(2)]
                kT2 = [kb.sb(ges, f"kT{i}", [128, HQG, 512], BF16) for i in range(2)]
                vT2 = [kb.sb(ges, f"vT{i}", [128, HVG, 512], BF16) for i in range(2)]
                zs = [kb.sb(ges, f"zs{i}", [128, 512], F32) for i in range(2)]
                zg2 = [kb.sb(ges, f"zg{i}", [128, 4, HVG * 128], F32) for i in range(2)]
                dno = [kb.sb(ges, f"dno{i}", [128, HVG, 512], BF16) for i in range(2)]
                Sf = [kb.sb(ges, f"Sf{i}", [128, 128], F32) for i in range(HVG)]
                Sb = [kb.sb(ges, f"Sb{i}", [128, 128], BF16) for i in range(HVG)]
                ob2 = [kb.sb(ges, f"ob{i}", [128, HVG, 128], F32) for i in range(2)]
                junk = kb.sb(ges, "junk", [128, 128], F32)
                ss2 = [kb.sb(ges, f"ss{i}", [128, HVG], F32) for i in range(2)]
                rstd2 = [kb.sb(ges, f"rstd{i}", [128, HVG], F32) for i in range(2)]
                ktm = [kb.sb(ges, f"ktm{i}", [128, 128], BF16) for i in range(HQG)]
                KKbd = [kb.sb(ges, f"KKbd{i}", [128, 128], F32) for i in range(HQG)]
                KKoff = [kb.sb(ges, f"KKoff{i}", [128, 128], F32) for i in range(HQG)]
                QKT = [kb.sb(ges, f"QKT{i}", [128, 128], F32) for i in range(HQG)]

                class PB:
                    pass
                pb = []
                for hi in range(HVG):
                    B = PB()
                    for n_ in ("rh", "Ds", "DT", "T1", "o1"):
                        setattr(B, n_, kb.sb(ges, f"{n_}_{hi}", [128, 128], F32))
                    for n_ in ("Abd", "Aoff", "P0", "P1", "M0", "Y1", "Tb", "bek", "vn", "dntm"):
                        setattr(B, n_, kb.sb(ges, f"{n_}_{hi}", [128, 128], BF16))
                    for n_ in ("QKd", "M128", "bv", "kdec", "nwT"):
                        setattr(B, n_, [kb.sb(ges, f"{n_}{i}_{hi}", [128, 128], BF16) for i in range(2)])
                    B.P = [B.P0, B.P1]
                    B.QM = [kb.sb(ges, f"QM{i}_{hi}", [128, 256], BF16) for i in range(2)]
                    pb.append(B)
                banks = [kb.ps(ges, f"bk{i}", [128, 512], F32) for i in range(8)]
                slots = [[PSlot(kb, banks[i], j, f"sl{i}_{j}") for j in range(4)] for i in range(8)]
                hsA = [slots[2 * hi] for hi in range(HVG)]
                hsB = [slots[2 * hi + 1] for hi in range(HVG)]

                def bank_bufs(i):
                    return [banks[i]]

                for b_ in Sf + [halo]:
                    kb.op(DVE, lambda: nc.vector.memset(b_.t[:], 0.0), writes=[b_])
                for b_ in Sb:
                    kb.op(DVE, lambda: nc.vector.memset(b_.t[:], 0.0), writes=[b_])
                cwo = _P["dn_cw"][0]
                chunks = [("q", wq, c, hq0 + c) for c in range(HQG)] + [("k", wk, c, 8 + hq0 + c) for c in range(HQG)] + \
                         [("v", wv, c, 16 + hv0 + c) for c in range(HVG)]

                def load_h(t):
                    kb.dma(hT[t % 2].t[:], hT_d[:, :, t * 512:(t + 1) * 512].rearrange("c p t -> p c t"), writes=[hT[t % 2]])

                bi_ = [0]

                def stageA(t):
                    h = hT[t % 2]
                    qT, kT, vT, zg = qT2[t % 2], kT2[t % 2], vT2[t % 2], zg2[t % 2]
                    for ci, (kind, w, c, gch) in enumerate(chunks):
                        bk = bi_[0] % 8; bi_[0] += 1
                        pt, pbufs = banks[bk].t, bank_bufs(bk)
                        for k in range(8):
                            kb.op(PE, lambda: nc.tensor.matmul(pt[:], lhsT=w.t[:, k, c * 128:(c + 1) * 128], rhs=h.t[:, k, :], start=(k == 0), stop=(k == 7)),
                                  reads=[w, h], writes=pbufs, inc=(k == 7))
                        rw, cvb = raw[ci % 3], cv[ci % 3]
                        kb.op(ACT, lambda: nc.scalar.copy(out=rw.t[:, 0:3], in_=halo.t[:, ci, :]), reads=[halo], writes=[rw])
                        kb.op(ACT, lambda: nc.scalar.copy(out=rw.t[:, 3:515], in_=pt[:]), reads=pbufs, writes=[rw])
                        kb.op(ACT, lambda: nc.scalar.copy(out=halo.t[:, ci, :], in_=rw.t[:, 512:515]), reads=[rw], writes=[halo])
                        kb.op(DVE, lambda: nc.vector.tensor_scalar(out=cvb.t[:], in0=rw.t[:, 3:515], scalar1=params.t[:, cwo + gch * 4 + 3:cwo + gch * 4 + 4],
                                                                   scalar2=None, op0=ALU.mult), reads=[rw, params], writes=[cvb])
                        for kk in range(3):
                            kb.op(DVE, lambda: nc.vector.scalar_tensor_tensor(out=cvb.t[:], in0=rw.t[:, kk:kk + 512], scalar=params.t[:, cwo + gch * 4 + kk:cwo + gch * 4 + kk + 1],
                                                                              in1=cvb.t[:], op0=ALU.mult, op1=ALU.add), reads=[rw, params, cvb], writes=[cvb])
                        if kind == "v":
                            kb.op(ACT, lambda: nc.scalar.activation(out=vT.t[:, c, :], in_=cvb.t[:], func=AF.Silu), reads=[cvb], writes=[vT])
                        else:
                            f = qkf[ci]
                            kb.op(ACT, lambda: nc.scalar.activation(out=f.t[:], in_=cvb.t[:], func=AF.Silu), reads=[cvb], writes=[f])
                            kb.op(POOL, lambda: nc.gpsimd.tensor_tensor(out=sqb[ci].t[:], in0=f.t[:], in1=f.t[:], op=ALU.mult), reads=[f], writes=[sqb[ci]])
                    for blk in range(4):
                        bk = bi_[0] % 8; bi_[0] += 1
                        pt, pbufs = banks[bk].t, bank_bufs(bk)
                        for k in range(8):
                            kb.op(PE, lambda: nc.tensor.matmul(pt[:, 0:HVG * 128], lhsT=h.t[:, k, blk * 128:(blk + 1) * 128], rhs=wz.t[:, k, :], start=(k == 0), stop=(k == 7)),
                                  reads=[h, wz], writes=pbufs, inc=(k == 7))
                        z_ = zs[blk % 2]
                        kb.op(ACT, lambda: nc.scalar.activation(out=z_.t[:, 0:HVG * 128], in_=pt[:, 0:HVG * 128], func=AF.Silu), reads=pbufs, writes=[z_])
                        kb.op(POOL, lambda: nc.gpsimd.tensor_tensor(out=zg.t[:, blk, :], in0=z_.t[:, 0:HVG * 128], in1=nwr.t[:], op=ALU.mult), reads=[z_, nwr], writes=[zg])

                    for ci, (kind, w, c, gch) in enumerate(chunks):
                        if kind == "v":
                            continue
                        f, sq, r_ = qkf[ci], sqb[ci], rn[ci % 2]
                        dst = qT if kind == "q" else kT
                        bk2 = bi_[0] % 8; bi_[0] += 1
                        pt2, pbufs2 = banks[bk2].t, bank_bufs(bk2)
                        kb.op(PE, lambda: nc.tensor.matmul(pt2[:], lhsT=ones_b, rhs=sq.t[:], start=True, stop=True), reads=[cmb, sq], writes=pbufs2)
                        kb.op(ACT, lambda: nc.scalar.activation(out=r_.t[:], in_=pt2[:], func=AF.Ln, bias=eps6), reads=pbufs2 + [kb.eps_ln], writes=[r_])
                        kb.op(ACT, lambda: nc.scalar.activation(out=r_.t[:], in_=r_.t[:], func=AF.Exp, scale=-0.5), reads=[r_], writes=[r_])
                        kb.op(DVE, lambda: nc.vector.scalar_tensor_tensor(out=dst.t[:, c, :], in0=f.t[:], scalar=(128.0 ** -0.5 if kind == "q" else 1.0), in1=r_.t[:],
                                                                          op0=ALU.mult, op1=ALU.mult), reads=[f, r_], writes=[dst])

                def hq_ops(n):
                    t, blk = n // 4, n % 4
                    cs = slice(blk * 128, (blk + 1) * 128)
                    qT, kT = qT2[t % 2], kT2[t % 2]
                    for hq in range(HQG):
                        kc, qc = kT.t[:, hq, cs], qT.t[:, hq, cs]
                        s0, s1, s2 = hsA[2 * hq][0], hsA[2 * hq][1], hsA[2 * hq][2]
                        kb.op(PE, lambda: nc.tensor.transpose(s0.apb, kc, ident_b), reads=[kT, cmb], writes=[s0.b])
                        kb.op(PE, lambda: nc.tensor.matmul(s1.ap, lhsT=kc, rhs=kc, start=True, stop=True), reads=[kT], writes=[s1.b])
                        kb.op(PE, lambda: nc.tensor.matmul(s2.ap, lhsT=kc, rhs=qc, start=True, stop=True), reads=[kT, qT], writes=[s2.b])
                        kb.op(ACT, lambda: nc.scalar.copy(out=ktm[hq].t[:], in_=s0.apb), reads=[s0.b], writes=[ktm[hq]])
                        kb.op(DVE, lambda: nc.vector.tensor_tensor(out=KKbd[hq].t[:], in0=s1.ap, in1=BD_f, op=ALU.mult), reads=[s1.b, cm], writes=[KKbd[hq]])
                        kb.op(DVE, lambda: nc.vector.tensor_tensor(out=KKoff[hq].t[:], in0=s1.ap, in1=OFF_f, op=ALU.mult), reads=[s1.b, cm], writes=[KKoff[hq]])
                        kb.op(ACT, lambda: nc.scalar.copy(out=QKT[hq].t[:], in_=s2.ap), reads=[s2.b], writes=[QKT[hq]])

                def pre_seq(hi, n):
                    t, blk = n // 4, n % 4
                    cs = slice(blk * 128, (blk + 1) * 128)
                    vT = vT2[t % 2]
                    pr = n % 2
                    hh_ = hv0 + hi
                    hq = hi // 2
                    B, sl = pb[hi], hsA[hi]
                    sc = lambda nm: tm[nm].t[:, n, hh_:hh_ + 1]
                    QKd, M128, bv, kdec, nwT = B.QKd[pr], B.M128[pr], B.bv[pr], B.kdec[pr], B.nwT[pr]
                    kb.op(ACT, lambda: nc.scalar.activation(out=B.rh.t[:], in_=U_f, func=AF.Identity, scale=sc("negg")), reads=[cm, tm["negg"]], writes=[B.rh])
                    kb.op(ACT, lambda: nc.scalar.activation(out=B.bek.t[:], in_=ktm[hq].t[:], func=AF.Identity, scale=sc("beG")), reads=[ktm[hq], tm["beG"]], writes=[B.bek])
                    kb.op(ACT, lambda: nc.scalar.activation(out=kdec.t[:], in_=ktm[hq].t[:], func=AF.Identity, scale=sc("kds")), reads=[ktm[hq], tm["kds"]], writes=[kdec])
                    yield
                    kb.op(PE, lambda: nc.tensor.matmul(sl[0].ap, lhsT=L_f, rhs=B.rh.t[:], start=True, stop=False), reads=[cm, B.rh], writes=[sl[0].b], inc=False)
                    kb.op(PE, lambda: nc.tensor.matmul(sl[0].ap, lhsT=ident_b, rhs=mS_b, start=False, stop=True), reads=[cmb], writes=[sl[0].b], inc=False)
                    kb.op(PE, lambda: nc.tensor.matmul(sl[1].ap, lhsT=B.rh.t[:], rhs=L_f, start=True, stop=False), reads=[cm, B.rh], writes=[sl[1].b], inc=False)
                    kb.op(PE, lambda: nc.tensor.matmul(sl[1].ap, lhsT=ident_b, rhs=mIT_b, start=False, stop=True), reads=[cmb], writes=[sl[1].b], inc=False)
                    kb.op(PE, lambda: nc.tensor.transpose(sl[3].apb, vT.t[:, hi, cs], ident_b), reads=[vT, cmb], writes=[sl[3].b])
                    kb.op(ACT, lambda: nc.scalar.activation(out=B.Ds.t[:], in_=sl[0].ap, func=AF.Exp, scale=-1.0), reads=[sl[0].b], writes=[B.Ds])
                    kb.op(ACT, lambda: nc.scalar.activation(out=B.DT.t[:], in_=sl[1].ap, func=AF.Exp, scale=-1.0), reads=[sl[1].b], writes=[B.DT])
                    kb.op(DVE, lambda: nc.vector.tensor_scalar(out=bv.t[:], in0=sl[3].apb, scalar1=sc("beta"), scalar2=None, op0=ALU.mult), reads=[sl[3].b, tm["beta"]], writes=[bv])
                    yield
                    kb.op(DVE, lambda: nc.vector.tensor_scalar(out=B.T1.t[:], in0=B.Ds.t[:], scalar1=sc("beta"), scalar2=None, op0=ALU.mult), reads=[B.Ds, tm["beta"]], writes=[B.T1])
                    kb.op(DVE, lambda: nc.vector.tensor_tensor(out=B.Abd.t[:], in0=B.T1.t[:], in1=KKbd[hq].t[:], op=ALU.mult), reads=[B.T1, KKbd[hq]], writes=[B.Abd])
                    kb.op(DVE, lambda: nc.vector.tensor_tensor(out=B.Aoff.t[:], in0=B.T1.t[:], in1=KKoff[hq].t[:], op=ALU.mult), reads=[B.T1, KKoff[hq]], writes=[B.Aoff])
                    kb.op(DVE, lambda: nc.vector.tensor_tensor(out=QKd.t[:], in0=QKT[hq].t[:], in1=B.DT.t[:], op=ALU.mult), reads=[QKT[hq], B.DT], writes=[QKd])
                    yield
                    bankA = banks[2 * hi]
                    kb.op(PE, lambda: nc.tensor.transpose(sl[2].apb, B.Abd.t[:], ident_b), reads=[B.Abd, cmb], writes=[sl[2].b])
                    kb.op(ACT, lambda: nc.scalar.copy(out=B.QM[0].t[:, 0:128], in_=sl[2].apb), reads=[sl[2].b], writes=[B.QM[0]])
                    kb.op(DVE, lambda: nc.vector.tensor_tensor(out=B.QM[0].t[:, 128:256], in0=ident_f, in1=sl[2].apb, op=ALU.subtract), reads=[cm, sl[2].b], writes=[B.QM[0]])
                    kb.op(DVE, lambda: nc.vector.tensor_tensor(out=B.QM[1].t[:, 128:256], in0=ident_f, in1=sl[2].apb, op=ALU.subtract), reads=[cm, sl[2].b], writes=[B.QM[1]])
                    yield
                    P = B.Abd
                    for lv in range(1, 6):
                        cur, nxt, Pn = B.QM[(lv - 1) % 2], B.QM[lv % 2], B.P[lv % 2]
                        lo, hi_ = (0, 128) if lv == 1 else ((0, 256) if lv < 5 else (128, 256))
                        kb.op(PE, lambda: nc.tensor.matmul(sl[2].ap, lhsT=cur.t[:, 0:128], rhs=P.t[:], start=True, stop=True), reads=[P, cur], writes=[sl[2].b], inc=False)
                        kb.op(PE, lambda: nc.tensor.matmul(bankA.t[:, lo:hi_], lhsT=P.t[:], rhs=cur.t[:, lo:hi_], start=True, stop=True), reads=[P, cur], writes=[bankA])
                        kb.op(ACT, lambda: nc.scalar.copy(out=Pn.t[:], in_=sl[2].ap), reads=[bankA], writes=[Pn])
                        if lv < 5:
                            kb.op(ACT, lambda: nc.scalar.copy(out=nxt.t[:, 0:128], in_=bankA.t[:, 0:128]), reads=[bankA], writes=[nxt])
                        if lv >= 2:
                            kb.op(DVE, lambda: nc.vector.tensor_tensor(out=nxt.t[:, 128:256], in0=cur.t[:, 128:256], in1=bankA.t[:, 128:256], op=ALU.add), reads=[cur, bankA], writes=[nxt])
                        yield
                        P = Pn
                    M4 = B.QM[1]
                    kb.op(PE, lambda: nc.tensor.matmul(sl[0].ap, lhsT=P.t[:], rhs=M4.t[:, 128:256], start=True, stop=True), reads=[P, M4], writes=[bankA])
                    kb.op(DVE, lambda: nc.vector.tensor_tensor(out=B.M0.t[:], in0=M4.t[:, 128:256], in1=sl[0].ap, op=ALU.add), reads=[M4, bankA], writes=[B.M0])
                    yield
                    M = B.M0
                    kb.op(PE, lambda: nc.tensor.matmul(sl[2].ap, lhsT=B.Aoff.t[:], rhs=M.t[:], start=True, stop=True), reads=[B.Aoff, M], writes=[bankA], inc=False)
                    kb.op(PE, lambda: nc.tensor.transpose(sl[3].apb, M.t[:], ident_b), reads=[M, cmb], writes=[bankA])
                    kb.op(ACT, lambda: nc.scalar.copy(out=B.Y1.t[:], in_=sl[2].ap), reads=[bankA], writes=[B.Y1])
                    kb.op(DVE, lambda: nc.vector.tensor_copy(out=B.Tb.t[:], in_=sl[3].apb), reads=[bankA], writes=[B.Tb])
                    yield
                    kb.op(PE, lambda: nc.tensor.matmul(sl[0].ap, lhsT=B.Tb.t[:], rhs=B.Y1.t[:], start=True, stop=True), reads=[B.Tb, B.Y1], writes=[bankA])
                    kb.op(DVE, lambda: nc.vector.tensor_tensor(out=M128.t[:], in0=M.t[:], in1=sl[0].ap, op=ALU.subtract), reads=[M, bankA], writes=[M128])
                    yield
                    kb.op(PE, lambda: nc.tensor.matmul(sl[1].ap, lhsT=B.bek.t[:], rhs=M128.t[:], start=True, stop=True), reads=[B.bek, M128], writes=[sl[1].b])
                    kb.op(ACT, lambda: nc.scalar.activation(out=nwT.t[:], in_=sl[1].ap, func=AF.Identity, scale=-1.0), reads=[sl[1].b], writes=[nwT])
                    yield

                def post_seq(hi, n):
                    t, blk = n // 4, n % 4
                    cs = slice(blk * 128, (blk + 1) * 128)
                    qT = qT2[t % 2]
                    pr = n % 2
                    hh_ = hv0 + hi
                    hq = hi // 2
                    B, sl = pb[hi], hsB[hi]
                    sc = lambda nm: tm[nm].t[:, n, hh_:hh_ + 1]
                    QKd, M128, bv, kdec, nwT = B.QKd[pr], B.M128[pr], B.bv[pr], B.kdec[pr], B.nwT[pr]
                    kb.op(PE, lambda: nc.tensor.matmul(sl[0].ap, lhsT=M128.t[:], rhs=bv.t[:], start=True, stop=False), reads=[M128, bv], writes=[sl[0].b], inc=False)
                    kb.op(PE, lambda: nc.tensor.matmul(sl[0].ap, lhsT=nwT.t[:], rhs=Sb[hi].t[:], start=False, stop=True), reads=[nwT, Sb[hi]], writes=[sl[0].b], inc=False)
                    kb.op(PE, lambda: nc.tensor.matmul(sl[1].ap, lhsT=qT.t[:, hq, cs], rhs=Sb[hi].t[:], start=True, stop=True), reads=[qT, Sb[hi]], writes=[sl[1].b])
                    kb.op(ACT, lambda: nc.scalar.copy(out=B.vn.t[:], in_=sl[0].ap), reads=[sl[0].b], writes=[B.vn])
                    kb.op(ACT, lambda: nc.scalar.activation(out=B.o1.t[:], in_=sl[1].ap, func=AF.Identity, scale=sc("eG")), reads=[sl[1].b, tm["eG"]], writes=[B.o1])
                    yield
                    kb.op(PE, lambda: nc.tensor.matmul(sl[2].ap, lhsT=kdec.t[:], rhs=B.vn.t[:], start=True, stop=True), reads=[kdec, B.vn], writes=[sl[2].b], inc=False)
                    kb.op(PE, lambda: nc.tensor.matmul(sl[3].ap, lhsT=QKd.t[:], rhs=B.vn.t[:], start=True, stop=True), reads=[QKd, B.vn], writes=[sl[3].b])
                    kb.op(DVE, lambda: nc.vector.scalar_tensor_tensor(out=Sf[hi].t[:], in0=Sf[hi].t[:], scalar=sc("gl"), in1=sl[2].ap, op0=ALU.mult, op1=ALU.add),
                          reads=[Sf[hi], tm["gl"], sl[2].b], writes=[Sf[hi]])
                    kb.op(DVE, lambda: nc.vector.tensor_tensor(out=ob2[pr].t[:, hi, :], in0=B.o1.t[:], in1=sl[3].ap, op=ALU.add), reads=[B.o1, sl[3].b], writes=[ob2[pr]])
                    yield
                    kb.op(ACT, lambda: nc.scalar.copy(out=Sb[hi].t[:], in_=Sf[hi].t[:]), reads=[Sf[hi]], writes=[Sb[hi]])
                    yield

                def norm_seq(n):
                    t, blk = n // 4, n % 4
                    cs = slice(blk * 128, (blk + 1) * 128)
                    zg, dn_o = zg2[t % 2], dno[t % 2]
                    ob, ss, rstd = ob2[n % 2], ss2[n % 2], rstd2[n % 2]
                    for hi in range(HVG):
                        kb.op(ACT, lambda: nc.scalar.activation(out=junk.t[:], in_=ob.t[:, hi, :], func=AF.Square, accum_out=ss.t[:, hi:hi + 1]), reads=[ob], writes=[junk, ss])
                    yield
                    kb.op(ACT, lambda: nc.scalar.activation(out=rstd.t[:], in_=ss.t[:], func=AF.Ln, scale=1.0 / 128.0, bias=eps6), reads=[ss, kb.eps_ln], writes=[rstd])
                    kb.op(ACT, lambda: nc.scalar.activation(out=rstd.t[:], in_=rstd.t[:], func=AF.Exp, scale=-0.5), reads=[rstd], writes=[rstd])
                    yield
                    for hi in range(HVG):
                        B = pb[hi]
                        kb.op(DVE, lambda: nc.vector.scalar_tensor_tensor(out=B.dntm.t[:], in0=ob.t[:, hi, :], scalar=rstd.t[:, hi:hi + 1], in1=zg.t[:, blk, hi * 128:(hi + 1) * 128],
                                                                          op0=ALU.mult, op1=ALU.mult), reads=[ob, rstd, zg], writes=[B.dntm])
                    yield
                    for hi in range(HVG):
                        B, sl = pb[hi], hsB[hi]
                        kb.op(PE, lambda: nc.tensor.transpose(sl[0].apb, B.dntm.t[:], ident_b), reads=[B.dntm, cmb], writes=[sl[0].b])
                        kb.op(ACT, lambda: nc.scalar.copy(out=dn_o.t[:, hi, cs], in_=sl[0].apb), reads=[sl[0].b], writes=[dn_o])
                    if blk == 3:
                        kb.dma(dnT_d[hv0:hv0 + HVG, :, t * 512:(t + 1) * 512].rearrange("c p t -> p c t"), dn_o.t[:], reads=[dn_o])
                    yield

                def run(gens, delays=None):
                    gens = list(gens)
                    delays = list(delays) if delays is not None else [0] * len(gens)
                    rnd = 0
                    while gens:
                        for i in range(len(gens) - 1, -1, -1):
                            pass
                        keep_g, keep_d = [], []
                        for g, d in zip(gens, delays):
                            if rnd < d:
                                keep_g.append(g); keep_d.append(d)
                                continue
                            try:
                                next(g)
                                keep_g.append(g); keep_d.append(d)
                            except StopIteration:
                                pass
                        gens, delays = keep_g, keep_d
                        rnd += 1

                STAG = getattr(build_program, "stagger", 0)
                load_h(0)
                if NT > 1:
                    load_h(1)
                stageA(0)
                hq_ops(0)
                run([pre_seq(hi, 0) for hi in range(HVG)])
                pend_norm = []
                for n in range(NCH):
                    gens = []
                    if n + 1 < NCH:
                        if (n + 1) % 4 == 0:
                            t1 = (n + 1) // 4
                            stageA(t1)
                            if t1 + 1 < NT:
                                load_h(t1 + 1)
                        hq_ops(n + 1)
                        gens = [pre_seq(hi, n + 1) for hi in range(HVG)]
                    posts = [post_seq(hi, n) for hi in range(HVG)]
                    run(posts + pend_norm + gens, [0] * (len(posts) + len(pend_norm)) + [STAG * (i // 2) for i in range(len(gens))])
                    pend_norm = [norm_seq(n)]
                run(pend_norm)
                kb.barrier()


def _consts():
    i = np.arange(128)
    ident = np.eye(128, dtype=np.float32)
    L = (i[:, None] <= i[None, :]).astype(np.float32)
    U = (i[:, None] > i[None, :]).astype(np.float32)
    mS = np.where(i[:, None] > i[None, :], 0.0, BIG).astype(np.float32)
    mIT = np.where(i[None, :] >= i[:, None], 0.0, BIG).astype(np.float32)
    blk = i // 64
    BD = (blk[:, None] == blk[None, :]).astype(np.float32)
    OFF = ((blk[:, None] == 1) & (blk[None, :] == 0)).astype(np.float32)
    ones = np.ones((128, 128), np.float32)
    return np.concatenate([ident, L, U, mS, mIT, BD, OFF, ones], axis=1)


def _fm(v, n):
    return np.ascontiguousarray(np.asarray(v, np.float32).reshape(n, 128).T)


def _band(w):
    full = np.zeros((DRNN, DRNN), np.float32)
    for b in range(16):
        full[b * 80:(b + 1) * 80, b * 80:(b + 1) * 80] = w[b]
    out = np.zeros((128, 10, 3, 128), np.float32)
    for m in range(10):
        for d in range(3):
            k = m + d - 1
            if 0 <= k < 10:
                out[:, m, d, :] = full[k * 128:(k + 1) * 128, m * 128:(m + 1) * 128]
    return out


def make_in_maps(inp, S, batches):
    l = 0
    f = lambda a: np.ascontiguousarray(np.asarray(a, np.float32))
    NCH = S // 128
    rows = np.zeros((128, NR), np.float32)
    b_ada = f(inp["b_ada"][l])
    def setr(name, v):
        o, w = _R[name]
        rows[:, o:o + w] = np.asarray(v, np.float32)[None, :]
    setr("b_gt1", b_ada[2048:3072]); setr("b_gt2", b_ada[5120:6144])
    setr("ln1_g", inp["ln1_g"][l]); setr("ln1_b", inp["ln1_b"][l]); setr("ln2_g", inp["ln2_g"][l]); setr("ln2_b", inp["ln2_b"][l])
    setr("nw", np.tile(f(inp["dn_norm_w"][l]), 16))
    hrep = np.zeros((128, 2, NCH * 16), np.float32)
    hrep[:, 0, :] = np.tile(f(inp["dn_dt_bias"][l]), NCH)[None, :]
    hrep[:, 1, :] = np.tile(f(inp["dn_a_log"][l]), NCH)[None, :]
    cmat = _consts()
    shared = dict(
        w_ada=f(inp["w_ada"][l]), w_in=f(inp["w_in"][l]), gate_a=_band(f(inp["rg_w_a"][l])), gate_x=_band(f(inp["rg_w_x"][l])),
        w_proj_a=f(inp["w_proj_a"][l]), w_proj_b=f(inp["w_proj_b"][l]), w_out=f(inp["w_out"][l]),
        ffn_w_gate=f(inp["ffn_w_gate"][l]), ffn_w_up=f(inp["ffn_w_up"][l]), ffn_w_down=f(inp["ffn_w_down"][l]),
        rows=rows, cmat=cmat, hrep=hrep)
    maps = []
    for b in batches:
        params = np.zeros((128, NP), np.float32)
        def setp(name, v):
            o, w = _P[name]
            params[:, o:o + w] = v
        setp("b_ada", _fm(b_ada, 48))
        setp("c", _fm(inp["c"][b], 8))
        setp("rg_cw", np.stack([_fm(inp["rg_conv_w"][l][k], 10) for k in range(4)], axis=2).reshape(128, 40))
        setp("rg_cb", _fm(inp["rg_conv_b"][l], 10)); setp("rg_ba", _fm(inp["rg_b_a"][l], 10)); setp("rg_bx", _fm(inp["rg_b_x"][l], 10))
        setp("rg_lam", _fm(inp["rg_lambda"][l], 10))
        setp("dn_cw", np.stack([_fm(inp["dn_conv_w"][l][k], 32) for k in range(4)], axis=2).reshape(128, 128))
        setp("ffn_cw", np.stack([_fm(inp["ffn_conv_w"][l][k], 22) for k in range(3)], axis=2).reshape(128, 66))
        setp("ffn_cb", _fm(inp["ffn_conv_b"][l], 22))
        xb = f(inp["x"][b])
        m = dict(shared)
        m["x"] = xb
        m["xT"] = np.ascontiguousarray(xb.T.reshape(8, 128, S))
        m["params"] = params
        maps.append(m)
    return maps


_NC_CACHE = {}


def kernel(**inputs):
    S = inputs["x"].shape[1]
    B = inputs["x"].shape[0]
    if S not in _NC_CACHE:
        _NC_CACHE[S] = build_program(S)
    nc = _NC_CACHE[S]
    maps = make_in_maps(inputs, S, list(range(B)))
    res = run_bass_kernel_spmd(nc, maps, core_ids=list(range(B)))
    return np.stack([np.asarray(r["out"], np.float32) for r in res.results], axis=0)
```

```python
import contextlib
import numpy as np
import concourse.bass as bass
import concourse.mybir as mybir
from concourse.bass_utils import run_bass_kernel_spmd

F32 = mybir.dt.float32
BF16 = mybir.dt.bfloat16
AF = mybir.ActivationFunctionType
ALU = mybir.AluOpType

D = 1024
DRNN = 1280
DFF = 2816
DIN = 10784
NHV = 16
C_XR, C_GR, C_Q, C_K, C_V, C_Z, C_A, C_B, C_GA, C_GB = 0, 1280, 2560, 3584, 4608, 6656, 8704, 8720, 8736, 9760
ALPHA = 2.0 ** 0.25
BIG = 32768.0
GELU_C = 1.5957691216057308

_P = {}
_off = 0
for _n, _w in [("b_ada", 48), ("c", 8), ("rg_cw", 40), ("rg_cb", 10), ("rg_ba", 10), ("rg_bx", 10), ("rg_lam", 10),
               ("dn_cw", 128), ("ffn_cw", 66), ("ffn_cb", 22)]:
    _P[_n] = (_off, _w)
    _off += _w
NP = _off
_R = {}
_off = 0
for _n, _w in [("b_gt1", 1024), ("b_gt2", 1024), ("ln1_g", 1024), ("ln1_b", 1024), ("ln2_g", 1024), ("ln2_b", 1024),
               ("nw", 2048)]:
    _R[_n] = (_off, _w)
    _off += _w
NR = _off
CM_IDENT, CM_L, CM_U, CM_MS, CM_MIT, CM_BD, CM_OFF, CM_ONES = range(8)
NCM = 8


class Buf:
    def __init__(self, kb, name, t):
        self.kb = kb
        self.name = name
        self.t = t
        self.w = None
        self.r = []
        self.lsem = None
        self.ssem = None
        self.excl = False

    def add_reader(self, tok):
        for i, (s, v) in enumerate(self.r):
            if s == tok[0]:
                if v < tok[1]:
                    self.r[i] = tok
                return
        self.r.append(tok)


class EngW:
    LIMIT = 24000

    def __init__(self, kb, eng, name):
        self.kb = kb
        self.eng = eng
        self.name = name
        self.sem = None
        self.cnt = 0
        self.seen = {}
        self.pending = []

    def wait(self, toks):
        best = {}
        for s, v in toks:
            if best.get(s, 0) < v:
                best[s] = v
        for s, v in best.items():
            if self.seen.get(s, 0) >= v:
                continue
            self.eng.wait_ge(self.kb.sems[s], v)
            self.seen[s] = v

    def bump(self, inst):
        if self.sem is None or self.cnt >= self.LIMIT:
            self.sem = self.kb.new_sem(f"e_{self.name}_{len(self.kb.sems)}")
            self.cnt = 0
        inst.then_inc(self.kb.sems[self.sem], 1)
        self.cnt += 1
        return (self.sem, self.cnt)

    def cur(self):
        return None if self.sem is None else (self.sem, self.cnt)


class KB:
    def __init__(self, nc, es):
        self.nc = nc
        self.es = es
        self.sems = []
        self.semcnt = []
        self.free_dma_sems = []
        self.pe = EngW(self, nc.tensor, "pe")
        self.act = EngW(self, nc.scalar, "act")
        self.dve = EngW(self, nc.vector, "dve")
        self.pool = EngW(self, nc.gpsimd, "pool")
        self.sp = EngW(self, nc.sync, "sp")
        self.engs = [self.pe, self.act, self.dve, self.pool, self.sp]
        self.dma_toks = {}
        self.phase_bufs = []
        self.n_inst = 0

    def new_sem(self, name):
        h = self.es.enter_context(self.nc.semaphore(name))
        self.sems.append(h)
        self.semcnt.append(0)
        return len(self.sems) - 1

    def dma_sem(self):
        if self.free_dma_sems:
            return self.free_dma_sems.pop()
        return self.new_sem(f"d_{len(self.sems)}")

    def sb(self, pes, name, shape, dt):
        self.n_alloc = getattr(self, "n_alloc", 0) + 1
        t = pes.enter_context(self.nc.sbuf_tensor(f"sb{self.n_alloc}_{name}", list(shape), dt))
        b = Buf(self, name, t)
        self.phase_bufs.append(b)
        return b

    def sbs(self, pes, name, shape, dt):
        b = self.sb(pes, name, shape, dt)
        b.sub = [Buf(self, f"{name}[{i}]", b.t) for i in range(shape[1])]
        self.phase_bufs.extend(b.sub)
        return b

    def ps(self, pes, name, shape, dt):
        self.n_alloc = getattr(self, "n_alloc", 0) + 1
        t = pes.enter_context(self.nc.psum_tensor(f"ps{self.n_alloc}_{name}", list(shape), dt))
        b = Buf(self, name, t)
        b.excl = True
        self.phase_bufs.append(b)
        return b

    def op(self, E, fn, reads=(), writes=(), inc=True):
        ex = [b for b in reads if b.excl]
        if ex:
            reads = [b for b in reads if not b.excl]
            writes = list(writes) + [b for b in ex if b not in writes]
        toks = []
        for b in reads:
            if b.w is not None:
                toks.append(b.w)
        for b in writes:
            if b.w is not None:
                toks.append(b.w)
            toks.extend(b.r)
        E.wait(toks)
        inst = fn()
        self.n_inst += 1
        if inc:
            tok = E.bump(inst)
            sets = E.pending + [(reads, writes)]
            E.pending = []
            for rd, wr in sets:
                for b in rd:
                    b.add_reader(tok)
                for b in wr:
                    b.w = tok
                    b.r = []
        else:
            E.pending.append((reads, writes))
        return inst

    def dma(self, out_ap, in_ap, reads=(), writes=(), Q=None):
        Q = Q or self.sp
        toks = []
        for b in reads:
            if b.w is not None:
                toks.append(b.w)
        for b in writes:
            if b.w is not None:
                toks.append(b.w)
            toks.extend(b.r)
        Q.wait(toks)
        inst = Q.eng.dma_start(out=out_ap, in_=in_ap)
        self.n_inst += 1
        if writes:
            tb = writes[0]
            if tb.lsem is None:
                tb.lsem = self.dma_sem()
            s = tb.lsem
        else:
            tb = reads[0]
            if tb.ssem is None:
                tb.ssem = self.dma_sem()
            s = tb.ssem
        self.semcnt[s] += 16
        inst.then_inc(self.sems[s], 16)
        tok = (s, self.semcnt[s])
        self.dma_toks[s] = tok
        for b in writes:
            b.w = tok
            b.r = []
        for b in reads:
            b.add_reader(tok)
        return tok

    def barrier(self, keep=()):
        toks = list(self.dma_toks.values())
        for E in self.engs:
            assert not E.pending
            c = E.cur()
            if c is not None:
                toks.append(c)
        for E in self.engs:
            E.wait(toks)
        for b in self.phase_bufs:
            if b.lsem is not None:
                self.free_dma_sems.append(b.lsem)
                b.lsem = None
            if b.ssem is not None:
                self.free_dma_sems.append(b.ssem)
                b.ssem = None
        self.dma_toks = {}

    def final_wait(self):
        toks = list(self.dma_toks.values())
        self.sp.wait(toks)


def _cm(cm, idx, dt=None):
    return cm.t[:, idx * 128:(idx + 1) * 128]


def build_program(S, debug=False):
    assert S % 512 == 0
    NT = S // 512
    NCH = S // 128
    nc = bass.Bass("TRN2", target_bir_lowering=False)
    okind = "ExternalOutput" if debug else "Internal"

    def din(name, shape, dt=F32):
        return nc.dram_tensor(name, list(shape), dt, kind="ExternalInput").ap()

    xT_d = din("xT", [8, 128, S])
    x_d = din("x", [S, D])
    w_ada_d = din("w_ada", [D, 6 * D])
    w_in_d = din("w_in", [D, DIN])
    gate_a_d = din("gate_a", [128, 10, 3, 128])
    gate_x_d = din("gate_x", [128, 10, 3, 128])
    w_pa_d = din("w_proj_a", [DRNN, D])
    w_pb_d = din("w_proj_b", [2048, D])
    w_out_d = din("w_out", [D, D])
    w_fg_d = din("ffn_w_gate", [D, DFF])
    w_fu_d = din("ffn_w_up", [D, DFF])
    w_fd_d = din("ffn_w_down", [DFF, D])
    params_d = din("params", [128, NP])
    rows_d = din("rows", [128, NR])
    cmat_d = din("cmat", [128, NCM * 128])
    hrep_d = din("hrep", [128, 2, NCH * 16])
    out_d = nc.dram_tensor("out", [S, D], F32, kind="ExternalOutput").ap()
    hT_d = nc.dram_tensor("hT_s", [8, 128, S], BF16, kind=okind).ap()
    recT_d = nc.dram_tensor("recT_s", [10, 128, S], BF16, kind=okind).ap()
    dnT_d = nc.dram_tensor("dnT_s", [16, 128, S], BF16, kind=okind).ap()
    x1_d = nc.dram_tensor("x1_s", [S, D], F32, kind=okind).ap()

    with contextlib.ExitStack() as es:
        kb = KB(nc, es)
        PE, ACT, DVE, POOL = kb.pe, kb.act, kb.dve, kb.pool
        params = kb.sb(es, "params", [128, NP], F32)
        cm = kb.sb(es, "cm", [128, NCM * 128], F32)
        cmb = kb.sb(es, "cmb", [128, NCM * 128], BF16)
        ada = kb.sb(es, "ada", [128, 48], F32)
        gt1r = kb.sb(es, "gt1r", [128, 1024], F32)
        gt2r = kb.sb(es, "gt2r", [128, 1024], F32)
        kb.dma(params.t[:], params_d[:, :], writes=[params])
        kb.dma(cm.t[:], cmat_d[:, :], writes=[cm])
        kb.op(ACT, lambda: nc.scalar.copy(out=cmb.t[:], in_=cm.t[:]), reads=[cm], writes=[cmb])

        def pcol(name, j=0, n=1):
            o, w = _P[name]
            return params.t[:, o + j:o + j + n]

        ident_f = cm.t[:, CM_IDENT * 128:(CM_IDENT + 1) * 128]
        ident_b = cmb.t[:, CM_IDENT * 128:(CM_IDENT + 1) * 128]
        ones_f = cm.t[:, CM_ONES * 128:(CM_ONES + 1) * 128]
        ones_b = cmb.t[:, CM_ONES * 128:(CM_ONES + 1) * 128]

        def load_w_bf16(pes, dst, dram, r0, nk, c0, ncols, stg, dst_c0=0):
            PW = stg[0].t.shape[-1]
            i = load_w_bf16.cnt
            for cc in range(0, ncols, PW):
                w = min(PW, ncols - cc)
                st = stg[i % len(stg)]
                src = dram[r0:r0 + nk * 128, c0 + cc:c0 + cc + w].rearrange("(k p) c -> p k c", p=128)
                kb.dma(st.t[:, 0:nk, 0:w], src, writes=[st])
                o = dst.t[:, 0:nk, dst_c0 + cc:dst_c0 + cc + w]
                if i % 2 == 0:
                    kb.op(ACT, lambda o=o, st=st, w=w: nc.scalar.copy(out=o, in_=st.t[:, 0:nk, 0:w]), reads=[st], writes=[dst])
                else:
                    kb.op(DVE, lambda o=o, st=st, w=w: nc.vector.tensor_copy(out=o, in_=st.t[:, 0:nk, 0:w]), reads=[st], writes=[dst])
                i += 1
            load_w_bf16.cnt = i
        load_w_bf16.cnt = 0

        with contextlib.ExitStack() as pes:
            scf = kb.sb(pes, "scf", [128, 8], F32)
            screp = kb.sb(pes, "screp", [128, 8, 128], F32)
            wst = [kb.sb(pes, f"wst{i}", [128, 8, 1024], F32) for i in range(2)]
            rows01 = kb.sb(pes, "rows01", [128, 2048], F32)
            pa = kb.ps(pes, "pa", [128, 512], F32)
            pb = [kb.ps(pes, f"pb{i}", [128, 512], F32) for i in range(2)]
            kb.dma(rows01.t[:], rows_d[:, 0:2048], writes=[rows01])
            kb.op(ACT, lambda: nc.scalar.activation(out=scf.t[:], in_=pcol("c", 0, 8), func=AF.Silu), reads=[params], writes=[scf])
            for k in range(8):
                kb.op(DVE, lambda k=k: nc.vector.tensor_copy(out=screp.t[:, k, :], in_=scf.t[:, k:k + 1].to_broadcast([128, 128])),
                      reads=[scf], writes=[screp])
            for jb in range(6):
                st = wst[jb % 2]
                kb.dma(st.t[:], w_ada_d[:, jb * 1024:(jb + 1) * 1024].rearrange("(k p) c -> p k c", p=128), writes=[st])
                if jb in (2, 5):
                    dst = gt1r if jb == 2 else gt2r
                    for cb in range(2):
                        for k in range(8):
                            kb.op(PE, lambda k=k, cb=cb, st=st: nc.tensor.matmul(pb[cb].t[:], lhsT=screp.t[:, k, :], rhs=st.t[:, k, cb * 512:(cb + 1) * 512],
                                                                               start=(k == 0), stop=(k == 7)),
                                  reads=[screp, st], writes=[pb[cb]], inc=(k == 7))
                        ro = (0 if jb == 2 else 1024) + cb * 512
                        kb.op(DVE, lambda cb=cb, ro=ro, dst=dst: nc.vector.scalar_tensor_tensor(
                            out=dst.t[:, cb * 512:(cb + 1) * 512], in0=pb[cb].t[:], scalar=1.0, in1=rows01.t[:, ro:ro + 512],
                            op0=ALU.add, op1=ALU.add), reads=[pb[cb], rows01], writes=[dst])
                for j in range(8):
                    col = jb * 8 + j
                    for k in range(8):
                        kb.op(PE, lambda k=k, j=j, col=col, st=st: nc.tensor.matmul(pa.t[:, col:col + 1], lhsT=st.t[:, k, j * 128:(j + 1) * 128],
                                                                                 rhs=scf.t[:, k:k + 1], start=(k == 0), stop=(k == 7)),
                              reads=[scf, st], writes=[pa], inc=(k == 7))
            kb.op(DVE, lambda: nc.vector.tensor_tensor(out=ada.t[:], in0=pa.t[:, 0:48], in1=pcol("b_ada", 0, 48), op=ALU.add),
                  reads=[pa, params], writes=[ada])
            for o in (8, 32):
                kb.op(DVE, lambda o=o: nc.vector.tensor_scalar(out=ada.t[:, o:o + 8], in0=ada.t[:, o:o + 8], scalar1=1.0, scalar2=None, op0=ALU.add),
                      reads=[ada], writes=[ada])
            xin = [kb.sb(pes, f"xin{i}", [128, 8, 512], F32) for i in range(2)]
            hto = [kb.sb(pes, f"hto{i}", [128, 8, 512], BF16) for i in range(2)]
            for t in range(NT):
                xi, ho = xin[t % 2], hto[t % 2]
                kb.dma(xi.t[:], xT_d[:, :, t * 512:(t + 1) * 512].rearrange("c p t -> p c t"), writes=[xi])
                for c in range(8):
                    if c % 2 == 0:
                        kb.op(ACT, lambda c=c, xi=xi, ho=ho: nc.scalar.activation(out=ho.t[:, c, :], in_=xi.t[:, c, :], func=AF.Identity,
                                                                              scale=ada.t[:, 8 + c:9 + c], bias=ada.t[:, c:c + 1]),
                              reads=[xi, ada], writes=[ho])
                    else:
                        kb.op(DVE, lambda c=c, xi=xi, ho=ho: nc.vector.tensor_scalar(out=ho.t[:, c, :], in0=xi.t[:, c, :], scalar1=ada.t[:, 8 + c:9 + c],
                                                                                 scalar2=ada.t[:, c:c + 1], op0=ALU.mult, op1=ALU.add),
                              reads=[xi, ada], writes=[ho])
                kb.dma(hT_d[:, :, t * 512:(t + 1) * 512].rearrange("c p t -> p c t"), ho.t[:], reads=[ho])
            kb.barrier()

        PHASES = build_program.phases
        eps_ln = kb.sb(es, "eps_ln", [128, 2], F32)
        kb.eps_ln = eps_ln
        kb.op(DVE, lambda: nc.vector.memset(eps_ln.t[:, 0:1], 1e-5), writes=[eps_ln])
        kb.op(DVE, lambda: nc.vector.memset(eps_ln.t[:, 1:2], 1e-6), writes=[eps_ln])
        if "rg" in PHASES:
            phase_rg(nc, kb, S, params, pcol, cm, cmb, load_w_bf16, w_in_d, gate_a_d, gate_x_d, hT_d, recT_d)
        if "gdn" in PHASES:
            phase_gdn(nc, kb, S, params, pcol, cm, cmb, load_w_bf16, w_in_d, rows_d, hrep_d, hT_d, dnT_d)
        if "mix" in PHASES:
            phase_mix(nc, kb, S, load_w_bf16, gt1r, w_in_d, w_pa_d, w_pb_d, w_out_d, rows_d, x_d, hT_d, recT_d, dnT_d, x1_d)
        if "ffn" in PHASES:
            phase_ffn(nc, kb, S, params, pcol, cm, load_w_bf16, ada, gt2r, w_fg_d, w_fu_d, w_fd_d, rows_d, x1_d, out_d)
        kb.final_wait()
    return nc


build_program.phases = ("rg", "gdn", "mix", "ffn")


def gelu_tanh(nc, kb, src_ap, src_bufs, out_ap, out_bufs, tmp, tmp2=None, mul_ap=None, mul_bufs=(), defer=None):
    ACT, DVE = kb.act, kb.dve
    if mul_ap is None:
        kb.op(ACT, lambda: nc.scalar.activation(out=out_ap, in_=src_ap, func=AF.Gelu_apprx_tanh), reads=src_bufs, writes=out_bufs)
    else:
        kb.op(ACT, lambda: nc.scalar.activation(out=tmp.t[:], in_=src_ap, func=AF.Gelu_apprx_tanh), reads=src_bufs, writes=[tmp])
        def fin():
            kb.op(DVE, lambda: nc.vector.tensor_tensor(out=out_ap, in0=tmp.t[:], in1=mul_ap, op=ALU.mult), reads=[tmp] + list(mul_bufs), writes=out_bufs)
        if defer is None:
            fin()
        else:
            defer.append(fin)


def phase_rg(nc, kb, S, params, pcol, cm, cmb, load_w_bf16, w_in_d, gate_a_d, gate_x_d, hT_d, recT_d):
    PE, ACT, DVE, POOL = kb.pe, kb.act, kb.dve, kb.pool
    TT = 256
    NT = S // TT
    with contextlib.ExitStack() as pes:
        wx = kb.sb(pes, "wx", [128, 8, 2560], BF16)
        wga = kb.sb(pes, "wga", [128, 10, 3, 128], BF16)
        wgx = kb.sb(pes, "wgx", [128, 10, 3, 128], BF16)
        cl = kb.sb(pes, "cl", [128, 20], F32)
        with contextlib.ExitStack() as ses:
            stg = [kb.sb(ses, f"stg{i}", [128, 8, 512], F32) for i in range(2)]
            gst = kb.sb(ses, "gst", [128, 10, 3, 128], F32)
            load_w_bf16(ses, wx, w_in_d, 0, 8, 0, 2560, stg)
            kb.dma(gst.t[:], gate_a_d[:, :, :, :], writes=[gst])
            kb.op(ACT, lambda: nc.scalar.copy(out=wga.t[:], in_=gst.t[:]), reads=[gst], writes=[wga])
            kb.dma(gst.t[:], gate_x_d[:, :, :, :], writes=[gst])
            kb.op(ACT, lambda: nc.scalar.copy(out=wgx.t[:], in_=gst.t[:]), reads=[gst], writes=[wgx])
            kb.barrier(keep=[wx, wga, wgx, cl])
        kb.op(ACT, lambda: nc.scalar.activation(out=cl.t[:, 0:10], in_=pcol("rg_lam", 0, 10), func=AF.Exp, scale=-1.0), reads=[params], writes=[cl])
        kb.op(ACT, lambda: nc.scalar.activation(out=cl.t[:, 0:10], in_=cl.t[:, 0:10], func=AF.Ln, bias=1.0), reads=[cl], writes=[cl])
        kb.op(DVE, lambda: nc.vector.tensor_scalar(out=cl.t[:, 10:20], in0=cl.t[:, 0:10], scalar1=-16.0, scalar2=None, op0=ALU.mult), reads=[cl], writes=[cl])
        kb.op(DVE, lambda: nc.vector.tensor_scalar(out=cl.t[:, 0:10], in0=cl.t[:, 0:10], scalar1=-8.0, scalar2=None, op0=ALU.mult), reads=[cl], writes=[cl])

        hT = [kb.sb(pes, f"hT{i}", [128, 8, TT], BF16) for i in range(2)]
        raw = [kb.sb(pes, f"raw{i}", [128, TT + 3], F32) for i in range(3)]
        halo = kb.sbs(pes, "halo", [128, 10, 3], F32)
        hst = kb.sbs(pes, "hst", [128, 10, 1], F32)
        xr = kb.sbs(pes, "xr", [128, 10, TT], F32)
        xrb = kb.sbs(pes, "xrb", [128, 10, TT], BF16)
        ga = kb.sbs(pes, "ga", [128, 10, TT], F32)
        gi = kb.sbs(pes, "gi", [128, 10, TT], F32)
        mu = kb.sbs(pes, "mu", [128, 10, TT], F32)
        hh = kb.sbs(pes, "hh", [128, 10, TT], F32)
        gg = kb.sbs(pes, "gg", [128, 10, TT], F32)
        tmp = [kb.sb(pes, f"gtmp{i}", [128, TT], F32) for i in range(3)]
        rec = [kb.sbs(pes, f"rec{i}", [128, 10, TT], BF16) for i in range(2)]
        pp = [kb.ps(pes, f"pp{i}", [128, 512], F32) for i in range(6)]
        kb.op(DVE, lambda: nc.vector.memset(halo.t[:], 0.0), writes=halo.sub)
        kb.op(DVE, lambda: nc.vector.memset(hst.t[:], 0.0), writes=hst.sub)
        cwo = _P["rg_cw"][0]

        def load_h(t):
            kb.dma(hT[t % 2].t[:], hT_d[:, :, t * TT:(t + 1) * TT].rearrange("c p t -> p c t"), writes=[hT[t % 2]])

        load_h(0)
        pi = 0
        for t in range(NT):
            if t + 1 < NT:
                load_h(t + 1)
            h = hT[t % 2]
            for j in range(10):
                p = pp[pi % 6]; pi += 1
                for k in range(8):
                    kb.op(PE, lambda: nc.tensor.matmul(p.t[:, 0:TT], lhsT=wx.t[:, k, j * 128:(j + 1) * 128], rhs=h.t[:, k, :],
                                                       start=(k == 0), stop=(k == 7)), reads=[wx, h], writes=[p], inc=(k == 7))
                rw = raw[j % 3]
                kb.op(ACT, lambda: nc.scalar.copy(out=rw.t[:, 0:3], in_=halo.t[:, j, :]), reads=[halo.sub[j]], writes=[rw])
                kb.op(ACT, lambda: nc.scalar.copy(out=rw.t[:, 3:TT + 3], in_=p.t[:, 0:TT]), reads=[p], writes=[rw])
                kb.op(ACT, lambda: nc.scalar.copy(out=halo.t[:, j, :], in_=rw.t[:, TT:TT + 3]), reads=[rw], writes=[halo.sub[j]])
                kb.op(DVE, lambda: nc.vector.tensor_scalar(out=xr.t[:, j, :], in0=rw.t[:, 3:TT + 3], scalar1=params.t[:, cwo + j * 4 + 3:cwo + j * 4 + 4],
                                                           scalar2=pcol("rg_cb", j), op0=ALU.mult, op1=ALU.add), reads=[rw, params], writes=[xr.sub[j]])
                for kk in range(3):
                    kb.op(DVE, lambda: nc.vector.scalar_tensor_tensor(
                        out=xr.t[:, j, :], in0=rw.t[:, kk:kk + TT], scalar=params.t[:, cwo + j * 4 + kk:cwo + j * 4 + kk + 1], in1=xr.t[:, j, :],
                        op0=ALU.mult, op1=ALU.add), reads=[rw, params, xr.sub[j]], writes=[xr.sub[j]])
            for j in range(10):
                p = pp[pi % 6]; pi += 1
                for k in range(8):
                    kb.op(PE, lambda: nc.tensor.matmul(p.t[:, 0:TT], lhsT=wx.t[:, k, 1280 + j * 128:1280 + (j + 1) * 128], rhs=h.t[:, k, :],
                                                       start=(k == 0), stop=(k == 7)), reads=[wx, h], writes=[p], inc=(k == 7))
                kb.op(ACT, lambda: nc.scalar.activation(out=gg.t[:, j, :], in_=p.t[:, 0:TT], func=AF.Gelu_apprx_tanh), reads=[p], writes=[gg.sub[j]])
            for j in range(10):
                kb.op(ACT, lambda: nc.scalar.copy(out=xrb.t[:, j, :], in_=xr.t[:, j, :]), reads=[xr.sub[j]], writes=[xrb.sub[j]])
            for (wg, dst, bname) in ((wga, ga, "rg_ba"), (wgx, gi, "rg_bx")):
                for m in range(10):
                    p = pp[pi % 6]; pi += 1
                    ks = [k for k in (m - 1, m, m + 1) if 0 <= k < 10]
                    for n_, k in enumerate(ks):
                        kb.op(PE, lambda: nc.tensor.matmul(
                            p.t[:, 0:TT], lhsT=wg.t[:, m, k - m + 1, :], rhs=xrb.t[:, k, :], start=(n_ == 0), stop=(n_ == len(ks) - 1)),
                            reads=[wg, xrb.sub[k]], writes=[p], inc=(n_ == len(ks) - 1))
                    kb.op(ACT, lambda: nc.scalar.activation(out=dst.t[:, m, :], in_=p.t[:, 0:TT], func=AF.Sigmoid, bias=pcol(bname, m)),
                          reads=[p, params], writes=[dst.sub[m]])
            for m in range(10):
                kb.op(ACT, lambda: nc.scalar.activation(out=mu.t[:, m, :], in_=ga.t[:, m, :], func=AF.Exp, scale=cl.t[:, 10 + m:11 + m]), reads=[ga.sub[m], cl], writes=[mu.sub[m]])
                kb.op(ACT, lambda: nc.scalar.activation(out=ga.t[:, m, :], in_=ga.t[:, m, :], func=AF.Exp, scale=cl.t[:, m:m + 1]), reads=[ga.sub[m], cl], writes=[ga.sub[m]])
            for m in range(10):
                kb.op(ACT, lambda: nc.scalar.activation(out=mu.t[:, m, :], in_=mu.t[:, m, :], func=AF.Sqrt, scale=-1.0, bias=1.0), reads=[mu.sub[m]], writes=[mu.sub[m]])
            for m in range(10):
                kb.op(POOL, lambda: nc.gpsimd.tensor_tensor(out=gi.t[:, m, :], in0=gi.t[:, m, :], in1=xr.t[:, m, :], op=ALU.mult), reads=[gi.sub[m], xr.sub[m]], writes=[gi.sub[m]])
                kb.op(DVE, lambda: nc.vector.tensor_tensor(out=gi.t[:, m, :], in0=gi.t[:, m, :], in1=mu.t[:, m, :], op=ALU.mult), reads=[gi.sub[m], mu.sub[m]], writes=[gi.sub[m]])
                kb.op(DVE, lambda: nc.vector.tensor_tensor_scan(out=hh.t[:, m, :], data0=ga.t[:, m, :], data1=gi.t[:, m, :], initial=hst.t[:, m, :],
                                                                op0=ALU.mult, op1=ALU.add), reads=[ga.sub[m], gi.sub[m], hst.sub[m]], writes=[hh.sub[m]])
                kb.op(ACT, lambda: nc.scalar.copy(out=hst.t[:, m, :], in_=hh.t[:, m, TT - 1:TT]), reads=[hh.sub[m]], writes=[hst.sub[m]])
            rc = rec[t % 2]
            for j in range(10):
                kb.op(DVE, lambda: nc.vector.tensor_tensor(out=rc.t[:, j, :], in0=hh.t[:, j, :], in1=gg.t[:, j, :], op=ALU.mult), reads=[hh.sub[j], gg.sub[j]], writes=[rc.sub[j]])
            kb.dma(recT_d[:, :, t * TT:(t + 1) * TT].rearrange("c p t -> p c t"), rc.t[:], reads=rc.sub)
        kb.barrier()


def layer_norm_rows(nc, kb, y, nblk, g_ap, b_ap, rowbuf, stats, mv, sd, out_buf):
    ACT, DVE, POOL = kb.act, kb.dve, kb.pool
    for blk in range(nblk):
        for hf in range(2):
            kb.op(DVE, lambda: nc.vector.bn_stats(out=stats.t[:, blk, hf, :], in_=y.t[:, blk, hf * 512:(hf + 1) * 512]), reads=[y], writes=[stats])
        kb.op(DVE, lambda: nc.vector.bn_aggr(out=mv.t[:, blk, :], in_=stats.t[:, blk, :, :].rearrange("p a b -> p (a b)")), reads=[stats], writes=[mv])
        kb.op(ACT, lambda: nc.scalar.activation(out=sd.t[:, blk:blk + 1], in_=mv.t[:, blk, 1:2], func=AF.Ln, bias=kb.eps_ln.t[:, 0:1]), reads=[mv, kb.eps_ln], writes=[sd])
        kb.op(ACT, lambda: nc.scalar.activation(out=sd.t[:, blk:blk + 1], in_=sd.t[:, blk:blk + 1], func=AF.Exp, scale=-0.5), reads=[sd], writes=[sd])
        kb.op(DVE, lambda: nc.vector.tensor_scalar(out=y.t[:, blk, :], in0=y.t[:, blk, :], scalar1=mv.t[:, blk, 0:1], scalar2=sd.t[:, blk:blk + 1],
                                                   op0=ALU.subtract, op1=ALU.mult), reads=[y, mv, sd], writes=[y])
        kb.op(POOL, lambda: nc.gpsimd.tensor_tensor(out=y.t[:, blk, :], in0=y.t[:, blk, :], in1=g_ap, op=ALU.mult), reads=[y, rowbuf], writes=[y])
        kb.op(DVE, lambda: nc.vector.tensor_tensor(out=out_buf.t[:, blk, :], in0=y.t[:, blk, :], in1=b_ap, op=ALU.add), reads=[y, rowbuf], writes=[out_buf])


def phase_mix(nc, kb, S, load_w_bf16, gt1r, w_in_d, w_pa_d, w_pb_d, w_out_d, rows_d, x_d, hT_d, recT_d, dnT_d, x1_d):
    PE, ACT, DVE, POOL = kb.pe, kb.act, kb.dve, kb.pool
    TT = 256
    NT = S // TT
    NB = TT // 128
    with contextlib.ExitStack() as pes:
        wg = kb.sb(pes, "wg", [128, 8, 2048], BF16)
        wpa = kb.sb(pes, "wpa", [128, 10, 1024], BF16)
        wpb = kb.sb(pes, "wpb", [128, 16, 1024], BF16)
        wo = kb.sb(pes, "wo", [128, 8, 1024], BF16)
        lnr = kb.sb(pes, "lnr", [128, 2048], F32)
        with contextlib.ExitStack() as ses:
            stg = [kb.sb(ses, f"stg{i}", [128, 16, 256], F32) for i in range(2)]
            load_w_bf16(ses, wg, w_in_d, 0, 8, C_GA, 2048, stg)
            load_w_bf16(ses, wpa, w_pa_d, 0, 10, 0, 1024, stg)
            load_w_bf16(ses, wpb, w_pb_d, 0, 16, 0, 1024, stg)
            load_w_bf16(ses, wo, w_out_d, 0, 8, 0, 1024, stg)
            kb.dma(lnr.t[:], rows_d[:, _R["ln1_g"][0]:_R["ln1_g"][0] + 2048], writes=[lnr])
            kb.barrier(keep=[wg, wpa, wpb, wo, lnr])
        hT = [kb.sb(pes, f"hT{i}", [128, 8, TT], BF16) for i in range(2)]
        rcT = [kb.sb(pes, f"rcT{i}", [128, 10, TT], BF16) for i in range(2)]
        dnT = [kb.sb(pes, f"dnT{i}", [128, 16, TT], BF16) for i in range(2)]
        xt = [kb.sb(pes, f"xt{i}", [128, NB, 1024], F32) for i in range(2)]
        mg = kb.sbs(pes, "mg", [128, 8, TT], BF16)
        sg = [kb.sb(pes, f"sg{i}", [128, TT], F32) for i in range(4)]
        t1 = [kb.sb(pes, f"t1{i}", [128, TT], F32) for i in range(4)]
        y = kb.sb(pes, "y", [128, NB, 1024], F32)
        xo = [y, y]
        stats = kb.sb(pes, "stats", [128, NB, 2, 6], F32)
        mv = kb.sb(pes, "mv", [128, NB, 2], F32)
        sd = kb.sb(pes, "sd", [128, NB], F32)
        pp = [kb.ps(pes, f"pp{i}", [128, 512], F32) for i in range(8)]

        def load(t):
            i = t % 2
            sl = slice(t * TT, (t + 1) * TT)
            kb.dma(hT[i].t[:], hT_d[:, :, sl].rearrange("c p t -> p c t"), writes=[hT[i]])
            kb.dma(rcT[i].t[:], recT_d[:, :, sl].rearrange("c p t -> p c t"), writes=[rcT[i]])
            kb.dma(dnT[i].t[:], dnT_d[:, :, sl].rearrange("c p t -> p c t"), writes=[dnT[i]])
            kb.dma(xt[i].t[:], x_d[sl, :].rearrange("(b p) f -> p b f", p=128), writes=[xt[i]])

        load(0)
        pi = 0
        for t in range(NT):
            if t + 1 < NT:
                load(t + 1)
            i = t % 2
            h, rc, dn, xx = hT[i], rcT[i], dnT[i], xt[i]
            deferred = []
            for m in range(8):
                ms = slice(m * 128, (m + 1) * 128)
                pga = pp[pi % 8]; pya = pp[(pi + 1) % 8]; pgb = pp[(pi + 2) % 8]; pyb = pp[(pi + 3) % 8]; pi += 4
                for k in range(8):
                    kb.op(PE, lambda: nc.tensor.matmul(pga.t[:, 0:TT], lhsT=wg.t[:, k, m * 128:(m + 1) * 128], rhs=h.t[:, k, :], start=(k == 0), stop=(k == 7)),
                          reads=[wg, h], writes=[pga], inc=(k == 7))
                for k in range(10):
                    kb.op(PE, lambda: nc.tensor.matmul(pya.t[:, 0:TT], lhsT=wpa.t[:, k, ms], rhs=rc.t[:, k, :], start=(k == 0), stop=(k == 9)),
                          reads=[wpa, rc], writes=[pya], inc=(k == 9))
                for k in range(8):
                    kb.op(PE, lambda: nc.tensor.matmul(pgb.t[:, 0:TT], lhsT=wg.t[:, k, 1024 + m * 128:1024 + (m + 1) * 128], rhs=h.t[:, k, :], start=(k == 0), stop=(k == 7)),
                          reads=[wg, h], writes=[pgb], inc=(k == 7))
                for k in range(16):
                    kb.op(PE, lambda: nc.tensor.matmul(pyb.t[:, 0:TT], lhsT=wpb.t[:, k, ms], rhs=dn.t[:, k, :], start=(k == 0), stop=(k == 15)),
                          reads=[wpb, dn], writes=[pyb], inc=(k == 15))
                s0, s1, ta, tb = sg[2 * (m % 2)], sg[2 * (m % 2) + 1], t1[2 * (m % 2)], t1[2 * (m % 2) + 1]
                kb.op(ACT, lambda: nc.scalar.activation(out=s0.t[:], in_=pga.t[:, 0:TT], func=AF.Sigmoid), reads=[pga], writes=[s0])
                kb.op(ACT, lambda: nc.scalar.activation(out=s1.t[:], in_=pgb.t[:, 0:TT], func=AF.Sigmoid), reads=[pgb], writes=[s1])
                def fin(m=m, s0=s0, s1=s1, ta=ta, tb=tb, pya=pya, pyb=pyb):
                    kb.op(DVE, lambda: nc.vector.tensor_tensor(out=ta.t[:], in0=s0.t[:], in1=pya.t[:, 0:TT], op=ALU.mult), reads=[s0, pya], writes=[ta])
                    kb.op(DVE, lambda: nc.vector.tensor_tensor(out=tb.t[:], in0=s1.t[:], in1=pyb.t[:, 0:TT], op=ALU.mult), reads=[s1, pyb], writes=[tb])
                    kb.op(POOL, lambda: nc.gpsimd.tensor_tensor(out=mg.t[:, m, :], in0=ta.t[:], in1=tb.t[:], op=ALU.add), reads=[ta, tb], writes=[mg.sub[m]])
                prev = list(deferred); del deferred[:]
                deferred.append(fin)
                for f_ in prev:
                    f_()
            for f_ in deferred:
                f_()
            del deferred[:]
            for blk in range(NB):
                for cb in range(2):
                    p = pp[pi % 8]; pi += 1
                    cs = slice(cb * 512, (cb + 1) * 512)
                    for k in range(8):
                        kb.op(PE, lambda: nc.tensor.matmul(p.t[:], lhsT=mg.t[:, k, blk * 128:(blk + 1) * 128], rhs=wo.t[:, k, cs], start=(k == 0), stop=(k == 7)),
                              reads=[mg.sub[k], wo], writes=[p], inc=(k == 7))
                    kb.op(DVE, lambda: nc.vector.tensor_tensor(out=y.t[:, blk, cs], in0=p.t[:], in1=gt1r.t[:, cs], op=ALU.mult), reads=[p, gt1r], writes=[y])
                    kb.op(DVE, lambda: nc.vector.scalar_tensor_tensor(out=y.t[:, blk, cs], in0=xx.t[:, blk, cs], scalar=ALPHA, in1=y.t[:, blk, cs],
                                                                      op0=ALU.mult, op1=ALU.add), reads=[xx, y], writes=[y])
            o = xo[i]
            layer_norm_rows(nc, kb, y, NB, lnr.t[:, 0:1024], lnr.t[:, 1024:2048], lnr, stats, mv, sd, o)
            kb.dma(x1_d[t * TT:(t + 1) * TT, :].rearrange("(b p) f -> p b f", p=128), o.t[:], reads=[o])
        kb.barrier()


def phase_ffn(nc, kb, S, params, pcol, cm, load_w_bf16, ada, gt2r, w_fg_d, w_fu_d, w_fd_d, rows_d, x1_d, out_d):
    PE, ACT, DVE, POOL = kb.pe, kb.act, kb.dve, kb.pool
    TT = 256
    NT = S // TT
    NB = TT // 128
    ident_f = cm.t[:, CM_IDENT * 128:(CM_IDENT + 1) * 128]
    with contextlib.ExitStack() as pes:
        wfg = kb.sb(pes, "wfg", [128, 8, DFF], BF16)
        wfu = kb.sb(pes, "wfu", [128, 8, DFF], BF16)
        wfd = kb.sb(pes, "wfd", [128, 22, 1024], BF16)
        lnr = kb.sb(pes, "lnr", [128, 2048], F32)
        with contextlib.ExitStack() as ses:
            stg = [kb.sb(ses, f"stg{i}", [128, 22, 128], F32) for i in range(2)]
            load_w_bf16(ses, wfg, w_fg_d, 0, 8, 0, DFF, stg)
            load_w_bf16(ses, wfu, w_fu_d, 0, 8, 0, DFF, stg)
            load_w_bf16(ses, wfd, w_fd_d, 0, 22, 0, 1024, stg)
            kb.dma(lnr.t[:], rows_d[:, _R["ln2_g"][0]:_R["ln2_g"][0] + 2048], writes=[lnr])
            kb.barrier(keep=[wfg, wfu, wfd, lnr])
        xt = [kb.sb(pes, f"xt{i}", [128, NB, 1024], F32) for i in range(2)]
        h2 = kb.sbs(pes, "h2", [128, 8, TT], BF16)
        act = kb.sbs(pes, "act", [128, 22, TT], BF16)
        raw = [kb.sb(pes, f"raw{i}", [128, TT + 2], F32) for i in range(3)]
        cv = [kb.sb(pes, f"cv{i}", [128, TT], F32) for i in range(3)]
        tmp = [kb.sb(pes, f"tmp{i}", [128, TT], F32) for i in range(3)]
        halo = kb.sbs(pes, "halo", [128, 22, 2], F32)
        y = kb.sb(pes, "y", [128, NB, 1024], F32)
        xo = [y, y]
        stats = kb.sb(pes, "stats", [128, NB, 2, 6], F32)
        mv = kb.sb(pes, "mv", [128, NB, 2], F32)
        sd = kb.sb(pes, "sd", [128, NB], F32)
        pp = [kb.ps(pes, f"pp{i}", [128, 512], F32) for i in range(8)]
        kb.op(DVE, lambda: nc.vector.memset(halo.t[:], 0.0), writes=halo.sub)
        cwo = _P["ffn_cw"][0]

        def load(t):
            kb.dma(xt[t % 2].t[:], x1_d[t * TT:(t + 1) * TT, :].rearrange("(b p) f -> p b f", p=128), writes=[xt[t % 2]])

        load(0)
        pi = 0
        for t in range(NT):
            if t + 1 < NT:
                load(t + 1)
            xx = xt[t % 2]
            for blk in range(NB):
                for c in range(8):
                    p = pp[pi % 8]; pi += 1
                    kb.op(PE, lambda: nc.tensor.matmul(p.t[:, 0:128], lhsT=xx.t[:, blk, c * 128:(c + 1) * 128], rhs=ident_f, start=True, stop=True), reads=[xx, cm], writes=[p])
                    kb.op(ACT, lambda: nc.scalar.activation(out=h2.t[:, c, blk * 128:(blk + 1) * 128], in_=p.t[:, 0:128], func=AF.Identity,
                                                            scale=ada.t[:, 32 + c:33 + c], bias=ada.t[:, 24 + c:25 + c]), reads=[p, ada], writes=[h2.sub[c]])
            deferred = []
            for m in range(22):
                ms = slice(m * 128, (m + 1) * 128)
                pg = pp[pi % 8]; pu = pp[(pi + 1) % 8]; pi += 2
                for k in range(8):
                    kb.op(PE, lambda: nc.tensor.matmul(pg.t[:, 0:TT], lhsT=wfg.t[:, k, ms], rhs=h2.t[:, k, :], start=(k == 0), stop=(k == 7)),
                          reads=[wfg, h2.sub[k]], writes=[pg], inc=(k == 7))
                for k in range(8):
                    kb.op(PE, lambda: nc.tensor.matmul(pu.t[:, 0:TT], lhsT=wfu.t[:, k, ms], rhs=h2.t[:, k, :], start=(k == 0), stop=(k == 7)),
                          reads=[wfu, h2.sub[k]], writes=[pu], inc=(k == 7))
                rw = raw[m % 3]
                c_ = cv[m % 3]
                kb.op(ACT, lambda: nc.scalar.copy(out=rw.t[:, 0:2], in_=halo.t[:, m, :]), reads=[halo.sub[m]], writes=[rw])
                kb.op(ACT, lambda: nc.scalar.copy(out=rw.t[:, 2:TT + 2], in_=pg.t[:, 0:TT]), reads=[pg], writes=[rw])
                kb.op(ACT, lambda: nc.scalar.copy(out=halo.t[:, m, :], in_=rw.t[:, TT:TT + 2]), reads=[rw], writes=[halo.sub[m]])
                kb.op(DVE, lambda: nc.vector.tensor_scalar(out=c_.t[:], in0=rw.t[:, 2:TT + 2], scalar1=params.t[:, cwo + m * 3 + 2:cwo + m * 3 + 3],
                                                           scalar2=pcol("ffn_cb", m), op0=ALU.mult, op1=ALU.add), reads=[rw, params], writes=[c_])
                for kk in range(2):
                    kb.op(DVE, lambda: nc.vector.scalar_tensor_tensor(out=c_.t[:], in0=rw.t[:, kk:kk + TT], scalar=params.t[:, cwo + m * 3 + kk:cwo + m * 3 + kk + 1],
                                                                      in1=c_.t[:], op0=ALU.mult, op1=ALU.add), reads=[rw, params, c_], writes=[c_])
                prev = list(deferred); del deferred[:]
                gelu_tanh(nc, kb, c_.t[:], [c_], act.t[:, m, :], [act.sub[m]], tmp[m % 3], mul_ap=pu.t[:, 0:TT], mul_bufs=[pu], defer=deferred)
                for f_ in prev:
                    f_()
            for f_ in deferred:
                f_()
            del deferred[:]
            for blk in range(NB):
                for cb in range(2):
                    p = pp[pi % 8]; pi += 1
                    cs = slice(cb * 512, (cb + 1) * 512)
                    for k in range(22):
                        kb.op(PE, lambda: nc.tensor.matmul(p.t[:], lhsT=act.t[:, k, blk * 128:(blk + 1) * 128], rhs=wfd.t[:, k, cs], start=(k == 0), stop=(k == 21)),
                              reads=[act.sub[k], wfd], writes=[p], inc=(k == 21))
                    kb.op(DVE, lambda: nc.vector.tensor_tensor(out=y.t[:, blk, cs], in0=p.t[:], in1=gt2r.t[:, cs], op=ALU.mult), reads=[p, gt2r], writes=[y])
                    kb.op(DVE, lambda: nc.vector.scalar_tensor_tensor(out=y.t[:, blk, cs], in0=xx.t[:, blk, cs], scalar=ALPHA, in1=y.t[:, blk, cs],
                                                                      op0=ALU.mult, op1=ALU.add), reads=[xx, y], writes=[y])
            o = xo[t % 2]
            layer_norm_rows(nc, kb, y, NB, lnr.t[:, 0:1024], lnr.t[:, 1024:2048], lnr, stats, mv, sd, o)
            kb.dma(out_d[t * TT:(t + 1) * TT, :].rearrange("(b p) f -> p b f", p=128), o.t[:], reads=[o])
        kb.barrier()


class PSlot:
    def __init__(self, kb, bank, i, name):
        self.b = bank
        self.ap = bank.t[:, i * 128:(i + 1) * 128]
        self.apb = self.ap.bitcast(BF16)[:, 0:128]


def phase_gdn(nc, kb, S, params, pcol, cm, cmb, load_w_bf16, w_in_d, rows_d, hrep_d, hT_d, dnT_d):
    PE, ACT, DVE, POOL = kb.pe, kb.act, kb.dve, kb.pool
    NCH = S // 128
    NT = S // 512
    HVG, HQG = 4, 2
    NG = 16 // HVG
    NCOL = NCH * 16
    cf = lambda i: cm.t[:, i * 128:(i + 1) * 128]
    cb = lambda i: cmb.t[:, i * 128:(i + 1) * 128]
    ident_f, ident_b, L_f, U_f, ones_f, ones_b = cf(CM_IDENT), cb(CM_IDENT), cf(CM_L), cf(CM_U), cf(CM_ONES), cb(CM_ONES)
    mS_b, mIT_b, BD_f, OFF_f = cb(CM_MS), cb(CM_MIT), cf(CM_BD), cf(CM_OFF)
    eps6 = kb.eps_ln.t[:, 1:2]
    with contextlib.ExitStack() as pes:
        tm = {n: kb.sb(pes, "tm_" + n, [128, NCH, 16], F32) for n in ("beta", "negg", "eG", "beG", "kds", "gl")}
        flat = lambda bf: bf.t[:].rearrange("p n h -> p (n h)")
        with contextlib.ExitStack() as ses:
            wab = kb.sb(ses, "wab", [128, 8, 32], BF16)
            stg = [kb.sb(ses, "stg0", [128, 8, 32], F32)]
            hT = [kb.sb(ses, f"hT{i}", [128, 8, 512], BF16) for i in range(2)]
            ab = kb.sb(ses, "ab", [128, NCH, 32], F32)
            hrep = kb.sb(ses, "hrep", [128, 2, NCOL], F32)
            X = kb.sb(ses, "X", [128, NCH, 16], F32)
            Y = kb.sb(ses, "Y", [128, NCH, 16], F32)
            Z = kb.sb(ses, "Z", [128, NCH, 16], F32)
            pp = [kb.ps(ses, f"pp{i}", [128, 512], F32) for i in range(4)]
            load_w_bf16(ses, wab, w_in_d, 0, 8, C_A, 32, stg)
            kb.dma(hrep.t[:], hrep_d[:, :, :], writes=[hrep])
            kb.dma(hT[0].t[:], hT_d[:, :, 0:512].rearrange("c p t -> p c t"), writes=[hT[0]])
            for t in range(NT):
                if t + 1 < NT:
                    kb.dma(hT[(t + 1) % 2].t[:], hT_d[:, :, (t + 1) * 512:(t + 2) * 512].rearrange("c p t -> p c t"), writes=[hT[(t + 1) % 2]])
                h = hT[t % 2]
                p = pp[t % 2]
                for blk in range(4):
                    for k in range(8):
                        kb.op(PE, lambda: nc.tensor.matmul(p.t[:, blk * 32:(blk + 1) * 32], lhsT=h.t[:, k, blk * 128:(blk + 1) * 128], rhs=wab.t[:, k, :],
                                                           start=(k == 0), stop=(k == 7)), reads=[h, wab], writes=[p], inc=(k == 7))
                kb.op(ACT, lambda: nc.scalar.copy(out=ab.t[:, t * 4:(t + 1) * 4, :], in_=p.t[:, 0:128].rearrange("p (b c) -> p b c", c=32)), reads=[p], writes=[ab])
            a_v, b_v = ab.t[:, :, 0:16], ab.t[:, :, 16:32]
            dtb = hrep.t[:, 0, :].rearrange("p (n h) -> p n h", h=16)
            alog = hrep.t[:, 1, :].rearrange("p (n h) -> p n h", h=16)
            kb.op(ACT, lambda: nc.scalar.activation(out=tm["beta"].t[:], in_=b_v, func=AF.Sigmoid), reads=[ab], writes=[tm["beta"]])
            kb.op(DVE, lambda: nc.vector.tensor_tensor(out=X.t[:], in0=a_v, in1=dtb, op=ALU.add), reads=[ab, hrep], writes=[X])
            kb.op(DVE, lambda: nc.vector.tensor_scalar(out=Y.t[:], in0=X.t[:], scalar1=-1.0, scalar2=None, op0=ALU.mult), reads=[X], writes=[Y])
            kb.op(DVE, lambda: nc.vector.tensor_tensor(out=Y.t[:], in0=Y.t[:], in1=X.t[:], op=ALU.max), reads=[X, Y], writes=[Y])
            kb.op(ACT, lambda: nc.scalar.activation(out=Y.t[:], in_=Y.t[:], func=AF.Exp, scale=-1.0), reads=[Y], writes=[Y])
            kb.op(ACT, lambda: nc.scalar.activation(out=Y.t[:], in_=Y.t[:], func=AF.Ln, bias=1.0), reads=[Y], writes=[Y])
            kb.op(DVE, lambda: nc.vector.tensor_scalar(out=X.t[:], in0=X.t[:], scalar1=0.0, scalar2=None, op0=ALU.max), reads=[X], writes=[X])
            kb.op(DVE, lambda: nc.vector.tensor_tensor(out=X.t[:], in0=X.t[:], in1=Y.t[:], op=ALU.add), reads=[X, Y], writes=[X])
            kb.op(ACT, lambda: nc.scalar.activation(out=Z.t[:], in_=alog, func=AF.Exp), reads=[hrep], writes=[Z])
            kb.op(DVE, lambda: nc.vector.tensor_tensor(out=tm["negg"].t[:], in0=X.t[:], in1=Z.t[:], op=ALU.mult), reads=[X, Z], writes=[tm["negg"]])
            ng_f = flat(tm["negg"])
            for c0 in range(0, NCOL, 512):
                w = min(512, NCOL - c0)
                p1, p2 = pp[2], pp[3]
                kb.op(PE, lambda: nc.tensor.matmul(p1.t[:, 0:w], lhsT=L_f, rhs=ng_f[:, c0:c0 + w], start=True, stop=True), reads=[cm, tm["negg"]], writes=[p1])
                kb.op(PE, lambda: nc.tensor.matmul(p2.t[:, 0:w], lhsT=ones_f, rhs=ng_f[:, c0:c0 + w], start=True, stop=True), reads=[cm, tm["negg"]], writes=[p2])
                Xf, Yf = flat(X), flat(Y)
                kb.op(ACT, lambda: nc.scalar.copy(out=Xf[:, c0:c0 + w], in_=p1.t[:, 0:w]), reads=[p1], writes=[X])
                kb.op(ACT, lambda: nc.scalar.activation(out=flat(tm["eG"])[:, c0:c0 + w], in_=p1.t[:, 0:w], func=AF.Exp, scale=-1.0), reads=[p1], writes=[tm["eG"]])
                kb.op(ACT, lambda: nc.scalar.activation(out=flat(tm["gl"])[:, c0:c0 + w], in_=p2.t[:, 0:w], func=AF.Exp, scale=-1.0), reads=[p2], writes=[tm["gl"]])
                kb.op(DVE, lambda: nc.vector.tensor_tensor(out=Yf[:, c0:c0 + w], in0=p2.t[:, 0:w], in1=Xf[:, c0:c0 + w], op=ALU.subtract), reads=[p2, X], writes=[Y])
                kb.op(ACT, lambda: nc.scalar.activation(out=flat(tm["kds"])[:, c0:c0 + w], in_=Yf[:, c0:c0 + w], func=AF.Exp, scale=-1.0), reads=[Y], writes=[tm["kds"]])
            kb.op(DVE, lambda: nc.vector.tensor_tensor(out=tm["beG"].t[:], in0=tm["beta"].t[:], in1=tm["eG"].t[:], op=ALU.mult), reads=[tm["beta"], tm["eG"]], writes=[tm["beG"]])
            kb.barrier()

        STOP = getattr(build_program, "gdn_stop", 0)
        for gi in range(NG if STOP != 1 else 0):
            hv0, hq0 = gi * HVG, gi * HQG
            with contextlib.ExitStack() as ges:
                wq = kb.sb(ges, "wq", [128, 8, HQG * 128], BF16)
                wk = kb.sb(ges, "wk", [128, 8, HQG * 128], BF16)
                wv = kb.sb(ges, "wv", [128, 8, HVG * 128], BF16)
                wz = kb.sb(ges, "wz", [128, 8, HVG * 128], BF16)
                nwr = kb.sb(ges, "nwr", [128, HVG * 128], F32)
                with contextlib.ExitStack() as ses:
                    stg = [kb.sb(ses, f"stg{i}", [128, 8, 512], F32) for i in range(2)]
                    load_w_bf16(ses, wq, w_in_d, 0, 8, C_Q + hq0 * 128, HQG * 128, stg)
                    load_w_bf16(ses, wk, w_in_d, 0, 8, C_K + hq0 * 128, HQG * 128, stg)
                    load_w_bf16(ses, wv, w_in_d, 0, 8, C_V + hv0 * 128, HVG * 128, stg)
                    load_w_bf16(ses, wz, w_in_d, 0, 8, C_Z + hv0 * 128, HVG * 128, stg)
                    kb.dma(nwr.t[:], rows_d[:, _R["nw"][0]:_R["nw"][0] + HVG * 128], writes=[nwr])
                    kb.barrier()
                hT = [kb.sb(ges, f"hT{i}", [128, 8, 512], BF16) for i in range(2)]
                raw = [kb.sb(ges, f"raw{i}", [128, 515], F32) for i in range(3)]
                cv = [kb.sb(ges, f"cv{i}", [128, 512], F32) for i in range(3)]
                NCK = 2 * HQG + HVG
                halo = kb.sb(ges, "halo", [128, NCK, 3], F32)
                qkf = [kb.sb(ges, f"qkf{i}", [128, 512], F32) for i in range(2 * HQG)]
                sqb = [kb.sb(ges, f"sqb{i}", [128, 512], BF16) for i in range(2 * HQG)]
                rn = [kb.sb(ges, f"rn{i}", [128, 512], F32) for i in range(2)]
                qT2 = [kb.sb(ges, f"qT{i}", [128, HQG, 512], BF16) for i in range(2)]
                kT2 = [kb.sb(ges, f"kT{i}", [128, HQG, 512], BF16) for i in range(2)]
                vT2 = [kb.sb(ges, f"vT{i}", [128, HVG, 512], BF16) for i in range(2)]
                zs = [kb.sb(ges, f"zs{i}", [128, 512], F32) for i in range(2)]
                zg2 = [kb.sb(ges, f"zg{i}", [128, 4, HVG * 128], F32) for i in range(2)]
                dno = [kb.sb(ges, f"dno{i}", [128, HVG, 512], BF16) for i in range(2)]
                Sf = [kb.sb(ges, f"Sf{i}", [128, 128], F32) for i in range(HVG)]
                Sb = [kb.sb(ges, f"Sb{i}", [128, 128], BF16) for i in range(HVG)]
                ob2 = [kb.sb(ges, f"ob{i}", [128, HVG, 128], F32) for i in range(2)]
                junk = kb.sb(ges, "junk", [128, 128], F32)
                ss2 = [kb.sb(ges, f"ss{i}", [128, HVG], F32) for i in range(2)]
                rstd2 = [kb.sb(ges, f"rstd{i}", [128, HVG], F32) for i in range(2)]
                ktm = [kb.sb(ges, f"ktm{i}", [128, 128], BF16) for i in range(HQG)]
                KKbd = [kb.sb(ges, f"KKbd{i}", [128, 128], F32) for i in range(HQG)]
                KKoff = [kb.sb(ges, f"KKoff{i}", [128, 128], F32) for i in range(HQG)]
                QKT = [kb.sb(ges, f"QKT{i}", [128, 128], F32) for i in range(HQG)]

                class PB:
                    pass
                pb = []
                for hi in range(HVG):
                    B = PB()
                    for n_ in ("rh", "Ds", "DT", "T1", "o1"):
                        setattr(B, n_, kb.sb(ges, f"{n_}_{hi}", [128, 128], F32))
                    for n_ in ("Abd", "Aoff", "P0", "P1", "M0", "Y1", "Tb", "bek", "vn", "dntm"):
                        setattr(B, n_, kb.sb(ges, f"{n_}_{hi}", [128, 128], BF16))
                    for n_ in ("QKd", "M128", "bv", "kdec", "nwT"):
                        setattr(B, n_, [kb.sb(ges, f"{n_}{i}_{hi}", [128, 128], BF16) for i in range(2)])
                    B.P = [B.P0, B.P1]
                    B.QM = [kb.sb(ges, f"QM{i}_{hi}", [128, 256], BF16) for i in range(2)]
                    pb.append(B)
                banks = [kb.ps(ges, f"bk{i}", [128, 512], F32) for i in range(8)]
                slots = [[PSlot(kb, banks[i], j, f"sl{i}_{j}") for j in range(4)] for i in range(8)]
                hsA = [slots[2 * hi] for hi in range(HVG)]
                hsB = [slots[2 * hi + 1] for hi in range(HVG)]

                def bank_bufs(i):
                    return [banks[i]]

                for b_ in Sf + [halo]:
                    kb.op(DVE, lambda: nc.vector.memset(b_.t[:], 0.0), writes=[b_])
                for b_ in Sb:
                    kb.op(DVE, lambda: nc.vector.memset(b_.t[:], 0.0), writes=[b_])
                cwo = _P["dn_cw"][0]
                chunks = [("q", wq, c, hq0 + c) for c in range(HQG)] + [("k", wk, c, 8 + hq0 + c) for c in range(HQG)] + \
                         [("v", wv, c, 16 + hv0 + c) for c in range(HVG)]

                def load_h(t):
                    kb.dma(hT[t % 2].t[:], hT_d[:, :, t * 512:(t + 1) * 512].rearrange("c p t -> p c t"), writes=[hT[t % 2]])

                bi_ = [0]

                def stageA(t):
                    h = hT[t % 2]
                    qT, kT, vT, zg = qT2[t % 2], kT2[t % 2], vT2[t % 2], zg2[t % 2]
                    for ci, (kind, w, c, gch) in enumerate(chunks):
                        bk = bi_[0] % 8; bi_[0] += 1
                        pt, pbufs = banks[bk].t, bank_bufs(bk)
                        for k in range(8):
                            kb.op(PE, lambda: nc.tensor.matmul(pt[:], lhsT=w.t[:, k, c * 128:(c + 1) * 128], rhs=h.t[:, k, :], start=(k == 0), stop=(k == 7)),
                                  reads=[w, h], writes=pbufs, inc=(k == 7))
                        rw, cvb = raw[ci % 3], cv[ci % 3]
                        kb.op(ACT, lambda: nc.scalar.copy(out=rw.t[:, 0:3], in_=halo.t[:, ci, :]), reads=[halo], writes=[rw])
                        kb.op(ACT, lambda: nc.scalar.copy(out=rw.t[:, 3:515], in_=pt[:]), reads=pbufs, writes=[rw])
                        kb.op(ACT, lambda: nc.scalar.copy(out=halo.t[:, ci, :], in_=rw.t[:, 512:515]), reads=[rw], writes=[halo])
                        kb.op(DVE, lambda: nc.vector.tensor_scalar(out=cvb.t[:], in0=rw.t[:, 3:515], scalar1=params.t[:, cwo + gch * 4 + 3:cwo + gch * 4 + 4],
                                                                   scalar2=None, op0=ALU.mult), reads=[rw, params], writes=[cvb])
                        for kk in range(3):
                            kb.op(DVE, lambda: nc.vector.scalar_tensor_tensor(out=cvb.t[:], in0=rw.t[:, kk:kk + 512], scalar=params.t[:, cwo + gch * 4 + kk:cwo + gch * 4 + kk + 1],
                                                                              in1=cvb.t[:], op0=ALU.mult, op1=ALU.add), reads=[rw, params, cvb], writes=[cvb])
                        if kind == "v":
                            kb.op(ACT, lambda: nc.scalar.activation(out=vT.t[:, c, :], in_=cvb.t[:], func=AF.Silu), reads=[cvb], writes=[vT])
                        else:
                            f = qkf[ci]
                            kb.op(ACT, lambda: nc.scalar.activation(out=f.t[:], in_=cvb.t[:], func=AF.Silu), reads=[cvb], writes=[f])
                            kb.op(POOL, lambda: nc.gpsimd.tensor_tensor(out=sqb[ci].t[:], in0=f.t[:], in1=f.t[:], op=ALU.mult), reads=[f], writes=[sqb[ci]])
                    for blk in range(4):
                        bk = bi_[0] % 8; bi_[0] += 1
                        pt, pbufs = banks[bk].t, bank_bufs(bk)
                        for k in range(8):
                            kb.op(PE, lambda: nc.tensor.matmul(pt[:, 0:HVG * 128], lhsT=h.t[:, k, blk * 128:(blk + 1) * 128], rhs=wz.t[:, k, :], start=(k == 0), stop=(k == 7)),
                                  reads=[h, wz], writes=pbufs, inc=(k == 7))
                        z_ = zs[blk % 2]
                        kb.op(ACT, lambda: nc.scalar.activation(out=z_.t[:, 0:HVG * 128], in_=pt[:, 0:HVG * 128], func=AF.Silu), reads=pbufs, writes=[z_])
                        kb.op(POOL, lambda: nc.gpsimd.tensor_tensor(out=zg.t[:, blk, :], in0=z_.t[:, 0:HVG * 128], in1=nwr.t[:], op=ALU.mult), reads=[z_, nwr], writes=[zg])

                    for ci, (kind, w, c, gch) in enumerate(chunks):
                        if kind == "v":
                            continue
                        f, sq, r_ = qkf[ci], sqb[ci], rn[ci % 2]
                        dst = qT if kind == "q" else kT
                        bk2 = bi_[0] % 8; bi_[0] += 1
                        pt2, pbufs2 = banks[bk2].t, bank_bufs(bk2)
                        kb.op(PE, lambda: nc.tensor.matmul(pt2[:], lhsT=ones_b, rhs=sq.t[:], start=True, stop=True), reads=[cmb, sq], writes=pbufs2)
                        kb.op(ACT, lambda: nc.scalar.activation(out=r_.t[:], in_=pt2[:], func=AF.Ln, bias=eps6), reads=pbufs2 + [kb.eps_ln], writes=[r_])
                        kb.op(ACT, lambda: nc.scalar.activation(out=r_.t[:], in_=r_.t[:], func=AF.Exp, scale=-0.5), reads=[r_], writes=[r_])
                        kb.op(DVE, lambda: nc.vector.scalar_tensor_tensor(out=dst.t[:, c, :], in0=f.t[:], scalar=(128.0 ** -0.5 if kind == "q" else 1.0), in1=r_.t[:],
                                                                          op0=ALU.mult, op1=ALU.mult), reads=[f, r_], writes=[dst])

                def hq_ops(n):
                    t, blk = n // 4, n % 4
                    cs = slice(blk * 128, (blk + 1) * 128)
                    qT, kT = qT2[t % 2], kT2[t % 2]
                    for hq in range(HQG):
                        kc, qc = kT.t[:, hq, cs], qT.t[:, hq, cs]
                        s0, s1, s2 = hsA[2 * hq][0], hsA[2 * hq][1], hsA[2 * hq][2]
                        kb.op(PE, lambda: nc.tensor.transpose(s0.apb, kc, ident_b), reads=[kT, cmb], writes=[s0.b])
                        kb.op(PE, lambda: nc.tensor.matmul(s1.ap, lhsT=kc, rhs=kc, start=True, stop=True), reads=[kT], writes=[s1.b])
                        kb.op(PE, lambda: nc.tensor.matmul(s2.ap, lhsT=kc, rhs=qc, start=True, stop=True), reads=[kT, qT], writes=[s2.b])
                        kb.op(ACT, lambda: nc.scalar.copy(out=ktm[hq].t[:], in_=s0.apb), reads=[s0.b], writes=[ktm[hq]])
                        kb.op(DVE, lambda: nc.vector.tensor_tensor(out=KKbd[hq].t[:], in0=s1.ap, in1=BD_f, op=ALU.mult), reads=[s1.b, cm], writes=[KKbd[hq]])
                        kb.op(DVE, lambda: nc.vector.tensor_tensor(out=KKoff[hq].t[:], in0=s1.ap, in1=OFF_f, op=ALU.mult), reads=[s1.b, cm], writes=[KKoff[hq]])
                        kb.op(ACT, lambda: nc.scalar.copy(out=QKT[hq].t[:], in_=s2.ap), reads=[s2.b], writes=[QKT[hq]])

                def pre_seq(hi, n):
                    t, blk = n // 4, n % 4
                    cs = slice(blk * 128, (blk + 1) * 128)
                    vT = vT2[t % 2]
                    pr = n % 2
                    hh_ = hv0 + hi
                    hq = hi // 2
                    B, sl = pb[hi], hsA[hi]
                    sc = lambda nm: tm[nm].t[:, n, hh_:hh_ + 1]
                    QKd, M128, bv, kdec, nwT = B.QKd[pr], B.M128[pr], B.bv[pr], B.kdec[pr], B.nwT[pr]
                    kb.op(ACT, lambda: nc.scalar.activation(out=B.rh.t[:], in_=U_f, func=AF.Identity, scale=sc("negg")), reads=[cm, tm["negg"]], writes=[B.rh])
                    kb.op(ACT, lambda: nc.scalar.activation(out=B.bek.t[:], in_=ktm[hq].t[:], func=AF.Identity, scale=sc("beG")), reads=[ktm[hq], tm["beG"]], writes=[B.bek])
                    kb.op(ACT, lambda: nc.scalar.activation(out=kdec.t[:], in_=ktm[hq].t[:], func=AF.Identity, scale=sc("kds")), reads=[ktm[hq], tm["kds"]], writes=[kdec])
                    yield
                    kb.op(PE, lambda: nc.tensor.matmul(sl[0].ap, lhsT=L_f, rhs=B.rh.t[:], start=True, stop=False), reads=[cm, B.rh], writes=[sl[0].b], inc=False)
                    kb.op(PE, lambda: nc.tensor.matmul(sl[0].ap, lhsT=ident_b, rhs=mS_b, start=False, stop=True), reads=[cmb], writes=[sl[0].b], inc=False)
                    kb.op(PE, lambda: nc.tensor.matmul(sl[1].ap, lhsT=B.rh.t[:], rhs=L_f, start=True, stop=False), reads=[cm, B.rh], writes=[sl[1].b], inc=False)
                    kb.op(PE, lambda: nc.tensor.matmul(sl[1].ap, lhsT=ident_b, rhs=mIT_b, start=False, stop=True), reads=[cmb], writes=[sl[1].b], inc=False)
                    kb.op(PE, lambda: nc.tensor.transpose(sl[3].apb, vT.t[:, hi, cs], ident_b), reads=[vT, cmb], writes=[sl[3].b])
                    kb.op(ACT, lambda: nc.scalar.activation(out=B.Ds.t[:], in_=sl[0].ap, func=AF.Exp, scale=-1.0), reads=[sl[0].b], writes=[B.Ds])
                    kb.op(ACT, lambda: nc.scalar.activation(out=B.DT.t[:], in_=sl[1].ap, func=AF.Exp, scale=-1.0), reads=[sl[1].b], writes=[B.DT])
                    kb.op(DVE, lambda: nc.vector.tensor_scalar(out=bv.t[:], in0=sl[3].apb, scalar1=sc("beta"), scalar2=None, op0=ALU.mult), reads=[sl[3].b, tm["beta"]], writes=[bv])
                    yield
                    kb.op(DVE, lambda: nc.vector.tensor_scalar(out=B.T1.t[:], in0=B.Ds.t[:], scalar1=sc("beta"), scalar2=None, op0=ALU.mult), reads=[B.Ds, tm["beta"]], writes=[B.T1])
                    kb.op(DVE, lambda: nc.vector.tensor_tensor(out=B.Abd.t[:], in0=B.T1.t[:], in1=KKbd[hq].t[:], op=ALU.mult), reads=[B.T1, KKbd[hq]], writes=[B.Abd])
                    kb.op(DVE, lambda: nc.vector.tensor_tensor(out=B.Aoff.t[:], in0=B.T1.t[:], in1=KKoff[hq].t[:], op=ALU.mult), reads=[B.T1, KKoff[hq]], writes=[B.Aoff])
                    kb.op(DVE, lambda: nc.vector.tensor_tensor(out=QKd.t[:], in0=QKT[hq].t[:], in1=B.DT.t[:], op=ALU.mult), reads=[QKT[hq], B.DT], writes=[QKd])
                    yield
                    bankA = banks[2 * hi]
                    kb.op(PE, lambda: nc.tensor.transpose(sl[2].apb, B.Abd.t[:], ident_b), reads=[B.Abd, cmb], writes=[sl[2].b])
                    kb.op(ACT, lambda: nc.scalar.copy(out=B.QM[0].t[:, 0:128], in_=sl[2].apb), reads=[sl[2].b], writes=[B.QM[0]])
                    kb.op(DVE, lambda: nc.vector.tensor_tensor(out=B.QM[0].t[:, 128:256], in0=ident_f, in1=sl[2].apb, op=ALU.subtract), reads=[cm, sl[2].b], writes=[B.QM[0]])
                    kb.op(DVE, lambda: nc.vector.tensor_tensor(out=B.QM[1].t[:, 128:256], in0=ident_f, in1=sl[2].apb, op=ALU.subtract), reads=[cm, sl[2].b], writes=[B.QM[1]])
                    yield
                    P = B.Abd
                    for lv in range(1, 6):
                        cur, nxt, Pn = B.QM[(lv - 1) % 2], B.QM[lv % 2], B.P[lv % 2]
                        lo, hi_ = (0, 128) if lv == 1 else ((0, 256) if lv < 5 else (128, 256))
                        kb.op(PE, lambda: nc.tensor.matmul(sl[2].ap, lhsT=cur.t[:, 0:128], rhs=P.t[:], start=True, stop=True), reads=[P, cur], writes=[sl[2].b], inc=False)
                        kb.op(PE, lambda: nc.tensor.matmul(bankA.t[:, lo:hi_], lhsT=P.t[:], rhs=cur.t[:, lo:hi_], start=True, stop=True), reads=[P, cur], writes=[bankA])
                        kb.op(ACT, lambda: nc.scalar.copy(out=Pn.t[:], in_=sl[2].ap), reads=[bankA], writes=[Pn])
                        if lv < 5:
                            kb.op(ACT, lambda: nc.scalar.copy(out=nxt.t[:, 0:128], in_=bankA.t[:, 0:128]), reads=[bankA], writes=[nxt])
                        if lv >= 2:
                            kb.op(DVE, lambda: nc.vector.tensor_tensor(out=nxt.t[:, 128:256], in0=cur.t[:, 128:256], in1=bankA.t[:, 128:256], op=ALU.add), reads=[cur, bankA], writes=[nxt])
                        yield
                        P = Pn
                    M4 = B.QM[1]
                    kb.op(PE, lambda: nc.tensor.matmul(sl[0].ap, lhsT=P.t[:], rhs=M4.t[:, 128:256], start=True, stop=True), reads=[P, M4], writes=[bankA])
                    kb.op(DVE, lambda: nc.vector.tensor_tensor(out=B.M0.t[:], in0=M4.t[:, 128:256], in1=sl[0].ap, op=ALU.add), reads=[M4, bankA], writes=[B.M0])
                    yield
                    M = B.M0
                    kb.op(PE, lambda: nc.tensor.matmul(sl[2].ap, lhsT=B.Aoff.t[:], rhs=M.t[:], start=True, stop=True), reads=[B.Aoff, M], writes=[bankA], inc=False)
                    kb.op(PE, lambda: nc.tensor.transpose(sl[3].apb, M.t[:], ident_b), reads=[M, cmb], writes=[bankA])
                    kb.op(ACT, lambda: nc.scalar.copy(out=B.Y1.t[:], in_=sl[2].ap), reads=[bankA], writes=[B.Y1])
                    kb.op(DVE, lambda: nc.vector.tensor_copy(out=B.Tb.t[:], in_=sl[3].apb), reads=[bankA], writes=[B.Tb])
                    yield
                    kb.op(PE, lambda: nc.tensor.matmul(sl[0].ap, lhsT=B.Tb.t[:], rhs=B.Y1.t[:], start=True, stop=True), reads=[B.Tb, B.Y1], writes=[bankA])
                    kb.op(DVE, lambda: nc.vector.tensor_tensor(out=M128.t[:], in0=M.t[:], in1=sl[0].ap, op=ALU.subtract), reads=[M, bankA], writes=[M128])
                    yield
                    kb.op(PE, lambda: nc.tensor.matmul(sl[1].ap, lhsT=B.bek.t[:], rhs=M128.t[:], start=True, stop=True), reads=[B.bek, M128], writes=[sl[1].b])
                    kb.op(ACT, lambda: nc.scalar.activation(out=nwT.t[:], in_=sl[1].ap, func=AF.Identity, scale=-1.0), reads=[sl[1].b], writes=[nwT])
                    yield

                def post_seq(hi, n):
                    t, blk = n // 4, n % 4
                    cs = slice(blk * 128, (blk + 1) * 128)
                    qT = qT2[t % 2]
                    pr = n % 2
                    hh_ = hv0 + hi
                    hq = hi // 2
                    B, sl = pb[hi], hsB[hi]
                    sc = lambda nm: tm[nm].t[:, n, hh_:hh_ + 1]
                    QKd, M128, bv, kdec, nwT = B.QKd[pr], B.M128[pr], B.bv[pr], B.kdec[pr], B.nwT[pr]
                    kb.op(PE, lambda: nc.tensor.matmul(sl[0].ap, lhsT=M128.t[:], rhs=bv.t[:], start=True, stop=False), reads=[M128, bv], writes=[sl[0].b], inc=False)
                    kb.op(PE, lambda: nc.tensor.matmul(sl[0].ap, lhsT=nwT.t[:], rhs=Sb[hi].t[:], start=False, stop=True), reads=[nwT, Sb[hi]], writes=[sl[0].b], inc=False)
                    kb.op(PE, lambda: nc.tensor.matmul(sl[1].ap, lhsT=qT.t[:, hq, cs], rhs=Sb[hi].t[:], start=True, stop=True), reads=[qT, Sb[hi]], writes=[sl[1].b])
                    kb.op(ACT, lambda: nc.scalar.copy(out=B.vn.t[:], in_=sl[0].ap), reads=[sl[0].b], writes=[B.vn])
                    kb.op(ACT, lambda: nc.scalar.activation(out=B.o1.t[:], in_=sl[1].ap, func=AF.Identity, scale=sc("eG")), reads=[sl[1].b, tm["eG"]], writes=[B.o1])
                    yield
                    kb.op(PE, lambda: nc.tensor.matmul(sl[2].ap, lhsT=kdec.t[:], rhs=B.vn.t[:], start=True, stop=True), reads=[kdec, B.vn], writes=[sl[2].b], inc=False)
                    kb.op(PE, lambda: nc.tensor.matmul(sl[3].ap, lhsT=QKd.t[:], rhs=B.vn.t[:], start=True, stop=True), reads=[QKd, B.vn], writes=[sl[3].b])
                    kb.op(DVE, lambda: nc.vector.scalar_tensor_tensor(out=Sf[hi].t[:], in0=Sf[hi].t[:], scalar=sc("gl"), in1=sl[2].ap, op0=ALU.mult, op1=ALU.add),
                          reads=[Sf[hi], tm["gl"], sl[2].b], writes=[Sf[hi]])
                    kb.op(DVE, lambda: nc.vector.tensor_tensor(out=ob2[pr].t[:, hi, :], in0=B.o1.t[:], in1=sl[3].ap, op=ALU.add), reads=[B.o1, sl[3].b], writes=[ob2[pr]])
                    yield
                    kb.op(ACT, lambda: nc.scalar.copy(out=Sb[hi].t[:], in_=Sf[hi].t[:]), reads=[Sf[hi]], writes=[Sb[hi]])
                    yield

                def norm_seq(n):
                    t, blk = n // 4, n % 4
                    cs = slice(blk * 128, (blk + 1) * 128)
                    zg, dn_o = zg2[t % 2], dno[t % 2]
                    ob, ss, rstd = ob2[n % 2], ss2[n % 2], rstd2[n % 2]
                    for hi in range(HVG):
                        kb.op(ACT, lambda: nc.scalar.activation(out=junk.t[:], in_=ob.t[:, hi, :], func=AF.Square, accum_out=ss.t[:, hi:hi + 1]), reads=[ob], writes=[junk, ss])
                    yield
                    kb.op(ACT, lambda: nc.scalar.activation(out=rstd.t[:], in_=ss.t[:], func=AF.Ln, scale=1.0 / 128.0, bias=eps6), reads=[ss, kb.eps_ln], writes=[rstd])
                    kb.op(ACT, lambda: nc.scalar.activation(out=rstd.t[:], in_=rstd.t[:], func=AF.Exp, scale=-0.5), reads=[rstd], writes=[rstd])
                    yield
                    for hi in range(HVG):
                        B = pb[hi]
                        kb.op(DVE, lambda: nc.vector.scalar_tensor_tensor(out=B.dntm.t[:], in0=ob.t[:, hi, :], scalar=rstd.t[:, hi:hi + 1], in1=zg.t[:, blk, hi * 128:(hi + 1) * 128],
                                                                          op0=ALU.mult, op1=ALU.mult), reads=[ob, rstd, zg], writes=[B.dntm])
                    yield
                    for hi in range(HVG):
                        B, sl = pb[hi], hsB[hi]
                        kb.op(PE, lambda: nc.tensor.transpose(sl[0].apb, B.dntm.t[:], ident_b), reads=[B.dntm, cmb], writes=[sl[0].b])
                        kb.op(ACT, lambda: nc.scalar.copy(out=dn_o.t[:, hi, cs], in_=sl[0].apb), reads=[sl[0].b], writes=[dn_o])
                    if blk == 3:
                        kb.dma(dnT_d[hv0:hv0 + HVG, :, t * 512:(t + 1) * 512].rearrange("c p t -> p c t"), dn_o.t[:], reads=[dn_o])
                    yield

                def run(gens, delays=None):
                    gens = list(gens)
                    delays = list(delays) if delays is not None else [0] * len(gens)
                    rnd = 0
                    while gens:
                        for i in range(len(gens) - 1, -1, -1):
                            pass
                        keep_g, keep_d = [], []
                        for g, d in zip(gens, delays):
                            if rnd < d:
                                keep_g.append(g); keep_d.append(d)
                                continue
                            try:
                                next(g)
                                keep_g.append(g); keep_d.append(d)
                            except StopIteration:
                                pass
                        gens, delays = keep_g, keep_d
                        rnd += 1

                STAG = getattr(build_program, "stagger", 0)
                load_h(0)
                if NT > 1:
                    load_h(1)
                stageA(0)
                hq_ops(0)
                run([pre_seq(hi, 0) for hi in range(HVG)])
                pend_norm = []
                for n in range(NCH):
                    gens = []
                    if n + 1 < NCH:
                        if (n + 1) % 4 == 0:
                            t1 = (n + 1) // 4
                            stageA(t1)
                            if t1 + 1 < NT:
                                load_h(t1 + 1)
                        hq_ops(n + 1)
                        gens = [pre_seq(hi, n + 1) for hi in range(HVG)]
                    posts = [post_seq(hi, n) for hi in range(HVG)]
                    run(posts + pend_norm + gens, [0] * (len(posts) + len(pend_norm)) + [STAG * (i // 2) for i in range(len(gens))])
                    pend_norm = [norm_seq(n)]
                run(pend_norm)
                kb.barrier()


def _consts():
    i = np.arange(128)
    ident = np.eye(128, dtype=np.float32)
    L = (i[:, None] <= i[None, :]).astype(np.float32)
    U = (i[:, None] > i[None, :]).astype(np.float32)
    mS = np.where(i[:, None] > i[None, :], 0.0, BIG).astype(np.float32)
    mIT = np.where(i[None, :] >= i[:, None], 0.0, BIG).astype(np.float32)
    blk = i // 64
    BD = (blk[:, None] == blk[None, :]).astype(np.float32)
    OFF = ((blk[:, None] == 1) & (blk[None, :] == 0)).astype(np.float32)
    ones = np.ones((128, 128), np.float32)
    return np.concatenate([ident, L, U, mS, mIT, BD, OFF, ones], axis=1)


def _fm(v, n):
    return np.ascontiguousarray(np.asarray(v, np.float32).reshape(n, 128).T)


def _band(w):
    full = np.zeros((DRNN, DRNN), np.float32)
    for b in range(16):
        full[b * 80:(b + 1) * 80, b * 80:(b + 1) * 80] = w[b]
    out = np.zeros((128, 10, 3, 128), np.float32)
    for m in range(10):
        for d in range(3):
            k = m + d - 1
            if 0 <= k < 10:
                out[:, m, d, :] = full[k * 128:(k + 1) * 128, m * 128:(m + 1) * 128]
    return out


def make_in_maps(inp, S, batches):
    l = 0
    f = lambda a: np.ascontiguousarray(np.asarray(a, np.float32))
    NCH = S // 128
    rows = np.zeros((128, NR), np.float32)
    b_ada = f(inp["b_ada"][l])
    def setr(name, v):
        o, w = _R[name]
        rows[:, o:o + w] = np.asarray(v, np.float32)[None, :]
    setr("b_gt1", b_ada[2048:3072]); setr("b_gt2", b_ada[5120:6144])
    setr("ln1_g", inp["ln1_g"][l]); setr("ln1_b", inp["ln1_b"][l]); setr("ln2_g", inp["ln2_g"][l]); setr("ln2_b", inp["ln2_b"][l])
    setr("nw", np.tile(f(inp["dn_norm_w"][l]), 16))
    hrep = np.zeros((128, 2, NCH * 16), np.float32)
    hrep[:, 0, :] = np.tile(f(inp["dn_dt_bias"][l]), NCH)[None, :]
    hrep[:, 1, :] = np.tile(f(inp["dn_a_log"][l]), NCH)[None, :]
    cmat = _consts()
    shared = dict(
        w_ada=f(inp["w_ada"][l]), w_in=f(inp["w_in"][l]), gate_a=_band(f(inp["rg_w_a"][l])), gate_x=_band(f(inp["rg_w_x"][l])),
        w_proj_a=f(inp["w_proj_a"][l]), w_proj_b=f(inp["w_proj_b"][l]), w_out=f(inp["w_out"][l]),
        ffn_w_gate=f(inp["ffn_w_gate"][l]), ffn_w_up=f(inp["ffn_w_up"][l]), ffn_w_down=f(inp["ffn_w_down"][l]),
        rows=rows, cmat=cmat, hrep=hrep)
    maps = []
    for b in batches:
        params = np.zeros((128, NP), np.float32)
        def setp(name, v):
            o, w = _P[name]
            params[:, o:o + w] = v
        setp("b_ada", _fm(b_ada, 48))
        setp("c", _fm(inp["c"][b], 8))
        setp("rg_cw", np.stack([_fm(inp["rg_conv_w"][l][k], 10) for k in range(4)], axis=2).reshape(128, 40))
        setp("rg_cb", _fm(inp["rg_conv_b"][l], 10)); setp("rg_ba", _fm(inp["rg_b_a"][l], 10)); setp("rg_bx", _fm(inp["rg_b_x"][l], 10))
        setp("rg_lam", _fm(inp["rg_lambda"][l], 10))
        setp("dn_cw", np.stack([_fm(inp["dn_conv_w"][l][k], 32) for k in range(4)], axis=2).reshape(128, 128))
        setp("ffn_cw", np.stack([_fm(inp["ffn_conv_w"][l][k], 22) for k in range(3)], axis=2).reshape(128, 66))
        setp("ffn_cb", _fm(inp["ffn_conv_b"][l], 22))
        xb = f(inp["x"][b])
        m = dict(shared)
        m["x"] = xb
        m["xT"] = np.ascontiguousarray(xb.T.reshape(8, 128, S))
        m["params"] = params
        maps.append(m)
    return maps


_NC_CACHE = {}


def kernel(**inputs):
    S = inputs["x"].shape[1]
    B = inputs["x"].shape[0]
    if S not in _NC_CACHE:
        _NC_CACHE[S] = build_program(S)
    nc = _NC_CACHE[S]
    maps = make_in_maps(inputs, S, list(range(B)))
    res = run_bass_kernel_spmd(nc, maps, core_ids=list(range(B)))
    return np.stack([np.asarray(r["out"], np.float32) for r in res.results], axis=0)
```

```python
import contextlib
import numpy as np
import concourse.bass as bass
import concourse.mybir as mybir
from concourse.bass_utils import run_bass_kernel_spmd

F32 = mybir.dt.float32
BF16 = mybir.dt.bfloat16
AF = mybir.ActivationFunctionType
ALU = mybir.AluOpType

D = 1024
DRNN = 1280
DFF = 2816
DIN = 10784
NHV = 16
C_XR, C_GR, C_Q, C_K, C_V, C_Z, C_A, C_B, C_GA, C_GB = 0, 1280, 2560, 3584, 4608, 6656, 8704, 8720, 8736, 9760
ALPHA = 2.0 ** 0.25
BIG = 32768.0
GELU_C = 1.5957691216057308

_P = {}
_off = 0
for _n, _w in [("b_ada", 48), ("c", 8), ("rg_cw", 40), ("rg_cb", 10), ("rg_ba", 10), ("rg_bx", 10), ("rg_lam", 10),
               ("dn_cw", 128), ("ffn_cw", 66), ("ffn_cb", 22)]:
    _P[_n] = (_off, _w)
    _off += _w
NP = _off
_R = {}
_off = 0
for _n, _w in [("b_gt1", 1024), ("b_gt2", 1024), ("ln1_g", 1024), ("ln1_b", 1024), ("ln2_g", 1024), ("ln2_b", 1024),
               ("nw", 2048)]:
    _R[_n] = (_off, _w)
    _off += _w
NR = _off
CM_IDENT, CM_L, CM_U, CM_MS, CM_MIT, CM_BD, CM_OFF, CM_ONES = range(8)
NCM = 8


class Buf:
    def __init__(self, kb, name, t):
        self.kb = kb
        self.name = name
        self.t = t
        self.w = None
        self.r = []
        self.lsem = None
        self.ssem = None
        self.excl = False

    def add_reader(self, tok):
        for i, (s, v) in enumerate(self.r):
            if s == tok[0]:
                if v < tok[1]:
                    self.r[i] = tok
                return
        self.r.append(tok)


class EngW:
    LIMIT = 24000

    def __init__(self, kb, eng, name):
        self.kb = kb
        self.eng = eng
        self.name = name
        self.sem = None
        self.cnt = 0
        self.seen = {}
        self.pending = []

    def wait(self, toks):
        best = {}
        for s, v in toks:
            if best.get(s, 0) < v:
                best[s] = v
        for s, v in best.items():
            if self.seen.get(s, 0) >= v:
                continue
            self.eng.wait_ge(self.kb.sems[s], v)
            self.seen[s] = v

    def bump(self, inst):
        if self.sem is None or self.cnt >= self.LIMIT:
            self.sem = self.kb.new_sem(f"e_{self.name}_{len(self.kb.sems)}")
            self.cnt = 0
        inst.then_inc(self.kb.sems[self.sem], 1)
        self.cnt += 1
        return (self.sem, self.cnt)

    def cur(self):
        return None if self.sem is None else (self.sem, self.cnt)


class KB:
    def __init__(self, nc, es):
        self.nc = nc
        self.es = es
        self.sems = []
        self.semcnt = []
        self.free_dma_sems = []
        self.pe = EngW(self, nc.tensor, "pe")
        self.act = EngW(self, nc.scalar, "act")
        self.dve = EngW(self, nc.vector, "dve")
        self.pool = EngW(self, nc.gpsimd, "pool")
        self.sp = EngW(self, nc.sync, "sp")
        self.engs = [self.pe, self.act, self.dve, self.pool, self.sp]
        self.dma_toks = {}
        self.phase_bufs = []
        self.n_inst = 0

    def new_sem(self, name):
        h = self.es.enter_context(self.nc.semaphore(name))
        self.sems.append(h)
        self.semcnt.append(0)
        return len(self.sems) - 1

    def dma_sem(self):
        if self.free_dma_sems:
            return self.free_dma_sems.pop()
        return self.new_sem(f"d_{len(self.sems)}")

    def sb(self, pes, name, shape, dt):
        self.n_alloc = getattr(self, "n_alloc", 0) + 1
        t = pes.enter_context(self.nc.sbuf_tensor(f"sb{self.n_alloc}_{name}", list(shape), dt))
        b = Buf(self, name, t)
        self.phase_bufs.append(b)
        return b

    def sbs(self, pes, name, shape, dt):
        b = self.sb(pes, name, shape, dt)
        b.sub = [Buf(self, f"{name}[{i}]", b.t) for i in range(shape[1])]
        self.phase_bufs.extend(b.sub)
        return b

    def ps(self, pes, name, shape, dt):
        self.n_alloc = getattr(self, "n_alloc", 0) + 1
        t = pes.enter_context(self.nc.psum_tensor(f"ps{self.n_alloc}_{name}", list(shape), dt))
        b = Buf(self, name, t)
        b.excl = True
        self.phase_bufs.append(b)
        return b

    def op(self, E, fn, reads=(), writes=(), inc=True):
        ex = [b for b in reads if b.excl]
        if ex:
            reads = [b for b in reads if not b.excl]
            writes = list(writes) + [b for b in ex if b not in writes]
        toks = []
        for b in reads:
            if b.w is not None:
                toks.append(b.w)
        for b in writes:
            if b.w is not None:
                toks.append(b.w)
            toks.extend(b.r)
        E.wait(toks)
        inst = fn()
        self.n_inst += 1
        if inc:
            tok = E.bump(inst)
            sets = E.pending + [(reads, writes)]
            E.pending = []
            for rd, wr in sets:
                for b in rd:
                    b.add_reader(tok)
                for b in wr:
                    b.w = tok
                    b.r = []
        else:
            E.pending.append((reads, writes))
        return inst

    def dma(self, out_ap, in_ap, reads=(), writes=(), Q=None):
        Q = Q or self.sp
        toks = []
        for b in reads:
            if b.w is not None:
                toks.append(b.w)
        for b in writes:
            if b.w is not None:
                toks.append(b.w)
            toks.extend(b.r)
        Q.wait(toks)
        inst = Q.eng.dma_start(out=out_ap, in_=in_ap)
        self.n_inst += 1
        if writes:
            tb = writes[0]
            if tb.lsem is None:
                tb.lsem = self.dma_sem()
            s = tb.lsem
        else:
            tb = reads[0]
            if tb.ssem is None:
                tb.ssem = self.dma_sem()
            s = tb.ssem
        self.semcnt[s] += 16
        inst.then_inc(self.sems[s], 16)
        tok = (s, self.semcnt[s])
        self.dma_toks[s] = tok
        for b in writes:
            b.w = tok
            b.r = []
        for b in reads:
            b.add_reader(tok)
        return tok

    def barrier(self, keep=()):
        toks = list(self.dma_toks.values())
        for E in self.engs:
            assert not E.pending
            c = E.cur()
            if c is not None:
                toks.append(c)
        for E in self.engs:
            E.wait(toks)
        for b in self.phase_bufs:
            if b.lsem is not None:
                self.free_dma_sems.append(b.lsem)
                b.lsem = None
            if b.ssem is not None:
                self.free_dma_sems.append(b.ssem)
                b.ssem = None
        self.dma_toks = {}

    def final_wait(self):
        toks = list(self.dma_toks.values())
        self.sp.wait(toks)


def _cm(cm, idx, dt=None):
    return cm.t[:, idx * 128:(idx + 1) * 128]


def build_program(S, debug=False):
    assert S % 512 == 0
    NT = S // 512
    NCH = S // 128
    nc = bass.Bass("TRN2", target_bir_lowering=False)
    okind = "ExternalOutput" if debug else "Internal"

    def din(name, shape, dt=F32):
        return nc.dram_tensor(name, list(shape), dt, kind="ExternalInput").ap()

    xT_d = din("xT", [8, 128, S])
    x_d = din("x", [S, D])
    w_ada_d = din("w_ada", [D, 6 * D])
    w_in_d = din("w_in", [D, DIN])
    gate_a_d = din("gate_a", [128, 10, 3, 128])
    gate_x_d = din("gate_x", [128, 10, 3, 128])
    w_pa_d = din("w_proj_a", [DRNN, D])
    w_pb_d = din("w_proj_b", [2048, D])
    w_out_d = din("w_out", [D, D])
    w_fg_d = din("ffn_w_gate", [D, DFF])
    w_fu_d = din("ffn_w_up", [D, DFF])
    w_fd_d = din("ffn_w_down", [DFF, D])
    params_d = din("params", [128, NP])
    rows_d = din("rows", [128, NR])
    cmat_d = din("cmat", [128, NCM * 128])
    hrep_d = din("hrep", [128, 2, NCH * 16])
    out_d = nc.dram_tensor("out", [S, D], F32, kind="ExternalOutput").ap()
    hT_d = nc.dram_tensor("hT_s", [8, 128, S], BF16, kind=okind).ap()
    recT_d = nc.dram_tensor("recT_s", [10, 128, S], BF16, kind=okind).ap()
    dnT_d = nc.dram_tensor("dnT_s", [16, 128, S], BF16, kind=okind).ap()
    x1_d = nc.dram_tensor("x1_s", [S, D], F32, kind=okind).ap()

    with contextlib.ExitStack() as es:
        kb = KB(nc, es)
        PE, ACT, DVE, POOL = kb.pe, kb.act, kb.dve, kb.pool
        params = kb.sb(es, "params", [128, NP], F32)
        cm = kb.sb(es, "cm", [128, NCM * 128], F32)
        cmb = kb.sb(es, "cmb", [128, NCM * 128], BF16)
        ada = kb.sb(es, "ada", [128, 48], F32)
        gt1r = kb.sb(es, "gt1r", [128, 1024], F32)
        gt2r = kb.sb(es, "gt2r", [128, 1024], F32)
        kb.dma(params.t[:], params_d[:, :], writes=[params])
        kb.dma(cm.t[:], cmat_d[:, :], writes=[cm])
        kb.op(ACT, lambda: nc.scalar.copy(out=cmb.t[:], in_=cm.t[:]), reads=[cm], writes=[cmb])

        def pcol(name, j=0, n=1):
            o, w = _P[name]
            return params.t[:, o + j:o + j + n]

        ident_f = cm.t[:, CM_IDENT * 128:(CM_IDENT + 1) * 128]
        ident_b = cmb.t[:, CM_IDENT * 128:(CM_IDENT + 1) * 128]
        ones_f = cm.t[:, CM_ONES * 128:(CM_ONES + 1) * 128]
        ones_b = cmb.t[:, CM_ONES * 128:(CM_ONES + 1) * 128]

        def load_w_bf16(pes, dst, dram, r0, nk, c0, ncols, stg, dst_c0=0):
            PW = stg[0].t.shape[-1]
            i = load_w_bf16.cnt
            for cc in range(0, ncols, PW):
                w = min(PW, ncols - cc)
                st = stg[i % len(stg)]
                src = dram[r0:r0 + nk * 128, c0 + cc:c0 + cc + w].rearrange("(k p) c -> p k c", p=128)
                kb.dma(st.t[:, 0:nk, 0:w], src, writes=[st])
                o = dst.t[:, 0:nk, dst_c0 + cc:dst_c0 + cc + w]
                if i % 2 == 0:
                    kb.op(ACT, lambda o=o, st=st, w=w: nc.scalar.copy(out=o, in_=st.t[:, 0:nk, 0:w]), reads=[st], writes=[dst])
                else:
                    kb.op(DVE, lambda o=o, st=st, w=w: nc.vector.tensor_copy(out=o, in_=st.t[:, 0:nk, 0:w]), reads=[st], writes=[dst])
                i += 1
            load_w_bf16.cnt = i
        load_w_bf16.cnt = 0

        with contextlib.ExitStack() as pes:
            scf = kb.sb(pes, "scf", [128, 8], F32)
            screp = kb.sb(pes, "screp", [128, 8, 128], F32)
            wst = [kb.sb(pes, f"wst{i}", [128, 8, 1024], F32) for i in range(2)]
            rows01 = kb.sb(pes, "rows01", [128, 2048], F32)
            pa = kb.ps(pes, "pa", [128, 512], F32)
            pb = [kb.ps(pes, f"pb{i}", [128, 512], F32) for i in range(2)]
            kb.dma(rows01.t[:], rows_d[:, 0:2048], writes=[rows01])
            kb.op(ACT, lambda: nc.scalar.activation(out=scf.t[:], in_=pcol("c", 0, 8), func=AF.Silu), reads=[params], writes=[scf])
            for k in range(8):
                kb.op(DVE, lambda k=k: nc.vector.tensor_copy(out=screp.t[:, k, :], in_=scf.t[:, k:k + 1].to_broadcast([128, 128])),
                      reads=[scf], writes=[screp])
            for jb in range(6):
                st = wst[jb % 2]
                kb.dma(st.t[:], w_ada_d[:, jb * 1024:(jb + 1) * 1024].rearrange("(k p) c -> p k c", p=128), writes=[st])
                if jb in (2, 5):
                    dst = gt1r if jb == 2 else gt2r
                    for cb in range(2):
                        for k in range(8):
                            kb.op(PE, lambda k=k, cb=cb, st=st: nc.tensor.matmul(pb[cb].t[:], lhsT=screp.t[:, k, :], rhs=st.t[:, k, cb * 512:(cb + 1) * 512],
                                                                               start=(k == 0), stop=(k == 7)),
                                  reads=[screp, st], writes=[pb[cb]], inc=(k == 7))
                        ro = (0 if jb == 2 else 1024) + cb * 512
                        kb.op(DVE, lambda cb=cb, ro=ro, dst=dst: nc.vector.scalar_tensor_tensor(
                            out=dst.t[:, cb * 512:(cb + 1) * 512], in0=pb[cb].t[:], scalar=1.0, in1=rows01.t[:, ro:ro + 512],
                            op0=ALU.add, op1=ALU.add), reads=[pb[cb], rows01], writes=[dst])
                for j in range(8):
                    col = jb * 8 + j
                    for k in range(8):
                        kb.op(PE, lambda k=k, j=j, col=col, st=st: nc.tensor.matmul(pa.t[:, col:col + 1], lhsT=st.t[:, k, j * 128:(j + 1) * 128],
                                                                                 rhs=scf.t[:, k:k + 1], start=(k == 0), stop=(k == 7)),
                              reads=[scf, st], writes=[pa], inc=(k == 7))
            kb.op(DVE, lambda: nc.vector.tensor_tensor(out=ada.t[:], in0=pa.t[:, 0:48], in1=pcol("b_ada", 0, 48), op=ALU.add),
                  reads=[pa, params], writes=[ada])
            for o in (8, 32):
                kb.op(DVE, lambda o=o: nc.vector.tensor_scalar(out=ada.t[:, o:o + 8], in0=ada.t[:, o:o + 8], scalar1=1.0, scalar2=None, op0=ALU.add),
                      reads=[ada], writes=[ada])
            xin = [kb.sb(pes, f"xin{i}", [128, 8, 512], F32) for i in range(2)]
            hto = [kb.sb(pes, f"hto{i}", [128, 8, 512], BF16) for i in range(2)]
            for t in range(NT):
                xi, ho = xin[t % 2], hto[t % 2]
                kb.dma(xi.t[:], xT_d[:, :, t * 512:(t + 1) * 512].rearrange("c p t -> p c t"), writes=[xi])
                for c in range(8):
                    if c % 2 == 0:
                        kb.op(ACT, lambda c=c, xi=xi, ho=ho: nc.scalar.activation(out=ho.t[:, c, :], in_=xi.t[:, c, :], func=AF.Identity,
                                                                              scale=ada.t[:, 8 + c:9 + c], bias=ada.t[:, c:c + 1]),
                              reads=[xi, ada], writes=[ho])
                    else:
                        kb.op(DVE, lambda c=c, xi=xi, ho=ho: nc.vector.tensor_scalar(out=ho.t[:, c, :], in0=xi.t[:, c, :], scalar1=ada.t[:, 8 + c:9 + c],
                                                                                 scalar2=ada.t[:, c:c + 1], op0=ALU.mult, op1=ALU.add),
                              reads=[xi, ada], writes=[ho])
                kb.dma(hT_d[:, :, t * 512:(t + 1) * 512].rearrange("c p t -> p c t"), ho.t[:], reads=[ho])
            kb.barrier()

        PHASES = build_program.phases
        eps_ln = kb.sb(es, "eps_ln", [128, 2], F32)
        kb.eps_ln = eps_ln
        kb.op(DVE, lambda: nc.vector.memset(eps_ln.t[:, 0:1], 1e-5), writes=[eps_ln])
        kb.op(DVE, lambda: nc.vector.memset(eps_ln.t[:, 1:2], 1e-6), writes=[eps_ln])
        if "rg" in PHASES:
            phase_rg(nc, kb, S, params, pcol, cm, cmb, load_w_bf16, w_in_d, gate_a_d, gate_x_d, hT_d, recT_d)
        if "gdn" in PHASES:
            phase_gdn(nc, kb, S, params, pcol, cm, cmb, load_w_bf16, w_in_d, rows_d, hrep_d, hT_d, dnT_d)
        if "mix" in PHASES:
            phase_mix(nc, kb, S, load_w_bf16, gt1r, w_in_d, w_pa_d, w_pb_d, w_out_d, rows_d, x_d, hT_d, recT_d, dnT_d, x1_d)
        if "ffn" in PHASES:
            phase_ffn(nc, kb, S, params, pcol, cm, load_w_bf16, ada, gt2r, w_fg_d, w_fu_d, w_fd_d, rows_d, x1_d, out_d)
        kb.final_wait()
    return nc


build_program.phases = ("rg", "gdn", "mix", "ffn")


def gelu_tanh(nc, kb, src_ap, src_bufs, out_ap, out_bufs, tmp, tmp2=None, mul_ap=None, mul_bufs=(), defer=None):
    ACT, DVE = kb.act, kb.dve
    if mul_ap is None:
        kb.op(ACT, lambda: nc.scalar.activation(out=out_ap, in_=src_ap, func=AF.Gelu_apprx_tanh), reads=src_bufs, writes=out_bufs)
    else:
        kb.op(ACT, lambda: nc.scalar.activation(out=tmp.t[:], in_=src_ap, func=AF.Gelu_apprx_tanh), reads=src_bufs, writes=[tmp])
        def fin():
            kb.op(DVE, lambda: nc.vector.tensor_tensor(out=out_ap, in0=tmp.t[:], in1=mul_ap, op=ALU.mult), reads=[tmp] + list(mul_bufs), writes=out_bufs)
        if defer is None:
            fin()
        else:
            defer.append(fin)


def phase_rg(nc, kb, S, params, pcol, cm, cmb, load_w_bf16, w_in_d, gate_a_d, gate_x_d, hT_d, recT_d):
    PE, ACT, DVE, POOL = kb.pe, kb.act, kb.dve, kb.pool
    TT = 256
    NT = S // TT
    with contextlib.ExitStack() as pes:
        wx = kb.sb(pes, "wx", [128, 8, 2560], BF16)
        wga = kb.sb(pes, "wga", [128, 10, 3, 128], BF16)
        wgx = kb.sb(pes, "wgx", [128, 10, 3, 128], BF16)
        cl = kb.sb(pes, "cl", [128, 20], F32)
        with contextlib.ExitStack() as ses:
            stg = [kb.sb(ses, f"stg{i}", [128, 8, 512], F32) for i in range(2)]
            gst = kb.sb(ses, "gst", [128, 10, 3, 128], F32)
            load_w_bf16(ses, wx, w_in_d, 0, 8, 0, 2560, stg)
            kb.dma(gst.t[:], gate_a_d[:, :, :, :], writes=[gst])
            kb.op(ACT, lambda: nc.scalar.copy(out=wga.t[:], in_=gst.t[:]), reads=[gst], writes=[wga])
            kb.dma(gst.t[:], gate_x_d[:, :, :, :], writes=[gst])
            kb.op(ACT, lambda: nc.scalar.copy(out=wgx.t[:], in_=gst.t[:]), reads=[gst], writes=[wgx])
            kb.barrier(keep=[wx, wga, wgx, cl])
        kb.op(ACT, lambda: nc.scalar.activation(out=cl.t[:, 0:10], in_=pcol("rg_lam", 0, 10), func=AF.Exp, scale=-1.0), reads=[params], writes=[cl])
        kb.op(ACT, lambda: nc.scalar.activation(out=cl.t[:, 0:10], in_=cl.t[:, 0:10], func=AF.Ln, bias=1.0), reads=[cl], writes=[cl])
        kb.op(DVE, lambda: nc.vector.tensor_scalar(out=cl.t[:, 10:20], in0=cl.t[:, 0:10], scalar1=-16.0, scalar2=None, op0=ALU.mult), reads=[cl], writes=[cl])
        kb.op(DVE, lambda: nc.vector.tensor_scalar(out=cl.t[:, 0:10], in0=cl.t[:, 0:10], scalar1=-8.0, scalar2=None, op0=ALU.mult), reads=[cl], writes=[cl])

        hT = [kb.sb(pes, f"hT{i}", [128, 8, TT], BF16) for i in range(2)]
        raw = [kb.sb(pes, f"raw{i}", [128, TT + 3], F32) for i in range(3)]
        halo = kb.sbs(pes, "halo", [128, 10, 3], F32)
        hst = kb.sbs(pes, "hst", [128, 10, 1], F32)
        xr = kb.sbs(pes, "xr", [128, 10, TT], F32)
        xrb = kb.sbs(pes, "xrb", [128, 10, TT], BF16)
        ga = kb.sbs(pes, "ga", [128, 10, TT], F32)
        gi = kb.sbs(pes, "gi", [128, 10, TT], F32)
        mu = kb.sbs(pes, "mu", [128, 10, TT], F32)
        hh = kb.sbs(pes, "hh", [128, 10, TT], F32)
        gg = kb.sbs(pes, "gg", [128, 10, TT], F32)
        tmp = [kb.sb(pes, f"gtmp{i}", [128, TT], F32) for i in range(3)]
        rec = [kb.sbs(pes, f"rec{i}", [128, 10, TT], BF16) for i in range(2)]
        pp = [kb.ps(pes, f"pp{i}", [128, 512], F32) for i in range(6)]
        kb.op(DVE, lambda: nc.vector.memset(halo.t[:], 0.0), writes=halo.sub)
        kb.op(DVE, lambda: nc.vector.memset(hst.t[:], 0.0), writes=hst.sub)
        cwo = _P["rg_cw"][0]

        def load_h(t):
            kb.dma(hT[t % 2].t[:], hT_d[:, :, t * TT:(t + 1) * TT].rearrange("c p t -> p c t"), writes=[hT[t % 2]])

        load_h(0)
        pi = 0
        for t in range(NT):
            if t + 1 < NT:
                load_h(t + 1)
            h = hT[t % 2]
            for j in range(10):
                p = pp[pi % 6]; pi += 1
                for k in range(8):
                    kb.op(PE, lambda: nc.tensor.matmul(p.t[:, 0:TT], lhsT=wx.t[:, k, j * 128:(j + 1) * 128], rhs=h.t[:, k, :],
                                                       start=(k == 0), stop=(k == 7)), reads=[wx, h], writes=[p], inc=(k == 7))
                rw = raw[j % 3]
                kb.op(ACT, lambda: nc.scalar.copy(out=rw.t[:, 0:3], in_=halo.t[:, j, :]), reads=[halo.sub[j]], writes=[rw])
                kb.op(ACT, lambda: nc.scalar.copy(out=rw.t[:, 3:TT + 3], in_=p.t[:, 0:TT]), reads=[p], writes=[rw])
                kb.op(ACT, lambda: nc.scalar.copy(out=halo.t[:, j, :], in_=rw.t[:, TT:TT + 3]), reads=[rw], writes=[halo.sub[j]])
                kb.op(DVE, lambda: nc.vector.tensor_scalar(out=xr.t[:, j, :], in0=rw.t[:, 3:TT + 3], scalar1=params.t[:, cwo + j * 4 + 3:cwo + j * 4 + 4],
                                                           scalar2=pcol("rg_cb", j), op0=ALU.mult, op1=ALU.add), reads=[rw, params], writes=[xr.sub[j]])
                for kk in range(3):
                    kb.op(DVE, lambda: nc.vector.scalar_tensor_tensor(
                        out=xr.t[:, j, :], in0=rw.t[:, kk:kk + TT], scalar=params.t[:, cwo + j * 4 + kk:cwo + j * 4 + kk + 1], in1=xr.t[:, j, :],
                        op0=ALU.mult, op1=ALU.add), reads=[rw, params, xr.sub[j]], writes=[xr.sub[j]])
            for j in range(10):
                p = pp[pi % 6]; pi += 1
                for k in range(8):
                    kb.op(PE, lambda: nc.tensor.matmul(p.t[:, 0:TT], lhsT=wx.t[:, k, 1280 + j * 128:1280 + (j + 1) * 128], rhs=h.t[:, k, :],
                                                       start=(k == 0), stop=(k == 7)), reads=[wx, h], writes=[p], inc=(k == 7))
                kb.op(ACT, lambda: nc.scalar.activation(out=gg.t[:, j, :], in_=p.t[:, 0:TT], func=AF.Gelu_apprx_tanh), reads=[p], writes=[gg.sub[j]])
            for j in range(10):
                kb.op(ACT, lambda: nc.scalar.copy(out=xrb.t[:, j, :], in_=xr.t[:, j, :]), reads=[xr.sub[j]], writes=[xrb.sub[j]])
            for (wg, dst, bname) in ((wga, ga, "rg_ba"), (wgx, gi, "rg_bx")):
                for m in range(10):
                    p = pp[pi % 6]; pi += 1
                    ks = [k for k in (m - 1, m, m + 1) if 0 <= k < 10]
                    for n_, k in enumerate(ks):
                        kb.op(PE, lambda: nc.tensor.matmul(
                            p.t[:, 0:TT], lhsT=wg.t[:, m, k - m + 1, :], rhs=xrb.t[:, k, :], start=(n_ == 0), stop=(n_ == len(ks) - 1)),
                            reads=[wg, xrb.sub[k]], writes=[p], inc=(n_ == len(ks) - 1))
                    kb.op(ACT, lambda: nc.scalar.activation(out=dst.t[:, m, :], in_=p.t[:, 0:TT], func=AF.Sigmoid, bias=pcol(bname, m)),
                          reads=[p, params], writes=[dst.sub[m]])
            for m in range(10):
                kb.op(ACT, lambda: nc.scalar.activation(out=mu.t[:, m, :], in_=ga.t[:, m, :], func=AF.Exp, scale=cl.t[:, 10 + m:11 + m]), reads=[ga.sub[m], cl], writes=[mu.sub[m]])
                kb.op(ACT, lambda: nc.scalar.activation(out=ga.t[:, m, :], in_=ga.t[:, m, :], func=AF.Exp, scale=cl.t[:, m:m + 1]), reads=[ga.sub[m], cl], writes=[ga.sub[m]])
            for m in range(10):
                kb.op(DVE, lambda: nc.vector.tensor_scalar(out=mu.t[:, m, :], in0=mu.t[:, m, :], scalar1=1.0, scalar2=None, op0=ALU.min), reads=[mu.sub[m]], writes=[mu.sub[m]])
            for m in range(10):
                kb.op(ACT, lambda: nc.scalar.activation(out=mu.t[:, m, :], in_=mu.t[:, m, :], func=AF.Sqrt, scale=-1.0, bias=1.0), reads=[mu.sub[m]], writes=[mu.sub[m]])
            for m in range(10):
                kb.op(POOL, lambda: nc.gpsimd.tensor_tensor(out=gi.t[:, m, :], in0=gi.t[:, m, :], in1=xr.t[:, m, :], op=ALU.mult), reads=[gi.sub[m], xr.sub[m]], writes=[gi.sub[m]])
                kb.op(DVE, lambda: nc.vector.tensor_tensor(out=gi.t[:, m, :], in0=gi.t[:, m, :], in1=mu.t[:, m, :], op=ALU.mult), reads=[gi.sub[m], mu.sub[m]], writes=[gi.sub[m]])
                kb.op(DVE, lambda: nc.vector.tensor_tensor_scan(out=hh.t[:, m, :], data0=ga.t[:, m, :], data1=gi.t[:, m, :], initial=hst.t[:, m, :],
                                                                op0=ALU.mult, op1=ALU.add), reads=[ga.sub[m], gi.sub[m], hst.sub[m]], writes=[hh.sub[m]])
                kb.op(ACT, lambda: nc.scalar.copy(out=hst.t[:, m, :], in_=hh.t[:, m, TT - 1:TT]), reads=[hh.sub[m]], writes=[hst.sub[m]])
            rc = rec[t % 2]
            for j in range(10):
                kb.op(DVE, lambda: nc.vector.tensor_tensor(out=rc.t[:, j, :], in0=hh.t[:, j, :], in1=gg.t[:, j, :], op=ALU.mult), reads=[hh.sub[j], gg.sub[j]], writes=[rc.sub[j]])
            kb.dma(recT_d[:, :, t * TT:(t + 1) * TT].rearrange("c p t -> p c t"), rc.t[:], reads=rc.sub)
        kb.barrier()


def layer_norm_rows(nc, kb, y, nblk, g_ap, b_ap, rowbuf, stats, mv, sd, out_buf):
    ACT, DVE, POOL = kb.act, kb.dve, kb.pool
    for blk in range(nblk):
        for hf in range(2):
            kb.op(DVE, lambda: nc.vector.bn_stats(out=stats.t[:, blk, hf, :], in_=y.t[:, blk, hf * 512:(hf + 1) * 512]), reads=[y], writes=[stats])
        kb.op(DVE, lambda: nc.vector.bn_aggr(out=mv.t[:, blk, :], in_=stats.t[:, blk, :, :].rearrange("p a b -> p (a b)")), reads=[stats], writes=[mv])
        kb.op(ACT, lambda: nc.scalar.activation(out=sd.t[:, blk:blk + 1], in_=mv.t[:, blk, 1:2], func=AF.Ln, bias=kb.eps_ln.t[:, 0:1]), reads=[mv, kb.eps_ln], writes=[sd])
        kb.op(ACT, lambda: nc.scalar.activation(out=sd.t[:, blk:blk + 1], in_=sd.t[:, blk:blk + 1], func=AF.Exp, scale=-0.5), reads=[sd], writes=[sd])
        kb.op(DVE, lambda: nc.vector.tensor_scalar(out=y.t[:, blk, :], in0=y.t[:, blk, :], scalar1=mv.t[:, blk, 0:1], scalar2=sd.t[:, blk:blk + 1],
                                                   op0=ALU.subtract, op1=ALU.mult), reads=[y, mv, sd], writes=[y])
        kb.op(POOL, lambda: nc.gpsimd.tensor_tensor(out=y.t[:, blk, :], in0=y.t[:, blk, :], in1=g_ap, op=ALU.mult), reads=[y, rowbuf], writes=[y])
        kb.op(DVE, lambda: nc.vector.tensor_tensor(out=out_buf.t[:, blk, :], in0=y.t[:, blk, :], in1=b_ap, op=ALU.add), reads=[y, rowbuf], writes=[out_buf])


def phase_mix(nc, kb, S, load_w_bf16, gt1r, w_in_d, w_pa_d, w_pb_d, w_out_d, rows_d, x_d, hT_d, recT_d, dnT_d, x1_d):
    PE, ACT, DVE, POOL = kb.pe, kb.act, kb.dve, kb.pool
    TT = 256
    NT = S // TT
    NB = TT // 128
    with contextlib.ExitStack() as pes:
        wg = kb.sb(pes, "wg", [128, 8, 2048], BF16)
        wpa = kb.sb(pes, "wpa", [128, 10, 1024], BF16)
        wpb = kb.sb(pes, "wpb", [128, 16, 1024], BF16)
        wo = kb.sb(pes, "wo", [128, 8, 1024], BF16)
        lnr = kb.sb(pes, "lnr", [128, 2048], F32)
        with contextlib.ExitStack() as ses:
            stg = [kb.sb(ses, f"stg{i}", [128, 16, 256], F32) for i in range(2)]
            load_w_bf16(ses, wg, w_in_d, 0, 8, C_GA, 2048, stg)
            load_w_bf16(ses, wpa, w_pa_d, 0, 10, 0, 1024, stg)
            load_w_bf16(ses, wpb, w_pb_d, 0, 16, 0, 1024, stg)
            load_w_bf16(ses, wo, w_out_d, 0, 8, 0, 1024, stg)
            kb.dma(lnr.t[:], rows_d[:, _R["ln1_g"][0]:_R["ln1_g"][0] + 2048], writes=[lnr])
            kb.barrier(keep=[wg, wpa, wpb, wo, lnr])
        hT = [kb.sb(pes, f"hT{i}", [128, 8, TT], BF16) for i in range(2)]
        rcT = [kb.sb(pes, f"rcT{i}", [128, 10, TT], BF16) for i in range(2)]
        dnT = [kb.sb(pes, f"dnT{i}", [128, 16, TT], BF16) for i in range(2)]
        xt = [kb.sb(pes, f"xt{i}", [128, NB, 1024], F32) for i in range(2)]
        mg = kb.sbs(pes, "mg", [128, 8, TT], BF16)
        sg = [kb.sb(pes, f"sg{i}", [128, TT], F32) for i in range(4)]
        t1 = [kb.sb(pes, f"t1{i}", [128, TT], F32) for i in range(4)]
        y = kb.sb(pes, "y", [128, NB, 1024], F32)
        xo = [y, y]
        stats = kb.sb(pes, "stats", [128, NB, 2, 6], F32)
        mv = kb.sb(pes, "mv", [128, NB, 2], F32)
        sd = kb.sb(pes, "sd", [128, NB], F32)
        pp = [kb.ps(pes, f"pp{i}", [128, 512], F32) for i in range(8)]

        def load(t):
            i = t % 2
            sl = slice(t * TT, (t + 1) * TT)
            kb.dma(hT[i].t[:], hT_d[:, :, sl].rearrange("c p t -> p c t"), writes=[hT[i]])
            kb.dma(rcT[i].t[:], recT_d[:, :, sl].rearrange("c p t -> p c t"), writes=[rcT[i]])
            kb.dma(dnT[i].t[:], dnT_d[:, :, sl].rearrange("c p t -> p c t"), writes=[dnT[i]])
            kb.dma(xt[i].t[:], x_d[sl, :].rearrange("(b p) f -> p b f", p=128), writes=[xt[i]])

        load(0)
        pi = 0
        for t in range(NT):
            if t + 1 < NT:
                load(t + 1)
            i = t % 2
            h, rc, dn, xx = hT[i], rcT[i], dnT[i], xt[i]
            deferred = []
            for m in range(8):
                ms = slice(m * 128, (m + 1) * 128)
                pga = pp[pi % 8]; pya = pp[(pi + 1) % 8]; pgb = pp[(pi + 2) % 8]; pyb = pp[(pi + 3) % 8]; pi += 4
                for k in range(8):
                    kb.op(PE, lambda: nc.tensor.matmul(pga.t[:, 0:TT], lhsT=wg.t[:, k, m * 128:(m + 1) * 128], rhs=h.t[:, k, :], start=(k == 0), stop=(k == 7)),
                          reads=[wg, h], writes=[pga], inc=(k == 7))
                for k in range(10):
                    kb.op(PE, lambda: nc.tensor.matmul(pya.t[:, 0:TT], lhsT=wpa.t[:, k, ms], rhs=rc.t[:, k, :], start=(k == 0), stop=(k == 9)),
                          reads=[wpa, rc], writes=[pya], inc=(k == 9))
                for k in range(8):
                    kb.op(PE, lambda: nc.tensor.matmul(pgb.t[:, 0:TT], lhsT=wg.t[:, k, 1024 + m * 128:1024 + (m + 1) * 128], rhs=h.t[:, k, :], start=(k == 0), stop=(k == 7)),
                          reads=[wg, h], writes=[pgb], inc=(k == 7))
                for k in range(16):
                    kb.op(PE, lambda: nc.tensor.matmul(pyb.t[:, 0:TT], lhsT=wpb.t[:, k, ms], rhs=dn.t[:, k, :], start=(k == 0), stop=(k == 15)),
                          reads=[wpb, dn], writes=[pyb], inc=(k == 15))
                s0, s1, ta, tb = sg[2 * (m % 2)], sg[2 * (m % 2) + 1], t1[2 * (m % 2)], t1[2 * (m % 2) + 1]
                kb.op(ACT, lambda: nc.scalar.activation(out=s0.t[:], in_=pga.t[:, 0:TT], func=AF.Sigmoid), reads=[pga], writes=[s0])
                kb.op(ACT, lambda: nc.scalar.activation(out=s1.t[:], in_=pgb.t[:, 0:TT], func=AF.Sigmoid), reads=[pgb], writes=[s1])
                def fin(m=m, s0=s0, s1=s1, ta=ta, tb=tb, pya=pya, pyb=pyb):
                    kb.op(DVE, lambda: nc.vector.tensor_tensor(out=ta.t[:], in0=s0.t[:], in1=pya.t[:, 0:TT], op=ALU.mult), reads=[s0, pya], writes=[ta])
                    kb.op(DVE, lambda: nc.vector.tensor_tensor(out=tb.t[:], in0=s1.t[:], in1=pyb.t[:, 0:TT], op=ALU.mult), reads=[s1, pyb], writes=[tb])
                    kb.op(POOL, lambda: nc.gpsimd.tensor_tensor(out=mg.t[:, m, :], in0=ta.t[:], in1=tb.t[:], op=ALU.add), reads=[ta, tb], writes=[mg.sub[m]])
                prev = list(deferred); del deferred[:]
                deferred.append(fin)
                for f_ in prev:
                    f_()
            for f_ in deferred:
                f_()
            del deferred[:]
            for blk in range(NB):
                for cb in range(2):
                    p = pp[pi % 8]; pi += 1
                    cs = slice(cb * 512, (cb + 1) * 512)
                    for k in range(8):
                        kb.op(PE, lambda: nc.tensor.matmul(p.t[:], lhsT=mg.t[:, k, blk * 128:(blk + 1) * 128], rhs=wo.t[:, k, cs], start=(k == 0), stop=(k == 7)),
                              reads=[mg.sub[k], wo], writes=[p], inc=(k == 7))
                    kb.op(DVE, lambda: nc.vector.tensor_tensor(out=y.t[:, blk, cs], in0=p.t[:], in1=gt1r.t[:, cs], op=ALU.mult), reads=[p, gt1r], writes=[y])
                    kb.op(DVE, lambda: nc.vector.scalar_tensor_tensor(out=y.t[:, blk, cs], in0=xx.t[:, blk, cs], scalar=ALPHA, in1=y.t[:, blk, cs],
                                                                      op0=ALU.mult, op1=ALU.add), reads=[xx, y], writes=[y])
            o = xo[i]
            layer_norm_rows(nc, kb, y, NB, lnr.t[:, 0:1024], lnr.t[:, 1024:2048], lnr, stats, mv, sd, o)
            kb.dma(x1_d[t * TT:(t + 1) * TT, :].rearrange("(b p) f -> p b f", p=128), o.t[:], reads=[o])
        kb.barrier()


def phase_ffn(nc, kb, S, params, pcol, cm, load_w_bf16, ada, gt2r, w_fg_d, w_fu_d, w_fd_d, rows_d, x1_d, out_d):
    PE, ACT, DVE, POOL = kb.pe, kb.act, kb.dve, kb.pool
    TT = 256
    NT = S // TT
    NB = TT // 128
    ident_f = cm.t[:, CM_IDENT * 128:(CM_IDENT + 1) * 128]
    with contextlib.ExitStack() as pes:
        wfg = kb.sb(pes, "wfg", [128, 8, DFF], BF16)
        wfu = kb.sb(pes, "wfu", [128, 8, DFF], BF16)
        wfd = kb.sb(pes, "wfd", [128, 22, 1024], BF16)
        lnr = kb.sb(pes, "lnr", [128, 2048], F32)
        with contextlib.ExitStack() as ses:
            stg = [kb.sb(ses, f"stg{i}", [128, 22, 128], F32) for i in range(2)]
            load_w_bf16(ses, wfg, w_fg_d, 0, 8, 0, DFF, stg)
            load_w_bf16(ses, wfu, w_fu_d, 0, 8, 0, DFF, stg)
            load_w_bf16(ses, wfd, w_fd_d, 0, 22, 0, 1024, stg)
            kb.dma(lnr.t[:], rows_d[:, _R["ln2_g"][0]:_R["ln2_g"][0] + 2048], writes=[lnr])
            kb.barrier(keep=[wfg, wfu, wfd, lnr])
        xt = [kb.sb(pes, f"xt{i}", [128, NB, 1024], F32) for i in range(2)]
        h2 = kb.sbs(pes, "h2", [128, 8, TT], BF16)
        act = kb.sbs(pes, "act", [128, 22, TT], BF16)
        raw = [kb.sb(pes, f"raw{i}", [128, TT + 2], F32) for i in range(3)]
        cv = [kb.sb(pes, f"cv{i}", [128, TT], F32) for i in range(3)]
        tmp = [kb.sb(pes, f"tmp{i}", [128, TT], F32) for i in range(3)]
        halo = kb.sbs(pes, "halo", [128, 22, 2], F32)
        y = kb.sb(pes, "y", [128, NB, 1024], F32)
        xo = [y, y]
        stats = kb.sb(pes, "stats", [128, NB, 2, 6], F32)
        mv = kb.sb(pes, "mv", [128, NB, 2], F32)
        sd = kb.sb(pes, "sd", [128, NB], F32)
        pp = [kb.ps(pes, f"pp{i}", [128, 512], F32) for i in range(8)]
        kb.op(DVE, lambda: nc.vector.memset(halo.t[:], 0.0), writes=halo.sub)
        cwo = _P["ffn_cw"][0]

        def load(t):
            kb.dma(xt[t % 2].t[:], x1_d[t * TT:(t + 1) * TT, :].rearrange("(b p) f -> p b f", p=128), writes=[xt[t % 2]])

        load(0)
        pi = 0
        for t in range(NT):
            if t + 1 < NT:
                load(t + 1)
            xx = xt[t % 2]
            for blk in range(NB):
                for c in range(8):
                    p = pp[pi % 8]; pi += 1
                    kb.op(PE, lambda: nc.tensor.matmul(p.t[:, 0:128], lhsT=xx.t[:, blk, c * 128:(c + 1) * 128], rhs=ident_f, start=True, stop=True), reads=[xx, cm], writes=[p])
                    kb.op(ACT, lambda: nc.scalar.activation(out=h2.t[:, c, blk * 128:(blk + 1) * 128], in_=p.t[:, 0:128], func=AF.Identity,
                                                            scale=ada.t[:, 32 + c:33 + c], bias=ada.t[:, 24 + c:25 + c]), reads=[p, ada], writes=[h2.sub[c]])
            deferred = []
            for m in range(22):
                ms = slice(m * 128, (m + 1) * 128)
                pg = pp[pi % 8]; pu = pp[(pi + 1) % 8]; pi += 2
                for k in range(8):
                    kb.op(PE, lambda: nc.tensor.matmul(pg.t[:, 0:TT], lhsT=wfg.t[:, k, ms], rhs=h2.t[:, k, :], start=(k == 0), stop=(k == 7)),
                          reads=[wfg, h2.sub[k]], writes=[pg], inc=(k == 7))
                for k in range(8):
                    kb.op(PE, lambda: nc.tensor.matmul(pu.t[:, 0:TT], lhsT=wfu.t[:, k, ms], rhs=h2.t[:, k, :], start=(k == 0), stop=(k == 7)),
                          reads=[wfu, h2.sub[k]], writes=[pu], inc=(k == 7))
                rw = raw[m % 3]
                c_ = cv[m % 3]
                kb.op(ACT, lambda: nc.scalar.copy(out=rw.t[:, 0:2], in_=halo.t[:, m, :]), reads=[halo.sub[m]], writes=[rw])
                kb.op(ACT, lambda: nc.scalar.copy(out=rw.t[:, 2:TT + 2], in_=pg.t[:, 0:TT]), reads=[pg], writes=[rw])
                kb.op(ACT, lambda: nc.scalar.copy(out=halo.t[:, m, :], in_=rw.t[:, TT:TT + 2]), reads=[rw], writes=[halo.sub[m]])
                kb.op(DVE, lambda: nc.vector.tensor_scalar(out=c_.t[:], in0=rw.t[:, 2:TT + 2], scalar1=params.t[:, cwo + m * 3 + 2:cwo + m * 3 + 3],
                                                           scalar2=pcol("ffn_cb", m), op0=ALU.mult, op1=ALU.add), reads=[rw, params], writes=[c_])
                for kk in range(2):
                    kb.op(DVE, lambda: nc.vector.scalar_tensor_tensor(out=c_.t[:], in0=rw.t[:, kk:kk + TT], scalar=params.t[:, cwo + m * 3 + kk:cwo + m * 3 + kk + 1],
                                                                      in1=c_.t[:], op0=ALU.mult, op1=ALU.add), reads=[rw, params, c_], writes=[c_])
                prev = list(deferred); del deferred[:]
                gelu_tanh(nc, kb, c_.t[:], [c_], act.t[:, m, :], [act.sub[m]], tmp[m % 3], mul_ap=pu.t[:, 0:TT], mul_bufs=[pu], defer=deferred)
                for f_ in prev:
                    f_()
            for f_ in deferred:
                f_()
            del deferred[:]
            for blk in range(NB):
                for cb in range(2):
                    p = pp[pi % 8]; pi += 1
                    cs = slice(cb * 512, (cb + 1) * 512)
                    for k in range(22):
                        kb.op(PE, lambda: nc.tensor.matmul(p.t[:], lhsT=act.t[:, k, blk * 128:(blk + 1) * 128], rhs=wfd.t[:, k, cs], start=(k == 0), stop=(k == 21)),
                              reads=[act.sub[k], wfd], writes=[p], inc=(k == 21))
                    kb.op(DVE, lambda: nc.vector.tensor_tensor(out=y.t[:, blk, cs], in0=p.t[:], in1=gt2r.t[:, cs], op=ALU.mult), reads=[p, gt2r], writes=[y])
                    kb.op(DVE, lambda: nc.vector.scalar_tensor_tensor(out=y.t[:, blk, cs], in0=xx.t[:, blk, cs], scalar=ALPHA, in1=y.t[:, blk, cs],
                                                                      op0=ALU.mult, op1=ALU.add), reads=[xx, y], writes=[y])
            o = xo[t % 2]
            layer_norm_rows(nc, kb, y, NB, lnr.t[:, 0:1024], lnr.t[:, 1024:2048], lnr, stats, mv, sd, o)
            kb.dma(out_d[t * TT:(t + 1) * TT, :].rearrange("(b p) f -> p b f", p=128), o.t[:], reads=[o])
        kb.barrier()


class PSlot:
    def __init__(self, kb, bank, i, name):
        self.b = bank
        self.ap = bank.t[:, i * 128:(i + 1) * 128]
        self.apb = self.ap.bitcast(BF16)[:, 0:128]


def phase_gdn(nc, kb, S, params, pcol, cm, cmb, load_w_bf16, w_in_d, rows_d, hrep_d, hT_d, dnT_d):
    PE, ACT, DVE, POOL = kb.pe, kb.act, kb.dve, kb.pool
    NCH = S // 128
    NT = S // 512
    HVG, HQG = 4, 2
    NG = 16 // HVG
    NCOL = NCH * 16
    cf = lambda i: cm.t[:, i * 128:(i + 1) * 128]
    cb = lambda i: cmb.t[:, i * 128:(i + 1) * 128]
    ident_f, ident_b, L_f, U_f, ones_f, ones_b = cf(CM_IDENT), cb(CM_IDENT), cf(CM_L), cf(CM_U), cf(CM_ONES), cb(CM_ONES)
    mS_b, mIT_b, BD_f, OFF_f = cb(CM_MS), cb(CM_MIT), cf(CM_BD), cf(CM_OFF)
    eps6 = kb.eps_ln.t[:, 1:2]
    with contextlib.ExitStack() as pes:
        tm = {n: kb.sb(pes, "tm_" + n, [128, NCH, 16], F32) for n in ("beta", "negg", "eG", "beG", "kds", "gl")}
        flat = lambda bf: bf.t[:].rearrange("p n h -> p (n h)")
        with contextlib.ExitStack() as ses:
            wab = kb.sb(ses, "wab", [128, 8, 32], BF16)
            stg = [kb.sb(ses, "stg0", [128, 8, 32], F32)]
            hT = [kb.sb(ses, f"hT{i}", [128, 8, 512], BF16) for i in range(2)]
            ab = kb.sb(ses, "ab", [128, NCH, 32], F32)
            hrep = kb.sb(ses, "hrep", [128, 2, NCOL], F32)
            X = kb.sb(ses, "X", [128, NCH, 16], F32)
            Y = kb.sb(ses, "Y", [128, NCH, 16], F32)
            Z = kb.sb(ses, "Z", [128, NCH, 16], F32)
            pp = [kb.ps(ses, f"pp{i}", [128, 512], F32) for i in range(4)]
            load_w_bf16(ses, wab, w_in_d, 0, 8, C_A, 32, stg)
            kb.dma(hrep.t[:], hrep_d[:, :, :], writes=[hrep])
            kb.dma(hT[0].t[:], hT_d[:, :, 0:512].rearrange("c p t -> p c t"), writes=[hT[0]])
            for t in range(NT):
                if t + 1 < NT:
                    kb.dma(hT[(t + 1) % 2].t[:], hT_d[:, :, (t + 1) * 512:(t + 2) * 512].rearrange("c p t -> p c t"), writes=[hT[(t + 1) % 2]])
                h = hT[t % 2]
                p = pp[t % 2]
                for blk in range(4):
                    for k in range(8):
                        kb.op(PE, lambda: nc.tensor.matmul(p.t[:, blk * 32:(blk + 1) * 32], lhsT=h.t[:, k, blk * 128:(blk + 1) * 128], rhs=wab.t[:, k, :],
                                                           start=(k == 0), stop=(k == 7)), reads=[h, wab], writes=[p], inc=(k == 7))
                kb.op(ACT, lambda: nc.scalar.copy(out=ab.t[:, t * 4:(t + 1) * 4, :], in_=p.t[:, 0:128].rearrange("p (b c) -> p b c", c=32)), reads=[p], writes=[ab])
            a_v, b_v = ab.t[:, :, 0:16], ab.t[:, :, 16:32]
            dtb = hrep.t[:, 0, :].rearrange("p (n h) -> p n h", h=16)
            alog = hrep.t[:, 1, :].rearrange("p (n h) -> p n h", h=16)
            kb.op(ACT, lambda: nc.scalar.activation(out=tm["beta"].t[:], in_=b_v, func=AF.Sigmoid), reads=[ab], writes=[tm["beta"]])
            kb.op(DVE, lambda: nc.vector.tensor_tensor(out=X.t[:], in0=a_v, in1=dtb, op=ALU.add), reads=[ab, hrep], writes=[X])
            kb.op(DVE, lambda: nc.vector.tensor_scalar(out=Y.t[:], in0=X.t[:], scalar1=-1.0, scalar2=None, op0=ALU.mult), reads=[X], writes=[Y])
            kb.op(DVE, lambda: nc.vector.tensor_tensor(out=Y.t[:], in0=Y.t[:], in1=X.t[:], op=ALU.max), reads=[X, Y], writes=[Y])
            kb.op(ACT, lambda: nc.scalar.activation(out=Y.t[:], in_=Y.t[:], func=AF.Exp, scale=-1.0), reads=[Y], writes=[Y])
            kb.op(ACT, lambda: nc.scalar.activation(out=Y.t[:], in_=Y.t[:], func=AF.Ln, bias=1.0), reads=[Y], writes=[Y])
            kb.op(DVE, lambda: nc.vector.tensor_scalar(out=X.t[:], in0=X.t[:], scalar1=0.0, scalar2=None, op0=ALU.max), reads=[X], writes=[X])
            kb.op(DVE, lambda: nc.vector.tensor_tensor(out=X.t[:], in0=X.t[:], in1=Y.t[:], op=ALU.add), reads=[X, Y], writes=[X])
            kb.op(ACT, lambda: nc.scalar.activation(out=Z.t[:], in_=alog, func=AF.Exp), reads=[hrep], writes=[Z])
            kb.op(DVE, lambda: nc.vector.tensor_tensor(out=tm["negg"].t[:], in0=X.t[:], in1=Z.t[:], op=ALU.mult), reads=[X, Z], writes=[tm["negg"]])
            ng_f = flat(tm["negg"])
            for c0 in range(0, NCOL, 512):
                w = min(512, NCOL - c0)
                p1, p2 = pp[2], pp[3]
                kb.op(PE, lambda: nc.tensor.matmul(p1.t[:, 0:w], lhsT=L_f, rhs=ng_f[:, c0:c0 + w], start=True, stop=True), reads=[cm, tm["negg"]], writes=[p1])
                kb.op(PE, lambda: nc.tensor.matmul(p2.t[:, 0:w], lhsT=ones_f, rhs=ng_f[:, c0:c0 + w], start=True, stop=True), reads=[cm, tm["negg"]], writes=[p2])
                Xf, Yf = flat(X), flat(Y)
                kb.op(ACT, lambda: nc.scalar.copy(out=Xf[:, c0:c0 + w], in_=p1.t[:, 0:w]), reads=[p1], writes=[X])
                kb.op(ACT, lambda: nc.scalar.activation(out=flat(tm["eG"])[:, c0:c0 + w], in_=p1.t[:, 0:w], func=AF.Exp, scale=-1.0), reads=[p1], writes=[tm["eG"]])
                kb.op(ACT, lambda: nc.scalar.activation(out=flat(tm["gl"])[:, c0:c0 + w], in_=p2.t[:, 0:w], func=AF.Exp, scale=-1.0), reads=[p2], writes=[tm["gl"]])
                kb.op(DVE, lambda: nc.vector.tensor_tensor(out=Yf[:, c0:c0 + w], in0=p2.t[:, 0:w], in1=Xf[:, c0:c0 + w], op=ALU.subtract), reads=[p2, X], writes=[Y])
                kb.op(ACT, lambda: nc.scalar.activation(out=flat(tm["kds"])[:, c0:c0 + w], in_=Yf[:, c0:c0 + w], func=AF.Exp, scale=-1.0), reads=[Y], writes=[tm["kds"]])
            kb.op(DVE, lambda: nc.vector.tensor_tensor(out=tm["beG"].t[:], in0=tm["beta"].t[:], in1=tm["eG"].t[:], op=ALU.mult), reads=[tm["beta"], tm["eG"]], writes=[tm["beG"]])
            kb.barrier()

        STOP = getattr(build_program, "gdn_stop", 0)
        for gi in range(NG if STOP != 1 else 0):
            hv0, hq0 = gi * HVG, gi * HQG
            with contextlib.ExitStack() as ges:
                wq = kb.sb(ges, "wq", [128, 8, HQG * 128], BF16)
                wk = kb.sb(ges, "wk", [128, 8, HQG * 128], BF16)
                wv = kb.sb(ges, "wv", [128, 8, HVG * 128], BF16)
                wz = kb.sb(ges, "wz", [128, 8, HVG * 128], BF16)
                nwr = kb.sb(ges, "nwr", [128, HVG * 128], F32)
                with contextlib.ExitStack() as ses:
                    stg = [kb.sb(ses, f"stg{i}", [128, 8, 512], F32) for i in range(2)]
                    load_w_bf16(ses, wq, w_in_d, 0, 8, C_Q + hq0 * 128, HQG * 128, stg)
                    load_w_bf16(ses, wk, w_in_d, 0, 8, C_K + hq0 * 128, HQG * 128, stg)
                    load_w_bf16(ses, wv, w_in_d, 0, 8, C_V + hv0 * 128, HVG * 128, stg)
                    load_w_bf16(ses, wz, w_in_d, 0, 8, C_Z + hv0 * 128, HVG * 128, stg)
                    kb.dma(nwr.t[:], rows_d[:, _R["nw"][0]:_R["nw"][0] + HVG * 128], writes=[nwr])
                    kb.barrier()
                hT = [kb.sb(ges, f"hT{i}", [128, 8, 512], BF16) for i in range(2)]
                raw = [kb.sb(ges, f"raw{i}", [128, 515], F32) for i in range(3)]
                cv = [kb.sb(ges, f"cv{i}", [128, 512], F32) for i in range(3)]
                NCK = 2 * HQG + HVG
                halo = kb.sb(ges, "halo", [128, NCK, 3], F32)
                qkf = [kb.sb(ges, f"qkf{i}", [128, 512], F32) for i in range(2 * HQG)]
                sqb = [kb.sb(ges, f"sqb{i}", [128, 512], BF16) for i in range(2 * HQG)]
                rn = [kb.sb(ges, f"rn{i}", [128, 512], F32) for i in range(2)]
                qT2 = [kb.sb(ges, f"qT{i}", [128, HQG, 512], BF16) for i in range(2)]
                kT2 = [kb.sb(ges, f"kT{i}", [128, HQG, 512], BF16) for i in range(2)]
                vT2 = [kb.sb(ges, f"vT{i}", [128, HVG, 512], BF16) for i in range(2)]
                zs = [kb.sb(ges, f"zs{i}", [128, 512], F32) for i in range(2)]
                zg2 = [kb.sb(ges, f"zg{i}", [128, 4, HVG * 128], F32) for i in range(2)]
                dno = [kb.sb(ges, f"dno{i}", [128, HVG, 512], BF16) for i in range(2)]
                Sf = [kb.sb(ges, f"Sf{i}", [128, 128], F32) for i in range(HVG)]
                Sb = [kb.sb(ges, f"Sb{i}", [128, 128], BF16) for i in range(HVG)]
                ob2 = [kb.sb(ges, f"ob{i}", [128, HVG, 128], F32) for i in range(2)]
                junk = kb.sb(ges, "junk", [128, 128], F32)
                ss2 = [kb.sb(ges, f"ss{i}", [128, HVG], F32) for i in range(2)]
                rstd2 = [kb.sb(ges, f"rstd{i}", [128, HVG], F32) for i in range(2)]
                ktm = [kb.sb(ges, f"ktm{i}", [128, 128], BF16) for i in range(HQG)]
                KKbd = [kb.sb(ges, f"KKbd{i}", [128, 128], F32) for i in range(HQG)]
                KKoff = [kb.sb(ges, f"KKoff{i}", [128, 128], F32) for i in range(HQG)]
                QKT = [kb.sb(ges, f"QKT{i}", [128, 128], F32) for i in range(HQG)]

                class PB:
                    pass
                pb = []
                for hi in range(HVG):
                    B = PB()
                    for n_ in ("rh", "Ds", "DT", "T1", "o1"):
                        setattr(B, n_, kb.sb(ges, f"{n_}_{hi}", [128, 128], F32))
                    for n_ in ("Abd", "Aoff", "P0", "P1", "M0", "Y1", "Tb", "bek", "vn", "dntm"):
                        setattr(B, n_, kb.sb(ges, f"{n_}_{hi}", [128, 128], BF16))
                    for n_ in ("QKd", "M128", "bv", "kdec", "nwT"):
                        setattr(B, n_, [kb.sb(ges, f"{n_}{i}_{hi}", [128, 128], BF16) for i in range(2)])
                    B.P = [B.P0, B.P1]
                    B.QM = [kb.sb(ges, f"QM{i}_{hi}", [128, 256], BF16) for i in range(2)]
                    pb.append(B)
                banks = [kb.ps(ges, f"bk{i}", [128, 512], F32) for i in range(8)]
                slots = [[PSlot(kb, banks[i], j, f"sl{i}_{j}") for j in range(4)] for i in range(8)]
                hsA = [slots[2 * hi] for hi in range(HVG)]
                hsB = [slots[2 * hi + 1] for hi in range(HVG)]

                def bank_bufs(i):
                    return [banks[i]]

                for b_ in Sf + [halo]:
                    kb.op(DVE, lambda: nc.vector.memset(b_.t[:], 0.0), writes=[b_])
                for b_ in Sb:
                    kb.op(DVE, lambda: nc.vector.memset(b_.t[:], 0.0), writes=[b_])
                cwo = _P["dn_cw"][0]
                chunks = [("q", wq, c, hq0 + c) for c in range(HQG)] + [("k", wk, c, 8 + hq0 + c) for c in range(HQG)] + \
                         [("v", wv, c, 16 + hv0 + c) for c in range(HVG)]

                def load_h(t):
                    kb.dma(hT[t % 2].t[:], hT_d[:, :, t * 512:(t + 1) * 512].rearrange("c p t -> p c t"), writes=[hT[t % 2]])

                bi_ = [0]

                def stageA(t):
                    h = hT[t % 2]
                    qT, kT, vT, zg = qT2[t % 2], kT2[t % 2], vT2[t % 2], zg2[t % 2]
                    for ci, (kind, w, c, gch) in enumerate(chunks):
                        bk = bi_[0] % 8; bi_[0] += 1
                        pt, pbufs = banks[bk].t, bank_bufs(bk)
                        for k in range(8):
                            kb.op(PE, lambda: nc.tensor.matmul(pt[:], lhsT=w.t[:, k, c * 128:(c + 1) * 128], rhs=h.t[:, k, :], start=(k == 0), stop=(k == 7)),
                                  reads=[w, h], writes=pbufs, inc=(k == 7))
                        rw, cvb = raw[ci % 3], cv[ci % 3]
                        kb.op(ACT, lambda: nc.scalar.copy(out=rw.t[:, 0:3], in_=halo.t[:, ci, :]), reads=[halo], writes=[rw])
                        kb.op(ACT, lambda: nc.scalar.copy(out=rw.t[:, 3:515], in_=pt[:]), reads=pbufs, writes=[rw])
                        kb.op(ACT, lambda: nc.scalar.copy(out=halo.t[:, ci, :], in_=rw.t[:, 512:515]), reads=[rw], writes=[halo])
                        kb.op(DVE, lambda: nc.vector.tensor_scalar(out=cvb.t[:], in0=rw.t[:, 3:515], scalar1=params.t[:, cwo + gch * 4 + 3:cwo + gch * 4 + 4],
                                                                   scalar2=None, op0=ALU.mult), reads=[rw, params], writes=[cvb])
                        for kk in range(3):
                            kb.op(DVE, lambda: nc.vector.scalar_tensor_tensor(out=cvb.t[:], in0=rw.t[:, kk:kk + 512], scalar=params.t[:, cwo + gch * 4 + kk:cwo + gch * 4 + kk + 1],
                                                                              in1=cvb.t[:], op0=ALU.mult, op1=ALU.add), reads=[rw, params, cvb], writes=[cvb])
                        if kind == "v":
                            kb.op(ACT, lambda: nc.scalar.activation(out=vT.t[:, c, :], in_=cvb.t[:], func=AF.Silu), reads=[cvb], writes=[vT])
                        else:
                            f = qkf[ci]
                            kb.op(ACT, lambda: nc.scalar.activation(out=f.t[:], in_=cvb.t[:], func=AF.Silu), reads=[cvb], writes=[f])
                            kb.op(POOL, lambda: nc.gpsimd.tensor_tensor(out=sqb[ci].t[:], in0=f.t[:], in1=f.t[:], op=ALU.mult), reads=[f], writes=[sqb[ci]])
                    for blk in range(4):
                        bk = bi_[0] % 8; bi_[0] += 1
                        pt, pbufs = banks[bk].t, bank_bufs(bk)
                        for k in range(8):
                            kb.op(PE, lambda: nc.tensor.matmul(pt[:, 0:HVG * 128], lhsT=h.t[:, k, blk * 128:(blk + 1) * 128], rhs=wz.t[:, k, :], start=(k == 0), stop=(k == 7)),
                                  reads=[h, wz], writes=pbufs, inc=(k == 7))
                        z_ = zs[blk % 2]
                        kb.op(ACT, lambda: nc.scalar.activation(out=z_.t[:, 0:HVG * 128], in_=pt[:, 0:HVG * 128], func=AF.Silu), reads=pbufs, writes=[z_])
                        kb.op(POOL, lambda: nc.gpsimd.tensor_tensor(out=zg.t[:, blk, :], in0=z_.t[:, 0:HVG * 128], in1=nwr.t[:], op=ALU.mult), reads=[z_, nwr], writes=[zg])

                    for ci, (kind, w, c, gch) in enumerate(chunks):
                        if kind == "v":
                            continue
                        f, sq, r_ = qkf[ci], sqb[ci], rn[ci % 2]
                        dst = qT if kind == "q" else kT
                        bk2 = bi_[0] % 8; bi_[0] += 1
                        pt2, pbufs2 = banks[bk2].t, bank_bufs(bk2)
                        kb.op(PE, lambda: nc.tensor.matmul(pt2[:], lhsT=ones_b, rhs=sq.t[:], start=True, stop=True), reads=[cmb, sq], writes=pbufs2)
                        kb.op(ACT, lambda: nc.scalar.activation(out=r_.t[:], in_=pt2[:], func=AF.Ln, bias=eps6), reads=pbufs2 + [kb.eps_ln], writes=[r_])
                        kb.op(ACT, lambda: nc.scalar.activation(out=r_.t[:], in_=r_.t[:], func=AF.Exp, scale=-0.5), reads=[r_], writes=[r_])
                        kb.op(DVE, lambda: nc.vector.scalar_tensor_tensor(out=dst.t[:, c, :], in0=f.t[:], scalar=(128.0 ** -0.5 if kind == "q" else 1.0), in1=r_.t[:],
                                                                          op0=ALU.mult, op1=ALU.mult), reads=[f, r_], writes=[dst])

                def hq_ops(n):
                    t, blk = n // 4, n % 4
                    cs = slice(blk * 128, (blk + 1) * 128)
                    qT, kT = qT2[t % 2], kT2[t % 2]
                    for hq in range(HQG):
                        kc, qc = kT.t[:, hq, cs], qT.t[:, hq, cs]
                        s0, s1, s2 = hsA[2 * hq][0], hsA[2 * hq][1], hsA[2 * hq][2]
                        kb.op(PE, lambda: nc.tensor.transpose(s0.apb, kc, ident_b), reads=[kT, cmb], writes=[s0.b])
                        kb.op(PE, lambda: nc.tensor.matmul(s1.ap, lhsT=kc, rhs=kc, start=True, stop=True), reads=[kT], writes=[s1.b])
                        kb.op(PE, lambda: nc.tensor.matmul(s2.ap, lhsT=kc, rhs=qc, start=True, stop=True), reads=[kT, qT], writes=[s2.b])
                        kb.op(ACT, lambda: nc.scalar.copy(out=ktm[hq].t[:], in_=s0.apb), reads=[s0.b], writes=[ktm[hq]])
                        kb.op(DVE, lambda: nc.vector.tensor_tensor(out=KKbd[hq].t[:], in0=s1.ap, in1=BD_f, op=ALU.mult), reads=[s1.b, cm], writes=[KKbd[hq]])
                        kb.op(DVE, lambda: nc.vector.tensor_tensor(out=KKoff[hq].t[:], in0=s1.ap, in1=OFF_f, op=ALU.mult), reads=[s1.b, cm], writes=[KKoff[hq]])
                        kb.op(ACT, lambda: nc.scalar.copy(out=QKT[hq].t[:], in_=s2.ap), reads=[s2.b], writes=[QKT[hq]])

                def pre_seq(hi, n):
                    t, blk = n // 4, n % 4
                    cs = slice(blk * 128, (blk + 1) * 128)
                    vT = vT2[t % 2]
                    pr = n % 2
                    hh_ = hv0 + hi
                    hq = hi // 2
                    B, sl = pb[hi], hsA[hi]
                    sc = lambda nm: tm[nm].t[:, n, hh_:hh_ + 1]
                    QKd, M128, bv, kdec, nwT = B.QKd[pr], B.M128[pr], B.bv[pr], B.kdec[pr], B.nwT[pr]
                    kb.op(ACT, lambda: nc.scalar.activation(out=B.rh.t[:], in_=U_f, func=AF.Identity, scale=sc("negg")), reads=[cm, tm["negg"]], writes=[B.rh])
                    kb.op(ACT, lambda: nc.scalar.activation(out=B.bek.t[:], in_=ktm[hq].t[:], func=AF.Identity, scale=sc("beG")), reads=[ktm[hq], tm["beG"]], writes=[B.bek])
                    kb.op(ACT, lambda: nc.scalar.activation(out=kdec.t[:], in_=ktm[hq].t[:], func=AF.Identity, scale=sc("kds")), reads=[ktm[hq], tm["kds"]], writes=[kdec])
                    yield
                    kb.op(PE, lambda: nc.tensor.matmul(sl[0].ap, lhsT=L_f, rhs=B.rh.t[:], start=True, stop=False), reads=[cm, B.rh], writes=[sl[0].b], inc=False)
                    kb.op(PE, lambda: nc.tensor.matmul(sl[0].ap, lhsT=ident_b, rhs=mS_b, start=False, stop=True), reads=[cmb], writes=[sl[0].b], inc=False)
                    kb.op(PE, lambda: nc.tensor.matmul(sl[1].ap, lhsT=B.rh.t[:], rhs=L_f, start=True, stop=False), reads=[cm, B.rh], writes=[sl[1].b], inc=False)
                    kb.op(PE, lambda: nc.tensor.matmul(sl[1].ap, lhsT=ident_b, rhs=mIT_b, start=False, stop=True), reads=[cmb], writes=[sl[1].b], inc=False)
                    kb.op(PE, lambda: nc.tensor.transpose(sl[3].apb, vT.t[:, hi, cs], ident_b), reads=[vT, cmb], writes=[sl[3].b])
                    kb.op(ACT, lambda: nc.scalar.activation(out=B.Ds.t[:], in_=sl[0].ap, func=AF.Exp, scale=-1.0), reads=[sl[0].b], writes=[B.Ds])
                    kb.op(ACT, lambda: nc.scalar.activation(out=B.DT.t[:], in_=sl[1].ap, func=AF.Exp, scale=-1.0), reads=[sl[1].b], writes=[B.DT])
                    kb.op(DVE, lambda: nc.vector.tensor_scalar(out=bv.t[:], in0=sl[3].apb, scalar1=sc("beta"), scalar2=None, op0=ALU.mult), reads=[sl[3].b, tm["beta"]], writes=[bv])
                    yield
                    kb.op(DVE, lambda: nc.vector.tensor_scalar(out=B.T1.t[:], in0=B.Ds.t[:], scalar1=sc("beta"), scalar2=None, op0=ALU.mult), reads=[B.Ds, tm["beta"]], writes=[B.T1])
                    kb.op(DVE, lambda: nc.vector.tensor_tensor(out=B.Abd.t[:], in0=B.T1.t[:], in1=KKbd[hq].t[:], op=ALU.mult), reads=[B.T1, KKbd[hq]], writes=[B.Abd])
                    kb.op(DVE, lambda: nc.vector.tensor_tensor(out=B.Aoff.t[:], in0=B.T1.t[:], in1=KKoff[hq].t[:], op=ALU.mult), reads=[B.T1, KKoff[hq]], writes=[B.Aoff])
                    kb.op(DVE, lambda: nc.vector.tensor_tensor(out=QKd.t[:], in0=QKT[hq].t[:], in1=B.DT.t[:], op=ALU.mult), reads=[QKT[hq], B.DT], writes=[QKd])
                    yield
                    bankA = banks[2 * hi]
                    kb.op(PE, lambda: nc.tensor.transpose(sl[2].apb, B.Abd.t[:], ident_b), reads=[B.Abd, cmb], writes=[sl[2].b])
                    kb.op(ACT, lambda: nc.scalar.copy(out=B.QM[0].t[:, 0:128], in_=sl[2].apb), reads=[sl[2].b], writes=[B.QM[0]])
                    kb.op(DVE, lambda: nc.vector.tensor_tensor(out=B.QM[0].t[:, 128:256], in0=ident_f, in1=sl[2].apb, op=ALU.subtract), reads=[cm, sl[2].b], writes=[B.QM[0]])
                    kb.op(DVE, lambda: nc.vector.tensor_tensor(out=B.QM[1].t[:, 128:256], in0=ident_f, in1=sl[2].apb, op=ALU.subtract), reads=[cm, sl[2].b], writes=[B.QM[1]])
                    yield
                    P = B.Abd
                    for lv in range(1, 6):
                        cur, nxt, Pn = B.QM[(lv - 1) % 2], B.QM[lv % 2], B.P[lv % 2]
                        lo, hi_ = (0, 128) if lv == 1 else ((0, 256) if lv < 5 else (128, 256))
                        kb.op(PE, lambda: nc.tensor.matmul(sl[2].ap, lhsT=cur.t[:, 0:128], rhs=P.t[:], start=True, stop=True), reads=[P, cur], writes=[sl[2].b], inc=False)
                        kb.op(PE, lambda: nc.tensor.matmul(bankA.t[:, lo:hi_], lhsT=P.t[:], rhs=cur.t[:, lo:hi_], start=True, stop=True), reads=[P, cur], writes=[bankA])
                        kb.op(ACT, lambda: nc.scalar.copy(out=Pn.t[:], in_=sl[2].ap), reads=[bankA], writes=[Pn])
                        if lv < 5:
                            kb.op(ACT, lambda: nc.scalar.copy(out=nxt.t[:, 0:128], in_=bankA.t[:, 0:128]), reads=[bankA], writes=[nxt])
                        if lv >= 2:
                            kb.op(DVE, lambda: nc.vector.tensor_tensor(out=nxt.t[:, 128:256], in0=cur.t[:, 128:256], in1=bankA.t[:, 128:256], op=ALU.add), reads=[cur, bankA], writes=[nxt])
                        yield
                        P = Pn
                    M4 = B.QM[1]
                    kb.op(PE, lambda: nc.tensor.matmul(sl[0].ap, lhsT=P.t[:], rhs=M4.t[:, 128:256], start=True, stop=True), reads=[P, M4], writes=[bankA])
                    kb.op(DVE, lambda: nc.vector.tensor_tensor(out=B.M0.t[:], in0=M4.t[:, 128:256], in1=sl[0].ap, op=ALU.add), reads=[M4, bankA], writes=[B.M0])
                    yield
                    M = B.M0
                    kb.op(PE, lambda: nc.tensor.matmul(sl[2].ap, lhsT=B.Aoff.t[:], rhs=M.t[:], start=True, stop=True), reads=[B.Aoff, M], writes=[bankA], inc=False)
                    kb.op(PE, lambda: nc.tensor.transpose(sl[3].apb, M.t[:], ident_b), reads=[M, cmb], writes=[bankA])
                    kb.op(ACT, lambda: nc.scalar.copy(out=B.Y1.t[:], in_=sl[2].ap), reads=[bankA], writes=[B.Y1])
                    kb.op(DVE, lambda: nc.vector.tensor_copy(out=B.Tb.t[:], in_=sl[3].apb), reads=[bankA], writes=[B.Tb])
                    yield
                    kb.op(PE, lambda: nc.tensor.matmul(sl[0].ap, lhsT=B.Tb.t[:], rhs=B.Y1.t[:], start=True, stop=True), reads=[B.Tb, B.Y1], writes=[bankA])
                    kb.op(DVE, lambda: nc.vector.tensor_tensor(out=M128.t[:], in0=M.t[:], in1=sl[0].ap, op=ALU.subtract), reads=[M, bankA], writes=[M128])
                    yield
                    kb.op(PE, lambda: nc.tensor.matmul(sl[1].ap, lhsT=B.bek.t[:], rhs=M128.t[:], start=True, stop=True), reads=[B.bek, M128], writes=[sl[1].b])
                    kb.op(ACT, lambda: nc.scalar.activation(out=nwT.t[:], in_=sl[1].ap, func=AF.Identity, scale=-1.0), reads=[sl[1].b], writes=[nwT])
                    yield

                def post_seq(hi, n):
                    t, blk = n // 4, n % 4
                    cs = slice(blk * 128, (blk + 1) * 128)
                    qT = qT2[t % 2]
                    pr = n % 2
                    hh_ = hv0 + hi
                    hq = hi // 2
                    B, sl = pb[hi], hsB[hi]
                    sc = lambda nm: tm[nm].t[:, n, hh_:hh_ + 1]
                    QKd, M128, bv, kdec, nwT = B.QKd[pr], B.M128[pr], B.bv[pr], B.kdec[pr], B.nwT[pr]
                    kb.op(PE, lambda: nc.tensor.matmul(sl[0].ap, lhsT=M128.t[:], rhs=bv.t[:], start=True, stop=False), reads=[M128, bv], writes=[sl[0].b], inc=False)
                    kb.op(PE, lambda: nc.tensor.matmul(sl[0].ap, lhsT=nwT.t[:], rhs=Sb[hi].t[:], start=False, stop=True), reads=[nwT, Sb[hi]], writes=[sl[0].b], inc=False)
                    kb.op(PE, lambda: nc.tensor.matmul(sl[1].ap, lhsT=qT.t[:, hq, cs], rhs=Sb[hi].t[:], start=True, stop=True), reads=[qT, Sb[hi]], writes=[sl[1].b])
                    kb.op(ACT, lambda: nc.scalar.copy(out=B.vn.t[:], in_=sl[0].ap), reads=[sl[0].b], writes=[B.vn])
                    kb.op(ACT, lambda: nc.scalar.activation(out=B.o1.t[:], in_=sl[1].ap, func=AF.Identity, scale=sc("eG")), reads=[sl[1].b, tm["eG"]], writes=[B.o1])
                    yield
                    kb.op(PE, lambda: nc.tensor.matmul(sl[2].ap, lhsT=kdec.t[:], rhs=B.vn.t[:], start=True, stop=True), reads=[kdec, B.vn], writes=[sl[2].b], inc=False)
                    kb.op(PE, lambda: nc.tensor.matmul(sl[3].ap, lhsT=QKd.t[:], rhs=B.vn.t[:], start=True, stop=True), reads=[QKd, B.vn], writes=[sl[3].b])
                    kb.op(DVE, lambda: nc.vector.scalar_tensor_tensor(out=Sf[hi].t[:], in0=Sf[hi].t[:], scalar=sc("gl"), in1=sl[2].ap, op0=ALU.mult, op1=ALU.add),
                          reads=[Sf[hi], tm["gl"], sl[2].b], writes=[Sf[hi]])
                    kb.op(DVE, lambda: nc.vector.tensor_tensor(out=ob2[pr].t[:, hi, :], in0=B.o1.t[:], in1=sl[3].ap, op=ALU.add), reads=[B.o1, sl[3].b], writes=[ob2[pr]])
                    yield
                    kb.op(ACT, lambda: nc.scalar.copy(out=Sb[hi].t[:], in_=Sf[hi].t[:]), reads=[Sf[hi]], writes=[Sb[hi]])
                    yield

                def norm_seq(n):
                    t, blk = n // 4, n % 4
                    cs = slice(blk * 128, (blk + 1) * 128)
                    zg, dn_o = zg2[t % 2], dno[t % 2]
                    ob, ss, rstd = ob2[n % 2], ss2[n % 2], rstd2[n % 2]
                    for hi in range(HVG):
                        kb.op(ACT, lambda: nc.scalar.activation(out=junk.t[:], in_=ob.t[:, hi, :], func=AF.Square, accum_out=ss.t[:, hi:hi + 1]), reads=[ob], writes=[junk, ss])
                    yield
                    kb.op(ACT, lambda: nc.scalar.activation(out=rstd.t[:], in_=ss.t[:], func=AF.Ln, scale=1.0 / 128.0, bias=eps6), reads=[ss, kb.eps_ln], writes=[rstd])
                    kb.op(ACT, lambda: nc.scalar.activation(out=rstd.t[:], in_=rstd.t[:], func=AF.Exp, scale=-0.5), reads=[rstd], writes=[rstd])
                    yield
                    for hi in range(HVG):
                        B = pb[hi]
                        kb.op(DVE, lambda: nc.vector.scalar_tensor_tensor(out=B.dntm.t[:], in0=ob.t[:, hi, :], scalar=rstd.t[:, hi:hi + 1], in1=zg.t[:, blk, hi * 128:(hi + 1) * 128],
                                                                          op0=ALU.mult, op1=ALU.mult), reads=[ob, rstd, zg], writes=[B.dntm])
                    yield
                    for hi in range(HVG):
                        B, sl = pb[hi], hsB[hi]
                        kb.op(PE, lambda: nc.tensor.transpose(sl[0].apb, B.dntm.t[:], ident_b), reads=[B.dntm, cmb], writes=[sl[0].b])
                        kb.op(ACT, lambda: nc.scalar.copy(out=dn_o.t[:, hi, cs], in_=sl[0].apb), reads=[sl[0].b], writes=[dn_o])
                    if blk == 3:
                        kb.dma(dnT_d[hv0:hv0 + HVG, :, t * 512:(t + 1) * 512].rearrange("c p t -> p c t"), dn_o.t[:], reads=[dn_o])
                    yield

                def run(gens, delays=None):
                    gens = list(gens)
                    delays = list(delays) if delays is not None else [0] * len(gens)
                    rnd = 0
                    while gens:
                        for i in range(len(gens) - 1, -1, -1):
                            pass
                        keep_g, keep_d = [], []
                        for g, d in zip(gens, delays):
                            if rnd < d:
                                keep_g.append(g); keep_d.append(d)
                                continue
                            try:
                                next(g)
                                keep_g.append(g); keep_d.append(d)
                            except StopIteration:
                                pass
                        gens, delays = keep_g, keep_d
                        rnd += 1

                STAG = getattr(build_program, "stagger", 0)
                load_h(0)
                if NT > 1:
                    load_h(1)
                stageA(0)
                hq_ops(0)
                run([pre_seq(hi, 0) for hi in range(HVG)])
                pend_norm = []
                for n in range(NCH):
                    gens = []
                    if n + 1 < NCH:
                        if (n + 1) % 4 == 0:
                            t1 = (n + 1) // 4
                            stageA(t1)
                            if t1 + 1 < NT:
                                load_h(t1 + 1)
                        hq_ops(n + 1)
                        gens = [pre_seq(hi, n + 1) for hi in range(HVG)]
                    posts = [post_seq(hi, n) for hi in range(HVG)]
                    run(posts + pend_norm + gens, [0] * (len(posts) + len(pend_norm)) + [STAG * (i // 2) for i in range(len(gens))])
                    pend_norm = [norm_seq(n)]
                run(pend_norm)
                kb.barrier()


def _consts():
    i = np.arange(128)
    ident = np.eye(128, dtype=np.float32)
    L = (i[:, None] <= i[None, :]).astype(np.float32)
    U = (i[:, None] > i[None, :]).astype(np.float32)
    mS = np.where(i[:, None] > i[None, :], 0.0, BIG).astype(np.float32)
    mIT = np.where(i[None, :] >= i[:, None], 0.0, BIG).astype(np.float32)
    blk = i // 64
    BD = (blk[:, None] == blk[None, :]).astype(np.float32)
    OFF = ((blk[:, None] == 1) & (blk[None, :] == 0)).astype(np.float32)
    ones = np.ones((128, 128), np.float32)
    return np.concatenate([ident, L, U, mS, mIT, BD, OFF, ones], axis=1)


def _fm(v, n):
    return np.ascontiguousarray(np.asarray(v, np.float32).reshape(n, 128).T)


def _band(w):
    full = np.zeros((DRNN, DRNN), np.float32)
    for b in range(16):
        full[b * 80:(b + 1) * 80, b * 80:(b + 1) * 80] = w[b]
    out = np.zeros((128, 10, 3, 128), np.float32)
    for m in range(10):
        for d in range(3):
            k = m + d - 1
            if 0 <= k < 10:
                out[:, m, d, :] = full[k * 128:(k + 1) * 128, m * 128:(m + 1) * 128]
    return out


def make_in_maps(inp, S, batches):
    l = 0
    f = lambda a: np.ascontiguousarray(np.asarray(a, np.float32))
    NCH = S // 128
    rows = np.zeros((128, NR), np.float32)
    b_ada = f(inp["b_ada"][l])
    def setr(name, v):
        o, w = _R[name]
        rows[:, o:o + w] = np.asarray(v, np.float32)[None, :]
    setr("b_gt1", b_ada[2048:3072]); setr("b_gt2", b_ada[5120:6144])
    setr("ln1_g", inp["ln1_g"][l]); setr("ln1_b", inp["ln1_b"][l]); setr("ln2_g", inp["ln2_g"][l]); setr("ln2_b", inp["ln2_b"][l])
    setr("nw", np.tile(f(inp["dn_norm_w"][l]), 16))
    hrep = np.zeros((128, 2, NCH * 16), np.float32)
    hrep[:, 0, :] = np.tile(f(inp["dn_dt_bias"][l]), NCH)[None, :]
    hrep[:, 1, :] = np.tile(f(inp["dn_a_log"][l]), NCH)[None, :]
    cmat = _consts()
    shared = dict(
        w_ada=f(inp["w_ada"][l]), w_in=f(inp["w_in"][l]), gate_a=_band(f(inp["rg_w_a"][l])), gate_x=_band(f(inp["rg_w_x"][l])),
        w_proj_a=f(inp["w_proj_a"][l]), w_proj_b=f(inp["w_proj_b"][l]), w_out=f(inp["w_out"][l]),
        ffn_w_gate=f(inp["ffn_w_gate"][l]), ffn_w_up=f(inp["ffn_w_up"][l]), ffn_w_down=f(inp["ffn_w_down"][l]),
        rows=rows, cmat=cmat, hrep=hrep)
    maps = []
    for b in batches:
        params = np.zeros((128, NP), np.float32)
        def setp(name, v):
            o, w = _P[name]
            params[:, o:o + w] = v
        setp("b_ada", _fm(b_ada, 48))
        setp("c", _fm(inp["c"][b], 8))
        setp("rg_cw", np.stack([_fm(inp["rg_conv_w"][l][k], 10) for k in range(4)], axis=2).reshape(128, 40))
        setp("rg_cb", _fm(inp["rg_conv_b"][l], 10)); setp("rg_ba", _fm(inp["rg_b_a"][l], 10)); setp("rg_bx", _fm(inp["rg_b_x"][l], 10))
        setp("rg_lam", _fm(inp["rg_lambda"][l], 10))
        setp("dn_cw", np.stack([_fm(inp["dn_conv_w"][l][k], 32) for k in range(4)], axis=2).reshape(128, 128))
        setp("ffn_cw", np.stack([_fm(inp["ffn_conv_w"][l][k], 22) for k in range(3)], axis=2).reshape(128, 66))
        setp("ffn_cb", _fm(inp["ffn_conv_b"][l], 22))
        xb = f(inp["x"][b])
        m = dict(shared)
        m["x"] = xb
        m["xT"] = np.ascontiguousarray(xb.T.reshape(8, 128, S))
        m["params"] = params
        maps.append(m)
    return maps


_NC_CACHE = {}


def kernel(**inputs):
    S = inputs["x"].shape[1]
    B = inputs["x"].shape[0]
    if S not in _NC_CACHE:
        _NC_CACHE[S] = build_program(S)
    nc = _NC_CACHE[S]
    maps = make_in_maps(inputs, S, list(range(B)))
    res = run_bass_kernel_spmd(nc, maps, core_ids=list(range(B)))
    return np.stack([np.asarray(r["out"], np.float32) for r in res.results], axis=0)
```

```python
import contextlib
import numpy as np
import concourse.bass as bass
import concourse.mybir as mybir
from concourse.bass_utils import run_bass_kernel_spmd

F32 = mybir.dt.float32
BF16 = mybir.dt.bfloat16
AF = mybir.ActivationFunctionType
ALU = mybir.AluOpType

D = 1024
DRNN = 1280
DFF = 2816
DIN = 10784
NHV = 16
C_XR, C_GR, C_Q, C_K, C_V, C_Z, C_A, C_B, C_GA, C_GB = 0, 1280, 2560, 3584, 4608, 6656, 8704, 8720, 8736, 9760
ALPHA = 2.0 ** 0.25
BIG = 32768.0
GELU_C = 1.5957691216057308

_P = {}
_off = 0
for _n, _w in [("b_ada", 48), ("c", 8), ("rg_cw", 40), ("rg_cb", 10), ("rg_ba", 10), ("rg_bx", 10), ("rg_lam", 10),
               ("dn_cw", 128), ("ffn_cw", 66), ("ffn_cb", 22)]:
    _P[_n] = (_off, _w)
    _off += _w
NP = _off
_R = {}
_off = 0
for _n, _w in [("b_gt1", 1024), ("b_gt2", 1024), ("ln1_g", 1024), ("ln1_b", 1024), ("ln2_g", 1024), ("ln2_b", 1024),
               ("nw", 2048)]:
    _R[_n] = (_off, _w)
    _off += _w
NR = _off
CM_IDENT, CM_L, CM_U, CM_MS, CM_MIT, CM_BD, CM_OFF, CM_ONES = range(8)
NCM = 8


class Buf:
    def __init__(self, kb, name, t):
        self.kb = kb
        self.name = name
        self.t = t
        self.w = None
        self.r = []
        self.lsem = None
        self.ssem = None
        self.excl = False

    def add_reader(self, tok):
        for i, (s, v) in enumerate(self.r):
            if s == tok[0]:
                if v < tok[1]:
                    self.r[i] = tok
                return
        self.r.append(tok)


class EngW:
    LIMIT = 24000

    def __init__(self, kb, eng, name):
        self.kb = kb
        self.eng = eng
        self.name = name
        self.sem = None
        self.cnt = 0
        self.seen = {}
        self.pending = []

    def wait(self, toks):
        best = {}
        for s, v in toks:
            if best.get(s, 0) < v:
                best[s] = v
        for s, v in best.items():
            if self.seen.get(s, 0) >= v:
                continue
            self.eng.wait_ge(self.kb.sems[s], v)
            self.seen[s] = v

    def bump(self, inst):
        if self.sem is None or self.cnt >= self.LIMIT:
            self.sem = self.kb.new_sem(f"e_{self.name}_{len(self.kb.sems)}")
            self.cnt = 0
        inst.then_inc(self.kb.sems[self.sem], 1)
        self.cnt += 1
        return (self.sem, self.cnt)

    def cur(self):
        return None if self.sem is None else (self.sem, self.cnt)


class KB:
    def __init__(self, nc, es):
        self.nc = nc
        self.es = es
        self.sems = []
        self.semcnt = []
        self.free_dma_sems = []
        self.pe = EngW(self, nc.tensor, "pe")
        self.act = EngW(self, nc.scalar, "act")
        self.dve = EngW(self, nc.vector, "dve")
        self.pool = EngW(self, nc.gpsimd, "pool")
        self.sp = EngW(self, nc.sync, "sp")
        self.engs = [self.pe, self.act, self.dve, self.pool, self.sp]
        self.dma_toks = {}
        self.phase_bufs = []
        self.n_inst = 0

    def new_sem(self, name):
        h = self.es.enter_context(self.nc.semaphore(name))
        self.sems.append(h)
        self.semcnt.append(0)
        return len(self.sems) - 1

    def dma_sem(self):
        if self.free_dma_sems:
            return self.free_dma_sems.pop()
        return self.new_sem(f"d_{len(self.sems)}")

    def sb(self, pes, name, shape, dt):
        self.n_alloc = getattr(self, "n_alloc", 0) + 1
        t = pes.enter_context(self.nc.sbuf_tensor(f"sb{self.n_alloc}_{name}", list(shape), dt))
        b = Buf(self, name, t)
        self.phase_bufs.append(b)
        return b

    def sbs(self, pes, name, shape, dt):
        b = self.sb(pes, name, shape, dt)
        b.sub = [Buf(self, f"{name}[{i}]", b.t) for i in range(shape[1])]
        self.phase_bufs.extend(b.sub)
        return b

    def ps(self, pes, name, shape, dt):
        self.n_alloc = getattr(self, "n_alloc", 0) + 1
        t = pes.enter_context(self.nc.psum_tensor(f"ps{self.n_alloc}_{name}", list(shape), dt))
        b = Buf(self, name, t)
        b.excl = True
        self.phase_bufs.append(b)
        return b

    def op(self, E, fn, reads=(), writes=(), inc=True):
        ex = [b for b in reads if b.excl]
        if ex:
            reads = [b for b in reads if not b.excl]
            writes = list(writes) + [b for b in ex if b not in writes]
        toks = []
        for b in reads:
            if b.w is not None:
                toks.append(b.w)
        for b in writes:
            if b.w is not None:
                toks.append(b.w)
            toks.extend(b.r)
        E.wait(toks)
        inst = fn()
        self.n_inst += 1
        if inc:
            tok = E.bump(inst)
            sets = E.pending + [(reads, writes)]
            E.pending = []
            for rd, wr in sets:
                for b in rd:
                    b.add_reader(tok)
                for b in wr:
                    b.w = tok
                    b.r = []
        else:
            E.pending.append((reads, writes))
        return inst

    def dma(self, out_ap, in_ap, reads=(), writes=(), Q=None):
        Q = Q or self.sp
        toks = []
        for b in reads:
            if b.w is not None:
                toks.append(b.w)
        for b in writes:
            if b.w is not None:
                toks.append(b.w)
            toks.extend(b.r)
        Q.wait(toks)
        inst = Q.eng.dma_start(out=out_ap, in_=in_ap)
        self.n_inst += 1
        if writes:
            tb = writes[0]
            if tb.lsem is None:
                tb.lsem = self.dma_sem()
            s = tb.lsem
        else:
            tb = reads[0]
            if tb.ssem is None:
                tb.ssem = self.dma_sem()
            s = tb.ssem
        self.semcnt[s] += 16
        inst.then_inc(self.sems[s], 16)
        tok = (s, self.semcnt[s])
        self.dma_toks[s] = tok
        for b in writes:
            b.w = tok
            b.r = []
        for b in reads:
            b.add_reader(tok)
        return tok

    def barrier(self, keep=()):
        toks = list(self.dma_toks.values())
        for E in self.engs:
            assert not E.pending
            c = E.cur()
            if c is not None:
                toks.append(c)
        for E in self.engs:
            E.wait(toks)
        for b in self.phase_bufs:
            if b.lsem is not None:
                self.free_dma_sems.append(b.lsem)
                b.lsem = None
            if b.ssem is not None:
                self.free_dma_sems.append(b.ssem)
                b.ssem = None
        self.dma_toks = {}

    def final_wait(self):
        toks = list(self.dma_toks.values())
        self.sp.wait(toks)


def _cm(cm, idx, dt=None):
    return cm.t[:, idx * 128:(idx + 1) * 128]


def build_program(S, debug=False):
    assert S % 512 == 0
    NT = S // 512
    NCH = S // 128
    nc = bass.Bass("TRN2", target_bir_lowering=False)
    okind = "ExternalOutput" if debug else "Internal"

    def din(name, shape, dt=F32):
        return nc.dram_tensor(name, list(shape), dt, kind="ExternalInput").ap()

    xT_d = din("xT", [8, 128, S])
    x_d = din("x", [S, D])
    w_ada_d = din("w_ada", [D, 6 * D])
    w_in_d = din("w_in", [D, DIN])
    gate_a_d = din("gate_a", [128, 10, 3, 128])
    gate_x_d = din("gate_x", [128, 10, 3, 128])
    w_pa_d = din("w_proj_a", [DRNN, D])
    w_pb_d = din("w_proj_b", [2048, D])
    w_out_d = din("w_out", [D, D])
    w_fg_d = din("ffn_w_gate", [D, DFF])
    w_fu_d = din("ffn_w_up", [D, DFF])
    w_fd_d = din("ffn_w_down", [DFF, D])
    params_d = din("params", [128, NP])
    rows_d = din("rows", [128, NR])
    cmat_d = din("cmat", [128, NCM * 128])
    hrep_d = din("hrep", [128, 2, NCH * 16])
    out_d = nc.dram_tensor("out", [S, D], F32, kind="ExternalOutput").ap()
    hT_d = nc.dram_tensor("hT_s", [8, 128, S], BF16, kind=okind).ap()
    recT_d = nc.dram_tensor("recT_s", [10, 128, S], BF16, kind=okind).ap()
    dnT_d = nc.dram_tensor("dnT_s", [16, 128, S], BF16, kind=okind).ap()
    x1_d = nc.dram_tensor("x1_s", [S, D], F32, kind=okind).ap()

    with contextlib.ExitStack() as es:
        kb = KB(nc, es)
        PE, ACT, DVE, POOL = kb.pe, kb.act, kb.dve, kb.pool
        params = kb.sb(es, "params", [128, NP], F32)
        cm = kb.sb(es, "cm", [128, NCM * 128], F32)
        cmb = kb.sb(es, "cmb", [128, NCM * 128], BF16)
        ada = kb.sb(es, "ada", [128, 48], F32)
        gt1r = kb.sb(es, "gt1r", [128, 1024], F32)
        gt2r = kb.sb(es, "gt2r", [128, 1024], F32)
        kb.dma(params.t[:], params_d[:, :], writes=[params])
        kb.dma(cm.t[:], cmat_d[:, :], writes=[cm])
        kb.op(ACT, lambda: nc.scalar.copy(out=cmb.t[:], in_=cm.t[:]), reads=[cm], writes=[cmb])

        def pcol(name, j=0, n=1):
            o, w = _P[name]
            return params.t[:, o + j:o + j + n]

        ident_f = cm.t[:, CM_IDENT * 128:(CM_IDENT + 1) * 128]
        ident_b = cmb.t[:, CM_IDENT * 128:(CM_IDENT + 1) * 128]
        ones_f = cm.t[:, CM_ONES * 128:(CM_ONES + 1) * 128]
        ones_b = cmb.t[:, CM_ONES * 128:(CM_ONES + 1) * 128]

        def load_w_bf16(pes, dst, dram, r0, nk, c0, ncols, stg, dst_c0=0):
            PW = stg[0].t.shape[-1]
            i = load_w_bf16.cnt
            for cc in range(0, ncols, PW):
                w = min(PW, ncols - cc)
                st = stg[i % len(stg)]
                src = dram[r0:r0 + nk * 128, c0 + cc:c0 + cc + w].rearrange("(k p) c -> p k c", p=128)
                kb.dma(st.t[:, 0:nk, 0:w], src, writes=[st])
                o = dst.t[:, 0:nk, dst_c0 + cc:dst_c0 + cc + w]
                if i % 2 == 0:
                    kb.op(ACT, lambda o=o, st=st, w=w: nc.scalar.copy(out=o, in_=st.t[:, 0:nk, 0:w]), reads=[st], writes=[dst])
                else:
                    kb.op(DVE, lambda o=o, st=st, w=w: nc.vector.tensor_copy(out=o, in_=st.t[:, 0:nk, 0:w]), reads=[st], writes=[dst])
                i += 1
            load_w_bf16.cnt = i
        load_w_bf16.cnt = 0

        with contextlib.ExitStack() as pes:
            scf = kb.sb(pes, "scf", [128, 8], F32)
            screp = kb.sb(pes, "screp", [128, 8, 128], F32)
            wst = [kb.sb(pes, f"wst{i}", [128, 8, 1024], F32) for i in range(2)]
            rows01 = kb.sb(pes, "rows01", [128, 2048], F32)
            pa = kb.ps(pes, "pa", [128, 512], F32)
            pb = [kb.ps(pes, f"pb{i}", [128, 512], F32) for i in range(2)]
            kb.dma(rows01.t[:], rows_d[:, 0:2048], writes=[rows01])
            kb.op(ACT, lambda: nc.scalar.activation(out=scf.t[:], in_=pcol("c", 0, 8), func=AF.Silu), reads=[params], writes=[scf])
            for k in range(8):
                kb.op(DVE, lambda k=k: nc.vector.tensor_copy(out=screp.t[:, k, :], in_=scf.t[:, k:k + 1].to_broadcast([128, 128])),
                      reads=[scf], writes=[screp])
            for jb in range(6):
                st = wst[jb % 2]
                kb.dma(st.t[:], w_ada_d[:, jb * 1024:(jb + 1) * 1024].rearrange("(k p) c -> p k c", p=128), writes=[st])
                if jb in (2, 5):
                    dst = gt1r if jb == 2 else gt2r
                    for cb in range(2):
                        for k in range(8):
                            kb.op(PE, lambda k=k, cb=cb, st=st: nc.tensor.matmul(pb[cb].t[:], lhsT=screp.t[:, k, :], rhs=st.t[:, k, cb * 512:(cb + 1) * 512],
                                                                               start=(k == 0), stop=(k == 7)),
                                  reads=[screp, st], writes=[pb[cb]], inc=(k == 7))
                        ro = (0 if jb == 2 else 1024) + cb * 512
                        kb.op(DVE, lambda cb=cb, ro=ro, dst=dst: nc.vector.scalar_tensor_tensor(
                            out=dst.t[:, cb * 512:(cb + 1) * 512], in0=pb[cb].t[:], scalar=1.0, in1=rows01.t[:, ro:ro + 512],
                            op0=ALU.add, op1=ALU.add), reads=[pb[cb], rows01], writes=[dst])
                for j in range(8):
                    col = jb * 8 + j
                    for k in range(8):
                        kb.op(PE, lambda k=k, j=j, col=col, st=st: nc.tensor.matmul(pa.t[:, col:col + 1], lhsT=st.t[:, k, j * 128:(j + 1) * 128],
                                                                                 rhs=scf.t[:, k:k + 1], start=(k == 0), stop=(k == 7)),
                              reads=[scf, st], writes=[pa], inc=(k == 7))
            kb.op(DVE, lambda: nc.vector.tensor_tensor(out=ada.t[:], in0=pa.t[:, 0:48], in1=pcol("b_ada", 0, 48), op=ALU.add),
                  reads=[pa, params], writes=[ada])
            for o in (8, 32):
                kb.op(DVE, lambda o=o: nc.vector.tensor_scalar(out=ada.t[:, o:o + 8], in0=ada.t[:, o:o + 8], scalar1=1.0, scalar2=None, op0=ALU.add),
                      reads=[ada], writes=[ada])
            xin = [kb.sb(pes, f"xin{i}", [128, 8, 512], F32) for i in range(2)]
            hto = [kb.sb(pes, f"hto{i}", [128, 8, 512], BF16) for i in range(2)]
            for t in range(NT):
                xi, ho = xin[t % 2], hto[t % 2]
                kb.dma(xi.t[:], xT_d[:, :, t * 512:(t + 1) * 512].rearrange("c p t -> p c t"), writes=[xi])
                for c in range(8):
                    if c % 2 == 0:
                        kb.op(ACT, lambda c=c, xi=xi, ho=ho: nc.scalar.activation(out=ho.t[:, c, :], in_=xi.t[:, c, :], func=AF.Identity,
                                                                              scale=ada.t[:, 8 + c:9 + c], bias=ada.t[:, c:c + 1]),
                              reads=[xi, ada], writes=[ho])
                    else:
                        kb.op(DVE, lambda c=c, xi=xi, ho=ho: nc.vector.tensor_scalar(out=ho.t[:, c, :], in0=xi.t[:, c, :], scalar1=ada.t[:, 8 + c:9 + c],
                                                                                 scalar2=ada.t[:, c:c + 1], op0=ALU.mult, op1=ALU.add),
                              reads=[xi, ada], writes=[ho])
                kb.dma(hT_d[:, :, t * 512:(t + 1) * 512].rearrange("c p t -> p c t"), ho.t[:], reads=[ho])
            kb.barrier()

        PHASES = build_program.phases
        eps_ln = kb.sb(es, "eps_ln", [128, 2], F32)
        kb.eps_ln = eps_ln
        kb.op(DVE, lambda: nc.vector.memset(eps_ln.t[:, 0:1], 1e-5), writes=[eps_ln])
        kb.op(DVE, lambda: nc.vector.memset(eps_ln.t[:, 1:2], 1e-6), writes=[eps_ln])
        if "rg" in PHASES:
            phase_rg(nc, kb, S, params, pcol, cm, cmb, load_w_bf16, w_in_d, gate_a_d, gate_x_d, hT_d, recT_d)
        if "gdn" in PHASES:
            phase_gdn(nc, kb, S, params, pcol, cm, cmb, load_w_bf16, w_in_d, rows_d, hrep_d, hT_d, dnT_d)
        if "mix" in PHASES:
            phase_mix(nc, kb, S, load_w_bf16, gt1r, w_in_d, w_pa_d, w_pb_d, w_out_d, rows_d, x_d, hT_d, recT_d, dnT_d, x1_d)
        if "ffn" in PHASES:
            phase_ffn(nc, kb, S, params, pcol, cm, load_w_bf16, ada, gt2r, w_fg_d, w_fu_d, w_fd_d, rows_d, x1_d, out_d)
        kb.final_wait()
    return nc


build_program.phases = ("rg", "gdn", "mix", "ffn")


def gelu_tanh(nc, kb, src_ap, src_bufs, out_ap, out_bufs, tmp, tmp2=None, mul_ap=None, mul_bufs=(), defer=None):
    ACT, DVE = kb.act, kb.dve
    if mul_ap is None:
        kb.op(ACT, lambda: nc.scalar.activation(out=out_ap, in_=src_ap, func=AF.Gelu_apprx_tanh), reads=src_bufs, writes=out_bufs)
    else:
        kb.op(ACT, lambda: nc.scalar.activation(out=tmp.t[:], in_=src_ap, func=AF.Gelu_apprx_tanh), reads=src_bufs, writes=[tmp])
        def fin():
            kb.op(DVE, lambda: nc.vector.tensor_tensor(out=out_ap, in0=tmp.t[:], in1=mul_ap, op=ALU.mult), reads=[tmp] + list(mul_bufs), writes=out_bufs)
        if defer is None:
            fin()
        else:
            defer.append(fin)


def phase_rg(nc, kb, S, params, pcol, cm, cmb, load_w_bf16, w_in_d, gate_a_d, gate_x_d, hT_d, recT_d):
    PE, ACT, DVE, POOL = kb.pe, kb.act, kb.dve, kb.pool
    TT = 256
    NT = S // TT
    with contextlib.ExitStack() as pes:
        wx = kb.sb(pes, "wx", [128, 8, 2560], BF16)
        wga = kb.sb(pes, "wga", [128, 10, 3, 128], BF16)
        wgx = kb.sb(pes, "wgx", [128, 10, 3, 128], BF16)
        cl = kb.sb(pes, "cl", [128, 20], F32)
        with contextlib.ExitStack() as ses:
            stg = [kb.sb(ses, f"stg{i}", [128, 8, 512], F32) for i in range(2)]
            gst = kb.sb(ses, "gst", [128, 10, 3, 128], F32)
            load_w_bf16(ses, wx, w_in_d, 0, 8, 0, 2560, stg)
            kb.dma(gst.t[:], gate_a_d[:, :, :, :], writes=[gst])
            kb.op(ACT, lambda: nc.scalar.copy(out=wga.t[:], in_=gst.t[:]), reads=[gst], writes=[wga])
            kb.dma(gst.t[:], gate_x_d[:, :, :, :], writes=[gst])
            kb.op(ACT, lambda: nc.scalar.copy(out=wgx.t[:], in_=gst.t[:]), reads=[gst], writes=[wgx])
            kb.barrier(keep=[wx, wga, wgx, cl])
        kb.op(ACT, lambda: nc.scalar.activation(out=cl.t[:, 0:10], in_=pcol("rg_lam", 0, 10), func=AF.Exp, scale=-1.0), reads=[params], writes=[cl])
        kb.op(ACT, lambda: nc.scalar.activation(out=cl.t[:, 0:10], in_=cl.t[:, 0:10], func=AF.Ln, bias=1.0), reads=[cl], writes=[cl])
        kb.op(DVE, lambda: nc.vector.tensor_scalar(out=cl.t[:, 10:20], in0=cl.t[:, 0:10], scalar1=-16.0, scalar2=None, op0=ALU.mult), reads=[cl], writes=[cl])
        kb.op(DVE, lambda: nc.vector.tensor_scalar(out=cl.t[:, 0:10], in0=cl.t[:, 0:10], scalar1=-8.0, scalar2=None, op0=ALU.mult), reads=[cl], writes=[cl])

        hT = [kb.sb(pes, f"hT{i}", [128, 8, TT], BF16) for i in range(2)]
        raw = [kb.sb(pes, f"raw{i}", [128, TT + 3], F32) for i in range(3)]
        halo = kb.sbs(pes, "halo", [128, 10, 3], F32)
        hst = kb.sbs(pes, "hst", [128, 10, 1], F32)
        xr = kb.sbs(pes, "xr", [128, 10, TT], F32)
        xrb = kb.sbs(pes, "xrb", [128, 10, TT], BF16)
        ga = kb.sbs(pes, "ga", [128, 10, TT], F32)
        gi = kb.sbs(pes, "gi", [128, 10, TT], F32)
        mu = kb.sbs(pes, "mu", [128, 10, TT], F32)
        hh = kb.sbs(pes, "hh", [128, 10, TT], F32)
        gg = kb.sbs(pes, "gg", [128, 10, TT], F32)
        tmp = [kb.sb(pes, f"gtmp{i}", [128, TT], F32) for i in range(3)]
        rec = [kb.sbs(pes, f"rec{i}", [128, 10, TT], BF16) for i in range(2)]
        pp = [kb.ps(pes, f"pp{i}", [128, 512], F32) for i in range(6)]
        kb.op(DVE, lambda: nc.vector.memset(halo.t[:], 0.0), writes=halo.sub)
        kb.op(DVE, lambda: nc.vector.memset(hst.t[:], 0.0), writes=hst.sub)
        cwo = _P["rg_cw"][0]

        def load_h(t):
            kb.dma(hT[t % 2].t[:], hT_d[:, :, t * TT:(t + 1) * TT].rearrange("c p t -> p c t"), writes=[hT[t % 2]])

        load_h(0)
        pi = 0
        for t in range(NT):
            if t + 1 < NT:
                load_h(t + 1)
            h = hT[t % 2]
            for j in range(10):
                p = pp[pi % 6]; pi += 1
                for k in range(8):
                    kb.op(PE, lambda: nc.tensor.matmul(p.t[:, 0:TT], lhsT=wx.t[:, k, j * 128:(j + 1) * 128], rhs=h.t[:, k, :],
                                                       start=(k == 0), stop=(k == 7)), reads=[wx, h], writes=[p], inc=(k == 7))
                rw = raw[j % 3]
                kb.op(ACT, lambda: nc.scalar.copy(out=rw.t[:, 0:3], in_=halo.t[:, j, :]), reads=[halo.sub[j]], writes=[rw])
                kb.op(ACT, lambda: nc.scalar.copy(out=rw.t[:, 3:TT + 3], in_=p.t[:, 0:TT]), reads=[p], writes=[rw])
                kb.op(ACT, lambda: nc.scalar.copy(out=halo.t[:, j, :], in_=rw.t[:, TT:TT + 3]), reads=[rw], writes=[halo.sub[j]])
                kb.op(DVE, lambda: nc.vector.tensor_scalar(out=xr.t[:, j, :], in0=rw.t[:, 3:TT + 3], scalar1=params.t[:, cwo + j * 4 + 3:cwo + j * 4 + 4],
                                                           scalar2=pcol("rg_cb", j), op0=ALU.mult, op1=ALU.add), reads=[rw, params], writes=[xr.sub[j]])
                for kk in range(3):
                    kb.op(DVE, lambda: nc.vector.scalar_tensor_tensor(
                        out=xr.t[:, j, :], in0=rw.t[:, kk:kk + TT], scalar=params.t[:, cwo + j * 4 + kk:cwo + j * 4 + kk + 1], in1=xr.t[:, j, :],
                        op0=ALU.mult, op1=ALU.add), reads=[rw, params, xr.sub[j]], writes=[xr.sub[j]])
            for j in range(10):
                p = pp[pi % 6]; pi += 1
                for k in range(8):
                    kb.op(PE, lambda: nc.tensor.matmul(p.t[:, 0:TT], lhsT=wx.t[:, k, 1280 + j * 128:1280 + (j + 1) * 128], rhs=h.t[:, k, :],
                                                       start=(k == 0), stop=(k == 7)), reads=[wx, h], writes=[p], inc=(k == 7))
                kb.op(ACT, lambda: nc.scalar.activation(out=gg.t[:, j, :], in_=p.t[:, 0:TT], func=AF.Gelu_apprx_tanh), reads=[p], writes=[gg.sub[j]])
            for j in range(10):
                kb.op(ACT, lambda: nc.scalar.copy(out=xrb.t[:, j, :], in_=xr.t[:, j, :]), reads=[xr.sub[j]], writes=[xrb.sub[j]])
            for (wg, dst, bname) in ((wga, ga, "rg_ba"), (wgx, gi, "rg_bx")):
                for m in range(10):
                    p = pp[pi % 6]; pi += 1
                    ks = [k for k in (m - 1, m, m + 1) if 0 <= k < 10]
                    for n_, k in enumerate(ks):
                        kb.op(PE, lambda: nc.tensor.matmul(
                            p.t[:, 0:TT], lhsT=wg.t[:, m, k - m + 1, :], rhs=xrb.t[:, k, :], start=(n_ == 0), stop=(n_ == len(ks) - 1)),
                            reads=[wg, xrb.sub[k]], writes=[p], inc=(n_ == len(ks) - 1))
                    kb.op(ACT, lambda: nc.scalar.activation(out=dst.t[:, m, :], in_=p.t[:, 0:TT], func=AF.Sigmoid, bias=pcol(bname, m)),
                          reads=[p, params], writes=[dst.sub[m]])
            for m in range(10):
                kb.op(ACT, lambda: nc.scalar.activation(out=mu.t[:, m, :], in_=ga.t[:, m, :], func=AF.Exp, scale=cl.t[:, 10 + m:11 + m]), reads=[ga.sub[m], cl], writes=[mu.sub[m]])
                kb.op(ACT, lambda: nc.scalar.activation(out=ga.t[:, m, :], in_=ga.t[:, m, :], func=AF.Exp, scale=cl.t[:, m:m + 1]), reads=[ga.sub[m], cl], writes=[ga.sub[m]])
            for m in range(10):
                kb.op(DVE, lambda: nc.vector.tensor_scalar(out=mu.t[:, m, :], in0=mu.t[:, m, :], scalar1=1.0, scalar2=None, op0=ALU.min), reads=[mu.sub[m]], writes=[mu.sub[m]])
            for m in range(10):
                kb.op(ACT, lambda: nc.scalar.activation(out=mu.t[:, m, :], in_=mu.t[:, m, :], func=AF.Sqrt, scale=-1.0, bias=1.0), reads=[mu.sub[m]], writes=[mu.sub[m]])
            for m in range(10):
                kb.op(POOL, lambda: nc.gpsimd.tensor_tensor(out=gi.t[:, m, :], in0=gi.t[:, m, :], in1=xr.t[:, m, :], op=ALU.mult), reads=[gi.sub[m], xr.sub[m]], writes=[gi.sub[m]])
                kb.op(DVE, lambda: nc.vector.tensor_tensor(out=gi.t[:, m, :], in0=gi.t[:, m, :], in1=mu.t[:, m, :], op=ALU.mult), reads=[gi.sub[m], mu.sub[m]], writes=[gi.sub[m]])
                kb.op(DVE, lambda: nc.vector.tensor_tensor_scan(out=hh.t[:, m, :], data0=ga.t[:, m, :], data1=gi.t[:, m, :], initial=hst.t[:, m, :],
                                                                op0=ALU.mult, op1=ALU.add), reads=[ga.sub[m], gi.sub[m], hst.sub[m]], writes=[hh.sub[m]])
                kb.op(ACT, lambda: nc.scalar.copy(out=hst.t[:, m, :], in_=hh.t[:, m, TT - 1:TT]), reads=[hh.sub[m]], writes=[hst.sub[m]])
            rc = rec[t % 2]
            for j in range(10):
                kb.op(DVE, lambda: nc.vector.tensor_tensor(out=rc.t[:, j, :], in0=hh.t[:, j, :], in1=gg.t[:, j, :], op=ALU.mult), reads=[hh.sub[j], gg.sub[j]], writes=[rc.sub[j]])
            kb.dma(recT_d[:, :, t * TT:(t + 1) * TT].rearrange("c p t -> p c t"), rc.t[:], reads=rc.sub)
        kb.barrier()


def layer_norm_rows(nc, kb, y, nblk, g_ap, b_ap, rowbuf, stats, mv, sd, out_buf):
    ACT, DVE, POOL = kb.act, kb.dve, kb.pool
    for blk in range(nblk):
        for hf in range(2):
            kb.op(DVE, lambda: nc.vector.bn_stats(out=stats.t[:, blk, hf, :], in_=y.t[:, blk, hf * 512:(hf + 1) * 512]), reads=[y], writes=[stats])
        kb.op(DVE, lambda: nc.vector.bn_aggr(out=mv.t[:, blk, :], in_=stats.t[:, blk, :, :].rearrange("p a b -> p (a b)")), reads=[stats], writes=[mv])
        kb.op(ACT, lambda: nc.scalar.activation(out=sd.t[:, blk:blk + 1], in_=mv.t[:, blk, 1:2], func=AF.Ln, bias=kb.eps_ln.t[:, 0:1]), reads=[mv, kb.eps_ln], writes=[sd])
        kb.op(ACT, lambda: nc.scalar.activation(out=sd.t[:, blk:blk + 1], in_=sd.t[:, blk:blk + 1], func=AF.Exp, scale=-0.5), reads=[sd], writes=[sd])
        kb.op(DVE, lambda: nc.vector.tensor_scalar(out=y.t[:, blk, :], in0=y.t[:, blk, :], scalar1=mv.t[:, blk, 0:1], scalar2=sd.t[:, blk:blk + 1],
                                                   op0=ALU.subtract, op1=ALU.mult), reads=[y, mv, sd], writes=[y])
        kb.op(POOL, lambda: nc.gpsimd.tensor_tensor(out=y.t[:, blk, :], in0=y.t[:, blk, :], in1=g_ap, op=ALU.mult), reads=[y, rowbuf], writes=[y])
        kb.op(DVE, lambda: nc.vector.tensor_tensor(out=out_buf.t[:, blk, :], in0=y.t[:, blk, :], in1=b_ap, op=ALU.add), reads=[y, rowbuf], writes=[out_buf])


def phase_mix(nc, kb, S, load_w_bf16, gt1r, w_in_d, w_pa_d, w_pb_d, w_out_d, rows_d, x_d, hT_d, recT_d, dnT_d, x1_d):
    PE, ACT, DVE, POOL = kb.pe, kb.act, kb.dve, kb.pool
    TT = 256
    NT = S // TT
    NB = TT // 128
    with contextlib.ExitStack() as pes:
        wg = kb.sb(pes, "wg", [128, 8, 2048], BF16)
        wpa = kb.sb(pes, "wpa", [128, 10, 1024], BF16)
        wpb = kb.sb(pes, "wpb", [128, 16, 1024], BF16)
        wo = kb.sb(pes, "wo", [128, 8, 1024], BF16)
        lnr = kb.sb(pes, "lnr", [128, 2048], F32)
        with contextlib.ExitStack() as ses:
            stg = [kb.sb(ses, f"stg{i}", [128, 16, 256], F32) for i in range(2)]
            load_w_bf16(ses, wg, w_in_d, 0, 8, C_GA, 2048, stg)
            load_w_bf16(ses, wpa, w_pa_d, 0, 10, 0, 1024, stg)
            load_w_bf16(ses, wpb, w_pb_d, 0, 16, 0, 1024, stg)
            load_w_bf16(ses, wo, w_out_d, 0, 8, 0, 1024, stg)
            kb.dma(lnr.t[:], rows_d[:, _R["ln1_g"][0]:_R["ln1_g"][0] + 2048], writes=[lnr])
            kb.barrier(keep=[wg, wpa, wpb, wo, lnr])
        hT = [kb.sb(pes, f"hT{i}", [128, 8, TT], BF16) for i in range(2)]
        rcT = [kb.sb(pes, f"rcT{i}", [128, 10, TT], BF16) for i in range(2)]
        dnT = [kb.sb(pes, f"dnT{i}", [128, 16, TT], BF16) for i in range(2)]
        xt = [kb.sb(pes, f"xt{i}", [128, NB, 1024], F32) for i in range(2)]
        mg = kb.sbs(pes, "mg", [128, 8, TT], BF16)
        sg = [kb.sb(pes, f"sg{i}", [128, TT], F32) for i in range(4)]
        t1 = [kb.sb(pes, f"t1{i}", [128, TT], F32) for i in range(4)]
        y = kb.sb(pes, "y", [128, NB, 1024], F32)
        xo = [y, y]
        stats = kb.sb(pes, "stats", [128, NB, 2, 6], F32)
        mv = kb.sb(pes, "mv", [128, NB, 2], F32)
        sd = kb.sb(pes, "sd", [128, NB], F32)
        pp = [kb.ps(pes, f"pp{i}", [128, 512], F32) for i in range(8)]

        def load(t):
            i = t % 2
            sl = slice(t * TT, (t + 1) * TT)
            kb.dma(hT[i].t[:], hT_d[:, :, sl].rearrange("c p t -> p c t"), writes=[hT[i]])
            kb.dma(rcT[i].t[:], recT_d[:, :, sl].rearrange("c p t -> p c t"), writes=[rcT[i]])
            kb.dma(dnT[i].t[:], dnT_d[:, :, sl].rearrange("c p t -> p c t"), writes=[dnT[i]])
            kb.dma(xt[i].t[:], x_d[sl, :].rearrange("(b p) f -> p b f", p=128), writes=[xt[i]])

        load(0)
        pi = 0
        for t in range(NT):
            if t + 1 < NT:
                load(t + 1)
            i = t % 2
            h, rc, dn, xx = hT[i], rcT[i], dnT[i], xt[i]
            deferred = []
            for m in range(8):
                ms = slice(m * 128, (m + 1) * 128)
                pga = pp[pi % 8]; pya = pp[(pi + 1) % 8]; pgb = pp[(pi + 2) % 8]; pyb = pp[(pi + 3) % 8]; pi += 4
                for k in range(8):
                    kb.op(PE, lambda: nc.tensor.matmul(pga.t[:, 0:TT], lhsT=wg.t[:, k, m * 128:(m + 1) * 128], rhs=h.t[:, k, :], start=(k == 0), stop=(k == 7)),
                          reads=[wg, h], writes=[pga], inc=(k == 7))
                for k in range(10):
                    kb.op(PE, lambda: nc.tensor.matmul(pya.t[:, 0:TT], lhsT=wpa.t[:, k, ms], rhs=rc.t[:, k, :], start=(k == 0), stop=(k == 9)),
                          reads=[wpa, rc], writes=[pya], inc=(k == 9))
                for k in range(8):
                    kb.op(PE, lambda: nc.tensor.matmul(pgb.t[:, 0:TT], lhsT=wg.t[:, k, 1024 + m * 128:1024 + (m + 1) * 128], rhs=h.t[:, k, :], start=(k == 0), stop=(k == 7)),
                          reads=[wg, h], writes=[pgb], inc=(k == 7))
                for k in range(16):
                    kb.op(PE, lambda: nc.tensor.matmul(pyb.t[:, 0:TT], lhsT=wpb.t[:, k, ms], rhs=dn.t[:, k, :], start=(k == 0), stop=(k == 15)),
                          reads=[wpb, dn], writes=[pyb], inc=(k == 15))
                s0, s1, ta, tb = sg[2 * (m % 2)], sg[2 * (m % 2) + 1], t1[2 * (m % 2)], t1[2 * (m % 2) + 1]
                kb.op(ACT, lambda: nc.scalar.activation(out=s0.t[:], in_=pga.t[:, 0:TT], func=AF.Sigmoid), reads=[pga], writes=[s0])
                kb.op(ACT, lambda: nc.scalar.activation(out=s1.t[:], in_=pgb.t[:, 0:TT], func=AF.Sigmoid), reads=[pgb], writes=[s1])
                def fin(m=m, s0=s0, s1=s1, ta=ta, tb=tb, pya=pya, pyb=pyb):
                    kb.op(DVE, lambda: nc.vector.tensor_tensor(out=ta.t[:], in0=s0.t[:], in1=pya.t[:, 0:TT], op=ALU.mult), reads=[s0, pya], writes=[ta])
                    kb.op(DVE, lambda: nc.vector.tensor_tensor(out=tb.t[:], in0=s1.t[:], in1=pyb.t[:, 0:TT], op=ALU.mult), reads=[s1, pyb], writes=[tb])
                    kb.op(POOL, lambda: nc.gpsimd.tensor_tensor(out=mg.t[:, m, :], in0=ta.t[:], in1=tb.t[:], op=ALU.add), reads=[ta, tb], writes=[mg.sub[m]])
                prev = list(deferred); del deferred[:]
                deferred.append(fin)
                for f_ in prev:
                    f_()
            for f_ in deferred:
                f_()
            del deferred[:]
            for blk in range(NB):
                for cb in range(2):
                    p = pp[pi % 8]; pi += 1
                    cs = slice(cb * 512, (cb + 1) * 512)
                    for k in range(8):
                        kb.op(PE, lambda: nc.tensor.matmul(p.t[:], lhsT=mg.t[:, k, blk * 128:(blk + 1) * 128], rhs=wo.t[:, k, cs], start=(k == 0), stop=(k == 7)),
                              reads=[mg.sub[k], wo], writes=[p], inc=(k == 7))
                    kb.op(DVE, lambda: nc.vector.tensor_tensor(out=y.t[:, blk, cs], in0=p.t[:], in1=gt1r.t[:, cs], op=ALU.mult), reads=[p, gt1r], writes=[y])
                    kb.op(DVE, lambda: nc.vector.scalar_tensor_tensor(out=y.t[:, blk, cs], in0=xx.t[:, blk, cs], scalar=ALPHA, in1=y.t[:, blk, cs],
                                                                      op0=ALU.mult, op1=ALU.add), reads=[xx, y], writes=[y])
            o = xo[i]
            layer_norm_rows(nc, kb, y, NB, lnr.t[:, 0:1024], lnr.t[:, 1024:2048], lnr, stats, mv, sd, o)
            kb.dma(x1_d[t * TT:(t + 1) * TT, :].rearrange("(b p) f -> p b f", p=128), o.t[:], reads=[o])
        kb.barrier()


def phase_ffn(nc, kb, S, params, pcol, cm, load_w_bf16, ada, gt2r, w_fg_d, w_fu_d, w_fd_d, rows_d, x1_d, out_d):
    PE, ACT, DVE, POOL = kb.pe, kb.act, kb.dve, kb.pool
    TT = 256
    NT = S // TT
    NB = TT // 128
    ident_f = cm.t[:, CM_IDENT * 128:(CM_IDENT + 1) * 128]
    with contextlib.ExitStack() as pes:
        wfg = kb.sb(pes, "wfg", [128, 8, DFF], BF16)
        wfu = kb.sb(pes, "wfu", [128, 8, DFF], BF16)
        wfd = kb.sb(pes, "wfd", [128, 22, 1024], BF16)
        lnr = kb.sb(pes, "lnr", [128, 2048], F32)
        with contextlib.ExitStack() as ses:
            stg = [kb.sb(ses, f"stg{i}", [128, 22, 128], F32) for i in range(2)]
            load_w_bf16(ses, wfg, w_fg_d, 0, 8, 0, DFF, stg)
            load_w_bf16(ses, wfu, w_fu_d, 0, 8, 0, DFF, stg)
            load_w_bf16(ses, wfd, w_fd_d, 0, 22, 0, 1024, stg)
            kb.dma(lnr.t[:], rows_d[:, _R["ln2_g"][0]:_R["ln2_g"][0] + 2048], writes=[lnr])
            kb.barrier(keep=[wfg, wfu, wfd, lnr])
        xt = [kb.sb(pes, f"xt{i}", [128, NB, 1024], F32) for i in range(2)]
        h2 = kb.sbs(pes, "h2", [128, 8, TT], BF16)
        act = kb.sbs(pes, "act", [128, 22, TT], BF16)
        raw = [kb.sb(pes, f"raw{i}", [128, TT + 2], F32) for i in range(3)]
        cv = [kb.sb(pes, f"cv{i}", [128, TT], F32) for i in range(3)]
        tmp = [kb.sb(pes, f"tmp{i}", [128, TT], F32) for i in range(3)]
        halo = kb.sbs(pes, "halo", [128, 22, 2], F32)
        y = kb.sb(pes, "y", [128, NB, 1024], F32)
        xo = [y, y]
        stats = kb.sb(pes, "stats", [128, NB, 2, 6], F32)
        mv = kb.sb(pes, "mv", [128, NB, 2], F32)
        sd = kb.sb(pes, "sd", [128, NB], F32)
        pp = [kb.ps(pes, f"pp{i}", [128, 512], F32) for i in range(8)]
        kb.op(DVE, lambda: nc.vector.memset(halo.t[:], 0.0), writes=halo.sub)
        cwo = _P["ffn_cw"][0]

        def load(t):
            kb.dma(xt[t % 2].t[:], x1_d[t * TT:(t + 1) * TT, :].rearrange("(b p) f -> p b f", p=128), writes=[xt[t % 2]])

        load(0)
        pi = 0
        for t in range(NT):
            if t + 1 < NT:
                load(t + 1)
            xx = xt[t % 2]
            for blk in range(NB):
                for c in range(8):
                    p = pp[pi % 8]; pi += 1
                    kb.op(PE, lambda: nc.tensor.matmul(p.t[:, 0:128], lhsT=xx.t[:, blk, c * 128:(c + 1) * 128], rhs=ident_f, start=True, stop=True), reads=[xx, cm], writes=[p])
                    kb.op(ACT, lambda: nc.scalar.activation(out=h2.t[:, c, blk * 128:(blk + 1) * 128], in_=p.t[:, 0:128], func=AF.Identity,
                                                            scale=ada.t[:, 32 + c:33 + c], bias=ada.t[:, 24 + c:25 + c]), reads=[p, ada], writes=[h2.sub[c]])
            deferred = []
            deferred_g = []
            for m in range(22):
                ms = slice(m * 128, (m + 1) * 128)
                pg = pp[pi % 8]; pu = pp[(pi + 1) % 8]; pi += 2
                for k in range(8):
                    kb.op(PE, lambda: nc.tensor.matmul(pg.t[:, 0:TT], lhsT=wfg.t[:, k, ms], rhs=h2.t[:, k, :], start=(k == 0), stop=(k == 7)),
                          reads=[wfg, h2.sub[k]], writes=[pg], inc=(k == 7))
                for k in range(8):
                    kb.op(PE, lambda: nc.tensor.matmul(pu.t[:, 0:TT], lhsT=wfu.t[:, k, ms], rhs=h2.t[:, k, :], start=(k == 0), stop=(k == 7)),
                          reads=[wfu, h2.sub[k]], writes=[pu], inc=(k == 7))
                rw = raw[m % 3]
                c_ = cv[m % 3]
                kb.op(ACT, lambda: nc.scalar.copy(out=rw.t[:, 0:2], in_=halo.t[:, m, :]), reads=[halo.sub[m]], writes=[rw])
                kb.op(ACT, lambda: nc.scalar.copy(out=rw.t[:, 2:TT + 2], in_=pg.t[:, 0:TT]), reads=[pg], writes=[rw])
                kb.op(ACT, lambda: nc.scalar.copy(out=halo.t[:, m, :], in_=rw.t[:, TT:TT + 2]), reads=[rw], writes=[halo.sub[m]])
                kb.op(DVE, lambda: nc.vector.tensor_scalar(out=c_.t[:], in0=rw.t[:, 2:TT + 2], scalar1=params.t[:, cwo + m * 3 + 2:cwo + m * 3 + 3],
                                                           scalar2=pcol("ffn_cb", m), op0=ALU.mult, op1=ALU.add), reads=[rw, params], writes=[c_])
                for kk in range(2):
                    kb.op(DVE, lambda: nc.vector.scalar_tensor_tensor(out=c_.t[:], in0=rw.t[:, kk:kk + TT], scalar=params.t[:, cwo + m * 3 + kk:cwo + m * 3 + kk + 1],
                                                                      in1=c_.t[:], op0=ALU.mult, op1=ALU.add), reads=[rw, params, c_], writes=[c_])
                def gel(m=m, c_=c_, pu=pu):
                    gelu_tanh(nc, kb, c_.t[:], [c_], act.t[:, m, :], [act.sub[m]], tmp[m % 3], mul_ap=pu.t[:, 0:TT], mul_bufs=[pu], defer=deferred)
                prev_m = list(deferred); del deferred[:]
                prev_g = list(deferred_g); del deferred_g[:]
                deferred_g.append(gel)
                for f_ in prev_g:
                    f_()
                for f_ in prev_m:
                    f_()
            for f_ in deferred_g:
                f_()
            del deferred_g[:]
            for f_ in deferred:
                f_()
            del deferred[:]
            for blk in range(NB):
                for cb in range(2):
                    p = pp[pi % 8]; pi += 1
                    cs = slice(cb * 512, (cb + 1) * 512)
                    for k in range(22):
                        kb.op(PE, lambda: nc.tensor.matmul(p.t[:], lhsT=act.t[:, k, blk * 128:(blk + 1) * 128], rhs=wfd.t[:, k, cs], start=(k == 0), stop=(k == 21)),
                              reads=[act.sub[k], wfd], writes=[p], inc=(k == 21))
                    kb.op(DVE, lambda: nc.vector.tensor_tensor(out=y.t[:, blk, cs], in0=p.t[:], in1=gt2r.t[:, cs], op=ALU.mult), reads=[p, gt2r], writes=[y])
                    kb.op(DVE, lambda: nc.vector.scalar_tensor_tensor(out=y.t[:, blk, cs], in0=xx.t[:, blk, cs], scalar=ALPHA, in1=y.t[:, blk, cs],
                                                                      op0=ALU.mult, op1=ALU.add), reads=[xx, y], writes=[y])
            o = xo[t % 2]
            layer_norm_rows(nc, kb, y, NB, lnr.t[:, 0:1024], lnr.t[:, 1024:2048], lnr, stats, mv, sd, o)
            kb.dma(out_d[t * TT:(t + 1) * TT, :].rearrange("(b p) f -> p b f", p=128), o.t[:], reads=[o])
        kb.barrier()


class PSlot:
    def __init__(self, kb, bank, i, name):
        self.b = bank
        self.ap = bank.t[:, i * 128:(i + 1) * 128]
        self.apb = self.ap.bitcast(BF16)[:, 0:128]


def phase_gdn(nc, kb, S, params, pcol, cm, cmb, load_w_bf16, w_in_d, rows_d, hrep_d, hT_d, dnT_d):
    PE, ACT, DVE, POOL = kb.pe, kb.act, kb.dve, kb.pool
    NCH = S // 128
    NT = S // 512
    HVG, HQG = 4, 2
    NG = 16 // HVG
    NCOL = NCH * 16
    cf = lambda i: cm.t[:, i * 128:(i + 1) * 128]
    cb = lambda i: cmb.t[:, i * 128:(i + 1) * 128]
    ident_f, ident_b, L_f, U_f, ones_f, ones_b = cf(CM_IDENT), cb(CM_IDENT), cf(CM_L), cf(CM_U), cf(CM_ONES), cb(CM_ONES)
    mS_b, mIT_b, BD_f, OFF_f = cb(CM_MS), cb(CM_MIT), cf(CM_BD), cf(CM_OFF)
    eps6 = kb.eps_ln.t[:, 1:2]
    with contextlib.ExitStack() as pes:
        tm = {n: kb.sb(pes, "tm_" + n, [128, NCH, 16], F32) for n in ("beta", "negg", "eG", "beG", "kds", "gl")}
        flat = lambda bf: bf.t[:].rearrange("p n h -> p (n h)")
        with contextlib.ExitStack() as ses:
            wab = kb.sb(ses, "wab", [128, 8, 32], BF16)
            stg = [kb.sb(ses, "stg0", [128, 8, 32], F32)]
            hT = [kb.sb(ses, f"hT{i}", [128, 8, 512], BF16) for i in range(2)]
            ab = kb.sb(ses, "ab", [128, NCH, 32], F32)
            hrep = kb.sb(ses, "hrep", [128, 2, NCOL], F32)
            X = kb.sb(ses, "X", [128, NCH, 16], F32)
            Y = kb.sb(ses, "Y", [128, NCH, 16], F32)
            Z = kb.sb(ses, "Z", [128, NCH, 16], F32)
            pp = [kb.ps(ses, f"pp{i}", [128, 512], F32) for i in range(4)]
            load_w_bf16(ses, wab, w_in_d, 0, 8, C_A, 32, stg)
            kb.dma(hrep.t[:], hrep_d[:, :, :], writes=[hrep])
            kb.dma(hT[0].t[:], hT_d[:, :, 0:512].rearrange("c p t -> p c t"), writes=[hT[0]])
            for t in range(NT):
                if t + 1 < NT:
                    kb.dma(hT[(t + 1) % 2].t[:], hT_d[:, :, (t + 1) * 512:(t + 2) * 512].rearrange("c p t -> p c t"), writes=[hT[(t + 1) % 2]])
                h = hT[t % 2]
                p = pp[t % 2]
                for blk in range(4):
                    for k in range(8):
                        kb.op(PE, lambda: nc.tensor.matmul(p.t[:, blk * 32:(blk + 1) * 32], lhsT=h.t[:, k, blk * 128:(blk + 1) * 128], rhs=wab.t[:, k, :],
                                                           start=(k == 0), stop=(k == 7)), reads=[h, wab], writes=[p], inc=(k == 7))
                kb.op(ACT, lambda: nc.scalar.copy(out=ab.t[:, t * 4:(t + 1) * 4, :], in_=p.t[:, 0:128].rearrange("p (b c) -> p b c", c=32)), reads=[p], writes=[ab])
            a_v, b_v = ab.t[:, :, 0:16], ab.t[:, :, 16:32]
            dtb = hrep.t[:, 0, :].rearrange("p (n h) -> p n h", h=16)
            alog = hrep.t[:, 1, :].rearrange("p (n h) -> p n h", h=16)
            kb.op(ACT, lambda: nc.scalar.activation(out=tm["beta"].t[:], in_=b_v, func=AF.Sigmoid), reads=[ab], writes=[tm["beta"]])
            kb.op(DVE, lambda: nc.vector.tensor_tensor(out=X.t[:], in0=a_v, in1=dtb, op=ALU.add), reads=[ab, hrep], writes=[X])
            kb.op(DVE, lambda: nc.vector.tensor_scalar(out=Y.t[:], in0=X.t[:], scalar1=-1.0, scalar2=None, op0=ALU.mult), reads=[X], writes=[Y])
            kb.op(DVE, lambda: nc.vector.tensor_tensor(out=Y.t[:], in0=Y.t[:], in1=X.t[:], op=ALU.max), reads=[X, Y], writes=[Y])
            kb.op(ACT, lambda: nc.scalar.activation(out=Y.t[:], in_=Y.t[:], func=AF.Exp, scale=-1.0), reads=[Y], writes=[Y])
            kb.op(ACT, lambda: nc.scalar.activation(out=Y.t[:], in_=Y.t[:], func=AF.Ln, bias=1.0), reads=[Y], writes=[Y])
            kb.op(DVE, lambda: nc.vector.tensor_scalar(out=X.t[:], in0=X.t[:], scalar1=0.0, scalar2=None, op0=ALU.max), reads=[X], writes=[X])
            kb.op(DVE, lambda: nc.vector.tensor_tensor(out=X.t[:], in0=X.t[:], in1=Y.t[:], op=ALU.add), reads=[X, Y], writes=[X])
            kb.op(ACT, lambda: nc.scalar.activation(out=Z.t[:], in_=alog, func=AF.Exp), reads=[hrep], writes=[Z])
            kb.op(DVE, lambda: nc.vector.tensor_tensor(out=tm["negg"].t[:], in0=X.t[:], in1=Z.t[:], op=ALU.mult), reads=[X, Z], writes=[tm["negg"]])
            ng_f = flat(tm["negg"])
            for c0 in range(0, NCOL, 512):
                w = min(512, NCOL - c0)
                p1, p2 = pp[2], pp[3]
                kb.op(PE, lambda: nc.tensor.matmul(p1.t[:, 0:w], lhsT=L_f, rhs=ng_f[:, c0:c0 + w], start=True, stop=True), reads=[cm, tm["negg"]], writes=[p1])
                kb.op(PE, lambda: nc.tensor.matmul(p2.t[:, 0:w], lhsT=ones_f, rhs=ng_f[:, c0:c0 + w], start=True, stop=True), reads=[cm, tm["negg"]], writes=[p2])
                Xf, Yf = flat(X), flat(Y)
                kb.op(ACT, lambda: nc.scalar.copy(out=Xf[:, c0:c0 + w], in_=p1.t[:, 0:w]), reads=[p1], writes=[X])
                kb.op(ACT, lambda: nc.scalar.activation(out=flat(tm["eG"])[:, c0:c0 + w], in_=p1.t[:, 0:w], func=AF.Exp, scale=-1.0), reads=[p1], writes=[tm["eG"]])
                kb.op(ACT, lambda: nc.scalar.activation(out=flat(tm["gl"])[:, c0:c0 + w], in_=p2.t[:, 0:w], func=AF.Exp, scale=-1.0), reads=[p2], writes=[tm["gl"]])
                kb.op(DVE, lambda: nc.vector.tensor_tensor(out=Yf[:, c0:c0 + w], in0=p2.t[:, 0:w], in1=Xf[:, c0:c0 + w], op=ALU.subtract), reads=[p2, X], writes=[Y])
                kb.op(ACT, lambda: nc.scalar.activation(out=flat(tm["kds"])[:, c0:c0 + w], in_=Yf[:, c0:c0 + w], func=AF.Exp, scale=-1.0), reads=[Y], writes=[tm["kds"]])
            kb.op(DVE, lambda: nc.vector.tensor_tensor(out=tm["beG"].t[:], in0=tm["beta"].t[:], in1=tm["eG"].t[:], op=ALU.mult), reads=[tm["beta"], tm["eG"]], writes=[tm["beG"]])
            kb.barrier()

        STOP = getattr(build_program, "gdn_stop", 0)
        for gi in range(NG if STOP != 1 else 0):
            hv0, hq0 = gi * HVG, gi * HQG
            with contextlib.ExitStack() as ges:
                wq = kb.sb(ges, "wq", [128, 8, HQG * 128], BF16)
                wk = kb.sb(ges, "wk", [128, 8, HQG * 128], BF16)
                wv = kb.sb(ges, "wv", [128, 8, HVG * 128], BF16)
                wz = kb.sb(ges, "wz", [128, 8, HVG * 128], BF16)
                nwr = kb.sb(ges, "nwr", [128, HVG * 128], F32)
                with contextlib.ExitStack() as ses:
                    stg = [kb.sb(ses, f"stg{i}", [128, 8, 512], F32) for i in range(2)]
                    load_w_bf16(ses, wq, w_in_d, 0, 8, C_Q + hq0 * 128, HQG * 128, stg)
                    load_w_bf16(ses, wk, w_in_d, 0, 8, C_K + hq0 * 128, HQG * 128, stg)
                    load_w_bf16(ses, wv, w_in_d, 0, 8, C_V + hv0 * 128, HVG * 128, stg)
                    load_w_bf16(ses, wz, w_in_d, 0, 8, C_Z + hv0 * 128, HVG * 128, stg)
                    kb.dma(nwr.t[:], rows_d[:, _R["nw"][0]:_R["nw"][0] + HVG * 128], writes=[nwr])
                    kb.barrier()
                hT = [kb.sb(ges, f"hT{i}", [128, 8, 512], BF16) for i in range(2)]
                raw = [kb.sb(ges, f"raw{i}", [128, 515], F32) for i in range(3)]
                cv = [kb.sb(ges, f"cv{i}", [128, 512], F32) for i in range(3)]
                NCK = 2 * HQG + HVG
                halo = kb.sb(ges, "halo", [128, NCK, 3], F32)
                qkf = [kb.sb(ges, f"qkf{i}", [128, 512], F32) for i in range(2 * HQG)]
                sqb = [kb.sb(ges, f"sqb{i}", [128, 512], BF16) for i in range(2 * HQG)]
                rn = [kb.sb(ges, f"rn{i}", [128, 512], F32) for i in range(2)]
                qT2 = [kb.sb(ges, f"qT{i}", [128, HQG, 512], BF16) for i in range(2)]
                kT2 = [kb.sb(ges, f"kT{i}", [128, HQG, 512], BF16) for i in range(2)]
                vT2 = [kb.sb(ges, f"vT{i}", [128, HVG, 512], BF16) for i in range(2)]
                zs = [kb.sb(ges, f"zs{i}", [128, 512], F32) for i in range(2)]
                zg2 = [kb.sb(ges, f"zg{i}", [128, 4, HVG * 128], F32) for i in range(2)]
                dno = [kb.sb(ges, f"dno{i}", [128, HVG, 512], BF16) for i in range(2)]
                Sf = [kb.sb(ges, f"Sf{i}", [128, 128], F32) for i in range(HVG)]
                Sb = [kb.sb(ges, f"Sb{i}", [128, 128], BF16) for i in range(HVG)]
                ob2 = [kb.sb(ges, f"ob{i}", [128, HVG, 128], F32) for i in range(2)]
                junk = kb.sb(ges, "junk", [128, 128], F32)
                ss2 = [kb.sb(ges, f"ss{i}", [128, HVG], F32) for i in range(2)]
                rstd2 = [kb.sb(ges, f"rstd{i}", [128, HVG], F32) for i in range(2)]
                ktm = [kb.sb(ges, f"ktm{i}", [128, 128], BF16) for i in range(HQG)]
                KKbd = [kb.sb(ges, f"KKbd{i}", [128, 128], F32) for i in range(HQG)]
                KKoff = [kb.sb(ges, f"KKoff{i}", [128, 128], F32) for i in range(HQG)]
                QKT = [kb.sb(ges, f"QKT{i}", [128, 128], F32) for i in range(HQG)]

                class PB:
                    pass
                pb = []
                for hi in range(HVG):
                    B = PB()
                    for n_ in ("rh", "Ds", "DT", "T1", "o1"):
                        setattr(B, n_, kb.sb(ges, f"{n_}_{hi}", [128, 128], F32))
                    for n_ in ("Abd", "Aoff", "P0", "P1", "M0", "Y1", "Tb", "bek", "vn", "dntm"):
                        setattr(B, n_, kb.sb(ges, f"{n_}_{hi}", [128, 128], BF16))
                    for n_ in ("QKd", "M128", "bv", "kdec", "nwT"):
                        setattr(B, n_, [kb.sb(ges, f"{n_}{i}_{hi}", [128, 128], BF16) for i in range(2)])
                    B.P = [B.P0, B.P1]
                    B.QM = [kb.sb(ges, f"QM{i}_{hi}", [128, 256], BF16) for i in range(2)]
                    pb.append(B)
                banks = [kb.ps(ges, f"bk{i}", [128, 512], F32) for i in range(8)]
                slots = [[PSlot(kb, banks[i], j, f"sl{i}_{j}") for j in range(4)] for i in range(8)]
                hsA = [slots[2 * hi] for hi in range(HVG)]
                hsB = [slots[2 * hi + 1] for hi in range(HVG)]

                def bank_bufs(i):
                    return [banks[i]]

                for b_ in Sf + [halo]:
                    kb.op(DVE, lambda: nc.vector.memset(b_.t[:], 0.0), writes=[b_])
                for b_ in Sb:
                    kb.op(DVE, lambda: nc.vector.memset(b_.t[:], 0.0), writes=[b_])
                cwo = _P["dn_cw"][0]
                chunks = [("q", wq, c, hq0 + c) for c in range(HQG)] + [("k", wk, c, 8 + hq0 + c) for c in range(HQG)] + \
                         [("v", wv, c, 16 + hv0 + c) for c in range(HVG)]

                def load_h(t):
                    kb.dma(hT[t % 2].t[:], hT_d[:, :, t * 512:(t + 1) * 512].rearrange("c p t -> p c t"), writes=[hT[t % 2]])

                bi_ = [0]

                def stageA(t):
                    h = hT[t % 2]
                    qT, kT, vT, zg = qT2[t % 2], kT2[t % 2], vT2[t % 2], zg2[t % 2]
                    defA = []
                    for ci, (kind, w, c, gch) in enumerate(chunks):
                        bk = bi_[0] % 8; bi_[0] += 1
                        pt, pbufs = banks[bk].t, bank_bufs(bk)
                        for k in range(8):
                            kb.op(PE, lambda: nc.tensor.matmul(pt[:], lhsT=w.t[:, k, c * 128:(c + 1) * 128], rhs=h.t[:, k, :], start=(k == 0), stop=(k == 7)),
                                  reads=[w, h], writes=pbufs, inc=(k == 7))
                        rw, cvb = raw[ci % 3], cv[ci % 3]
                        kb.op(ACT, lambda: nc.scalar.copy(out=rw.t[:, 0:3], in_=halo.t[:, ci, :]), reads=[halo], writes=[rw])
                        kb.op(ACT, lambda: nc.scalar.copy(out=rw.t[:, 3:515], in_=pt[:]), reads=pbufs, writes=[rw])
                        kb.op(ACT, lambda: nc.scalar.copy(out=halo.t[:, ci, :], in_=rw.t[:, 512:515]), reads=[rw], writes=[halo])
                        kb.op(DVE, lambda: nc.vector.tensor_scalar(out=cvb.t[:], in0=rw.t[:, 3:515], scalar1=params.t[:, cwo + gch * 4 + 3:cwo + gch * 4 + 4],
                                                                   scalar2=None, op0=ALU.mult), reads=[rw, params], writes=[cvb])
                        for kk in range(3):
                            kb.op(DVE, lambda: nc.vector.scalar_tensor_tensor(out=cvb.t[:], in0=rw.t[:, kk:kk + 512], scalar=params.t[:, cwo + gch * 4 + kk:cwo + gch * 4 + kk + 1],
                                                                              in1=cvb.t[:], op0=ALU.mult, op1=ALU.add), reads=[rw, params, cvb], writes=[cvb])
                        def fin(kind=kind, c=c, ci=ci, cvb=cvb):
                            if kind == "v":
                                kb.op(ACT, lambda: nc.scalar.activation(out=vT.t[:, c, :], in_=cvb.t[:], func=AF.Silu), reads=[cvb], writes=[vT])
                            else:
                                f = qkf[ci]
                                kb.op(ACT, lambda: nc.scalar.activation(out=f.t[:], in_=cvb.t[:], func=AF.Silu), reads=[cvb], writes=[f])
                                kb.op(POOL, lambda: nc.gpsimd.tensor_tensor(out=sqb[ci].t[:], in0=f.t[:], in1=f.t[:], op=ALU.mult), reads=[f], writes=[sqb[ci]])
                        prevA = list(defA); del defA[:]
                        defA.append(fin)
                        for f_ in prevA:
                            f_()
                    for f_ in defA:
                        f_()
                    del defA[:]
                    for blk in range(4):
                        bk = bi_[0] % 8; bi_[0] += 1
                        pt, pbufs = banks[bk].t, bank_bufs(bk)
                        for k in range(8):
                            kb.op(PE, lambda: nc.tensor.matmul(pt[:, 0:HVG * 128], lhsT=h.t[:, k, blk * 128:(blk + 1) * 128], rhs=wz.t[:, k, :], start=(k == 0), stop=(k == 7)),
                                  reads=[h, wz], writes=pbufs, inc=(k == 7))
                        z_ = zs[blk % 2]
                        kb.op(ACT, lambda: nc.scalar.activation(out=z_.t[:, 0:HVG * 128], in_=pt[:, 0:HVG * 128], func=AF.Silu), reads=pbufs, writes=[z_])
                        kb.op(POOL, lambda: nc.gpsimd.tensor_tensor(out=zg.t[:, blk, :], in0=z_.t[:, 0:HVG * 128], in1=nwr.t[:], op=ALU.mult), reads=[z_, nwr], writes=[zg])

                    for ci, (kind, w, c, gch) in enumerate(chunks):
                        if kind == "v":
                            continue
                        f, sq, r_ = qkf[ci], sqb[ci], rn[ci % 2]
                        dst = qT if kind == "q" else kT
                        bk2 = bi_[0] % 8; bi_[0] += 1
                        pt2, pbufs2 = banks[bk2].t, bank_bufs(bk2)
                        kb.op(PE, lambda: nc.tensor.matmul(pt2[:], lhsT=ones_b, rhs=sq.t[:], start=True, stop=True), reads=[cmb, sq], writes=pbufs2)
                        kb.op(ACT, lambda: nc.scalar.activation(out=r_.t[:], in_=pt2[:], func=AF.Ln, bias=eps6), reads=pbufs2 + [kb.eps_ln], writes=[r_])
                        kb.op(ACT, lambda: nc.scalar.activation(out=r_.t[:], in_=r_.t[:], func=AF.Exp, scale=-0.5), reads=[r_], writes=[r_])
                        kb.op(DVE, lambda: nc.vector.scalar_tensor_tensor(out=dst.t[:, c, :], in0=f.t[:], scalar=(128.0 ** -0.5 if kind == "q" else 1.0), in1=r_.t[:],
                                                                          op0=ALU.mult, op1=ALU.mult), reads=[f, r_], writes=[dst])

                def hq_ops(n):
                    t, blk = n // 4, n % 4
                    cs = slice(blk * 128, (blk + 1) * 128)
                    qT, kT = qT2[t % 2], kT2[t % 2]
                    for hq in range(HQG):
                        kc, qc = kT.t[:, hq, cs], qT.t[:, hq, cs]
                        s0, s1, s2 = hsA[2 * hq][0], hsA[2 * hq][1], hsA[2 * hq][2]
                        kb.op(PE, lambda: nc.tensor.transpose(s0.apb, kc, ident_b), reads=[kT, cmb], writes=[s0.b])
                        kb.op(PE, lambda: nc.tensor.matmul(s1.ap, lhsT=kc, rhs=kc, start=True, stop=True), reads=[kT], writes=[s1.b])
                        kb.op(PE, lambda: nc.tensor.matmul(s2.ap, lhsT=kc, rhs=qc, start=True, stop=True), reads=[kT, qT], writes=[s2.b])
                        kb.op(ACT, lambda: nc.scalar.copy(out=ktm[hq].t[:], in_=s0.apb), reads=[s0.b], writes=[ktm[hq]])
                        kb.op(DVE, lambda: nc.vector.tensor_tensor(out=KKbd[hq].t[:], in0=s1.ap, in1=BD_f, op=ALU.mult), reads=[s1.b, cm], writes=[KKbd[hq]])
                        kb.op(DVE, lambda: nc.vector.tensor_tensor(out=KKoff[hq].t[:], in0=s1.ap, in1=OFF_f, op=ALU.mult), reads=[s1.b, cm], writes=[KKoff[hq]])
                        kb.op(ACT, lambda: nc.scalar.copy(out=QKT[hq].t[:], in_=s2.ap), reads=[s2.b], writes=[QKT[hq]])

                def pre_seq(hi, n):
                    t, blk = n // 4, n % 4
                    cs = slice(blk * 128, (blk + 1) * 128)
                    vT = vT2[t % 2]
                    pr = n % 2
                    hh_ = hv0 + hi
                    hq = hi // 2
                    B, sl = pb[hi], hsA[hi]
                    sc = lambda nm: tm[nm].t[:, n, hh_:hh_ + 1]
                    QKd, M128, bv, kdec, nwT = B.QKd[pr], B.M128[pr], B.bv[pr], B.kdec[pr], B.nwT[pr]
                    kb.op(ACT, lambda: nc.scalar.activation(out=B.rh.t[:], in_=U_f, func=AF.Identity, scale=sc("negg")), reads=[cm, tm["negg"]], writes=[B.rh])
                    kb.op(ACT, lambda: nc.scalar.activation(out=B.bek.t[:], in_=ktm[hq].t[:], func=AF.Identity, scale=sc("beG")), reads=[ktm[hq], tm["beG"]], writes=[B.bek])
                    kb.op(ACT, lambda: nc.scalar.activation(out=kdec.t[:], in_=ktm[hq].t[:], func=AF.Identity, scale=sc("kds")), reads=[ktm[hq], tm["kds"]], writes=[kdec])
                    yield
                    kb.op(PE, lambda: nc.tensor.matmul(sl[0].ap, lhsT=L_f, rhs=B.rh.t[:], start=True, stop=False), reads=[cm, B.rh], writes=[sl[0].b], inc=False)
                    kb.op(PE, lambda: nc.tensor.matmul(sl[0].ap, lhsT=ident_b, rhs=mS_b, start=False, stop=True), reads=[cmb], writes=[sl[0].b], inc=False)
                    kb.op(PE, lambda: nc.tensor.matmul(sl[1].ap, lhsT=B.rh.t[:], rhs=L_f, start=True, stop=False), reads=[cm, B.rh], writes=[sl[1].b], inc=False)
                    kb.op(PE, lambda: nc.tensor.matmul(sl[1].ap, lhsT=ident_b, rhs=mIT_b, start=False, stop=True), reads=[cmb], writes=[sl[1].b], inc=False)
                    kb.op(PE, lambda: nc.tensor.transpose(sl[3].apb, vT.t[:, hi, cs], ident_b), reads=[vT, cmb], writes=[sl[3].b])
                    kb.op(ACT, lambda: nc.scalar.activation(out=B.Ds.t[:], in_=sl[0].ap, func=AF.Exp, scale=-1.0), reads=[sl[0].b], writes=[B.Ds])
                    kb.op(ACT, lambda: nc.scalar.activation(out=B.DT.t[:], in_=sl[1].ap, func=AF.Exp, scale=-1.0), reads=[sl[1].b], writes=[B.DT])
                    kb.op(DVE, lambda: nc.vector.tensor_scalar(out=bv.t[:], in0=sl[3].apb, scalar1=sc("beta"), scalar2=None, op0=ALU.mult), reads=[sl[3].b, tm["beta"]], writes=[bv])
                    yield
                    kb.op(DVE, lambda: nc.vector.tensor_scalar(out=B.T1.t[:], in0=B.Ds.t[:], scalar1=sc("beta"), scalar2=None, op0=ALU.mult), reads=[B.Ds, tm["beta"]], writes=[B.T1])
                    kb.op(DVE, lambda: nc.vector.tensor_tensor(out=B.Abd.t[:], in0=B.T1.t[:], in1=KKbd[hq].t[:], op=ALU.mult), reads=[B.T1, KKbd[hq]], writes=[B.Abd])
                    kb.op(DVE, lambda: nc.vector.tensor_tensor(out=B.Aoff.t[:], in0=B.T1.t[:], in1=KKoff[hq].t[:], op=ALU.mult), reads=[B.T1, KKoff[hq]], writes=[B.Aoff])
                    kb.op(DVE, lambda: nc.vector.tensor_tensor(out=QKd.t[:], in0=QKT[hq].t[:], in1=B.DT.t[:], op=ALU.mult), reads=[QKT[hq], B.DT], writes=[QKd])
                    yield
                    bankA = banks[2 * hi]
                    kb.op(PE, lambda: nc.tensor.transpose(sl[2].apb, B.Abd.t[:], ident_b), reads=[B.Abd, cmb], writes=[sl[2].b])
                    kb.op(ACT, lambda: nc.scalar.copy(out=B.QM[0].t[:, 0:128], in_=sl[2].apb), reads=[sl[2].b], writes=[B.QM[0]])
                    kb.op(DVE, lambda: nc.vector.tensor_tensor(out=B.QM[0].t[:, 128:256], in0=ident_f, in1=sl[2].apb, op=ALU.subtract), reads=[cm, sl[2].b], writes=[B.QM[0]])
                    kb.op(DVE, lambda: nc.vector.tensor_tensor(out=B.QM[1].t[:, 128:256], in0=ident_f, in1=sl[2].apb, op=ALU.subtract), reads=[cm, sl[2].b], writes=[B.QM[1]])
                    yield
                    P = B.Abd
                    for lv in range(1, 6):
                        cur, nxt, Pn = B.QM[(lv - 1) % 2], B.QM[lv % 2], B.P[lv % 2]
                        lo, hi_ = (0, 128) if lv == 1 else ((0, 256) if lv < 5 else (128, 256))
                        kb.op(PE, lambda: nc.tensor.matmul(sl[2].ap, lhsT=cur.t[:, 0:128], rhs=P.t[:], start=True, stop=True), reads=[P, cur], writes=[sl[2].b], inc=False)
                        kb.op(PE, lambda: nc.tensor.matmul(bankA.t[:, lo:hi_], lhsT=P.t[:], rhs=cur.t[:, lo:hi_], start=True, stop=True), reads=[P, cur], writes=[bankA])
                        kb.op(ACT, lambda: nc.scalar.copy(out=Pn.t[:], in_=sl[2].ap), reads=[bankA], writes=[Pn])
                        if lv < 5:
                            kb.op(ACT, lambda: nc.scalar.copy(out=nxt.t[:, 0:128], in_=bankA.t[:, 0:128]), reads=[bankA], writes=[nxt])
                        if lv >= 2:
                            kb.op(DVE, lambda: nc.vector.tensor_tensor(out=nxt.t[:, 128:256], in0=cur.t[:, 128:256], in1=bankA.t[:, 128:256], op=ALU.add), reads=[cur, bankA], writes=[nxt])
                        yield
                        P = Pn
                    M4 = B.QM[1]
                    kb.op(PE, lambda: nc.tensor.matmul(sl[0].ap, lhsT=P.t[:], rhs=M4.t[:, 128:256], start=True, stop=True), reads=[P, M4], writes=[bankA])
                    kb.op(DVE, lambda: nc.vector.tensor_tensor(out=B.M0.t[:], in0=M4.t[:, 128:256], in1=sl[0].ap, op=ALU.add), reads=[M4, bankA], writes=[B.M0])
                    yield
                    M = B.M0
                    kb.op(PE, lambda: nc.tensor.matmul(sl[2].ap, lhsT=B.Aoff.t[:], rhs=M.t[:], start=True, stop=True), reads=[B.Aoff, M], writes=[bankA], inc=False)
                    kb.op(PE, lambda: nc.tensor.transpose(sl[3].apb, M.t[:], ident_b), reads=[M, cmb], writes=[bankA])
                    kb.op(ACT, lambda: nc.scalar.copy(out=B.Y1.t[:], in_=sl[2].ap), reads=[bankA], writes=[B.Y1])
                    kb.op(DVE, lambda: nc.vector.tensor_copy(out=B.Tb.t[:], in_=sl[3].apb), reads=[bankA], writes=[B.Tb])
                    yield
                    kb.op(PE, lambda: nc.tensor.matmul(sl[0].ap, lhsT=B.Tb.t[:], rhs=B.Y1.t[:], start=True, stop=True), reads=[B.Tb, B.Y1], writes=[bankA])
                    kb.op(DVE, lambda: nc.vector.tensor_tensor(out=M128.t[:], in0=M.t[:], in1=sl[0].ap, op=ALU.subtract), reads=[M, bankA], writes=[M128])
                    yield
                    kb.op(PE, lambda: nc.tensor.matmul(sl[1].ap, lhsT=B.bek.t[:], rhs=M128.t[:], start=True, stop=True), reads=[B.bek, M128], writes=[sl[1].b])
                    kb.op(ACT, lambda: nc.scalar.activation(out=nwT.t[:], in_=sl[1].ap, func=AF.Identity, scale=-1.0), reads=[sl[1].b], writes=[nwT])
                    yield

                def post_seq(hi, n):
                    t, blk = n // 4, n % 4
                    cs = slice(blk * 128, (blk + 1) * 128)
                    qT = qT2[t % 2]
                    pr = n % 2
                    hh_ = hv0 + hi
                    hq = hi // 2
                    B, sl = pb[hi], hsB[hi]
                    sc = lambda nm: tm[nm].t[:, n, hh_:hh_ + 1]
                    QKd, M128, bv, kdec, nwT = B.QKd[pr], B.M128[pr], B.bv[pr], B.kdec[pr], B.nwT[pr]
                    kb.op(PE, lambda: nc.tensor.matmul(sl[0].ap, lhsT=M128.t[:], rhs=bv.t[:], start=True, stop=False), reads=[M128, bv], writes=[sl[0].b], inc=False)
                    kb.op(PE, lambda: nc.tensor.matmul(sl[0].ap, lhsT=nwT.t[:], rhs=Sb[hi].t[:], start=False, stop=True), reads=[nwT, Sb[hi]], writes=[sl[0].b], inc=False)
                    kb.op(PE, lambda: nc.tensor.matmul(sl[1].ap, lhsT=qT.t[:, hq, cs], rhs=Sb[hi].t[:], start=True, stop=True), reads=[qT, Sb[hi]], writes=[sl[1].b])
                    kb.op(ACT, lambda: nc.scalar.copy(out=B.vn.t[:], in_=sl[0].ap), reads=[sl[0].b], writes=[B.vn])
                    kb.op(ACT, lambda: nc.scalar.activation(out=B.o1.t[:], in_=sl[1].ap, func=AF.Identity, scale=sc("eG")), reads=[sl[1].b, tm["eG"]], writes=[B.o1])
                    yield
                    kb.op(PE, lambda: nc.tensor.matmul(sl[2].ap, lhsT=kdec.t[:], rhs=B.vn.t[:], start=True, stop=True), reads=[kdec, B.vn], writes=[sl[2].b], inc=False)
                    kb.op(PE, lambda: nc.tensor.matmul(sl[3].ap, lhsT=QKd.t[:], rhs=B.vn.t[:], start=True, stop=True), reads=[QKd, B.vn], writes=[sl[3].b])
                    kb.op(DVE, lambda: nc.vector.scalar_tensor_tensor(out=Sf[hi].t[:], in0=Sf[hi].t[:], scalar=sc("gl"), in1=sl[2].ap, op0=ALU.mult, op1=ALU.add),
                          reads=[Sf[hi], tm["gl"], sl[2].b], writes=[Sf[hi]])
                    kb.op(DVE, lambda: nc.vector.tensor_tensor(out=ob2[pr].t[:, hi, :], in0=B.o1.t[:], in1=sl[3].ap, op=ALU.add), reads=[B.o1, sl[3].b], writes=[ob2[pr]])
                    yield
                    kb.op(ACT, lambda: nc.scalar.copy(out=Sb[hi].t[:], in_=Sf[hi].t[:]), reads=[Sf[hi]], writes=[Sb[hi]])
                    yield

                def norm_seq(n):
                    t, blk = n // 4, n % 4
                    cs = slice(blk * 128, (blk + 1) * 128)
                    zg, dn_o = zg2[t % 2], dno[t % 2]
                    ob, ss, rstd = ob2[n % 2], ss2[n % 2], rstd2[n % 2]
                    for hi in range(HVG):
                        kb.op(ACT, lambda: nc.scalar.activation(out=junk.t[:], in_=ob.t[:, hi, :], func=AF.Square, accum_out=ss.t[:, hi:hi + 1]), reads=[ob], writes=[junk, ss])
                    yield
                    kb.op(ACT, lambda: nc.scalar.activation(out=rstd.t[:], in_=ss.t[:], func=AF.Ln, scale=1.0 / 128.0, bias=eps6), reads=[ss, kb.eps_ln], writes=[rstd])
                    kb.op(ACT, lambda: nc.scalar.activation(out=rstd.t[:], in_=rstd.t[:], func=AF.Exp, scale=-0.5), reads=[rstd], writes=[rstd])
                    yield
                    for hi in range(HVG):
                        B = pb[hi]
                        kb.op(DVE, lambda: nc.vector.scalar_tensor_tensor(out=B.dntm.t[:], in0=ob.t[:, hi, :], scalar=rstd.t[:, hi:hi + 1], in1=zg.t[:, blk, hi * 128:(hi + 1) * 128],
                                                                          op0=ALU.mult, op1=ALU.mult), reads=[ob, rstd, zg], writes=[B.dntm])
                    yield
                    for hi in range(HVG):
                        B, sl = pb[hi], hsB[hi]
                        kb.op(PE, lambda: nc.tensor.transpose(sl[0].apb, B.dntm.t[:], ident_b), reads=[B.dntm, cmb], writes=[sl[0].b])
                        kb.op(ACT, lambda: nc.scalar.copy(out=dn_o.t[:, hi, cs], in_=sl[0].apb), reads=[sl[0].b], writes=[dn_o])
                    if blk == 3:
                        kb.dma(dnT_d[hv0:hv0 + HVG, :, t * 512:(t + 1) * 512].rearrange("c p t -> p c t"), dn_o.t[:], reads=[dn_o])
                    yield

                def run(gens, delays=None):
                    gens = list(gens)
                    delays = list(delays) if delays is not None else [0] * len(gens)
                    rnd = 0
                    while gens:
                        for i in range(len(gens) - 1, -1, -1):
                            pass
                        keep_g, keep_d = [], []
                        for g, d in zip(gens, delays):
                            if rnd < d:
                                keep_g.append(g); keep_d.append(d)
                                continue
                            try:
                                next(g)
                                keep_g.append(g); keep_d.append(d)
                            except StopIteration:
                                pass
                        gens, delays = keep_g, keep_d
                        rnd += 1

                STAG = getattr(build_program, "stagger", 0)
                load_h(0)
                if NT > 1:
                    load_h(1)
                stageA(0)
                hq_ops(0)
                run([pre_seq(hi, 0) for hi in range(HVG)])
                pend_norm = []
                for n in range(NCH):
                    gens = []
                    if n + 1 < NCH:
                        if (n + 1) % 4 == 0:
                            t1 = (n + 1) // 4
                            stageA(t1)
                            if t1 + 1 < NT:
                                load_h(t1 + 1)
                        hq_ops(n + 1)
                        gens = [pre_seq(hi, n + 1) for hi in range(HVG)]
                    posts = [post_seq(hi, n) for hi in range(HVG)]
                    run(posts + pend_norm + gens, [0] * (len(posts) + len(pend_norm)) + [STAG * (i // 2) for i in range(len(gens))])
                    pend_norm = [norm_seq(n)]
                run(pend_norm)
                kb.barrier()


def _consts():
    i = np.arange(128)
    ident = np.eye(128, dtype=np.float32)
    L = (i[:, None] <= i[None, :]).astype(np.float32)
    U = (i[:, None] > i[None, :]).astype(np.float32)
    mS = np.where(i[:, None] > i[None, :], 0.0, BIG).astype(np.float32)
    mIT = np.where(i[None, :] >= i[:, None], 0.0, BIG).astype(np.float32)
    blk = i // 64
    BD = (blk[:, None] == blk[None, :]).astype(np.float32)
    OFF = ((blk[:, None] == 1) & (blk[None, :] == 0)).astype(np.float32)
    ones = np.ones((128, 128), np.float32)
    return np.concatenate([ident, L, U, mS, mIT, BD, OFF, ones], axis=1)


def _fm(v, n):
    return np.ascontiguousarray(np.asarray(v, np.float32).reshape(n, 128).T)


def _band(w):
    full = np.zeros((DRNN, DRNN), np.float32)
    for b in range(16):
        full[b * 80:(b + 1) * 80, b * 80:(b + 1) * 80] = w[b]
    out = np.zeros((128, 10, 3, 128), np.float32)
    for m in range(10):
        for d in range(3):
            k = m + d - 1
            if 0 <= k < 10:
                out[:, m, d, :] = full[k * 128:(k + 1) * 128, m * 128:(m + 1) * 128]
    return out


def make_in_maps(inp, S, batches):
    l = 0
    f = lambda a: np.ascontiguousarray(np.asarray(a, np.float32))
    NCH = S // 128
    rows = np.zeros((128, NR), np.float32)
    b_ada = f(inp["b_ada"][l])
    def setr(name, v):
        o, w = _R[name]
        rows[:, o:o + w] = np.asarray(v, np.float32)[None, :]
    setr("b_gt1", b_ada[2048:3072]); setr("b_gt2", b_ada[5120:6144])
    setr("ln1_g", inp["ln1_g"][l]); setr("ln1_b", inp["ln1_b"][l]); setr("ln2_g", inp["ln2_g"][l]); setr("ln2_b", inp["ln2_b"][l])
    setr("nw", np.tile(f(inp["dn_norm_w"][l]), 16))
    hrep = np.zeros((128, 2, NCH * 16), np.float32)
    hrep[:, 0, :] = np.tile(f(inp["dn_dt_bias"][l]), NCH)[None, :]
    hrep[:, 1, :] = np.tile(f(inp["dn_a_log"][l]), NCH)[None, :]
    cmat = _consts()
    shared = dict(
        w_ada=f(inp["w_ada"][l]), w_in=f(inp["w_in"][l]), gate_a=_band(f(inp["rg_w_a"][l])), gate_x=_band(f(inp["rg_w_x"][l])),
        w_proj_a=f(inp["w_proj_a"][l]), w_proj_b=f(inp["w_proj_b"][l]), w_out=f(inp["w_out"][l]),
        ffn_w_gate=f(inp["ffn_w_gate"][l]), ffn_w_up=f(inp["ffn_w_up"][l]), ffn_w_down=f(inp["ffn_w_down"][l]),
        rows=rows, cmat=cmat, hrep=hrep)
    maps = []
    for b in batches:
        params = np.zeros((128, NP), np.float32)
        def setp(name, v):
            o, w = _P[name]
            params[:, o:o + w] = v
        setp("b_ada", _fm(b_ada, 48))
        setp("c", _fm(inp["c"][b], 8))
        setp("rg_cw", np.stack([_fm(inp["rg_conv_w"][l][k], 10) for k in range(4)], axis=2).reshape(128, 40))
        setp("rg_cb", _fm(inp["rg_conv_b"][l], 10)); setp("rg_ba", _fm(inp["rg_b_a"][l], 10)); setp("rg_bx", _fm(inp["rg_b_x"][l], 10))
        setp("rg_lam", _fm(inp["rg_lambda"][l], 10))
        setp("dn_cw", np.stack([_fm(inp["dn_conv_w"][l][k], 32) for k in range(4)], axis=2).reshape(128, 128))
        setp("ffn_cw", np.stack([_fm(inp["ffn_conv_w"][l][k], 22) for k in range(3)], axis=2).reshape(128, 66))
        setp("ffn_cb", _fm(inp["ffn_conv_b"][l], 22))
        xb = f(inp["x"][b])
        m = dict(shared)
        m["x"] = xb
        m["xT"] = np.ascontiguousarray(xb.T.reshape(8, 128, S))
        m["params"] = params
        maps.append(m)
    return maps


_NC_CACHE = {}


def kernel(**inputs):
    S = inputs["x"].shape[1]
    B = inputs["x"].shape[0]
    if S not in _NC_CACHE:
        _NC_CACHE[S] = build_program(S)
    nc = _NC_CACHE[S]
    maps = make_in_maps(inputs, S, list(range(B)))
    res = run_bass_kernel_spmd(nc, maps, core_ids=list(range(B)))
    return np.stack([np.asarray(r["out"], np.float32) for r in res.results], axis=0)
```

```python
import contextlib
import numpy as np
import concourse.bass as bass
import concourse.mybir as mybir
from concourse.bass_utils import run_bass_kernel_spmd

F32 = mybir.dt.float32
BF16 = mybir.dt.bfloat16
AF = mybir.ActivationFunctionType
ALU = mybir.AluOpType

D = 1024
DRNN = 1280
DFF = 2816
DIN = 10784
NHV = 16
C_XR, C_GR, C_Q, C_K, C_V, C_Z, C_A, C_B, C_GA, C_GB = 0, 1280, 2560, 3584, 4608, 6656, 8704, 8720, 8736, 9760
ALPHA = 2.0 ** 0.25
BIG = 32768.0
GELU_C = 1.5957691216057308

_P = {}
_off = 0
for _n, _w in [("b_ada", 48), ("c", 8), ("rg_cw", 40), ("rg_cb", 10), ("rg_ba", 10), ("rg_bx", 10), ("rg_lam", 10),
               ("dn_cw", 128), ("ffn_cw", 66), ("ffn_cb", 22)]:
    _P[_n] = (_off, _w)
    _off += _w
NP = _off
_R = {}
_off = 0
for _n, _w in [("b_gt1", 1024), ("b_gt2", 1024), ("ln1_g", 1024), ("ln1_b", 1024), ("ln2_g", 1024), ("ln2_b", 1024),
               ("nw", 2048)]:
    _R[_n] = (_off, _w)
    _off += _w
NR = _off
CM_IDENT, CM_L, CM_U, CM_MS, CM_MIT, CM_BD, CM_OFF, CM_ONES = range(8)
NCM = 8


class Buf:
    def __init__(self, kb, name, t):
        self.kb = kb
        self.name = name
        self.t = t
        self.w = None
        self.r = []
        self.lsem = None
        self.ssem = None
        self.excl = False

    def add_reader(self, tok):
        for i, (s, v) in enumerate(self.r):
            if s == tok[0]:
                if v < tok[1]:
                    self.r[i] = tok
                return
        self.r.append(tok)


class EngW:
    LIMIT = 24000

    def __init__(self, kb, eng, name):
        self.kb = kb
        self.eng = eng
        self.name = name
        self.sem = None
        self.cnt = 0
        self.seen = {}
        self.pending = []

    def wait(self, toks):
        best = {}
        for s, v in toks:
            if best.get(s, 0) < v:
                best[s] = v
        for s, v in best.items():
            if self.seen.get(s, 0) >= v:
                continue
            self.eng.wait_ge(self.kb.sems[s], v)
            self.seen[s] = v

    def bump(self, inst):
        if self.sem is None or self.cnt >= self.LIMIT:
            self.sem = self.kb.new_sem(f"e_{self.name}_{len(self.kb.sems)}")
            self.cnt = 0
        inst.then_inc(self.kb.sems[self.sem], 1)
        self.cnt += 1
        return (self.sem, self.cnt)

    def cur(self):
        return None if self.sem is None else (self.sem, self.cnt)


class KB:
    def __init__(self, nc, es):
        self.nc = nc
        self.es = es
        self.sems = []
        self.semcnt = []
        self.free_dma_sems = []
        self.pe = EngW(self, nc.tensor, "pe")
        self.act = EngW(self, nc.scalar, "act")
        self.dve = EngW(self, nc.vector, "dve")
        self.pool = EngW(self, nc.gpsimd, "pool")
        self.sp = EngW(self, nc.sync, "sp")
        self.engs = [self.pe, self.act, self.dve, self.pool, self.sp]
        self.dma_toks = {}
        self.phase_bufs = []
        self.n_inst = 0

    def new_sem(self, name):
        h = self.es.enter_context(self.nc.semaphore(name))
        self.sems.append(h)
        self.semcnt.append(0)
        return len(self.sems) - 1

    def dma_sem(self):
        if self.free_dma_sems:
            return self.free_dma_sems.pop()
        return self.new_sem(f"d_{len(self.sems)}")

    def sb(self, pes, name, shape, dt):
        self.n_alloc = getattr(self, "n_alloc", 0) + 1
        t = pes.enter_context(self.nc.sbuf_tensor(f"sb{self.n_alloc}_{name}", list(shape), dt))
        b = Buf(self, name, t)
        self.phase_bufs.append(b)
        return b

    def sbs(self, pes, name, shape, dt):
        b = self.sb(pes, name, shape, dt)
        b.sub = [Buf(self, f"{name}[{i}]", b.t) for i in range(shape[1])]
        self.phase_bufs.extend(b.sub)
        return b

    def ps(self, pes, name, shape, dt):
        self.n_alloc = getattr(self, "n_alloc", 0) + 1
        t = pes.enter_context(self.nc.psum_tensor(f"ps{self.n_alloc}_{name}", list(shape), dt))
        b = Buf(self, name, t)
        b.excl = True
        self.phase_bufs.append(b)
        return b

    def op(self, E, fn, reads=(), writes=(), inc=True):
        ex = [b for b in reads if b.excl]
        if ex:
            reads = [b for b in reads if not b.excl]
            writes = list(writes) + [b for b in ex if b not in writes]
        toks = []
        for b in reads:
            if b.w is not None:
                toks.append(b.w)
        for b in writes:
            if b.w is not None:
                toks.append(b.w)
            toks.extend(b.r)
        E.wait(toks)
        inst = fn()
        self.n_inst += 1
        if inc:
            tok = E.bump(inst)
            sets = E.pending + [(reads, writes)]
            E.pending = []
            for rd, wr in sets:
                for b in rd:
                    b.add_reader(tok)
                for b in wr:
                    b.w = tok
                    b.r = []
        else:
            E.pending.append((reads, writes))
        return inst

    def dma(self, out_ap, in_ap, reads=(), writes=(), Q=None):
        Q = Q or self.sp
        toks = []
        for b in reads:
            if b.w is not None:
                toks.append(b.w)
        for b in writes:
            if b.w is not None:
                toks.append(b.w)
            toks.extend(b.r)
        Q.wait(toks)
        inst = Q.eng.dma_start(out=out_ap, in_=in_ap)
        self.n_inst += 1
        if writes:
            tb = writes[0]
            if tb.lsem is None:
                tb.lsem = self.dma_sem()
            s = tb.lsem
        else:
            tb = reads[0]
            if tb.ssem is None:
                tb.ssem = self.dma_sem()
            s = tb.ssem
        self.semcnt[s] += 16
        inst.then_inc(self.sems[s], 16)
        tok = (s, self.semcnt[s])
        self.dma_toks[s] = tok
        for b in writes:
            b.w = tok
            b.r = []
        for b in reads:
            b.add_reader(tok)
        return tok

    def barrier(self, keep=()):
        toks = list(self.dma_toks.values())
        for E in self.engs:
            assert not E.pending
            c = E.cur()
            if c is not None:
                toks.append(c)
        for E in self.engs:
            E.wait(toks)
        for b in self.phase_bufs:
            if b.lsem is not None:
                self.free_dma_sems.append(b.lsem)
                b.lsem = None
            if b.ssem is not None:
                self.free_dma_sems.append(b.ssem)
                b.ssem = None
        self.dma_toks = {}

    def final_wait(self):
        toks = list(self.dma_toks.values())
        self.sp.wait(toks)


def _cm(cm, idx, dt=None):
    return cm.t[:, idx * 128:(idx + 1) * 128]


def build_program(S, debug=False):
    assert S % 512 == 0
    NT = S // 512
    NCH = S // 128
    nc = bass.Bass("TRN2", target_bir_lowering=False)
    okind = "ExternalOutput" if debug else "Internal"

    def din(name, shape, dt=F32):
        return nc.dram_tensor(name, list(shape), dt, kind="ExternalInput").ap()

    xT_d = din("xT", [8, 128, S])
    x_d = din("x", [S, D])
    w_ada_d = din("w_ada", [D, 6 * D])
    w_in_d = din("w_in", [D, DIN])
    gate_a_d = din("gate_a", [128, 10, 3, 128])
    gate_x_d = din("gate_x", [128, 10, 3, 128])
    w_pa_d = din("w_proj_a", [DRNN, D])
    w_pb_d = din("w_proj_b", [2048, D])
    w_out_d = din("w_out", [D, D])
    w_fg_d = din("ffn_w_gate", [D, DFF])
    w_fu_d = din("ffn_w_up", [D, DFF])
    w_fd_d = din("ffn_w_down", [DFF, D])
    params_d = din("params", [128, NP])
    rows_d = din("rows", [128, NR])
    cmat_d = din("cmat", [128, NCM * 128])
    hrep_d = din("hrep", [128, 2, NCH * 16])
    out_d = nc.dram_tensor("out", [S, D], F32, kind="ExternalOutput").ap()
    hT_d = nc.dram_tensor("hT_s", [8, 128, S], BF16, kind=okind).ap()
    recT_d = nc.dram_tensor("recT_s", [10, 128, S], BF16, kind=okind).ap()
    dnT_d = nc.dram_tensor("dnT_s", [16, 128, S], BF16, kind=okind).ap()
    x1_d = nc.dram_tensor("x1_s", [S, D], F32, kind=okind).ap()

    with contextlib.ExitStack() as es:
        kb = KB(nc, es)
        PE, ACT, DVE, POOL = kb.pe, kb.act, kb.dve, kb.pool
        params = kb.sb(es, "params", [128, NP], F32)
        cm = kb.sb(es, "cm", [128, NCM * 128], F32)
        cmb = kb.sb(es, "cmb", [128, NCM * 128], BF16)
        ada = kb.sb(es, "ada", [128, 48], F32)
        gt1r = kb.sb(es, "gt1r", [128, 1024], F32)
        gt2r = kb.sb(es, "gt2r", [128, 1024], F32)
        kb.dma(params.t[:], params_d[:, :], writes=[params])
        kb.dma(cm.t[:], cmat_d[:, :], writes=[cm])
        kb.op(ACT, lambda: nc.scalar.copy(out=cmb.t[:], in_=cm.t[:]), reads=[cm], writes=[cmb])

        def pcol(name, j=0, n=1):
            o, w = _P[name]
            return params.t[:, o + j:o + j + n]

        ident_f = cm.t[:, CM_IDENT * 128:(CM_IDENT + 1) * 128]
        ident_b = cmb.t[:, CM_IDENT * 128:(CM_IDENT + 1) * 128]
        ones_f = cm.t[:, CM_ONES * 128:(CM_ONES + 1) * 128]
        ones_b = cmb.t[:, CM_ONES * 128:(CM_ONES + 1) * 128]

        def load_w_bf16(pes, dst, dram, r0, nk, c0, ncols, stg, dst_c0=0):
            PW = stg[0].t.shape[-1]
            i = load_w_bf16.cnt
            for cc in range(0, ncols, PW):
                w = min(PW, ncols - cc)
                st = stg[i % len(stg)]
                src = dram[r0:r0 + nk * 128, c0 + cc:c0 + cc + w].rearrange("(k p) c -> p k c", p=128)
                kb.dma(st.t[:, 0:nk, 0:w], src, writes=[st])
                o = dst.t[:, 0:nk, dst_c0 + cc:dst_c0 + cc + w]
                if i % 2 == 0:
                    kb.op(ACT, lambda o=o, st=st, w=w: nc.scalar.copy(out=o, in_=st.t[:, 0:nk, 0:w]), reads=[st], writes=[dst])
                else:
                    kb.op(DVE, lambda o=o, st=st, w=w: nc.vector.tensor_copy(out=o, in_=st.t[:, 0:nk, 0:w]), reads=[st], writes=[dst])
                i += 1
            load_w_bf16.cnt = i
        load_w_bf16.cnt = 0

        with contextlib.ExitStack() as pes:
            scf = kb.sb(pes, "scf", [128, 8], F32)
            screp = kb.sb(pes, "screp", [128, 8, 128], F32)
            wst = [kb.sb(pes, f"wst{i}", [128, 8, 1024], F32) for i in range(2)]
            rows01 = kb.sb(pes, "rows01", [128, 2048], F32)
            pa = kb.ps(pes, "pa", [128, 512], F32)
            pb = [kb.ps(pes, f"pb{i}", [128, 512], F32) for i in range(2)]
            kb.dma(rows01.t[:], rows_d[:, 0:2048], writes=[rows01])
            kb.op(ACT, lambda: nc.scalar.activation(out=scf.t[:], in_=pcol("c", 0, 8), func=AF.Silu), reads=[params], writes=[scf])
            for k in range(8):
                kb.op(DVE, lambda k=k: nc.vector.tensor_copy(out=screp.t[:, k, :], in_=scf.t[:, k:k + 1].to_broadcast([128, 128])),
                      reads=[scf], writes=[screp])
            for jb in range(6):
                st = wst[jb % 2]
                kb.dma(st.t[:], w_ada_d[:, jb * 1024:(jb + 1) * 1024].rearrange("(k p) c -> p k c", p=128), writes=[st])
                if jb in (2, 5):
                    dst = gt1r if jb == 2 else gt2r
                    for cb in range(2):
                        for k in range(8):
                            kb.op(PE, lambda k=k, cb=cb, st=st: nc.tensor.matmul(pb[cb].t[:], lhsT=screp.t[:, k, :], rhs=st.t[:, k, cb * 512:(cb + 1) * 512],
                                                                               start=(k == 0), stop=(k == 7)),
                                  reads=[screp, st], writes=[pb[cb]], inc=(k == 7))
                        ro = (0 if jb == 2 else 1024) + cb * 512
                        kb.op(DVE, lambda cb=cb, ro=ro, dst=dst: nc.vector.scalar_tensor_tensor(
                            out=dst.t[:, cb * 512:(cb + 1) * 512], in0=pb[cb].t[:], scalar=1.0, in1=rows01.t[:, ro:ro + 512],
                            op0=ALU.add, op1=ALU.add), reads=[pb[cb], rows01], writes=[dst])
                for j in range(8):
                    col = jb * 8 + j
                    for k in range(8):
                        kb.op(PE, lambda k=k, j=j, col=col, st=st: nc.tensor.matmul(pa.t[:, col:col + 1], lhsT=st.t[:, k, j * 128:(j + 1) * 128],
                                                                                 rhs=scf.t[:, k:k + 1], start=(k == 0), stop=(k == 7)),
                              reads=[scf, st], writes=[pa], inc=(k == 7))
            kb.op(DVE, lambda: nc.vector.tensor_tensor(out=ada.t[:], in0=pa.t[:, 0:48], in1=pcol("b_ada", 0, 48), op=ALU.add),
                  reads=[pa, params], writes=[ada])
            for o in (8, 32):
                kb.op(DVE, lambda o=o: nc.vector.tensor_scalar(out=ada.t[:, o:o + 8], in0=ada.t[:, o:o + 8], scalar1=1.0, scalar2=None, op0=ALU.add),
                      reads=[ada], writes=[ada])
            xin = [kb.sb(pes, f"xin{i}", [128, 8, 512], F32) for i in range(2)]
            hto = [kb.sb(pes, f"hto{i}", [128, 8, 512], BF16) for i in range(2)]
            for t in range(NT):
                xi, ho = xin[t % 2], hto[t % 2]
                kb.dma(xi.t[:], xT_d[:, :, t * 512:(t + 1) * 512].rearrange("c p t -> p c t"), writes=[xi])
                for c in range(8):
                    if c % 2 == 0:
                        kb.op(ACT, lambda c=c, xi=xi, ho=ho: nc.scalar.activation(out=ho.t[:, c, :], in_=xi.t[:, c, :], func=AF.Identity,
                                                                              scale=ada.t[:, 8 + c:9 + c], bias=ada.t[:, c:c + 1]),
                              reads=[xi, ada], writes=[ho])
                    else:
                        kb.op(DVE, lambda c=c, xi=xi, ho=ho: nc.vector.tensor_scalar(out=ho.t[:, c, :], in0=xi.t[:, c, :], scalar1=ada.t[:, 8 + c:9 + c],
                                                                                 scalar2=ada.t[:, c:c + 1], op0=ALU.mult, op1=ALU.add),
                              reads=[xi, ada], writes=[ho])
                kb.dma(hT_d[:, :, t * 512:(t + 1) * 512].rearrange("c p t -> p c t"), ho.t[:], reads=[ho])
            kb.barrier()

        PHASES = build_program.phases
        eps_ln = kb.sb(es, "eps_ln", [128, 2], F32)
        kb.eps_ln = eps_ln
        kb.op(DVE, lambda: nc.vector.memset(eps_ln.t[:, 0:1], 1e-5), writes=[eps_ln])
        kb.op(DVE, lambda: nc.vector.memset(eps_ln.t[:, 1:2], 1e-6), writes=[eps_ln])
        if "rg" in PHASES:
            phase_rg(nc, kb, S, params, pcol, cm, cmb, load_w_bf16, w_in_d, gate_a_d, gate_x_d, hT_d, recT_d)
        if "gdn" in PHASES:
            phase_gdn(nc, kb, S, params, pcol, cm, cmb, load_w_bf16, w_in_d, rows_d, hrep_d, hT_d, dnT_d)
        if "mix" in PHASES:
            phase_mix(nc, kb, S, load_w_bf16, gt1r, w_in_d, w_pa_d, w_pb_d, w_out_d, rows_d, x_d, hT_d, recT_d, dnT_d, x1_d)
        if "ffn" in PHASES:
            phase_ffn(nc, kb, S, params, pcol, cm, load_w_bf16, ada, gt2r, w_fg_d, w_fu_d, w_fd_d, rows_d, x1_d, out_d)
        kb.final_wait()
    return nc


build_program.phases = ("rg", "gdn", "mix", "ffn")


def gelu_tanh(nc, kb, src_ap, src_bufs, out_ap, out_bufs, tmp, tmp2=None, mul_ap=None, mul_bufs=(), defer=None):
    ACT, DVE = kb.act, kb.dve
    if mul_ap is None:
        kb.op(ACT, lambda: nc.scalar.activation(out=out_ap, in_=src_ap, func=AF.Gelu_apprx_tanh), reads=src_bufs, writes=out_bufs)
    else:
        kb.op(ACT, lambda: nc.scalar.activation(out=tmp.t[:], in_=src_ap, func=AF.Gelu_apprx_tanh), reads=src_bufs, writes=[tmp])
        def fin():
            kb.op(DVE, lambda: nc.vector.tensor_tensor(out=out_ap, in0=tmp.t[:], in1=mul_ap, op=ALU.mult), reads=[tmp] + list(mul_bufs), writes=out_bufs)
        if defer is None:
            fin()
        else:
            defer.append(fin)


def phase_rg(nc, kb, S, params, pcol, cm, cmb, load_w_bf16, w_in_d, gate_a_d, gate_x_d, hT_d, recT_d):
    PE, ACT, DVE, POOL = kb.pe, kb.act, kb.dve, kb.pool
    TT = 256
    NT = S // TT
    with contextlib.ExitStack() as pes:
        wx = kb.sb(pes, "wx", [128, 8, 2560], BF16)
        wga = kb.sb(pes, "wga", [128, 10, 3, 128], BF16)
        wgx = kb.sb(pes, "wgx", [128, 10, 3, 128], BF16)
        cl = kb.sb(pes, "cl", [128, 20], F32)
        with contextlib.ExitStack() as ses:
            stg = [kb.sb(ses, f"stg{i}", [128, 8, 512], F32) for i in range(2)]
            gst = kb.sb(ses, "gst", [128, 10, 3, 128], F32)
            load_w_bf16(ses, wx, w_in_d, 0, 8, 0, 2560, stg)
            kb.dma(gst.t[:], gate_a_d[:, :, :, :], writes=[gst])
            kb.op(ACT, lambda: nc.scalar.copy(out=wga.t[:], in_=gst.t[:]), reads=[gst], writes=[wga])
            kb.dma(gst.t[:], gate_x_d[:, :, :, :], writes=[gst])
            kb.op(ACT, lambda: nc.scalar.copy(out=wgx.t[:], in_=gst.t[:]), reads=[gst], writes=[wgx])
            kb.barrier(keep=[wx, wga, wgx, cl])
        kb.op(ACT, lambda: nc.scalar.activation(out=cl.t[:, 0:10], in_=pcol("rg_lam", 0, 10), func=AF.Exp, scale=-1.0), reads=[params], writes=[cl])
        kb.op(ACT, lambda: nc.scalar.activation(out=cl.t[:, 0:10], in_=cl.t[:, 0:10], func=AF.Ln, bias=1.0), reads=[cl], writes=[cl])
        kb.op(DVE, lambda: nc.vector.tensor_scalar(out=cl.t[:, 10:20], in0=cl.t[:, 0:10], scalar1=-16.0, scalar2=None, op0=ALU.mult), reads=[cl], writes=[cl])
        kb.op(DVE, lambda: nc.vector.tensor_scalar(out=cl.t[:, 0:10], in0=cl.t[:, 0:10], scalar1=-8.0, scalar2=None, op0=ALU.mult), reads=[cl], writes=[cl])

        hT = [kb.sb(pes, f"hT{i}", [128, 8, TT], BF16) for i in range(2)]
        raw = [kb.sb(pes, f"raw{i}", [128, TT + 3], F32) for i in range(3)]
        halo = kb.sbs(pes, "halo", [128, 10, 3], F32)
        hst = kb.sbs(pes, "hst", [128, 10, 1], F32)
        xr2 = [kb.sbs(pes, f"xr{i}", [128, 10, TT], F32) for i in range(2)]
        xrb2 = [kb.sbs(pes, f"xrb{i}", [128, 10, TT], BF16) for i in range(2)]
        ga = kb.sbs(pes, "ga", [128, 10, TT], F32)
        gi = kb.sbs(pes, "gi", [128, 10, TT], F32)
        mu = kb.sbs(pes, "mu", [128, 10, TT], F32)
        hh = kb.sbs(pes, "hh", [128, 10, TT], F32)
        gg2 = [kb.sbs(pes, f"gg{i}", [128, 10, TT], F32) for i in range(2)]
        tmp = [kb.sb(pes, f"gtmp{i}", [128, TT], F32) for i in range(3)]
        rec = [kb.sbs(pes, f"rec{i}", [128, 10, TT], BF16) for i in range(2)]
        pp = [kb.ps(pes, f"pp{i}", [128, 512], F32) for i in range(6)]
        kb.op(DVE, lambda: nc.vector.memset(halo.t[:], 0.0), writes=halo.sub)
        kb.op(DVE, lambda: nc.vector.memset(hst.t[:], 0.0), writes=hst.sub)
        cwo = _P["rg_cw"][0]

        def load_h(t):
            kb.dma(hT[t % 2].t[:], hT_d[:, :, t * TT:(t + 1) * TT].rearrange("c p t -> p c t"), writes=[hT[t % 2]])

        load_h(0)
        if NT > 1:
            load_h(1)
        pi_ = [0]

        def front(t):
            xr, xrb, gg = xr2[t % 2], xrb2[t % 2], gg2[t % 2]
            pi = pi_[0]
            h = hT[t % 2]
            for j in range(10):
                p = pp[pi % 6]; pi += 1
                for k in range(8):
                    kb.op(PE, lambda: nc.tensor.matmul(p.t[:, 0:TT], lhsT=wx.t[:, k, j * 128:(j + 1) * 128], rhs=h.t[:, k, :],
                                                       start=(k == 0), stop=(k == 7)), reads=[wx, h], writes=[p], inc=(k == 7))
                rw = raw[j % 3]
                kb.op(ACT, lambda: nc.scalar.copy(out=rw.t[:, 0:3], in_=halo.t[:, j, :]), reads=[halo.sub[j]], writes=[rw])
                kb.op(ACT, lambda: nc.scalar.copy(out=rw.t[:, 3:TT + 3], in_=p.t[:, 0:TT]), reads=[p], writes=[rw])
                kb.op(ACT, lambda: nc.scalar.copy(out=halo.t[:, j, :], in_=rw.t[:, TT:TT + 3]), reads=[rw], writes=[halo.sub[j]])
                kb.op(DVE, lambda: nc.vector.tensor_scalar(out=xr.t[:, j, :], in0=rw.t[:, 3:TT + 3], scalar1=params.t[:, cwo + j * 4 + 3:cwo + j * 4 + 4],
                                                           scalar2=pcol("rg_cb", j), op0=ALU.mult, op1=ALU.add), reads=[rw, params], writes=[xr.sub[j]])
                for kk in range(3):
                    kb.op(DVE, lambda: nc.vector.scalar_tensor_tensor(
                        out=xr.t[:, j, :], in0=rw.t[:, kk:kk + TT], scalar=params.t[:, cwo + j * 4 + kk:cwo + j * 4 + kk + 1], in1=xr.t[:, j, :],
                        op0=ALU.mult, op1=ALU.add), reads=[rw, params, xr.sub[j]], writes=[xr.sub[j]])
            for j in range(10):
                p = pp[pi % 6]; pi += 1
                for k in range(8):
                    kb.op(PE, lambda: nc.tensor.matmul(p.t[:, 0:TT], lhsT=wx.t[:, k, 1280 + j * 128:1280 + (j + 1) * 128], rhs=h.t[:, k, :],
                                                       start=(k == 0), stop=(k == 7)), reads=[wx, h], writes=[p], inc=(k == 7))
                kb.op(ACT, lambda: nc.scalar.activation(out=gg.t[:, j, :], in_=p.t[:, 0:TT], func=AF.Gelu_apprx_tanh), reads=[p], writes=[gg.sub[j]])
            for j in range(10):
                kb.op(ACT, lambda: nc.scalar.copy(out=xrb.t[:, j, :], in_=xr.t[:, j, :]), reads=[xr.sub[j]], writes=[xrb.sub[j]])
            pi_[0] = pi

        def back(t):
            xr, xrb, gg = xr2[t % 2], xrb2[t % 2], gg2[t % 2]
            pi = pi_[0]
            for (wg, dst, bname) in ((wga, ga, "rg_ba"), (wgx, gi, "rg_bx")):
                for m in range(10):
                    p = pp[pi % 6]; pi += 1
                    ks = [k for k in (m - 1, m, m + 1) if 0 <= k < 10]
                    for n_, k in enumerate(ks):
                        kb.op(PE, lambda: nc.tensor.matmul(
                            p.t[:, 0:TT], lhsT=wg.t[:, m, k - m + 1, :], rhs=xrb.t[:, k, :], start=(n_ == 0), stop=(n_ == len(ks) - 1)),
                            reads=[wg, xrb.sub[k]], writes=[p], inc=(n_ == len(ks) - 1))
                    kb.op(ACT, lambda: nc.scalar.activation(out=dst.t[:, m, :], in_=p.t[:, 0:TT], func=AF.Sigmoid, bias=pcol(bname, m)),
                          reads=[p, params], writes=[dst.sub[m]])
            for m in range(10):
                kb.op(ACT, lambda: nc.scalar.activation(out=mu.t[:, m, :], in_=ga.t[:, m, :], func=AF.Exp, scale=cl.t[:, 10 + m:11 + m]), reads=[ga.sub[m], cl], writes=[mu.sub[m]])
                kb.op(ACT, lambda: nc.scalar.activation(out=ga.t[:, m, :], in_=ga.t[:, m, :], func=AF.Exp, scale=cl.t[:, m:m + 1]), reads=[ga.sub[m], cl], writes=[ga.sub[m]])
            for m in range(10):
                kb.op(DVE, lambda: nc.vector.tensor_scalar(out=mu.t[:, m, :], in0=mu.t[:, m, :], scalar1=1.0, scalar2=None, op0=ALU.min), reads=[mu.sub[m]], writes=[mu.sub[m]])
            for m in range(10):
                kb.op(ACT, lambda: nc.scalar.activation(out=mu.t[:, m, :], in_=mu.t[:, m, :], func=AF.Sqrt, scale=-1.0, bias=1.0), reads=[mu.sub[m]], writes=[mu.sub[m]])
            for m in range(10):
                kb.op(POOL, lambda: nc.gpsimd.tensor_tensor(out=gi.t[:, m, :], in0=gi.t[:, m, :], in1=xr.t[:, m, :], op=ALU.mult), reads=[gi.sub[m], xr.sub[m]], writes=[gi.sub[m]])
                kb.op(DVE, lambda: nc.vector.tensor_tensor(out=gi.t[:, m, :], in0=gi.t[:, m, :], in1=mu.t[:, m, :], op=ALU.mult), reads=[gi.sub[m], mu.sub[m]], writes=[gi.sub[m]])
                kb.op(DVE, lambda: nc.vector.tensor_tensor_scan(out=hh.t[:, m, :], data0=ga.t[:, m, :], data1=gi.t[:, m, :], initial=hst.t[:, m, :],
                                                                op0=ALU.mult, op1=ALU.add), reads=[ga.sub[m], gi.sub[m], hst.sub[m]], writes=[hh.sub[m]])
                kb.op(ACT, lambda: nc.scalar.copy(out=hst.t[:, m, :], in_=hh.t[:, m, TT - 1:TT]), reads=[hh.sub[m]], writes=[hst.sub[m]])
            rc = rec[t % 2]
            for j in range(10):
                kb.op(DVE, lambda: nc.vector.tensor_tensor(out=rc.t[:, j, :], in0=hh.t[:, j, :], in1=gg.t[:, j, :], op=ALU.mult), reads=[hh.sub[j], gg.sub[j]], writes=[rc.sub[j]])
            kb.dma(recT_d[:, :, t * TT:(t + 1) * TT].rearrange("c p t -> p c t"), rc.t[:], reads=rc.sub)
            pi_[0] = pi

        front(0)
        for t in range(NT):
            if t + 1 < NT:
                front(t + 1)
                if t + 2 < NT:
                    load_h(t + 2)
            back(t)
        kb.barrier()


def layer_norm_rows(nc, kb, y, nblk, g_ap, b_ap, rowbuf, stats, mv, sd, out_buf):
    ACT, DVE, POOL = kb.act, kb.dve, kb.pool
    for blk in range(nblk):
        for hf in range(2):
            kb.op(DVE, lambda: nc.vector.bn_stats(out=stats.t[:, blk, hf, :], in_=y.t[:, blk, hf * 512:(hf + 1) * 512]), reads=[y], writes=[stats])
        kb.op(DVE, lambda: nc.vector.bn_aggr(out=mv.t[:, blk, :], in_=stats.t[:, blk, :, :].rearrange("p a b -> p (a b)")), reads=[stats], writes=[mv])
        kb.op(ACT, lambda: nc.scalar.activation(out=sd.t[:, blk:blk + 1], in_=mv.t[:, blk, 1:2], func=AF.Ln, bias=kb.eps_ln.t[:, 0:1]), reads=[mv, kb.eps_ln], writes=[sd])
        kb.op(ACT, lambda: nc.scalar.activation(out=sd.t[:, blk:blk + 1], in_=sd.t[:, blk:blk + 1], func=AF.Exp, scale=-0.5), reads=[sd], writes=[sd])
        kb.op(DVE, lambda: nc.vector.tensor_scalar(out=y.t[:, blk, :], in0=y.t[:, blk, :], scalar1=mv.t[:, blk, 0:1], scalar2=sd.t[:, blk:blk + 1],
                                                   op0=ALU.subtract, op1=ALU.mult), reads=[y, mv, sd], writes=[y])
        kb.op(POOL, lambda: nc.gpsimd.tensor_tensor(out=y.t[:, blk, :], in0=y.t[:, blk, :], in1=g_ap, op=ALU.mult), reads=[y, rowbuf], writes=[y])
        kb.op(DVE, lambda: nc.vector.tensor_tensor(out=out_buf.t[:, blk, :], in0=y.t[:, blk, :], in1=b_ap, op=ALU.add), reads=[y, rowbuf], writes=[out_buf])


def phase_mix(nc, kb, S, load_w_bf16, gt1r, w_in_d, w_pa_d, w_pb_d, w_out_d, rows_d, x_d, hT_d, recT_d, dnT_d, x1_d):
    PE, ACT, DVE, POOL = kb.pe, kb.act, kb.dve, kb.pool
    TT = 256
    NT = S // TT
    NB = TT // 128
    with contextlib.ExitStack() as pes:
        wg = kb.sb(pes, "wg", [128, 8, 2048], BF16)
        wpa = kb.sb(pes, "wpa", [128, 10, 1024], BF16)
        wpb = kb.sb(pes, "wpb", [128, 16, 1024], BF16)
        wo = kb.sb(pes, "wo", [128, 8, 1024], BF16)
        lnr = kb.sb(pes, "lnr", [128, 2048], F32)
        with contextlib.ExitStack() as ses:
            stg = [kb.sb(ses, f"stg{i}", [128, 16, 256], F32) for i in range(2)]
            load_w_bf16(ses, wg, w_in_d, 0, 8, C_GA, 2048, stg)
            load_w_bf16(ses, wpa, w_pa_d, 0, 10, 0, 1024, stg)
            load_w_bf16(ses, wpb, w_pb_d, 0, 16, 0, 1024, stg)
            load_w_bf16(ses, wo, w_out_d, 0, 8, 0, 1024, stg)
            kb.dma(lnr.t[:], rows_d[:, _R["ln1_g"][0]:_R["ln1_g"][0] + 2048], writes=[lnr])
            kb.barrier(keep=[wg, wpa, wpb, wo, lnr])
        hT = [kb.sb(pes, f"hT{i}", [128, 8, TT], BF16) for i in range(2)]
        rcT = [kb.sb(pes, f"rcT{i}", [128, 10, TT], BF16) for i in range(2)]
        dnT = [kb.sb(pes, f"dnT{i}", [128, 16, TT], BF16) for i in range(2)]
        xt = [kb.sb(pes, f"xt{i}", [128, NB, 1024], F32) for i in range(2)]
        mg = kb.sbs(pes, "mg", [128, 8, TT], BF16)
        sg = [kb.sb(pes, f"sg{i}", [128, TT], F32) for i in range(4)]
        t1 = [kb.sb(pes, f"t1{i}", [128, TT], F32) for i in range(4)]
        y = kb.sb(pes, "y", [128, NB, 1024], F32)
        xo = [y, y]
        stats = kb.sb(pes, "stats", [128, NB, 2, 6], F32)
        mv = kb.sb(pes, "mv", [128, NB, 2], F32)
        sd = kb.sb(pes, "sd", [128, NB], F32)
        pp = [kb.ps(pes, f"pp{i}", [128, 512], F32) for i in range(8)]

        def load(t):
            i = t % 2
            sl = slice(t * TT, (t + 1) * TT)
            kb.dma(hT[i].t[:], hT_d[:, :, sl].rearrange("c p t -> p c t"), writes=[hT[i]])
            kb.dma(rcT[i].t[:], recT_d[:, :, sl].rearrange("c p t -> p c t"), writes=[rcT[i]])
            kb.dma(dnT[i].t[:], dnT_d[:, :, sl].rearrange("c p t -> p c t"), writes=[dnT[i]])
            kb.dma(xt[i].t[:], x_d[sl, :].rearrange("(b p) f -> p b f", p=128), writes=[xt[i]])

        load(0)
        pi = 0
        for t in range(NT):
            if t + 1 < NT:
                load(t + 1)
            i = t % 2
            h, rc, dn, xx = hT[i], rcT[i], dnT[i], xt[i]
            deferred = []
            for m in range(8):
                ms = slice(m * 128, (m + 1) * 128)
                pga = pp[pi % 8]; pya = pp[(pi + 1) % 8]; pgb = pp[(pi + 2) % 8]; pyb = pp[(pi + 3) % 8]; pi += 4
                for k in range(8):
                    kb.op(PE, lambda: nc.tensor.matmul(pga.t[:, 0:TT], lhsT=wg.t[:, k, m * 128:(m + 1) * 128], rhs=h.t[:, k, :], start=(k == 0), stop=(k == 7)),
                          reads=[wg, h], writes=[pga], inc=(k == 7))
                for k in range(10):
                    kb.op(PE, lambda: nc.tensor.matmul(pya.t[:, 0:TT], lhsT=wpa.t[:, k, ms], rhs=rc.t[:, k, :], start=(k == 0), stop=(k == 9)),
                          reads=[wpa, rc], writes=[pya], inc=(k == 9))
                for k in range(8):
                    kb.op(PE, lambda: nc.tensor.matmul(pgb.t[:, 0:TT], lhsT=wg.t[:, k, 1024 + m * 128:1024 + (m + 1) * 128], rhs=h.t[:, k, :], start=(k == 0), stop=(k == 7)),
                          reads=[wg, h], writes=[pgb], inc=(k == 7))
                for k in range(16):
                    kb.op(PE, lambda: nc.tensor.matmul(pyb.t[:, 0:TT], lhsT=wpb.t[:, k, ms], rhs=dn.t[:, k, :], start=(k == 0), stop=(k == 15)),
                          reads=[wpb, dn], writes=[pyb], inc=(k == 15))
                s0, s1, ta, tb = sg[2 * (m % 2)], sg[2 * (m % 2) + 1], t1[2 * (m % 2)], t1[2 * (m % 2) + 1]
                kb.op(ACT, lambda: nc.scalar.activation(out=s0.t[:], in_=pga.t[:, 0:TT], func=AF.Sigmoid), reads=[pga], writes=[s0])
                kb.op(ACT, lambda: nc.scalar.activation(out=s1.t[:], in_=pgb.t[:, 0:TT], func=AF.Sigmoid), reads=[pgb], writes=[s1])
                def fin(m=m, s0=s0, s1=s1, ta=ta, tb=tb, pya=pya, pyb=pyb):
                    kb.op(DVE, lambda: nc.vector.tensor_tensor(out=ta.t[:], in0=s0.t[:], in1=pya.t[:, 0:TT], op=ALU.mult), reads=[s0, pya], writes=[ta])
                    kb.op(DVE, lambda: nc.vector.tensor_tensor(out=tb.t[:], in0=s1.t[:], in1=pyb.t[:, 0:TT], op=ALU.mult), reads=[s1, pyb], writes=[tb])
                    kb.op(POOL, lambda: nc.gpsimd.tensor_tensor(out=mg.t[:, m, :], in0=ta.t[:], in1=tb.t[:], op=ALU.add), reads=[ta, tb], writes=[mg.sub[m]])
                prev = list(deferred); del deferred[:]
                deferred.append(fin)
                for f_ in prev:
                    f_()
            for f_ in deferred:
                f_()
            del deferred[:]
            for blk in range(NB):
                for cb in range(2):
                    p = pp[pi % 8]; pi += 1
                    cs = slice(cb * 512, (cb + 1) * 512)
                    for k in range(8):
                        kb.op(PE, lambda: nc.tensor.matmul(p.t[:], lhsT=mg.t[:, k, blk * 128:(blk + 1) * 128], rhs=wo.t[:, k, cs], start=(k == 0), stop=(k == 7)),
                              reads=[mg.sub[k], wo], writes=[p], inc=(k == 7))
                    kb.op(DVE, lambda: nc.vector.tensor_tensor(out=y.t[:, blk, cs], in0=p.t[:], in1=gt1r.t[:, cs], op=ALU.mult), reads=[p, gt1r], writes=[y])
                    kb.op(DVE, lambda: nc.vector.scalar_tensor_tensor(out=y.t[:, blk, cs], in0=xx.t[:, blk, cs], scalar=ALPHA, in1=y.t[:, blk, cs],
                                                                      op0=ALU.mult, op1=ALU.add), reads=[xx, y], writes=[y])
            o = xo[i]
            layer_norm_rows(nc, kb, y, NB, lnr.t[:, 0:1024], lnr.t[:, 1024:2048], lnr, stats, mv, sd, o)
            kb.dma(x1_d[t * TT:(t + 1) * TT, :].rearrange("(b p) f -> p b f", p=128), o.t[:], reads=[o])
        kb.barrier()


def phase_ffn(nc, kb, S, params, pcol, cm, load_w_bf16, ada, gt2r, w_fg_d, w_fu_d, w_fd_d, rows_d, x1_d, out_d):
    PE, ACT, DVE, POOL = kb.pe, kb.act, kb.dve, kb.pool
    TT = 256
    NT = S // TT
    NB = TT // 128
    ident_f = cm.t[:, CM_IDENT * 128:(CM_IDENT + 1) * 128]
    with contextlib.ExitStack() as pes:
        wfg = kb.sb(pes, "wfg", [128, 8, DFF], BF16)
        wfu = kb.sb(pes, "wfu", [128, 8, DFF], BF16)
        wfd = kb.sb(pes, "wfd", [128, 22, 1024], BF16)
        lnr = kb.sb(pes, "lnr", [128, 2048], F32)
        with contextlib.ExitStack() as ses:
            stg = [kb.sb(ses, f"stg{i}", [128, 22, 128], F32) for i in range(2)]
            load_w_bf16(ses, wfg, w_fg_d, 0, 8, 0, DFF, stg)
            load_w_bf16(ses, wfu, w_fu_d, 0, 8, 0, DFF, stg)
            load_w_bf16(ses, wfd, w_fd_d, 0, 22, 0, 1024, stg)
            kb.dma(lnr.t[:], rows_d[:, _R["ln2_g"][0]:_R["ln2_g"][0] + 2048], writes=[lnr])
            kb.barrier(keep=[wfg, wfu, wfd, lnr])
        xt = [kb.sb(pes, f"xt{i}", [128, NB, 1024], F32) for i in range(2)]
        h2 = kb.sbs(pes, "h2", [128, 8, TT], BF16)
        act = kb.sbs(pes, "act", [128, 22, TT], BF16)
        raw = [kb.sb(pes, f"raw{i}", [128, TT + 2], F32) for i in range(3)]
        cv = [kb.sb(pes, f"cv{i}", [128, TT], F32) for i in range(3)]
        tmp = [kb.sb(pes, f"tmp{i}", [128, TT], F32) for i in range(3)]
        halo = kb.sbs(pes, "halo", [128, 22, 2], F32)
        y = kb.sb(pes, "y", [128, NB, 1024], F32)
        xo = [y, y]
        stats = kb.sb(pes, "stats", [128, NB, 2, 6], F32)
        mv = kb.sb(pes, "mv", [128, NB, 2], F32)
        sd = kb.sb(pes, "sd", [128, NB], F32)
        pp = [kb.ps(pes, f"pp{i}", [128, 512], F32) for i in range(8)]
        kb.op(DVE, lambda: nc.vector.memset(halo.t[:], 0.0), writes=halo.sub)
        cwo = _P["ffn_cw"][0]

        def load(t):
            kb.dma(xt[t % 2].t[:], x1_d[t * TT:(t + 1) * TT, :].rearrange("(b p) f -> p b f", p=128), writes=[xt[t % 2]])

        load(0)
        pi = 0
        for t in range(NT):
            if t + 1 < NT:
                load(t + 1)
            xx = xt[t % 2]
            for blk in range(NB):
                for c in range(8):
                    p = pp[pi % 8]; pi += 1
                    kb.op(PE, lambda: nc.tensor.matmul(p.t[:, 0:128], lhsT=xx.t[:, blk, c * 128:(c + 1) * 128], rhs=ident_f, start=True, stop=True), reads=[xx, cm], writes=[p])
                    kb.op(ACT, lambda: nc.scalar.activation(out=h2.t[:, c, blk * 128:(blk + 1) * 128], in_=p.t[:, 0:128], func=AF.Identity,
                                                            scale=ada.t[:, 32 + c:33 + c], bias=ada.t[:, 24 + c:25 + c]), reads=[p, ada], writes=[h2.sub[c]])
            deferred = []
            deferred_g = []
            for m in range(22):
                ms = slice(m * 128, (m + 1) * 128)
                pg = pp[pi % 8]; pu = pp[(pi + 1) % 8]; pi += 2
                for k in range(8):
                    kb.op(PE, lambda: nc.tensor.matmul(pg.t[:, 0:TT], lhsT=wfg.t[:, k, ms], rhs=h2.t[:, k, :], start=(k == 0), stop=(k == 7)),
                          reads=[wfg, h2.sub[k]], writes=[pg], inc=(k == 7))
                for k in range(8):
                    kb.op(PE, lambda: nc.tensor.matmul(pu.t[:, 0:TT], lhsT=wfu.t[:, k, ms], rhs=h2.t[:, k, :], start=(k == 0), stop=(k == 7)),
                          reads=[wfu, h2.sub[k]], writes=[pu], inc=(k == 7))
                rw = raw[m % 3]
                c_ = cv[m % 3]
                kb.op(ACT, lambda: nc.scalar.copy(out=rw.t[:, 0:2], in_=halo.t[:, m, :]), reads=[halo.sub[m]], writes=[rw])
                kb.op(ACT, lambda: nc.scalar.copy(out=rw.t[:, 2:TT + 2], in_=pg.t[:, 0:TT]), reads=[pg], writes=[rw])
                kb.op(ACT, lambda: nc.scalar.copy(out=halo.t[:, m, :], in_=rw.t[:, TT:TT + 2]), reads=[rw], writes=[halo.sub[m]])
                kb.op(DVE, lambda: nc.vector.tensor_scalar(out=c_.t[:], in0=rw.t[:, 2:TT + 2], scalar1=params.t[:, cwo + m * 3 + 2:cwo + m * 3 + 3],
                                                           scalar2=pcol("ffn_cb", m), op0=ALU.mult, op1=ALU.add), reads=[rw, params], writes=[c_])
                for kk in range(2):
                    kb.op(DVE, lambda: nc.vector.scalar_tensor_tensor(out=c_.t[:], in0=rw.t[:, kk:kk + TT], scalar=params.t[:, cwo + m * 3 + kk:cwo + m * 3 + kk + 1],
                                                                      in1=c_.t[:], op0=ALU.mult, op1=ALU.add), reads=[rw, params, c_], writes=[c_])
                def gel(m=m, c_=c_, pu=pu):
                    gelu_tanh(nc, kb, c_.t[:], [c_], act.t[:, m, :], [act.sub[m]], tmp[m % 3], mul_ap=pu.t[:, 0:TT], mul_bufs=[pu], defer=deferred)
                prev_m = list(deferred); del deferred[:]
                prev_g = list(deferred_g); del deferred_g[:]
                deferred_g.append(gel)
                for f_ in prev_g:
                    f_()
                for f_ in prev_m:
                    f_()
            for f_ in deferred_g:
                f_()
            del deferred_g[:]
            for f_ in deferred:
                f_()
            del deferred[:]
            for blk in range(NB):
                for cb in range(2):
                    p = pp[pi % 8]; pi += 1
                    cs = slice(cb * 512, (cb + 1) * 512)
                    for k in range(22):
                        kb.op(PE, lambda: nc.tensor.matmul(p.t[:], lhsT=act.t[:, k, blk * 128:(blk + 1) * 128], rhs=wfd.t[:, k, cs], start=(k == 0), stop=(k == 21)),
                              reads=[act.sub[k], wfd], writes=[p], inc=(k == 21))
                    kb.op(DVE, lambda: nc.vector.tensor_tensor(out=y.t[:, blk, cs], in0=p.t[:], in1=gt2r.t[:, cs], op=ALU.mult), reads=[p, gt2r], writes=[y])
                    kb.op(DVE, lambda: nc.vector.scalar_tensor_tensor(out=y.t[:, blk, cs], in0=xx.t[:, blk, cs], scalar=ALPHA, in1=y.t[:, blk, cs],
                                                                      op0=ALU.mult, op1=ALU.add), reads=[xx, y], writes=[y])
            o = xo[t % 2]
            layer_norm_rows(nc, kb, y, NB, lnr.t[:, 0:1024], lnr.t[:, 1024:2048], lnr, stats, mv, sd, o)
            kb.dma(out_d[t * TT:(t + 1) * TT, :].rearrange("(b p) f -> p b f", p=128), o.t[:], reads=[o])
        kb.barrier()


class PSlot:
    def __init__(self, kb, bank, i, name):
        self.b = bank
        self.ap = bank.t[:, i * 128:(i + 1) * 128]
        self.apb = self.ap.bitcast(BF16)[:, 0:128]


def phase_gdn(nc, kb, S, params, pcol, cm, cmb, load_w_bf16, w_in_d, rows_d, hrep_d, hT_d, dnT_d):
    PE, ACT, DVE, POOL = kb.pe, kb.act, kb.dve, kb.pool
    NCH = S // 128
    NT = S // 512
    HVG, HQG = 4, 2
    NG = 16 // HVG
    NCOL = NCH * 16
    cf = lambda i: cm.t[:, i * 128:(i + 1) * 128]
    cb = lambda i: cmb.t[:, i * 128:(i + 1) * 128]
    ident_f, ident_b, L_f, U_f, ones_f, ones_b = cf(CM_IDENT), cb(CM_IDENT), cf(CM_L), cf(CM_U), cf(CM_ONES), cb(CM_ONES)
    mS_b, mIT_b, BD_f, OFF_f = cb(CM_MS), cb(CM_MIT), cf(CM_BD), cf(CM_OFF)
    eps6 = kb.eps_ln.t[:, 1:2]
    with contextlib.ExitStack() as pes:
        tm = {n: kb.sb(pes, "tm_" + n, [128, NCH, 16], F32) for n in ("beta", "negg", "eG", "beG", "kds", "gl")}
        flat = lambda bf: bf.t[:].rearrange("p n h -> p (n h)")
        with contextlib.ExitStack() as ses:
            wab = kb.sb(ses, "wab", [128, 8, 32], BF16)
            stg = [kb.sb(ses, "stg0", [128, 8, 32], F32)]
            hT = [kb.sb(ses, f"hT{i}", [128, 8, 512], BF16) for i in range(2)]
            ab = kb.sb(ses, "ab", [128, NCH, 32], F32)
            hrep = kb.sb(ses, "hrep", [128, 2, NCOL], F32)
            X = kb.sb(ses, "X", [128, NCH, 16], F32)
            Y = kb.sb(ses, "Y", [128, NCH, 16], F32)
            Z = kb.sb(ses, "Z", [128, NCH, 16], F32)
            pp = [kb.ps(ses, f"pp{i}", [128, 512], F32) for i in range(4)]
            load_w_bf16(ses, wab, w_in_d, 0, 8, C_A, 32, stg)
            kb.dma(hrep.t[:], hrep_d[:, :, :], writes=[hrep])
            kb.dma(hT[0].t[:], hT_d[:, :, 0:512].rearrange("c p t -> p c t"), writes=[hT[0]])
            for t in range(NT):
                if t + 1 < NT:
                    kb.dma(hT[(t + 1) % 2].t[:], hT_d[:, :, (t + 1) * 512:(t + 2) * 512].rearrange("c p t -> p c t"), writes=[hT[(t + 1) % 2]])
                h = hT[t % 2]
                p = pp[t % 2]
                for blk in range(4):
                    for k in range(8):
                        kb.op(PE, lambda: nc.tensor.matmul(p.t[:, blk * 32:(blk + 1) * 32], lhsT=h.t[:, k, blk * 128:(blk + 1) * 128], rhs=wab.t[:, k, :],
                                                           start=(k == 0), stop=(k == 7)), reads=[h, wab], writes=[p], inc=(k == 7))
                kb.op(ACT, lambda: nc.scalar.copy(out=ab.t[:, t * 4:(t + 1) * 4, :], in_=p.t[:, 0:128].rearrange("p (b c) -> p b c", c=32)), reads=[p], writes=[ab])
            a_v, b_v = ab.t[:, :, 0:16], ab.t[:, :, 16:32]
            dtb = hrep.t[:, 0, :].rearrange("p (n h) -> p n h", h=16)
            alog = hrep.t[:, 1, :].rearrange("p (n h) -> p n h", h=16)
            kb.op(ACT, lambda: nc.scalar.activation(out=tm["beta"].t[:], in_=b_v, func=AF.Sigmoid), reads=[ab], writes=[tm["beta"]])
            kb.op(DVE, lambda: nc.vector.tensor_tensor(out=X.t[:], in0=a_v, in1=dtb, op=ALU.add), reads=[ab, hrep], writes=[X])
            kb.op(DVE, lambda: nc.vector.tensor_scalar(out=Y.t[:], in0=X.t[:], scalar1=-1.0, scalar2=None, op0=ALU.mult), reads=[X], writes=[Y])
            kb.op(DVE, lambda: nc.vector.tensor_tensor(out=Y.t[:], in0=Y.t[:], in1=X.t[:], op=ALU.max), reads=[X, Y], writes=[Y])
            kb.op(ACT, lambda: nc.scalar.activation(out=Y.t[:], in_=Y.t[:], func=AF.Exp, scale=-1.0), reads=[Y], writes=[Y])
            kb.op(ACT, lambda: nc.scalar.activation(out=Y.t[:], in_=Y.t[:], func=AF.Ln, bias=1.0), reads=[Y], writes=[Y])
            kb.op(DVE, lambda: nc.vector.tensor_scalar(out=X.t[:], in0=X.t[:], scalar1=0.0, scalar2=None, op0=ALU.max), reads=[X], writes=[X])
            kb.op(DVE, lambda: nc.vector.tensor_tensor(out=X.t[:], in0=X.t[:], in1=Y.t[:], op=ALU.add), reads=[X, Y], writes=[X])
            kb.op(ACT, lambda: nc.scalar.activation(out=Z.t[:], in_=alog, func=AF.Exp), reads=[hrep], writes=[Z])
            kb.op(DVE, lambda: nc.vector.tensor_tensor(out=tm["negg"].t[:], in0=X.t[:], in1=Z.t[:], op=ALU.mult), reads=[X, Z], writes=[tm["negg"]])
            ng_f = flat(tm["negg"])
            for c0 in range(0, NCOL, 512):
                w = min(512, NCOL - c0)
                p1, p2 = pp[2], pp[3]
                kb.op(PE, lambda: nc.tensor.matmul(p1.t[:, 0:w], lhsT=L_f, rhs=ng_f[:, c0:c0 + w], start=True, stop=True), reads=[cm, tm["negg"]], writes=[p1])
                kb.op(PE, lambda: nc.tensor.matmul(p2.t[:, 0:w], lhsT=ones_f, rhs=ng_f[:, c0:c0 + w], start=True, stop=True), reads=[cm, tm["negg"]], writes=[p2])
                Xf, Yf = flat(X), flat(Y)
                kb.op(ACT, lambda: nc.scalar.copy(out=Xf[:, c0:c0 + w], in_=p1.t[:, 0:w]), reads=[p1], writes=[X])
                kb.op(ACT, lambda: nc.scalar.activation(out=flat(tm["eG"])[:, c0:c0 + w], in_=p1.t[:, 0:w], func=AF.Exp, scale=-1.0), reads=[p1], writes=[tm["eG"]])
                kb.op(ACT, lambda: nc.scalar.activation(out=flat(tm["gl"])[:, c0:c0 + w], in_=p2.t[:, 0:w], func=AF.Exp, scale=-1.0), reads=[p2], writes=[tm["gl"]])
                kb.op(DVE, lambda: nc.vector.tensor_tensor(out=Yf[:, c0:c0 + w], in0=p2.t[:, 0:w], in1=Xf[:, c0:c0 + w], op=ALU.subtract), reads=[p2, X], writes=[Y])
                kb.op(ACT, lambda: nc.scalar.activation(out=flat(tm["kds"])[:, c0:c0 + w], in_=Yf[:, c0:c0 + w], func=AF.Exp, scale=-1.0), reads=[Y], writes=[tm["kds"]])
            kb.op(DVE, lambda: nc.vector.tensor_tensor(out=tm["beG"].t[:], in0=tm["beta"].t[:], in1=tm["eG"].t[:], op=ALU.mult), reads=[tm["beta"], tm["eG"]], writes=[tm["beG"]])
            kb.barrier()

        STOP = getattr(build_program, "gdn_stop", 0)
        for gi in range(NG if STOP != 1 else 0):
            hv0, hq0 = gi * HVG, gi * HQG
            with contextlib.ExitStack() as ges:
                wq = kb.sb(ges, "wq", [128, 8, HQG * 128], BF16)
                wk = kb.sb(ges, "wk", [128, 8, HQG * 128], BF16)
                wv = kb.sb(ges, "wv", [128, 8, HVG * 128], BF16)
                wz = kb.sb(ges, "wz", [128, 8, HVG * 128], BF16)
                nwr = kb.sb(ges, "nwr", [128, HVG * 128], F32)
                with contextlib.ExitStack() as ses:
                    stg = [kb.sb(ses, f"stg{i}", [128, 8, 512], F32) for i in range(2)]
                    load_w_bf16(ses, wq, w_in_d, 0, 8, C_Q + hq0 * 128, HQG * 128, stg)
                    load_w_bf16(ses, wk, w_in_d, 0, 8, C_K + hq0 * 128, HQG * 128, stg)
                    load_w_bf16(ses, wv, w_in_d, 0, 8, C_V + hv0 * 128, HVG * 128, stg)
                    load_w_bf16(ses, wz, w_in_d, 0, 8, C_Z + hv0 * 128, HVG * 128, stg)
                    kb.dma(nwr.t[:], rows_d[:, _R["nw"][0]:_R["nw"][0] + HVG * 128], writes=[nwr])
                    kb.barrier()
                hT = [kb.sb(ges, f"hT{i}", [128, 8, 512], BF16) for i in range(2)]
                raw = [kb.sb(ges, f"raw{i}", [128, 515], F32) for i in range(3)]
                cv = [kb.sb(ges, f"cv{i}", [128, 512], F32) for i in range(3)]
                NCK = 2 * HQG + HVG
                halo = kb.sb(ges, "halo", [128, NCK, 3], F32)
                qkf = [kb.sb(ges, f"qkf{i}", [128, 512], F32) for i in range(2 * HQG)]
                sqb = [kb.sb(ges, f"sqb{i}", [128, 512], BF16) for i in range(2 * HQG)]
                rn = [kb.sb(ges, f"rn{i}", [128, 512], F32) for i in range(2)]
                qT2 = [kb.sb(ges, f"qT{i}", [128, HQG, 512], BF16) for i in range(2)]
                kT2 = [kb.sb(ges, f"kT{i}", [128, HQG, 512], BF16) for i in range(2)]
                vT2 = [kb.sb(ges, f"vT{i}", [128, HVG, 512], BF16) for i in range(2)]
                zs = [kb.sb(ges, f"zs{i}", [128, 512], F32) for i in range(2)]
                zg2 = [kb.sb(ges, f"zg{i}", [128, 4, HVG * 128], F32) for i in range(2)]
                dno = [kb.sb(ges, f"dno{i}", [128, HVG, 512], BF16) for i in range(2)]
                Sf = [kb.sb(ges, f"Sf{i}", [128, 128], F32) for i in range(HVG)]
                Sb = [kb.sb(ges, f"Sb{i}", [128, 128], BF16) for i in range(HVG)]
                ob2 = [kb.sb(ges, f"ob{i}", [128, HVG, 128], F32) for i in range(2)]
                junk = kb.sb(ges, "junk", [128, 128], F32)
                ss2 = [kb.sb(ges, f"ss{i}", [128, HVG], F32) for i in range(2)]
                rstd2 = [kb.sb(ges, f"rstd{i}", [128, HVG], F32) for i in range(2)]
                ktm = [kb.sb(ges, f"ktm{i}", [128, 128], BF16) for i in range(HQG)]
                KKbd = [kb.sb(ges, f"KKbd{i}", [128, 128], F32) for i in range(HQG)]
                KKoff = [kb.sb(ges, f"KKoff{i}", [128, 128], F32) for i in range(HQG)]
                QKT = [kb.sb(ges, f"QKT{i}", [128, 128], F32) for i in range(HQG)]

                class PB:
                    pass
                pb = []
                for hi in range(HVG):
                    B = PB()
                    for n_ in ("rh", "Ds", "DT", "T1", "o1"):
                        setattr(B, n_, kb.sb(ges, f"{n_}_{hi}", [128, 128], F32))
                    for n_ in ("Abd", "Aoff", "P0", "P1", "M0", "Y1", "Tb", "bek", "vn", "dntm"):
                        setattr(B, n_, kb.sb(ges, f"{n_}_{hi}", [128, 128], BF16))
                    for n_ in ("QKd", "M128", "bv", "kdec", "nwT"):
                        setattr(B, n_, [kb.sb(ges, f"{n_}{i}_{hi}", [128, 128], BF16) for i in range(2)])
                    B.P = [B.P0, B.P1]
                    B.QM = [kb.sb(ges, f"QM{i}_{hi}", [128, 256], BF16) for i in range(2)]
                    pb.append(B)
                banks = [kb.ps(ges, f"bk{i}", [128, 512], F32) for i in range(8)]
                slots = [[PSlot(kb, banks[i], j, f"sl{i}_{j}") for j in range(4)] for i in range(8)]
                hsA = [slots[2 * hi] for hi in range(HVG)]
                hsB = [slots[2 * hi + 1] for hi in range(HVG)]

                def bank_bufs(i):
                    return [banks[i]]

                for b_ in Sf + [halo]:
                    kb.op(DVE, lambda: nc.vector.memset(b_.t[:], 0.0), writes=[b_])
                for b_ in Sb:
                    kb.op(DVE, lambda: nc.vector.memset(b_.t[:], 0.0), writes=[b_])
                cwo = _P["dn_cw"][0]
                chunks = [("q", wq, c, hq0 + c) for c in range(HQG)] + [("k", wk, c, 8 + hq0 + c) for c in range(HQG)] + \
                         [("v", wv, c, 16 + hv0 + c) for c in range(HVG)]

                def load_h(t):
                    kb.dma(hT[t % 2].t[:], hT_d[:, :, t * 512:(t + 1) * 512].rearrange("c p t -> p c t"), writes=[hT[t % 2]])

                bi_ = [0]

                def stageA(t):
                    h = hT[t % 2]
                    qT, kT, vT, zg = qT2[t % 2], kT2[t % 2], vT2[t % 2], zg2[t % 2]
                    defA = []
                    for ci, (kind, w, c, gch) in enumerate(chunks):
                        bk = bi_[0] % 8; bi_[0] += 1
                        pt, pbufs = banks[bk].t, bank_bufs(bk)
                        for k in range(8):
                            kb.op(PE, lambda: nc.tensor.matmul(pt[:], lhsT=w.t[:, k, c * 128:(c + 1) * 128], rhs=h.t[:, k, :], start=(k == 0), stop=(k == 7)),
                                  reads=[w, h], writes=pbufs, inc=(k == 7))
                        rw, cvb = raw[ci % 3], cv[ci % 3]
                        kb.op(ACT, lambda: nc.scalar.copy(out=rw.t[:, 0:3], in_=halo.t[:, ci, :]), reads=[halo], writes=[rw])
                        kb.op(ACT, lambda: nc.scalar.copy(out=rw.t[:, 3:515], in_=pt[:]), reads=pbufs, writes=[rw])
                        kb.op(ACT, lambda: nc.scalar.copy(out=halo.t[:, ci, :], in_=rw.t[:, 512:515]), reads=[rw], writes=[halo])
                        kb.op(DVE, lambda: nc.vector.tensor_scalar(out=cvb.t[:], in0=rw.t[:, 3:515], scalar1=params.t[:, cwo + gch * 4 + 3:cwo + gch * 4 + 4],
                                                                   scalar2=None, op0=ALU.mult), reads=[rw, params], writes=[cvb])
                        for kk in range(3):
                            kb.op(DVE, lambda: nc.vector.scalar_tensor_tensor(out=cvb.t[:], in0=rw.t[:, kk:kk + 512], scalar=params.t[:, cwo + gch * 4 + kk:cwo + gch * 4 + kk + 1],
                                                                              in1=cvb.t[:], op0=ALU.mult, op1=ALU.add), reads=[rw, params, cvb], writes=[cvb])
                        def fin(kind=kind, c=c, ci=ci, cvb=cvb):
                            if kind == "v":
                                kb.op(ACT, lambda: nc.scalar.activation(out=vT.t[:, c, :], in_=cvb.t[:], func=AF.Silu), reads=[cvb], writes=[vT])
                            else:
                                f = qkf[ci]
                                kb.op(ACT, lambda: nc.scalar.activation(out=f.t[:], in_=cvb.t[:], func=AF.Silu), reads=[cvb], writes=[f])
                                kb.op(POOL, lambda: nc.gpsimd.tensor_tensor(out=sqb[ci].t[:], in0=f.t[:], in1=f.t[:], op=ALU.mult), reads=[f], writes=[sqb[ci]])
                        prevA = list(defA); del defA[:]
                        defA.append(fin)
                        for f_ in prevA:
                            f_()
                    for f_ in defA:
                        f_()
                    del defA[:]
                    for blk in range(4):
                        bk = bi_[0] % 8; bi_[0] += 1
                        pt, pbufs = banks[bk].t, bank_bufs(bk)
                        for k in range(8):
                            kb.op(PE, lambda: nc.tensor.matmul(pt[:, 0:HVG * 128], lhsT=h.t[:, k, blk * 128:(blk + 1) * 128], rhs=wz.t[:, k, :], start=(k == 0), stop=(k == 7)),
                                  reads=[h, wz], writes=pbufs, inc=(k == 7))
                        z_ = zs[blk % 2]
                        kb.op(ACT, lambda: nc.scalar.activation(out=z_.t[:, 0:HVG * 128], in_=pt[:, 0:HVG * 128], func=AF.Silu), reads=pbufs, writes=[z_])
                        kb.op(POOL, lambda: nc.gpsimd.tensor_tensor(out=zg.t[:, blk, :], in0=z_.t[:, 0:HVG * 128], in1=nwr.t[:], op=ALU.mult), reads=[z_, nwr], writes=[zg])

                    for ci, (kind, w, c, gch) in enumerate(chunks):
                        if kind == "v":
                            continue
                        f, sq, r_ = qkf[ci], sqb[ci], rn[ci % 2]
                        dst = qT if kind == "q" else kT
                        bk2 = bi_[0] % 8; bi_[0] += 1
                        pt2, pbufs2 = banks[bk2].t, bank_bufs(bk2)
                        kb.op(PE, lambda: nc.tensor.matmul(pt2[:], lhsT=ones_b, rhs=sq.t[:], start=True, stop=True), reads=[cmb, sq], writes=pbufs2)
                        kb.op(ACT, lambda: nc.scalar.activation(out=r_.t[:], in_=pt2[:], func=AF.Ln, bias=eps6), reads=pbufs2 + [kb.eps_ln], writes=[r_])
                        kb.op(ACT, lambda: nc.scalar.activation(out=r_.t[:], in_=r_.t[:], func=AF.Exp, scale=-0.5), reads=[r_], writes=[r_])
                        kb.op(DVE, lambda: nc.vector.scalar_tensor_tensor(out=dst.t[:, c, :], in0=f.t[:], scalar=(128.0 ** -0.5 if kind == "q" else 1.0), in1=r_.t[:],
                                                                          op0=ALU.mult, op1=ALU.mult), reads=[f, r_], writes=[dst])

                def hq_ops(n):
                    t, blk = n // 4, n % 4
                    cs = slice(blk * 128, (blk + 1) * 128)
                    qT, kT = qT2[t % 2], kT2[t % 2]
                    for hq in range(HQG):
                        kc, qc = kT.t[:, hq, cs], qT.t[:, hq, cs]
                        s0, s1, s2 = hsA[2 * hq][0], hsA[2 * hq][1], hsA[2 * hq][2]
                        kb.op(PE, lambda: nc.tensor.transpose(s0.apb, kc, ident_b), reads=[kT, cmb], writes=[s0.b])
                        kb.op(PE, lambda: nc.tensor.matmul(s1.ap, lhsT=kc, rhs=kc, start=True, stop=True), reads=[kT], writes=[s1.b])
                        kb.op(PE, lambda: nc.tensor.matmul(s2.ap, lhsT=kc, rhs=qc, start=True, stop=True), reads=[kT, qT], writes=[s2.b])
                        kb.op(ACT, lambda: nc.scalar.copy(out=ktm[hq].t[:], in_=s0.apb), reads=[s0.b], writes=[ktm[hq]])
                        kb.op(DVE, lambda: nc.vector.tensor_tensor(out=KKbd[hq].t[:], in0=s1.ap, in1=BD_f, op=ALU.mult), reads=[s1.b, cm], writes=[KKbd[hq]])
                        kb.op(DVE, lambda: nc.vector.tensor_tensor(out=KKoff[hq].t[:], in0=s1.ap, in1=OFF_f, op=ALU.mult), reads=[s1.b, cm], writes=[KKoff[hq]])
                        kb.op(ACT, lambda: nc.scalar.copy(out=QKT[hq].t[:], in_=s2.ap), reads=[s2.b], writes=[QKT[hq]])

                def pre_seq(hi, n):
                    t, blk = n // 4, n % 4
                    cs = slice(blk * 128, (blk + 1) * 128)
                    vT = vT2[t % 2]
                    pr = n % 2
                    hh_ = hv0 + hi
                    hq = hi // 2
                    B, sl = pb[hi], hsA[hi]
                    sc = lambda nm: tm[nm].t[:, n, hh_:hh_ + 1]
                    QKd, M128, bv, kdec, nwT = B.QKd[pr], B.M128[pr], B.bv[pr], B.kdec[pr], B.nwT[pr]
                    kb.op(ACT, lambda: nc.scalar.activation(out=B.rh.t[:], in_=U_f, func=AF.Identity, scale=sc("negg")), reads=[cm, tm["negg"]], writes=[B.rh])
                    kb.op(ACT, lambda: nc.scalar.activation(out=B.bek.t[:], in_=ktm[hq].t[:], func=AF.Identity, scale=sc("beG")), reads=[ktm[hq], tm["beG"]], writes=[B.bek])
                    kb.op(ACT, lambda: nc.scalar.activation(out=kdec.t[:], in_=ktm[hq].t[:], func=AF.Identity, scale=sc("kds")), reads=[ktm[hq], tm["kds"]], writes=[kdec])
                    yield
                    kb.op(PE, lambda: nc.tensor.matmul(sl[0].ap, lhsT=L_f, rhs=B.rh.t[:], start=True, stop=False), reads=[cm, B.rh], writes=[sl[0].b], inc=False)
                    kb.op(PE, lambda: nc.tensor.matmul(sl[0].ap, lhsT=ident_b, rhs=mS_b, start=False, stop=True), reads=[cmb], writes=[sl[0].b], inc=False)
                    kb.op(PE, lambda: nc.tensor.matmul(sl[1].ap, lhsT=B.rh.t[:], rhs=L_f, start=True, stop=False), reads=[cm, B.rh], writes=[sl[1].b], inc=False)
                    kb.op(PE, lambda: nc.tensor.matmul(sl[1].ap, lhsT=ident_b, rhs=mIT_b, start=False, stop=True), reads=[cmb], writes=[sl[1].b], inc=False)
                    kb.op(PE, lambda: nc.tensor.transpose(sl[3].apb, vT.t[:, hi, cs], ident_b), reads=[vT, cmb], writes=[sl[3].b])
                    kb.op(ACT, lambda: nc.scalar.activation(out=B.Ds.t[:], in_=sl[0].ap, func=AF.Exp, scale=-1.0), reads=[sl[0].b], writes=[B.Ds])
                    kb.op(ACT, lambda: nc.scalar.activation(out=B.DT.t[:], in_=sl[1].ap, func=AF.Exp, scale=-1.0), reads=[sl[1].b], writes=[B.DT])
                    kb.op(DVE, lambda: nc.vector.tensor_scalar(out=bv.t[:], in0=sl[3].apb, scalar1=sc("beta"), scalar2=None, op0=ALU.mult), reads=[sl[3].b, tm["beta"]], writes=[bv])
                    yield
                    kb.op(DVE, lambda: nc.vector.tensor_scalar(out=B.T1.t[:], in0=B.Ds.t[:], scalar1=sc("beta"), scalar2=None, op0=ALU.mult), reads=[B.Ds, tm["beta"]], writes=[B.T1])
                    kb.op(DVE, lambda: nc.vector.tensor_tensor(out=B.Abd.t[:], in0=B.T1.t[:], in1=KKbd[hq].t[:], op=ALU.mult), reads=[B.T1, KKbd[hq]], writes=[B.Abd])
                    kb.op(DVE, lambda: nc.vector.tensor_tensor(out=B.Aoff.t[:], in0=B.T1.t[:], in1=KKoff[hq].t[:], op=ALU.mult), reads=[B.T1, KKoff[hq]], writes=[B.Aoff])
                    kb.op(DVE, lambda: nc.vector.tensor_tensor(out=QKd.t[:], in0=QKT[hq].t[:], in1=B.DT.t[:], op=ALU.mult), reads=[QKT[hq], B.DT], writes=[QKd])
                    yield
                    bankA = banks[2 * hi]
                    kb.op(PE, lambda: nc.tensor.transpose(sl[2].apb, B.Abd.t[:], ident_b), reads=[B.Abd, cmb], writes=[sl[2].b])
                    kb.op(ACT, lambda: nc.scalar.copy(out=B.QM[0].t[:, 0:128], in_=sl[2].apb), reads=[sl[2].b], writes=[B.QM[0]])
                    kb.op(DVE, lambda: nc.vector.tensor_tensor(out=B.QM[0].t[:, 128:256], in0=ident_f, in1=sl[2].apb, op=ALU.subtract), reads=[cm, sl[2].b], writes=[B.QM[0]])
                    kb.op(DVE, lambda: nc.vector.tensor_tensor(out=B.QM[1].t[:, 128:256], in0=ident_f, in1=sl[2].apb, op=ALU.subtract), reads=[cm, sl[2].b], writes=[B.QM[1]])
                    yield
                    P = B.Abd
                    for lv in range(1, 6):
                        cur, nxt, Pn = B.QM[(lv - 1) % 2], B.QM[lv % 2], B.P[lv % 2]
                        lo, hi_ = (0, 128) if lv == 1 else ((0, 256) if lv < 5 else (128, 256))
                        kb.op(PE, lambda: nc.tensor.matmul(sl[2].ap, lhsT=cur.t[:, 0:128], rhs=P.t[:], start=True, stop=True), reads=[P, cur], writes=[sl[2].b], inc=False)
                        kb.op(PE, lambda: nc.tensor.matmul(bankA.t[:, lo:hi_], lhsT=P.t[:], rhs=cur.t[:, lo:hi_], start=True, stop=True), reads=[P, cur], writes=[bankA])
                        kb.op(ACT, lambda: nc.scalar.copy(out=Pn.t[:], in_=sl[2].ap), reads=[bankA], writes=[Pn])
                        if lv < 5:
                            kb.op(ACT, lambda: nc.scalar.copy(out=nxt.t[:, 0:128], in_=bankA.t[:, 0:128]), reads=[bankA], writes=[nxt])
                        if lv >= 2:
                            kb.op(DVE, lambda: nc.vector.tensor_tensor(out=nxt.t[:, 128:256], in0=cur.t[:, 128:256], in1=bankA.t[:, 128:256], op=ALU.add), reads=[cur, bankA], writes=[nxt])
                        yield
                        P = Pn
                    M4 = B.QM[1]
                    kb.op(PE, lambda: nc.tensor.matmul(sl[0].ap, lhsT=P.t[:], rhs=M4.t[:, 128:256], start=True, stop=True), reads=[P, M4], writes=[bankA])
                    kb.op(DVE, lambda: nc.vector.tensor_tensor(out=B.M0.t[:], in0=M4.t[:, 128:256], in1=sl[0].ap, op=ALU.add), reads=[M4, bankA], writes=[B.M0])
                    yield
                    M = B.M0
                    kb.op(PE, lambda: nc.tensor.matmul(sl[2].ap, lhsT=B.Aoff.t[:], rhs=M.t[:], start=True, stop=True), reads=[B.Aoff, M], writes=[bankA], inc=False)
                    kb.op(PE, lambda: nc.tensor.transpose(sl[3].apb, M.t[:], ident_b), reads=[M, cmb], writes=[bankA])
                    kb.op(ACT, lambda: nc.scalar.copy(out=B.Y1.t[:], in_=sl[2].ap), reads=[bankA], writes=[B.Y1])
                    kb.op(DVE, lambda: nc.vector.tensor_copy(out=B.Tb.t[:], in_=sl[3].apb), reads=[bankA], writes=[B.Tb])
                    yield
                    kb.op(PE, lambda: nc.tensor.matmul(sl[0].ap, lhsT=B.Tb.t[:], rhs=B.Y1.t[:], start=True, stop=True), reads=[B.Tb, B.Y1], writes=[bankA])
                    kb.op(DVE, lambda: nc.vector.tensor_tensor(out=M128.t[:], in0=M.t[:], in1=sl[0].ap, op=ALU.subtract), reads=[M, bankA], writes=[M128])
                    yield
                    kb.op(PE, lambda: nc.tensor.matmul(sl[1].ap, lhsT=B.bek.t[:], rhs=M128.t[:], start=True, stop=True), reads=[B.bek, M128], writes=[sl[1].b])
                    kb.op(ACT, lambda: nc.scalar.activation(out=nwT.t[:], in_=sl[1].ap, func=AF.Identity, scale=-1.0), reads=[sl[1].b], writes=[nwT])
                    yield

                def post_seq(hi, n):
                    t, blk = n // 4, n % 4
                    cs = slice(blk * 128, (blk + 1) * 128)
                    qT = qT2[t % 2]
                    pr = n % 2
                    hh_ = hv0 + hi
                    hq = hi // 2
                    B, sl = pb[hi], hsB[hi]
                    sc = lambda nm: tm[nm].t[:, n, hh_:hh_ + 1]
                    QKd, M128, bv, kdec, nwT = B.QKd[pr], B.M128[pr], B.bv[pr], B.kdec[pr], B.nwT[pr]
                    kb.op(PE, lambda: nc.tensor.matmul(sl[0].ap, lhsT=M128.t[:], rhs=bv.t[:], start=True, stop=False), reads=[M128, bv], writes=[sl[0].b], inc=False)
                    kb.op(PE, lambda: nc.tensor.matmul(sl[0].ap, lhsT=nwT.t[:], rhs=Sb[hi].t[:], start=False, stop=True), reads=[nwT, Sb[hi]], writes=[sl[0].b], inc=False)
                    kb.op(PE, lambda: nc.tensor.matmul(sl[1].ap, lhsT=qT.t[:, hq, cs], rhs=Sb[hi].t[:], start=True, stop=True), reads=[qT, Sb[hi]], writes=[sl[1].b])
                    kb.op(ACT, lambda: nc.scalar.copy(out=B.vn.t[:], in_=sl[0].ap), reads=[sl[0].b], writes=[B.vn])
                    kb.op(ACT, lambda: nc.scalar.activation(out=B.o1.t[:], in_=sl[1].ap, func=AF.Identity, scale=sc("eG")), reads=[sl[1].b, tm["eG"]], writes=[B.o1])
                    yield
                    kb.op(PE, lambda: nc.tensor.matmul(sl[2].ap, lhsT=kdec.t[:], rhs=B.vn.t[:], start=True, stop=True), reads=[kdec, B.vn], writes=[sl[2].b], inc=False)
                    kb.op(PE, lambda: nc.tensor.matmul(sl[3].ap, lhsT=QKd.t[:], rhs=B.vn.t[:], start=True, stop=True), reads=[QKd, B.vn], writes=[sl[3].b])
                    kb.op(DVE, lambda: nc.vector.scalar_tensor_tensor(out=Sf[hi].t[:], in0=Sf[hi].t[:], scalar=sc("gl"), in1=sl[2].ap, op0=ALU.mult, op1=ALU.add),
                          reads=[Sf[hi], tm["gl"], sl[2].b], writes=[Sf[hi]])
                    kb.op(DVE, lambda: nc.vector.tensor_tensor(out=ob2[pr].t[:, hi, :], in0=B.o1.t[:], in1=sl[3].ap, op=ALU.add), reads=[B.o1, sl[3].b], writes=[ob2[pr]])
                    yield
                    kb.op(ACT, lambda: nc.scalar.copy(out=Sb[hi].t[:], in_=Sf[hi].t[:]), reads=[Sf[hi]], writes=[Sb[hi]])
                    yield

                def norm_seq(n):
                    t, blk = n // 4, n % 4
                    cs = slice(blk * 128, (blk + 1) * 128)
                    zg, dn_o = zg2[t % 2], dno[t % 2]
                    ob, ss, rstd = ob2[n % 2], ss2[n % 2], rstd2[n % 2]
                    for hi in range(HVG):
                        kb.op(ACT, lambda: nc.scalar.activation(out=junk.t[:], in_=ob.t[:, hi, :], func=AF.Square, accum_out=ss.t[:, hi:hi + 1]), reads=[ob], writes=[junk, ss])
                    yield
                    kb.op(ACT, lambda: nc.scalar.activation(out=rstd.t[:], in_=ss.t[:], func=AF.Ln, scale=1.0 / 128.0, bias=eps6), reads=[ss, kb.eps_ln], writes=[rstd])
                    kb.op(ACT, lambda: nc.scalar.activation(out=rstd.t[:], in_=rstd.t[:], func=AF.Exp, scale=-0.5), reads=[rstd], writes=[rstd])
                    yield
                    for hi in range(HVG):
                        B = pb[hi]
                        kb.op(DVE, lambda: nc.vector.scalar_tensor_tensor(out=B.dntm.t[:], in0=ob.t[:, hi, :], scalar=rstd.t[:, hi:hi + 1], in1=zg.t[:, blk, hi * 128:(hi + 1) * 128],
                                                                          op0=ALU.mult, op1=ALU.mult), reads=[ob, rstd, zg], writes=[B.dntm])
                    yield
                    for hi in range(HVG):
                        B, sl = pb[hi], hsB[hi]
                        kb.op(PE, lambda: nc.tensor.transpose(sl[0].apb, B.dntm.t[:], ident_b), reads=[B.dntm, cmb], writes=[sl[0].b])
                        kb.op(ACT, lambda: nc.scalar.copy(out=dn_o.t[:, hi, cs], in_=sl[0].apb), reads=[sl[0].b], writes=[dn_o])
                    if blk == 3:
                        kb.dma(dnT_d[hv0:hv0 + HVG, :, t * 512:(t + 1) * 512].rearrange("c p t -> p c t"), dn_o.t[:], reads=[dn_o])
                    yield

                def run(gens, delays=None):
                    gens = list(gens)
                    delays = list(delays) if delays is not None else [0] * len(gens)
                    rnd = 0
                    while gens:
                        for i in range(len(gens) - 1, -1, -1):
                            pass
                        keep_g, keep_d = [], []
                        for g, d in zip(gens, delays):
                            if rnd < d:
                                keep_g.append(g); keep_d.append(d)
                                continue
                            try:
                                next(g)
                                keep_g.append(g); keep_d.append(d)
                            except StopIteration:
                                pass
                        gens, delays = keep_g, keep_d
                        rnd += 1

                STAG = getattr(build_program, "stagger", 0)
                load_h(0)
                if NT > 1:
                    load_h(1)
                stageA(0)
                hq_ops(0)
                run([pre_seq(hi, 0) for hi in range(HVG)])
                pend_norm = []
                for n in range(NCH):
                    gens = []
                    if n + 1 < NCH:
                        if (n + 1) % 4 == 0:
                            t1 = (n + 1) // 4
                            stageA(t1)
                            if t1 + 1 < NT:
                                load_h(t1 + 1)
                        hq_ops(n + 1)
                        gens = [pre_seq(hi, n + 1) for hi in range(HVG)]
                    posts = [post_seq(hi, n) for hi in range(HVG)]
                    run(posts + pend_norm + gens, [0] * (len(posts) + len(pend_norm)) + [STAG * (i // 2) for i in range(len(gens))])
                    pend_norm = [norm_seq(n)]
                run(pend_norm)
                kb.barrier()


def _consts():
    i = np.arange(128)
    ident = np.eye(128, dtype=np.float32)
    L = (i[:, None] <= i[None, :]).astype(np.float32)
    U = (i[:, None] > i[None, :]).astype(np.float32)
    mS = np.where(i[:, None] > i[None, :], 0.0, BIG).astype(np.float32)
    mIT = np.where(i[None, :] >= i[:, None], 0.0, BIG).astype(np.float32)
    blk = i // 64
    BD = (blk[:, None] == blk[None, :]).astype(np.float32)
    OFF = ((blk[:, None] == 1) & (blk[None, :] == 0)).astype(np.float32)
    ones = np.ones((128, 128), np.float32)
    return np.concatenate([ident, L, U, mS, mIT, BD, OFF, ones], axis=1)


def _fm(v, n):
    return np.ascontiguousarray(np.asarray(v, np.float32).reshape(n, 128).T)


def _band(w):
    full = np.zeros((DRNN, DRNN), np.float32)
    for b in range(16):
        full[b * 80:(b + 1) * 80, b * 80:(b + 1) * 80] = w[b]
    out = np.zeros((128, 10, 3, 128), np.float32)
    for m in range(10):
        for d in range(3):
            k = m + d - 1
            if 0 <= k < 10:
                out[:, m, d, :] = full[k * 128:(k + 1) * 128, m * 128:(m + 1) * 128]
    return out


def make_in_maps(inp, S, batches):
    l = 0
    f = lambda a: np.ascontiguousarray(np.asarray(a, np.float32))
    NCH = S // 128
    rows = np.zeros((128, NR), np.float32)
    b_ada = f(inp["b_ada"][l])
    def setr(name, v):
        o, w = _R[name]
        rows[:, o:o + w] = np.asarray(v, np.float32)[None, :]
    setr("b_gt1", b_ada[2048:3072]); setr("b_gt2", b_ada[5120:6144])
    setr("ln1_g", inp["ln1_g"][l]); setr("ln1_b", inp["ln1_b"][l]); setr("ln2_g", inp["ln2_g"][l]); setr("ln2_b", inp["ln2_b"][l])
    setr("nw", np.tile(f(inp["dn_norm_w"][l]), 16))
    hrep = np.zeros((128, 2, NCH * 16), np.float32)
    hrep[:, 0, :] = np.tile(f(inp["dn_dt_bias"][l]), NCH)[None, :]
    hrep[:, 1, :] = np.tile(f(inp["dn_a_log"][l]), NCH)[None, :]
    cmat = _consts()
    shared = dict(
        w_ada=f(inp["w_ada"][l]), w_in=f(inp["w_in"][l]), gate_a=_band(f(inp["rg_w_a"][l])), gate_x=_band(f(inp["rg_w_x"][l])),
        w_proj_a=f(inp["w_proj_a"][l]), w_proj_b=f(inp["w_proj_b"][l]), w_out=f(inp["w_out"][l]),
        ffn_w_gate=f(inp["ffn_w_gate"][l]), ffn_w_up=f(inp["ffn_w_up"][l]), ffn_w_down=f(inp["ffn_w_down"][l]),
        rows=rows, cmat=cmat, hrep=hrep)
    maps = []
    for b in batches:
        params = np.zeros((128, NP), np.float32)
        def setp(name, v):
            o, w = _P[name]
            params[:, o:o + w] = v
        setp("b_ada", _fm(b_ada, 48))
        setp("c", _fm(inp["c"][b], 8))
        setp("rg_cw", np.stack([_fm(inp["rg_conv_w"][l][k], 10) for k in range(4)], axis=2).reshape(128, 40))
        setp("rg_cb", _fm(inp["rg_conv_b"][l], 10)); setp("rg_ba", _fm(inp["rg_b_a"][l], 10)); setp("rg_bx", _fm(inp["rg_b_x"][l], 10))
        setp("rg_lam", _fm(inp["rg_lambda"][l], 10))
        setp("dn_cw", np.stack([_fm(inp["dn_conv_w"][l][k], 32) for k in range(4)], axis=2).reshape(128, 128))
        setp("ffn_cw", np.stack([_fm(inp["ffn_conv_w"][l][k], 22) for k in range(3)], axis=2).reshape(128, 66))
        setp("ffn_cb", _fm(inp["ffn_conv_b"][l], 22))
        xb = f(inp["x"][b])
        m = dict(shared)
        m["x"] = xb
        m["xT"] = np.ascontiguousarray(xb.T.reshape(8, 128, S))
        m["params"] = params
        maps.append(m)
    return maps


_NC_CACHE = {}


def kernel(**inputs):
    S = inputs["x"].shape[1]
    B = inputs["x"].shape[0]
    if S not in _NC_CACHE:
        _NC_CACHE[S] = build_program(S)
    nc = _NC_CACHE[S]
    maps = make_in_maps(inputs, S, list(range(B)))
    res = run_bass_kernel_spmd(nc, maps, core_ids=list(range(B)))
    return np.stack([np.asarray(r["out"], np.float32) for r in res.results], axis=0)
```

```python
import contextlib
import numpy as np
import concourse.bass as bass
import concourse.mybir as mybir
from concourse.bass_utils import run_bass_kernel_spmd

F32 = mybir.dt.float32
BF16 = mybir.dt.bfloat16
AF = mybir.ActivationFunctionType
ALU = mybir.AluOpType

D = 1024
DRNN = 1280
DFF = 2816
DIN = 10784
NHV = 16
C_XR, C_GR, C_Q, C_K, C_V, C_Z, C_A, C_B, C_GA, C_GB = 0, 1280, 2560, 3584, 4608, 6656, 8704, 8720, 8736, 9760
ALPHA = 2.0 ** 0.25
BIG = 32768.0
GELU_C = 1.5957691216057308

_P = {}
_off = 0
for _n, _w in [("b_ada", 48), ("c", 8), ("rg_cw", 40), ("rg_cb", 10), ("rg_ba", 10), ("rg_bx", 10), ("rg_lam", 10),
               ("dn_cw", 128), ("ffn_cw", 66), ("ffn_cb", 22)]:
    _P[_n] = (_off, _w)
    _off += _w
NP = _off
_R = {}
_off = 0
for _n, _w in [("b_gt1", 1024), ("b_gt2", 1024), ("ln1_g", 1024), ("ln1_b", 1024), ("ln2_g", 1024), ("ln2_b", 1024),
               ("nw", 2048)]:
    _R[_n] = (_off, _w)
    _off += _w
NR = _off
CM_IDENT, CM_L, CM_U, CM_MS, CM_MIT, CM_BD, CM_OFF, CM_ONES = range(8)
NCM = 8


class Buf:
    def __init__(self, kb, name, t):
        self.kb = kb
        self.name = name
        self.t = t
        self.w = None
        self.r = []
        self.lsem = None
        self.ssem = None
        self.excl = False

    def add_reader(self, tok):
        for i, (s, v) in enumerate(self.r):
            if s == tok[0]:
                if v < tok[1]:
                    self.r[i] = tok
                return
        self.r.append(tok)


class EngW:
    LIMIT = 24000

    def __init__(self, kb, eng, name):
        self.kb = kb
        self.eng = eng
        self.name = name
        self.sem = None
        self.cnt = 0
        self.seen = {}
        self.pending = []

    def wait(self, toks):
        best = {}
        for s, v in toks:
            if best.get(s, 0) < v:
                best[s] = v
        for s, v in best.items():
            if self.seen.get(s, 0) >= v:
                continue
            self.eng.wait_ge(self.kb.sems[s], v)
            self.seen[s] = v

    def bump(self, inst):
        if self.sem is None or self.cnt >= self.LIMIT:
            self.sem = self.kb.new_sem(f"e_{self.name}_{len(self.kb.sems)}")
            self.cnt = 0
        inst.then_inc(self.kb.sems[self.sem], 1)
        self.cnt += 1
        return (self.sem, self.cnt)

    def cur(self):
        return None if self.sem is None else (self.sem, self.cnt)


class KB:
    def __init__(self, nc, es):
        self.nc = nc
        self.es = es
        self.sems = []
        self.semcnt = []
        self.free_dma_sems = []
        self.pe = EngW(self, nc.tensor, "pe")
        self.act = EngW(self, nc.scalar, "act")
        self.dve = EngW(self, nc.vector, "dve")
        self.pool = EngW(self, nc.gpsimd, "pool")
        self.sp = EngW(self, nc.sync, "sp")
        self.engs = [self.pe, self.act, self.dve, self.pool, self.sp]
        self.dma_toks = {}
        self.phase_bufs = []
        self.n_inst = 0

    def new_sem(self, name):
        h = self.es.enter_context(self.nc.semaphore(name))
        self.sems.append(h)
        self.semcnt.append(0)
        return len(self.sems) - 1

    def dma_sem(self):
        if self.free_dma_sems:
            return self.free_dma_sems.pop()
        return self.new_sem(f"d_{len(self.sems)}")

    def sb(self, pes, name, shape, dt):
        self.n_alloc = getattr(self, "n_alloc", 0) + 1
        t = pes.enter_context(self.nc.sbuf_tensor(f"sb{self.n_alloc}_{name}", list(shape), dt))
        b = Buf(self, name, t)
        self.phase_bufs.append(b)
        return b

    def sbs(self, pes, name, shape, dt):
        b = self.sb(pes, name, shape, dt)
        b.sub = [Buf(self, f"{name}[{i}]", b.t) for i in range(shape[1])]
        self.phase_bufs.extend(b.sub)
        return b

    def ps(self, pes, name, shape, dt):
        self.n_alloc = getattr(self, "n_alloc", 0) + 1
        t = pes.enter_context(self.nc.psum_tensor(f"ps{self.n_alloc}_{name}", list(shape), dt))
        b = Buf(self, name, t)
        b.excl = True
        self.phase_bufs.append(b)
        return b

    def op(self, E, fn, reads=(), writes=(), inc=True):
        ex = [b for b in reads if b.excl]
        if ex:
            reads = [b for b in reads if not b.excl]
            writes = list(writes) + [b for b in ex if b not in writes]
        toks = []
        for b in reads:
            if b.w is not None:
                toks.append(b.w)
        for b in writes:
            if b.w is not None:
                toks.append(b.w)
            toks.extend(b.r)
        E.wait(toks)
        inst = fn()
        self.n_inst += 1
        if inc:
            tok = E.bump(inst)
            sets = E.pending + [(reads, writes)]
            E.pending = []
            for rd, wr in sets:
                for b in rd:
                    b.add_reader(tok)
                for b in wr:
                    b.w = tok
                    b.r = []
        else:
            E.pending.append((reads, writes))
        return inst

    def dma(self, out_ap, in_ap, reads=(), writes=(), Q=None):
        Q = Q or self.sp
        toks = []
        for b in reads:
            if b.w is not None:
                toks.append(b.w)
        for b in writes:
            if b.w is not None:
                toks.append(b.w)
            toks.extend(b.r)
        Q.wait(toks)
        inst = Q.eng.dma_start(out=out_ap, in_=in_ap)
        self.n_inst += 1
        if writes:
            tb = writes[0]
            if tb.lsem is None:
                tb.lsem = self.dma_sem()
            s = tb.lsem
        else:
            tb = reads[0]
            if tb.ssem is None:
                tb.ssem = self.dma_sem()
            s = tb.ssem
        self.semcnt[s] += 16
        inst.then_inc(self.sems[s], 16)
        tok = (s, self.semcnt[s])
        self.dma_toks[s] = tok
        for b in writes:
            b.w = tok
            b.r = []
        for b in reads:
            b.add_reader(tok)
        return tok

    def barrier(self, keep=()):
        toks = list(self.dma_toks.values())
        for E in self.engs:
            assert not E.pending
            c = E.cur()
            if c is not None:
                toks.append(c)
        for E in self.engs:
            E.wait(toks)
        for b in self.phase_bufs:
            if b.lsem is not None:
                self.free_dma_sems.append(b.lsem)
                b.lsem = None
            if b.ssem is not None:
                self.free_dma_sems.append(b.ssem)
                b.ssem = None
        self.dma_toks = {}

    def final_wait(self):
        toks = list(self.dma_toks.values())
        self.sp.wait(toks)


def _cm(cm, idx, dt=None):
    return cm.t[:, idx * 128:(idx + 1) * 128]


def build_program(S, debug=False):
    assert S % 512 == 0
    NT = S // 512
    NCH = S // 128
    nc = bass.Bass("TRN2", target_bir_lowering=False)
    okind = "ExternalOutput" if debug else "Internal"

    def din(name, shape, dt=F32):
        return nc.dram_tensor(name, list(shape), dt, kind="ExternalInput").ap()

    xT_d = din("xT", [8, 128, S])
    x_d = din("x", [S, D])
    w_ada_d = din("w_ada", [D, 6 * D])
    w_in_d = din("w_in", [D, DIN])
    gate_a_d = din("gate_a", [128, 10, 3, 128])
    gate_x_d = din("gate_x", [128, 10, 3, 128])
    w_pa_d = din("w_proj_a", [DRNN, D])
    w_pb_d = din("w_proj_b", [2048, D])
    w_out_d = din("w_out", [D, D])
    w_fg_d = din("ffn_w_gate", [D, DFF])
    w_fu_d = din("ffn_w_up", [D, DFF])
    w_fd_d = din("ffn_w_down", [DFF, D])
    params_d = din("params", [128, NP])
    rows_d = din("rows", [128, NR])
    cmat_d = din("cmat", [128, NCM * 128])
    hrep_d = din("hrep", [128, 2, NCH * 16])
    out_d = nc.dram_tensor("out", [S, D], F32, kind="ExternalOutput").ap()
    hT_d = nc.dram_tensor("hT_s", [8, 128, S], BF16, kind=okind).ap()
    recT_d = nc.dram_tensor("recT_s", [10, 128, S], BF16, kind=okind).ap()
    dnT_d = nc.dram_tensor("dnT_s", [16, 128, S], BF16, kind=okind).ap()
    x1_d = nc.dram_tensor("x1_s", [S, D], F32, kind=okind).ap()

    with contextlib.ExitStack() as es:
        kb = KB(nc, es)
        PE, ACT, DVE, POOL = kb.pe, kb.act, kb.dve, kb.pool
        params = kb.sb(es, "params", [128, NP], F32)
        cm = kb.sb(es, "cm", [128, NCM * 128], F32)
        cmb = kb.sb(es, "cmb", [128, NCM * 128], BF16)
        ada = kb.sb(es, "ada", [128, 48], F32)
        gt1r = kb.sb(es, "gt1r", [128, 1024], F32)
        gt2r = kb.sb(es, "gt2r", [128, 1024], F32)
        kb.dma(params.t[:], params_d[:, :], writes=[params])
        kb.dma(cm.t[:], cmat_d[:, :], writes=[cm])
        kb.op(ACT, lambda: nc.scalar.copy(out=cmb.t[:], in_=cm.t[:]), reads=[cm], writes=[cmb])

        def pcol(name, j=0, n=1):
            o, w = _P[name]
            return params.t[:, o + j:o + j + n]

        ident_f = cm.t[:, CM_IDENT * 128:(CM_IDENT + 1) * 128]
        ident_b = cmb.t[:, CM_IDENT * 128:(CM_IDENT + 1) * 128]
        ones_f = cm.t[:, CM_ONES * 128:(CM_ONES + 1) * 128]
        ones_b = cmb.t[:, CM_ONES * 128:(CM_ONES + 1) * 128]

        def load_w_bf16(pes, dst, dram, r0, nk, c0, ncols, stg, dst_c0=0):
            PW = stg[0].t.shape[-1]
            i = load_w_bf16.cnt
            for cc in range(0, ncols, PW):
                w = min(PW, ncols - cc)
                st = stg[i % len(stg)]
                src = dram[r0:r0 + nk * 128, c0 + cc:c0 + cc + w].rearrange("(k p) c -> p k c", p=128)
                kb.dma(st.t[:, 0:nk, 0:w], src, writes=[st])
                o = dst.t[:, 0:nk, dst_c0 + cc:dst_c0 + cc + w]
                if i % 2 == 0:
                    kb.op(ACT, lambda o=o, st=st, w=w: nc.scalar.copy(out=o, in_=st.t[:, 0:nk, 0:w]), reads=[st], writes=[dst])
                else:
                    kb.op(DVE, lambda o=o, st=st, w=w: nc.vector.tensor_copy(out=o, in_=st.t[:, 0:nk, 0:w]), reads=[st], writes=[dst])
                i += 1
            load_w_bf16.cnt = i
        load_w_bf16.cnt = 0

        with contextlib.ExitStack() as pes:
            scf = kb.sb(pes, "scf", [128, 8], F32)
            screp = kb.sb(pes, "screp", [128, 8, 128], F32)
            wst = [kb.sb(pes, f"wst{i}", [128, 8, 1024], F32) for i in range(2)]
            rows01 = kb.sb(pes, "rows01", [128, 2048], F32)
            pa = kb.ps(pes, "pa", [128, 512], F32)
            pb = [kb.ps(pes, f"pb{i}", [128, 512], F32) for i in range(2)]
            kb.dma(rows01.t[:], rows_d[:, 0:2048], writes=[rows01])
            kb.op(ACT, lambda: nc.scalar.activation(out=scf.t[:], in_=pcol("c", 0, 8), func=AF.Silu), reads=[params], writes=[scf])
            for k in range(8):
                kb.op(DVE, lambda k=k: nc.vector.tensor_copy(out=screp.t[:, k, :], in_=scf.t[:, k:k + 1].to_broadcast([128, 128])),
                      reads=[scf], writes=[screp])
            for jb in range(6):
                st = wst[jb % 2]
                kb.dma(st.t[:], w_ada_d[:, jb * 1024:(jb + 1) * 1024].rearrange("(k p) c -> p k c", p=128), writes=[st])
                if jb in (2, 5):
                    dst = gt1r if jb == 2 else gt2r
                    for cb in range(2):
                        for k in range(8):
                            kb.op(PE, lambda k=k, cb=cb, st=st: nc.tensor.matmul(pb[cb].t[:], lhsT=screp.t[:, k, :], rhs=st.t[:, k, cb * 512:(cb + 1) * 512],
                                                                               start=(k == 0), stop=(k == 7)),
                                  reads=[screp, st], writes=[pb[cb]], inc=(k == 7))
                        ro = (0 if jb == 2 else 1024) + cb * 512
                        kb.op(DVE, lambda cb=cb, ro=ro, dst=dst: nc.vector.scalar_tensor_tensor(
                            out=dst.t[:, cb * 512:(cb + 1) * 512], in0=pb[cb].t[:], scalar=1.0, in1=rows01.t[:, ro:ro + 512],
                            op0=ALU.add, op1=ALU.add), reads=[pb[cb], rows01], writes=[dst])
                for j in range(8):
                    col = jb * 8 + j
                    for k in range(8):
                        kb.op(PE, lambda k=k, j=j, col=col, st=st: nc.tensor.matmul(pa.t[:, col:col + 1], lhsT=st.t[:, k, j * 128:(j + 1) * 128],
                                                                                 rhs=scf.t[:, k:k + 1], start=(k == 0), stop=(k == 7)),
                              reads=[scf, st], writes=[pa], inc=(k == 7))
            kb.op(DVE, lambda: nc.vector.tensor_tensor(out=ada.t[:], in0=pa.t[:, 0:48], in1=pcol("b_ada", 0, 48), op=ALU.add),
                  reads=[pa, params], writes=[ada])
            for o in (8, 32):
                kb.op(DVE, lambda o=o: nc.vector.tensor_scalar(out=ada.t[:, o:o + 8], in0=ada.t[:, o:o + 8], scalar1=1.0, scalar2=None, op0=ALU.add),
                      reads=[ada], writes=[ada])
            xin = [kb.sb(pes, f"xin{i}", [128, 8, 512], F32) for i in range(2)]
            hto = [kb.sb(pes, f"hto{i}", [128, 8, 512], BF16) for i in range(2)]
            for t in range(NT):
                xi, ho = xin[t % 2], hto[t % 2]
                kb.dma(xi.t[:], xT_d[:, :, t * 512:(t + 1) * 512].rearrange("c p t -> p c t"), writes=[xi])
                for c in range(8):
                    if c % 2 == 0:
                        kb.op(ACT, lambda c=c, xi=xi, ho=ho: nc.scalar.activation(out=ho.t[:, c, :], in_=xi.t[:, c, :], func=AF.Identity,
                                                                              scale=ada.t[:, 8 + c:9 + c], bias=ada.t[:, c:c + 1]),
                              reads=[xi, ada], writes=[ho])
                    else:
                        kb.op(DVE, lambda c=c, xi=xi, ho=ho: nc.vector.tensor_scalar(out=ho.t[:, c, :], in0=xi.t[:, c, :], scalar1=ada.t[:, 8 + c:9 + c],
                                                                                 scalar2=ada.t[:, c:c + 1], op0=ALU.mult, op1=ALU.add),
                              reads=[xi, ada], writes=[ho])
                kb.dma(hT_d[:, :, t * 512:(t + 1) * 512].rearrange("c p t -> p c t"), ho.t[:], reads=[ho])
            kb.barrier()

        PHASES = build_program.phases
        eps_ln = kb.sb(es, "eps_ln", [128, 2], F32)
        kb.eps_ln = eps_ln
        kb.op(DVE, lambda: nc.vector.memset(eps_ln.t[:, 0:1], 1e-5), writes=[eps_ln])
        kb.op(DVE, lambda: nc.vector.memset(eps_ln.t[:, 1:2], 1e-6), writes=[eps_ln])
        if "rg" in PHASES:
            phase_rg(nc, kb, S, params, pcol, cm, cmb, load_w_bf16, w_in_d, gate_a_d, gate_x_d, hT_d, recT_d)
        if "gdn" in PHASES:
            phase_gdn(nc, kb, S, params, pcol, cm, cmb, load_w_bf16, w_in_d, rows_d, hrep_d, hT_d, dnT_d)
        if "mix" in PHASES:
            phase_mix(nc, kb, S, load_w_bf16, gt1r, w_in_d, w_pa_d, w_pb_d, w_out_d, rows_d, x_d, hT_d, recT_d, dnT_d, x1_d)
        if "ffn" in PHASES:
            phase_ffn(nc, kb, S, params, pcol, cm, load_w_bf16, ada, gt2r, w_fg_d, w_fu_d, w_fd_d, rows_d, x1_d, out_d)
        kb.final_wait()
    return nc


build_program.phases = ("rg", "gdn", "mix", "ffn")


def gelu_tanh(nc, kb, src_ap, src_bufs, out_ap, out_bufs, tmp, tmp2=None, mul_ap=None, mul_bufs=(), defer=None):
    ACT, DVE = kb.act, kb.dve
    if mul_ap is None:
        kb.op(ACT, lambda: nc.scalar.activation(out=out_ap, in_=src_ap, func=AF.Gelu_apprx_tanh), reads=src_bufs, writes=out_bufs)
    else:
        kb.op(ACT, lambda: nc.scalar.activation(out=tmp.t[:], in_=src_ap, func=AF.Gelu_apprx_tanh), reads=src_bufs, writes=[tmp])
        def fin():
            kb.op(DVE, lambda: nc.vector.tensor_tensor(out=out_ap, in0=tmp.t[:], in1=mul_ap, op=ALU.mult), reads=[tmp] + list(mul_bufs), writes=out_bufs)
        if defer is None:
            fin()
        else:
            defer.append(fin)


def phase_rg(nc, kb, S, params, pcol, cm, cmb, load_w_bf16, w_in_d, gate_a_d, gate_x_d, hT_d, recT_d):
    PE, ACT, DVE, POOL = kb.pe, kb.act, kb.dve, kb.pool
    TT = 256
    NT = S // TT
    with contextlib.ExitStack() as pes:
        wx = kb.sb(pes, "wx", [128, 8, 2560], BF16)
        wga = kb.sb(pes, "wga", [128, 10, 3, 128], BF16)
        wgx = kb.sb(pes, "wgx", [128, 10, 3, 128], BF16)
        cl = kb.sb(pes, "cl", [128, 20], F32)
        with contextlib.ExitStack() as ses:
            stg = [kb.sb(ses, f"stg{i}", [128, 8, 512], F32) for i in range(2)]
            gst = kb.sb(ses, "gst", [128, 10, 3, 128], F32)
            load_w_bf16(ses, wx, w_in_d, 0, 8, 0, 2560, stg)
            kb.dma(gst.t[:], gate_a_d[:, :, :, :], writes=[gst])
            kb.op(ACT, lambda: nc.scalar.copy(out=wga.t[:], in_=gst.t[:]), reads=[gst], writes=[wga])
            kb.dma(gst.t[:], gate_x_d[:, :, :, :], writes=[gst])
            kb.op(ACT, lambda: nc.scalar.copy(out=wgx.t[:], in_=gst.t[:]), reads=[gst], writes=[wgx])
            kb.barrier(keep=[wx, wga, wgx, cl])
        kb.op(ACT, lambda: nc.scalar.activation(out=cl.t[:, 0:10], in_=pcol("rg_lam", 0, 10), func=AF.Exp, scale=-1.0), reads=[params], writes=[cl])
        kb.op(ACT, lambda: nc.scalar.activation(out=cl.t[:, 0:10], in_=cl.t[:, 0:10], func=AF.Ln, bias=1.0), reads=[cl], writes=[cl])
        kb.op(DVE, lambda: nc.vector.tensor_scalar(out=cl.t[:, 10:20], in0=cl.t[:, 0:10], scalar1=-16.0, scalar2=None, op0=ALU.mult), reads=[cl], writes=[cl])
        kb.op(DVE, lambda: nc.vector.tensor_scalar(out=cl.t[:, 0:10], in0=cl.t[:, 0:10], scalar1=-8.0, scalar2=None, op0=ALU.mult), reads=[cl], writes=[cl])

        hT = [kb.sb(pes, f"hT{i}", [128, 8, TT], BF16) for i in range(2)]
        raw = [kb.sb(pes, f"raw{i}", [128, TT + 3], F32) for i in range(3)]
        halo = kb.sbs(pes, "halo", [128, 10, 3], F32)
        hst = kb.sbs(pes, "hst", [128, 10, 1], F32)
        xr2 = [kb.sbs(pes, f"xr{i}", [128, 10, TT], F32) for i in range(2)]
        xrb2 = [kb.sbs(pes, f"xrb{i}", [128, 10, TT], BF16) for i in range(2)]
        ga = kb.sbs(pes, "ga", [128, 10, TT], F32)
        gi = kb.sbs(pes, "gi", [128, 10, TT], F32)
        mu = kb.sbs(pes, "mu", [128, 10, TT], F32)
        hh = kb.sbs(pes, "hh", [128, 10, TT], F32)
        gg2 = [kb.sbs(pes, f"gg{i}", [128, 10, TT], F32) for i in range(2)]
        tmp = [kb.sb(pes, f"gtmp{i}", [128, TT], F32) for i in range(3)]
        rec = [kb.sbs(pes, f"rec{i}", [128, 10, TT], BF16) for i in range(2)]
        pp = [kb.ps(pes, f"pp{i}", [128, 512], F32) for i in range(6)]
        kb.op(DVE, lambda: nc.vector.memset(halo.t[:], 0.0), writes=halo.sub)
        kb.op(DVE, lambda: nc.vector.memset(hst.t[:], 0.0), writes=hst.sub)
        cwo = _P["rg_cw"][0]

        def load_h(t):
            kb.dma(hT[t % 2].t[:], hT_d[:, :, t * TT:(t + 1) * TT].rearrange("c p t -> p c t"), writes=[hT[t % 2]])

        load_h(0)
        if NT > 1:
            load_h(1)
        pi_ = [0]

        def front(t):
            xr, xrb, gg = xr2[t % 2], xrb2[t % 2], gg2[t % 2]
            pi = pi_[0]
            h = hT[t % 2]
            for j in range(10):
                p = pp[pi % 6]; pi += 1
                for k in range(8):
                    kb.op(PE, lambda: nc.tensor.matmul(p.t[:, 0:TT], lhsT=wx.t[:, k, j * 128:(j + 1) * 128], rhs=h.t[:, k, :],
                                                       start=(k == 0), stop=(k == 7)), reads=[wx, h], writes=[p], inc=(k == 7))
                rw = raw[j % 3]
                kb.op(ACT, lambda: nc.scalar.copy(out=rw.t[:, 0:3], in_=halo.t[:, j, :]), reads=[halo.sub[j]], writes=[rw])
                kb.op(ACT, lambda: nc.scalar.copy(out=rw.t[:, 3:TT + 3], in_=p.t[:, 0:TT]), reads=[p], writes=[rw])
                kb.op(ACT, lambda: nc.scalar.copy(out=halo.t[:, j, :], in_=rw.t[:, TT:TT + 3]), reads=[rw], writes=[halo.sub[j]])
                kb.op(DVE, lambda: nc.vector.tensor_scalar(out=xr.t[:, j, :], in0=rw.t[:, 3:TT + 3], scalar1=params.t[:, cwo + j * 4 + 3:cwo + j * 4 + 4],
                                                           scalar2=pcol("rg_cb", j), op0=ALU.mult, op1=ALU.add), reads=[rw, params], writes=[xr.sub[j]])
                for kk in range(3):
                    kb.op(DVE, lambda: nc.vector.scalar_tensor_tensor(
                        out=xr.t[:, j, :], in0=rw.t[:, kk:kk + TT], scalar=params.t[:, cwo + j * 4 + kk:cwo + j * 4 + kk + 1], in1=xr.t[:, j, :],
                        op0=ALU.mult, op1=ALU.add), reads=[rw, params, xr.sub[j]], writes=[xr.sub[j]])
            for j in range(10):
                p = pp[pi % 6]; pi += 1
                for k in range(8):
                    kb.op(PE, lambda: nc.tensor.matmul(p.t[:, 0:TT], lhsT=wx.t[:, k, 1280 + j * 128:1280 + (j + 1) * 128], rhs=h.t[:, k, :],
                                                       start=(k == 0), stop=(k == 7)), reads=[wx, h], writes=[p], inc=(k == 7))
                kb.op(ACT, lambda: nc.scalar.activation(out=gg.t[:, j, :], in_=p.t[:, 0:TT], func=AF.Gelu_apprx_tanh), reads=[p], writes=[gg.sub[j]])
            for j in range(10):
                kb.op(ACT, lambda: nc.scalar.copy(out=xrb.t[:, j, :], in_=xr.t[:, j, :]), reads=[xr.sub[j]], writes=[xrb.sub[j]])
            pi_[0] = pi

        def back(t):
            xr, xrb, gg = xr2[t % 2], xrb2[t % 2], gg2[t % 2]
            pi = pi_[0]
            for (wg, dst, bname) in ((wga, ga, "rg_ba"), (wgx, gi, "rg_bx")):
                for m in range(10):
                    p = pp[pi % 6]; pi += 1
                    ks = [k for k in (m - 1, m, m + 1) if 0 <= k < 10]
                    for n_, k in enumerate(ks):
                        kb.op(PE, lambda: nc.tensor.matmul(
                            p.t[:, 0:TT], lhsT=wg.t[:, m, k - m + 1, :], rhs=xrb.t[:, k, :], start=(n_ == 0), stop=(n_ == len(ks) - 1)),
                            reads=[wg, xrb.sub[k]], writes=[p], inc=(n_ == len(ks) - 1))
                    kb.op(ACT, lambda: nc.scalar.activation(out=dst.t[:, m, :], in_=p.t[:, 0:TT], func=AF.Sigmoid, bias=pcol(bname, m)),
                          reads=[p, params], writes=[dst.sub[m]])
            for m in range(10):
                kb.op(ACT, lambda: nc.scalar.activation(out=mu.t[:, m, :], in_=ga.t[:, m, :], func=AF.Exp, scale=cl.t[:, 10 + m:11 + m]), reads=[ga.sub[m], cl], writes=[mu.sub[m]])
                kb.op(ACT, lambda: nc.scalar.activation(out=ga.t[:, m, :], in_=ga.t[:, m, :], func=AF.Exp, scale=cl.t[:, m:m + 1]), reads=[ga.sub[m], cl], writes=[ga.sub[m]])
            for m in range(10):
                kb.op(DVE, lambda: nc.vector.tensor_scalar(out=mu.t[:, m, :], in0=mu.t[:, m, :], scalar1=1.0, scalar2=None, op0=ALU.min), reads=[mu.sub[m]], writes=[mu.sub[m]])
            for m in range(10):
                kb.op(ACT, lambda: nc.scalar.activation(out=mu.t[:, m, :], in_=mu.t[:, m, :], func=AF.Sqrt, scale=-1.0, bias=1.0), reads=[mu.sub[m]], writes=[mu.sub[m]])
            for m in range(10):
                kb.op(POOL, lambda: nc.gpsimd.tensor_tensor(out=gi.t[:, m, :], in0=gi.t[:, m, :], in1=xr.t[:, m, :], op=ALU.mult), reads=[gi.sub[m], xr.sub[m]], writes=[gi.sub[m]])
                kb.op(DVE, lambda: nc.vector.tensor_tensor(out=gi.t[:, m, :], in0=gi.t[:, m, :], in1=mu.t[:, m, :], op=ALU.mult), reads=[gi.sub[m], mu.sub[m]], writes=[gi.sub[m]])
                kb.op(DVE, lambda: nc.vector.tensor_tensor_scan(out=hh.t[:, m, :], data0=ga.t[:, m, :], data1=gi.t[:, m, :], initial=hst.t[:, m, :],
                                                                op0=ALU.mult, op1=ALU.add), reads=[ga.sub[m], gi.sub[m], hst.sub[m]], writes=[hh.sub[m]])
                kb.op(ACT, lambda: nc.scalar.copy(out=hst.t[:, m, :], in_=hh.t[:, m, TT - 1:TT]), reads=[hh.sub[m]], writes=[hst.sub[m]])
            rc = rec[t % 2]
            for j in range(10):
                kb.op(DVE, lambda: nc.vector.tensor_tensor(out=rc.t[:, j, :], in0=hh.t[:, j, :], in1=gg.t[:, j, :], op=ALU.mult), reads=[hh.sub[j], gg.sub[j]], writes=[rc.sub[j]])
            kb.dma(recT_d[:, :, t * TT:(t + 1) * TT].rearrange("c p t -> p c t"), rc.t[:], reads=rc.sub)
            pi_[0] = pi

        front(0)
        for t in range(NT):
            if t + 1 < NT:
                front(t + 1)
                if t + 2 < NT:
                    load_h(t + 2)
            back(t)
        kb.barrier()


def layer_norm_rows(nc, kb, y, nblk, g_ap, b_ap, rowbuf, stats, mv, sd, out_buf):
    ACT, DVE, POOL = kb.act, kb.dve, kb.pool
    for blk in range(nblk):
        for hf in range(2):
            kb.op(DVE, lambda: nc.vector.bn_stats(out=stats.t[:, blk, hf, :], in_=y.t[:, blk, hf * 512:(hf + 1) * 512]), reads=[y], writes=[stats])
        kb.op(DVE, lambda: nc.vector.bn_aggr(out=mv.t[:, blk, :], in_=stats.t[:, blk, :, :].rearrange("p a b -> p (a b)")), reads=[stats], writes=[mv])
        kb.op(ACT, lambda: nc.scalar.activation(out=sd.t[:, blk:blk + 1], in_=mv.t[:, blk, 1:2], func=AF.Ln, bias=kb.eps_ln.t[:, 0:1]), reads=[mv, kb.eps_ln], writes=[sd])
        kb.op(ACT, lambda: nc.scalar.activation(out=sd.t[:, blk:blk + 1], in_=sd.t[:, blk:blk + 1], func=AF.Exp, scale=-0.5), reads=[sd], writes=[sd])
        kb.op(DVE, lambda: nc.vector.tensor_scalar(out=y.t[:, blk, :], in0=y.t[:, blk, :], scalar1=mv.t[:, blk, 0:1], scalar2=sd.t[:, blk:blk + 1],
                                                   op0=ALU.subtract, op1=ALU.mult), reads=[y, mv, sd], writes=[y])
        kb.op(POOL, lambda: nc.gpsimd.tensor_tensor(out=y.t[:, blk, :], in0=y.t[:, blk, :], in1=g_ap, op=ALU.mult), reads=[y, rowbuf], writes=[y])
        kb.op(DVE, lambda: nc.vector.tensor_tensor(out=out_buf.t[:, blk, :], in0=y.t[:, blk, :], in1=b_ap, op=ALU.add), reads=[y, rowbuf], writes=[out_buf])


def phase_mix(nc, kb, S, load_w_bf16, gt1r, w_in_d, w_pa_d, w_pb_d, w_out_d, rows_d, x_d, hT_d, recT_d, dnT_d, x1_d):
    PE, ACT, DVE, POOL = kb.pe, kb.act, kb.dve, kb.pool
    TT = 256
    NT = S // TT
    NB = TT // 128
    with contextlib.ExitStack() as pes:
        wg = kb.sb(pes, "wg", [128, 8, 2048], BF16)
        wpa = kb.sb(pes, "wpa", [128, 10, 1024], BF16)
        wpb = kb.sb(pes, "wpb", [128, 16, 1024], BF16)
        wo = kb.sb(pes, "wo", [128, 8, 1024], BF16)
        lnr = kb.sb(pes, "lnr", [128, 2048], F32)
        with contextlib.ExitStack() as ses:
            stg = [kb.sb(ses, f"stg{i}", [128, 16, 256], F32) for i in range(2)]
            load_w_bf16(ses, wg, w_in_d, 0, 8, C_GA, 2048, stg)
            load_w_bf16(ses, wpa, w_pa_d, 0, 10, 0, 1024, stg)
            load_w_bf16(ses, wpb, w_pb_d, 0, 16, 0, 1024, stg)
            load_w_bf16(ses, wo, w_out_d, 0, 8, 0, 1024, stg)
            kb.dma(lnr.t[:], rows_d[:, _R["ln1_g"][0]:_R["ln1_g"][0] + 2048], writes=[lnr])
            kb.barrier(keep=[wg, wpa, wpb, wo, lnr])
        hT = [kb.sb(pes, f"hT{i}", [128, 8, TT], BF16) for i in range(2)]
        rcT = [kb.sb(pes, f"rcT{i}", [128, 10, TT], BF16) for i in range(2)]
        dnT = [kb.sb(pes, f"dnT{i}", [128, 16, TT], BF16) for i in range(2)]
        xt = [kb.sb(pes, f"xt{i}", [128, NB, 1024], F32) for i in range(2)]
        mg = kb.sbs(pes, "mg", [128, 8, TT], BF16)
        sg = [kb.sb(pes, f"sg{i}", [128, TT], F32) for i in range(4)]
        t1 = [kb.sb(pes, f"t1{i}", [128, TT], F32) for i in range(4)]
        y = kb.sb(pes, "y", [128, NB, 1024], F32)
        xo = [y, y]
        stats = kb.sb(pes, "stats", [128, NB, 2, 6], F32)
        mv = kb.sb(pes, "mv", [128, NB, 2], F32)
        sd = kb.sb(pes, "sd", [128, NB], F32)
        pp = [kb.ps(pes, f"pp{i}", [128, 512], F32) for i in range(8)]

        def load(t):
            i = t % 2
            sl = slice(t * TT, (t + 1) * TT)
            kb.dma(hT[i].t[:], hT_d[:, :, sl].rearrange("c p t -> p c t"), writes=[hT[i]])
            kb.dma(rcT[i].t[:], recT_d[:, :, sl].rearrange("c p t -> p c t"), writes=[rcT[i]])
            kb.dma(dnT[i].t[:], dnT_d[:, :, sl].rearrange("c p t -> p c t"), writes=[dnT[i]])
            kb.dma(xt[i].t[:], x_d[sl, :].rearrange("(b p) f -> p b f", p=128), writes=[xt[i]])

        load(0)
        pi = 0
        for t in range(NT):
            if t + 1 < NT:
                load(t + 1)
            i = t % 2
            h, rc, dn, xx = hT[i], rcT[i], dnT[i], xt[i]
            deferred = []
            for m in range(8):
                ms = slice(m * 128, (m + 1) * 128)
                pga = pp[pi % 8]; pya = pp[(pi + 1) % 8]; pgb = pp[(pi + 2) % 8]; pyb = pp[(pi + 3) % 8]; pi += 4
                for k in range(8):
                    kb.op(PE, lambda: nc.tensor.matmul(pga.t[:, 0:TT], lhsT=wg.t[:, k, m * 128:(m + 1) * 128], rhs=h.t[:, k, :], start=(k == 0), stop=(k == 7)),
                          reads=[wg, h], writes=[pga], inc=(k == 7))
                for k in range(10):
                    kb.op(PE, lambda: nc.tensor.matmul(pya.t[:, 0:TT], lhsT=wpa.t[:, k, ms], rhs=rc.t[:, k, :], start=(k == 0), stop=(k == 9)),
                          reads=[wpa, rc], writes=[pya], inc=(k == 9))
                for k in range(8):
                    kb.op(PE, lambda: nc.tensor.matmul(pgb.t[:, 0:TT], lhsT=wg.t[:, k, 1024 + m * 128:1024 + (m + 1) * 128], rhs=h.t[:, k, :], start=(k == 0), stop=(k == 7)),
                          reads=[wg, h], writes=[pgb], inc=(k == 7))
                for k in range(16):
                    kb.op(PE, lambda: nc.tensor.matmul(pyb.t[:, 0:TT], lhsT=wpb.t[:, k, ms], rhs=dn.t[:, k, :], start=(k == 0), stop=(k == 15)),
                          reads=[wpb, dn], writes=[pyb], inc=(k == 15))
                s0, s1, ta, tb = sg[2 * (m % 2)], sg[2 * (m % 2) + 1], t1[2 * (m % 2)], t1[2 * (m % 2) + 1]
                kb.op(ACT, lambda: nc.scalar.activation(out=s0.t[:], in_=pga.t[:, 0:TT], func=AF.Sigmoid), reads=[pga], writes=[s0])
                kb.op(ACT, lambda: nc.scalar.activation(out=s1.t[:], in_=pgb.t[:, 0:TT], func=AF.Sigmoid), reads=[pgb], writes=[s1])
                def fin(m=m, s0=s0, s1=s1, ta=ta, tb=tb, pya=pya, pyb=pyb):
                    kb.op(DVE, lambda: nc.vector.tensor_tensor(out=ta.t[:], in0=s0.t[:], in1=pya.t[:, 0:TT], op=ALU.mult), reads=[s0, pya], writes=[ta])
                    kb.op(DVE, lambda: nc.vector.tensor_tensor(out=tb.t[:], in0=s1.t[:], in1=pyb.t[:, 0:TT], op=ALU.mult), reads=[s1, pyb], writes=[tb])
                    kb.op(POOL, lambda: nc.gpsimd.tensor_tensor(out=mg.t[:, m, :], in0=ta.t[:], in1=tb.t[:], op=ALU.add), reads=[ta, tb], writes=[mg.sub[m]])
                prev = list(deferred); del deferred[:]
                deferred.append(fin)
                for f_ in prev:
                    f_()
            for f_ in deferred:
                f_()
            del deferred[:]
            for blk in range(NB):
                for cb in range(2):
                    p = pp[pi % 8]; pi += 1
                    cs = slice(cb * 512, (cb + 1) * 512)
                    for k in range(8):
                        kb.op(PE, lambda: nc.tensor.matmul(p.t[:], lhsT=mg.t[:, k, blk * 128:(blk + 1) * 128], rhs=wo.t[:, k, cs], start=(k == 0), stop=(k == 7)),
                              reads=[mg.sub[k], wo], writes=[p], inc=(k == 7))
                    kb.op(DVE, lambda: nc.vector.tensor_tensor(out=y.t[:, blk, cs], in0=p.t[:], in1=gt1r.t[:, cs], op=ALU.mult), reads=[p, gt1r], writes=[y])
                    kb.op(DVE, lambda: nc.vector.scalar_tensor_tensor(out=y.t[:, blk, cs], in0=xx.t[:, blk, cs], scalar=ALPHA, in1=y.t[:, blk, cs],
                                                                      op0=ALU.mult, op1=ALU.add), reads=[xx, y], writes=[y])
            o = xo[i]
            layer_norm_rows(nc, kb, y, NB, lnr.t[:, 0:1024], lnr.t[:, 1024:2048], lnr, stats, mv, sd, o)
            kb.dma(x1_d[t * TT:(t + 1) * TT, :].rearrange("(b p) f -> p b f", p=128), o.t[:], reads=[o])
        kb.barrier()


def phase_ffn(nc, kb, S, params, pcol, cm, load_w_bf16, ada, gt2r, w_fg_d, w_fu_d, w_fd_d, rows_d, x1_d, out_d):
    PE, ACT, DVE, POOL = kb.pe, kb.act, kb.dve, kb.pool
    TT = 256
    NT = S // TT
    NB = TT // 128
    ident_f = cm.t[:, CM_IDENT * 128:(CM_IDENT + 1) * 128]
    with contextlib.ExitStack() as pes:
        wfg = kb.sb(pes, "wfg", [128, 8, DFF], BF16)
        wfu = kb.sb(pes, "wfu", [128, 8, DFF], BF16)
        wfd = kb.sb(pes, "wfd", [128, 22, 1024], BF16)
        lnr = kb.sb(pes, "lnr", [128, 2048], F32)
        with contextlib.ExitStack() as ses:
            stg = [kb.sb(ses, f"stg{i}", [128, 22, 128], F32) for i in range(2)]
            load_w_bf16(ses, wfg, w_fg_d, 0, 8, 0, DFF, stg)
            load_w_bf16(ses, wfu, w_fu_d, 0, 8, 0, DFF, stg)
            load_w_bf16(ses, wfd, w_fd_d, 0, 22, 0, 1024, stg)
            kb.dma(lnr.t[:], rows_d[:, _R["ln2_g"][0]:_R["ln2_g"][0] + 2048], writes=[lnr])
            kb.barrier(keep=[wfg, wfu, wfd, lnr])
        xt = [kb.sb(pes, f"xt{i}", [128, NB, 1024], F32) for i in range(2)]
        h2 = kb.sbs(pes, "h2", [128, 8, TT], BF16)
        act = kb.sbs(pes, "act", [128, 22, TT], BF16)
        raw = [kb.sb(pes, f"raw{i}", [128, TT + 2], F32) for i in range(3)]
        cv = [kb.sb(pes, f"cv{i}", [128, TT], F32) for i in range(3)]
        tmp = [kb.sb(pes, f"tmp{i}", [128, TT], F32) for i in range(3)]
        halo = kb.sbs(pes, "halo", [128, 22, 2], F32)
        y = kb.sb(pes, "y", [128, NB, 1024], F32)
        xo = [y, y]
        stats = kb.sb(pes, "stats", [128, NB, 2, 6], F32)
        mv = kb.sb(pes, "mv", [128, NB, 2], F32)
        sd = kb.sb(pes, "sd", [128, NB], F32)
        pp = [kb.ps(pes, f"pp{i}", [128, 512], F32) for i in range(8)]
        kb.op(DVE, lambda: nc.vector.memset(halo.t[:], 0.0), writes=halo.sub)
        cwo = _P["ffn_cw"][0]

        def load(t):
            kb.dma(xt[t % 2].t[:], x1_d[t * TT:(t + 1) * TT, :].rearrange("(b p) f -> p b f", p=128), writes=[xt[t % 2]])

        load(0)
        pi = 0
        for t in range(NT):
            if t + 1 < NT:
                load(t + 1)
            xx = xt[t % 2]
            for blk in range(NB):
                for c in range(8):
                    p = pp[pi % 8]; pi += 1
                    kb.op(PE, lambda: nc.tensor.matmul(p.t[:, 0:128], lhsT=xx.t[:, blk, c * 128:(c + 1) * 128], rhs=ident_f, start=True, stop=True), reads=[xx, cm], writes=[p])
                    kb.op(ACT, lambda: nc.scalar.activation(out=h2.t[:, c, blk * 128:(blk + 1) * 128], in_=p.t[:, 0:128], func=AF.Identity,
                                                            scale=ada.t[:, 32 + c:33 + c], bias=ada.t[:, 24 + c:25 + c]), reads=[p, ada], writes=[h2.sub[c]])
            deferred = []
            deferred_g = []
            for m in range(22):
                ms = slice(m * 128, (m + 1) * 128)
                pg = pp[pi % 8]; pu = pp[(pi + 1) % 8]; pi += 2
                for k in range(8):
                    kb.op(PE, lambda: nc.tensor.matmul(pg.t[:, 0:TT], lhsT=wfg.t[:, k, ms], rhs=h2.t[:, k, :], start=(k == 0), stop=(k == 7)),
                          reads=[wfg, h2.sub[k]], writes=[pg], inc=(k == 7))
                for k in range(8):
                    kb.op(PE, lambda: nc.tensor.matmul(pu.t[:, 0:TT], lhsT=wfu.t[:, k, ms], rhs=h2.t[:, k, :], start=(k == 0), stop=(k == 7)),
                          reads=[wfu, h2.sub[k]], writes=[pu], inc=(k == 7))
                rw = raw[m % 3]
                c_ = cv[m % 3]
                kb.op(ACT, lambda: nc.scalar.copy(out=rw.t[:, 0:2], in_=halo.t[:, m, :]), reads=[halo.sub[m]], writes=[rw])
                kb.op(ACT, lambda: nc.scalar.copy(out=rw.t[:, 2:TT + 2], in_=pg.t[:, 0:TT]), reads=[pg], writes=[rw])
                kb.op(ACT, lambda: nc.scalar.copy(out=halo.t[:, m, :], in_=rw.t[:, TT:TT + 2]), reads=[rw], writes=[halo.sub[m]])
                kb.op(DVE, lambda: nc.vector.tensor_scalar(out=c_.t[:], in0=rw.t[:, 2:TT + 2], scalar1=params.t[:, cwo + m * 3 + 2:cwo + m * 3 + 3],
                                                           scalar2=pcol("ffn_cb", m), op0=ALU.mult, op1=ALU.add), reads=[rw, params], writes=[c_])
                for kk in range(2):
                    kb.op(DVE, lambda: nc.vector.scalar_tensor_tensor(out=c_.t[:], in0=rw.t[:, kk:kk + TT], scalar=params.t[:, cwo + m * 3 + kk:cwo + m * 3 + kk + 1],
                                                                      in1=c_.t[:], op0=ALU.mult, op1=ALU.add), reads=[rw, params, c_], writes=[c_])
                def gel(m=m, c_=c_, pu=pu):
                    gelu_tanh(nc, kb, c_.t[:], [c_], act.t[:, m, :], [act.sub[m]], tmp[m % 3], mul_ap=pu.t[:, 0:TT], mul_bufs=[pu], defer=deferred)
                prev_m = list(deferred); del deferred[:]
                prev_g = list(deferred_g); del deferred_g[:]
                deferred_g.append(gel)
                for f_ in prev_g:
                    f_()
                for f_ in prev_m:
                    f_()
            for f_ in deferred_g:
                f_()
            del deferred_g[:]
            for f_ in deferred:
                f_()
            del deferred[:]
            for blk in range(NB):
                for cb in range(2):
                    p = pp[pi % 8]; pi += 1
                    cs = slice(cb * 512, (cb + 1) * 512)
                    for k in range(22):
                        kb.op(PE, lambda: nc.tensor.matmul(p.t[:], lhsT=act.t[:, k, blk * 128:(blk + 1) * 128], rhs=wfd.t[:, k, cs], start=(k == 0), stop=(k == 21)),
                              reads=[act.sub[k], wfd], writes=[p], inc=(k == 21))
                    kb.op(DVE, lambda: nc.vector.tensor_tensor(out=y.t[:, blk, cs], in0=p.t[:], in1=gt2r.t[:, cs], op=ALU.mult), reads=[p, gt2r], writes=[y])
                    kb.op(DVE, lambda: nc.vector.scalar_tensor_tensor(out=y.t[:, blk, cs], in0=xx.t[:, blk, cs], scalar=ALPHA, in1=y.t[:, blk, cs],
                                                                      op0=ALU.mult, op1=ALU.add), reads=[xx, y], writes=[y])
            o = xo[t % 2]
            layer_norm_rows(nc, kb, y, NB, lnr.t[:, 0:1024], lnr.t[:, 1024:2048], lnr, stats, mv, sd, o)
            kb.dma(out_d[t * TT:(t + 1) * TT, :].rearrange("(b p) f -> p b f", p=128), o.t[:], reads=[o])
        kb.barrier()


class PSlot:
    def __init__(self, kb, bank, i, name):
        self.b = bank
        self.ap = bank.t[:, i * 128:(i + 1) * 128]
        self.apb = self.ap.bitcast(BF16)[:, 0:128]


def phase_gdn(nc, kb, S, params, pcol, cm, cmb, load_w_bf16, w_in_d, rows_d, hrep_d, hT_d, dnT_d):
    PE, ACT, DVE, POOL = kb.pe, kb.act, kb.dve, kb.pool
    NCH = S // 128
    NT = S // 512
    HVG, HQG = 4, 2
    NG = 16 // HVG
    NCOL = NCH * 16
    cf = lambda i: cm.t[:, i * 128:(i + 1) * 128]
    cb = lambda i: cmb.t[:, i * 128:(i + 1) * 128]
    ident_f, ident_b, L_f, U_f, ones_f, ones_b = cf(CM_IDENT), cb(CM_IDENT), cf(CM_L), cf(CM_U), cf(CM_ONES), cb(CM_ONES)
    mS_b, mIT_b, BD_f, OFF_f = cb(CM_MS), cb(CM_MIT), cf(CM_BD), cf(CM_OFF)
    eps6 = kb.eps_ln.t[:, 1:2]
    with contextlib.ExitStack() as pes:
        tm = {n: kb.sb(pes, "tm_" + n, [128, NCH, 16], F32) for n in ("beta", "negg", "eG", "beG", "kds", "gl")}
        flat = lambda bf: bf.t[:].rearrange("p n h -> p (n h)")
        with contextlib.ExitStack() as ses:
            wab = kb.sb(ses, "wab", [128, 8, 32], BF16)
            stg = [kb.sb(ses, "stg0", [128, 8, 32], F32)]
            hT = [kb.sb(ses, f"hT{i}", [128, 8, 512], BF16) for i in range(2)]
            ab = kb.sb(ses, "ab", [128, NCH, 32], F32)
            hrep = kb.sb(ses, "hrep", [128, 2, NCOL], F32)
            X = kb.sb(ses, "X", [128, NCH, 16], F32)
            Y = kb.sb(ses, "Y", [128, NCH, 16], F32)
            Z = kb.sb(ses, "Z", [128, NCH, 16], F32)
            pp = [kb.ps(ses, f"pp{i}", [128, 512], F32) for i in range(4)]
            load_w_bf16(ses, wab, w_in_d, 0, 8, C_A, 32, stg)
            kb.dma(hrep.t[:], hrep_d[:, :, :], writes=[hrep])
            kb.dma(hT[0].t[:], hT_d[:, :, 0:512].rearrange("c p t -> p c t"), writes=[hT[0]])
            for t in range(NT):
                if t + 1 < NT:
                    kb.dma(hT[(t + 1) % 2].t[:], hT_d[:, :, (t + 1) * 512:(t + 2) * 512].rearrange("c p t -> p c t"), writes=[hT[(t + 1) % 2]])
                h = hT[t % 2]
                p = pp[t % 2]
                for blk in range(4):
                    for k in range(8):
                        kb.op(PE, lambda: nc.tensor.matmul(p.t[:, blk * 32:(blk + 1) * 32], lhsT=h.t[:, k, blk * 128:(blk + 1) * 128], rhs=wab.t[:, k, :],
                                                           start=(k == 0), stop=(k == 7)), reads=[h, wab], writes=[p], inc=(k == 7))
                kb.op(ACT, lambda: nc.scalar.copy(out=ab.t[:, t * 4:(t + 1) * 4, :], in_=p.t[:, 0:128].rearrange("p (b c) -> p b c", c=32)), reads=[p], writes=[ab])
            a_v, b_v = ab.t[:, :, 0:16], ab.t[:, :, 16:32]
            dtb = hrep.t[:, 0, :].rearrange("p (n h) -> p n h", h=16)
            alog = hrep.t[:, 1, :].rearrange("p (n h) -> p n h", h=16)
            kb.op(ACT, lambda: nc.scalar.activation(out=tm["beta"].t[:], in_=b_v, func=AF.Sigmoid), reads=[ab], writes=[tm["beta"]])
            kb.op(DVE, lambda: nc.vector.tensor_tensor(out=X.t[:], in0=a_v, in1=dtb, op=ALU.add), reads=[ab, hrep], writes=[X])
            kb.op(DVE, lambda: nc.vector.tensor_scalar(out=Y.t[:], in0=X.t[:], scalar1=-1.0, scalar2=None, op0=ALU.mult), reads=[X], writes=[Y])
            kb.op(DVE, lambda: nc.vector.tensor_tensor(out=Y.t[:], in0=Y.t[:], in1=X.t[:], op=ALU.max), reads=[X, Y], writes=[Y])
            kb.op(ACT, lambda: nc.scalar.activation(out=Y.t[:], in_=Y.t[:], func=AF.Exp, scale=-1.0), reads=[Y], writes=[Y])
            kb.op(ACT, lambda: nc.scalar.activation(out=Y.t[:], in_=Y.t[:], func=AF.Ln, bias=1.0), reads=[Y], writes=[Y])
            kb.op(DVE, lambda: nc.vector.tensor_scalar(out=X.t[:], in0=X.t[:], scalar1=0.0, scalar2=None, op0=ALU.max), reads=[X], writes=[X])
            kb.op(DVE, lambda: nc.vector.tensor_tensor(out=X.t[:], in0=X.t[:], in1=Y.t[:], op=ALU.add), reads=[X, Y], writes=[X])
            kb.op(ACT, lambda: nc.scalar.activation(out=Z.t[:], in_=alog, func=AF.Exp), reads=[hrep], writes=[Z])
            kb.op(DVE, lambda: nc.vector.tensor_tensor(out=tm["negg"].t[:], in0=X.t[:], in1=Z.t[:], op=ALU.mult), reads=[X, Z], writes=[tm["negg"]])
            ng_f = flat(tm["negg"])
            for c0 in range(0, NCOL, 512):
                w = min(512, NCOL - c0)
                p1, p2 = pp[2], pp[3]
                kb.op(PE, lambda: nc.tensor.matmul(p1.t[:, 0:w], lhsT=L_f, rhs=ng_f[:, c0:c0 + w], start=True, stop=True), reads=[cm, tm["negg"]], writes=[p1])
                kb.op(PE, lambda: nc.tensor.matmul(p2.t[:, 0:w], lhsT=ones_f, rhs=ng_f[:, c0:c0 + w], start=True, stop=True), reads=[cm, tm["negg"]], writes=[p2])
                Xf, Yf = flat(X), flat(Y)
                kb.op(ACT, lambda: nc.scalar.copy(out=Xf[:, c0:c0 + w], in_=p1.t[:, 0:w]), reads=[p1], writes=[X])
                kb.op(ACT, lambda: nc.scalar.activation(out=flat(tm["eG"])[:, c0:c0 + w], in_=p1.t[:, 0:w], func=AF.Exp, scale=-1.0), reads=[p1], writes=[tm["eG"]])
                kb.op(ACT, lambda: nc.scalar.activation(out=flat(tm["gl"])[:, c0:c0 + w], in_=p2.t[:, 0:w], func=AF.Exp, scale=-1.0), reads=[p2], writes=[tm["gl"]])
                kb.op(DVE, lambda: nc.vector.tensor_tensor(out=Yf[:, c0:c0 + w], in0=p2.t[:, 0:w], in1=Xf[:, c0:c0 + w], op=ALU.subtract), reads=[p2, X], writes=[Y])
                kb.op(ACT, lambda: nc.scalar.activation(out=flat(tm["kds"])[:, c0:c0 + w], in_=Yf[:, c0:c0 + w], func=AF.Exp, scale=-1.0), reads=[Y], writes=[tm["kds"]])
            kb.op(DVE, lambda: nc.vector.tensor_tensor(out=tm["beG"].t[:], in0=tm["beta"].t[:], in1=tm["eG"].t[:], op=ALU.mult), reads=[tm["beta"], tm["eG"]], writes=[tm["beG"]])
            kb.barrier()

        STOP = getattr(build_program, "gdn_stop", 0)
        for gi in range(NG if STOP != 1 else 0):
            hv0, hq0 = gi * HVG, gi * HQG
            with contextlib.ExitStack() as ges:
                wq = kb.sb(ges, "wq", [128, 8, HQG * 128], BF16)
                wk = kb.sb(ges, "wk", [128, 8, HQG * 128], BF16)
                wv = kb.sb(ges, "wv", [128, 8, HVG * 128], BF16)
                wz = kb.sb(ges, "wz", [128, 8, HVG * 128], BF16)
                nwr = kb.sb(ges, "nwr", [128, HVG * 128], F32)
                with contextlib.ExitStack() as ses:
                    stg = [kb.sb(ses, f"stg{i}", [128, 8, 512], F32) for i in range(2)]
                    load_w_bf16(ses, wq, w_in_d, 0, 8, C_Q + hq0 * 128, HQG * 128, stg)
                    load_w_bf16(ses, wk, w_in_d, 0, 8, C_K + hq0 * 128, HQG * 128, stg)
                    load_w_bf16(ses, wv, w_in_d, 0, 8, C_V + hv0 * 128, HVG * 128, stg)
                    load_w_bf16(ses, wz, w_in_d, 0, 8, C_Z + hv0 * 128, HVG * 128, stg)
                    kb.dma(nwr.t[:], rows_d[:, _R["nw"][0]:_R["nw"][0] + HVG * 128], writes=[nwr])
                    kb.barrier()
                hT = [kb.sb(ges, f"hT{i}", [128, 8, 512], BF16) for i in range(2)]
                raw = [kb.sb(ges, f"raw{i}", [128, 515], F32) for i in range(3)]
                cv = [kb.sb(ges, f"cv{i}", [128, 512], F32) for i in range(3)]
                NCK = 2 * HQG + HVG
                halo = kb.sb(ges, "halo", [128, NCK, 3], F32)
                qkf = [kb.sb(ges, f"qkf{i}", [128, 512], F32) for i in range(2 * HQG)]
                sqb = [kb.sb(ges, f"sqb{i}", [128, 512], BF16) for i in range(2 * HQG)]
                rn = [kb.sb(ges, f"rn{i}", [128, 512], F32) for i in range(2)]
                qT2 = [kb.sb(ges, f"qT{i}", [128, HQG, 512], BF16) for i in range(2)]
                kT2 = [kb.sb(ges, f"kT{i}", [128, HQG, 512], BF16) for i in range(2)]
                vT2 = [kb.sb(ges, f"vT{i}", [128, HVG, 512], BF16) for i in range(2)]
                zs = [kb.sb(ges, f"zs{i}", [128, 512], F32) for i in range(2)]
                zg2 = [kb.sb(ges, f"zg{i}", [128, 4, HVG * 128], F32) for i in range(2)]
                dno = [kb.sb(ges, f"dno{i}", [128, HVG, 512], BF16) for i in range(2)]
                Sf = [kb.sb(ges, f"Sf{i}", [128, 128], F32) for i in range(HVG)]
                Sb = [kb.sb(ges, f"Sb{i}", [128, 128], BF16) for i in range(HVG)]
                ob2 = [kb.sb(ges, f"ob{i}", [128, HVG, 128], F32) for i in range(2)]
                junk = kb.sb(ges, "junk", [128, 128], F32)
                ss2 = [kb.sb(ges, f"ss{i}", [128, HVG], F32) for i in range(2)]
                rstd2 = [kb.sb(ges, f"rstd{i}", [128, HVG], F32) for i in range(2)]
                ktm = [kb.sb(ges, f"ktm{i}", [128, 128], BF16) for i in range(HQG)]
                KKbd = [kb.sb(ges, f"KKbd{i}", [128, 128], F32) for i in range(HQG)]
                KKoff = [kb.sb(ges, f"KKoff{i}", [128, 128], F32) for i in range(HQG)]
                QKT = [kb.sb(ges, f"QKT{i}", [128, 128], F32) for i in range(HQG)]

                class PB:
                    pass
                pb = []
                for hi in range(HVG):
                    B = PB()
                    for n_ in ("rh", "Ds", "DT", "T1", "o1"):
                        setattr(B, n_, kb.sb(ges, f"{n_}_{hi}", [128, 128], F32))
                    for n_ in ("Abd", "Aoff", "P0", "P1", "M0", "Y1", "Tb", "bek", "vn", "dntm"):
                        setattr(B, n_, kb.sb(ges, f"{n_}_{hi}", [128, 128], BF16))
                    for n_ in ("QKd", "M128", "bv", "kdec", "nwT"):
                        setattr(B, n_, [kb.sb(ges, f"{n_}{i}_{hi}", [128, 128], BF16) for i in range(2)])
                    B.P = [B.P0, B.P1]
                    B.QM = [kb.sb(ges, f"QM{i}_{hi}", [128, 256], BF16) for i in range(2)]
                    pb.append(B)
                banks = [kb.ps(ges, f"bk{i}", [128, 512], F32) for i in range(8)]
                slots = [[PSlot(kb, banks[i], j, f"sl{i}_{j}") for j in range(4)] for i in range(8)]
                hsA = [slots[2 * hi] for hi in range(HVG)]
                hsB = [slots[2 * hi + 1] for hi in range(HVG)]

                def bank_bufs(i):
                    return [banks[i]]

                for b_ in Sf + [halo]:
                    kb.op(DVE, lambda: nc.vector.memset(b_.t[:], 0.0), writes=[b_])
                for b_ in Sb:
                    kb.op(DVE, lambda: nc.vector.memset(b_.t[:], 0.0), writes=[b_])
                cwo = _P["dn_cw"][0]
                chunks = [("q", wq, c, hq0 + c) for c in range(HQG)] + [("k", wk, c, 8 + hq0 + c) for c in range(HQG)] + \
                         [("v", wv, c, 16 + hv0 + c) for c in range(HVG)]

                def load_h(t):
                    kb.dma(hT[t % 2].t[:], hT_d[:, :, t * 512:(t + 1) * 512].rearrange("c p t -> p c t"), writes=[hT[t % 2]])

                bi_ = [0]

                def stageA(t):
                    h = hT[t % 2]
                    qT, kT, vT, zg = qT2[t % 2], kT2[t % 2], vT2[t % 2], zg2[t % 2]
                    defA = []
                    for ci, (kind, w, c, gch) in enumerate(chunks):
                        bk = bi_[0] % 8; bi_[0] += 1
                        pt, pbufs = banks[bk].t, bank_bufs(bk)
                        for k in range(8):
                            kb.op(PE, lambda: nc.tensor.matmul(pt[:], lhsT=w.t[:, k, c * 128:(c + 1) * 128], rhs=h.t[:, k, :], start=(k == 0), stop=(k == 7)),
                                  reads=[w, h], writes=pbufs, inc=(k == 7))
                        rw, cvb = raw[ci % 3], cv[ci % 3]
                        kb.op(ACT, lambda: nc.scalar.copy(out=rw.t[:, 0:3], in_=halo.t[:, ci, :]), reads=[halo], writes=[rw])
                        kb.op(ACT, lambda: nc.scalar.copy(out=rw.t[:, 3:515], in_=pt[:]), reads=pbufs, writes=[rw])
                        kb.op(ACT, lambda: nc.scalar.copy(out=halo.t[:, ci, :], in_=rw.t[:, 512:515]), reads=[rw], writes=[halo])
                        kb.op(DVE, lambda: nc.vector.tensor_scalar(out=cvb.t[:], in0=rw.t[:, 3:515], scalar1=params.t[:, cwo + gch * 4 + 3:cwo + gch * 4 + 4],
                                                                   scalar2=None, op0=ALU.mult), reads=[rw, params], writes=[cvb])
                        for kk in range(3):
                            kb.op(DVE, lambda: nc.vector.scalar_tensor_tensor(out=cvb.t[:], in0=rw.t[:, kk:kk + 512], scalar=params.t[:, cwo + gch * 4 + kk:cwo + gch * 4 + kk + 1],
                                                                              in1=cvb.t[:], op0=ALU.mult, op1=ALU.add), reads=[rw, params, cvb], writes=[cvb])
                        def fin(kind=kind, c=c, ci=ci, cvb=cvb):
                            if kind == "v":
                                kb.op(ACT, lambda: nc.scalar.activation(out=vT.t[:, c, :], in_=cvb.t[:], func=AF.Silu), reads=[cvb], writes=[vT])
                            else:
                                f = qkf[ci]
                                kb.op(ACT, lambda: nc.scalar.activation(out=f.t[:], in_=cvb.t[:], func=AF.Silu), reads=[cvb], writes=[f])
                                kb.op(POOL, lambda: nc.gpsimd.tensor_tensor(out=sqb[ci].t[:], in0=f.t[:], in1=f.t[:], op=ALU.mult), reads=[f], writes=[sqb[ci]])
                        prevA = list(defA); del defA[:]
                        defA.append(fin)
                        for f_ in prevA:
                            f_()
                    for f_ in defA:
                        f_()
                    del defA[:]
                    for blk in range(4):
                        bk = bi_[0] % 8; bi_[0] += 1
                        pt, pbufs = banks[bk].t, bank_bufs(bk)
                        for k in range(8):
                            kb.op(PE, lambda: nc.tensor.matmul(pt[:, 0:HVG * 128], lhsT=h.t[:, k, blk * 128:(blk + 1) * 128], rhs=wz.t[:, k, :], start=(k == 0), stop=(k == 7)),
                                  reads=[h, wz], writes=pbufs, inc=(k == 7))
                        z_ = zs[blk % 2]
                        kb.op(ACT, lambda: nc.scalar.activation(out=z_.t[:, 0:HVG * 128], in_=pt[:, 0:HVG * 128], func=AF.Silu), reads=pbufs, writes=[z_])
                        kb.op(POOL, lambda: nc.gpsimd.tensor_tensor(out=zg.t[:, blk, :], in0=z_.t[:, 0:HVG * 128], in1=nwr.t[:], op=ALU.mult), reads=[z_, nwr], writes=[zg])

                    for ci, (kind, w, c, gch) in enumerate(chunks):
                        if kind == "v":
                            continue
                        f, sq, r_ = qkf[ci], sqb[ci], rn[ci % 2]
                        dst = qT if kind == "q" else kT
                        bk2 = bi_[0] % 8; bi_[0] += 1
                        pt2, pbufs2 = banks[bk2].t, bank_bufs(bk2)
                        kb.op(PE, lambda: nc.tensor.matmul(pt2[:], lhsT=ones_b, rhs=sq.t[:], start=True, stop=True), reads=[cmb, sq], writes=pbufs2)
                        kb.op(ACT, lambda: nc.scalar.activation(out=r_.t[:], in_=pt2[:], func=AF.Ln, bias=eps6), reads=pbufs2 + [kb.eps_ln], writes=[r_])
                        kb.op(ACT, lambda: nc.scalar.activation(out=r_.t[:], in_=r_.t[:], func=AF.Exp, scale=-0.5), reads=[r_], writes=[r_])
                        kb.op(DVE, lambda: nc.vector.scalar_tensor_tensor(out=dst.t[:, c, :], in0=f.t[:], scalar=(128.0 ** -0.5 if kind == "q" else 1.0), in1=r_.t[:],
                                                                          op0=ALU.mult, op1=ALU.mult), reads=[f, r_], writes=[dst])

                def hq_ops(n):
                    t, blk = n // 4, n % 4
                    cs = slice(blk * 128, (blk + 1) * 128)
                    qT, kT = qT2[t % 2], kT2[t % 2]
                    for hq in range(HQG):
                        kc, qc = kT.t[:, hq, cs], qT.t[:, hq, cs]
                        s0, s1, s2 = hsA[2 * hq][0], hsA[2 * hq][1], hsA[2 * hq][2]
                        kb.op(PE, lambda: nc.tensor.transpose(s0.apb, kc, ident_b), reads=[kT, cmb], writes=[s0.b])
                        kb.op(PE, lambda: nc.tensor.matmul(s1.ap, lhsT=kc, rhs=kc, start=True, stop=True), reads=[kT], writes=[s1.b])
                        kb.op(PE, lambda: nc.tensor.matmul(s2.ap, lhsT=kc, rhs=qc, start=True, stop=True), reads=[kT, qT], writes=[s2.b])
                        kb.op(ACT, lambda: nc.scalar.copy(out=ktm[hq].t[:], in_=s0.apb), reads=[s0.b], writes=[ktm[hq]])
                        kb.op(DVE, lambda: nc.vector.tensor_tensor(out=KKbd[hq].t[:], in0=s1.ap, in1=BD_f, op=ALU.mult), reads=[s1.b, cm], writes=[KKbd[hq]])
                        kb.op(DVE, lambda: nc.vector.tensor_tensor(out=KKoff[hq].t[:], in0=s1.ap, in1=OFF_f, op=ALU.mult), reads=[s1.b, cm], writes=[KKoff[hq]])
                        kb.op(ACT, lambda: nc.scalar.copy(out=QKT[hq].t[:], in_=s2.ap), reads=[s2.b], writes=[QKT[hq]])

                def pre_seq(hi, n):
                    t, blk = n // 4, n % 4
                    cs = slice(blk * 128, (blk + 1) * 128)
                    vT = vT2[t % 2]
                    pr = n % 2
                    hh_ = hv0 + hi
                    hq = hi // 2
                    B, sl = pb[hi], hsA[hi]
                    sc = lambda nm: tm[nm].t[:, n, hh_:hh_ + 1]
                    QKd, M128, bv, kdec, nwT = B.QKd[pr], B.M128[pr], B.bv[pr], B.kdec[pr], B.nwT[pr]
                    kb.op(ACT, lambda: nc.scalar.activation(out=B.rh.t[:], in_=U_f, func=AF.Identity, scale=sc("negg")), reads=[cm, tm["negg"]], writes=[B.rh])
                    kb.op(ACT, lambda: nc.scalar.activation(out=B.bek.t[:], in_=ktm[hq].t[:], func=AF.Identity, scale=sc("beG")), reads=[ktm[hq], tm["beG"]], writes=[B.bek])
                    kb.op(ACT, lambda: nc.scalar.activation(out=kdec.t[:], in_=ktm[hq].t[:], func=AF.Identity, scale=sc("kds")), reads=[ktm[hq], tm["kds"]], writes=[kdec])
                    yield
                    kb.op(PE, lambda: nc.tensor.matmul(sl[0].ap, lhsT=L_f, rhs=B.rh.t[:], start=True, stop=False), reads=[cm, B.rh], writes=[sl[0].b], inc=False)
                    kb.op(PE, lambda: nc.tensor.matmul(sl[0].ap, lhsT=ident_b, rhs=mS_b, start=False, stop=True), reads=[cmb], writes=[sl[0].b], inc=False)
                    kb.op(PE, lambda: nc.tensor.matmul(sl[1].ap, lhsT=B.rh.t[:], rhs=L_f, start=True, stop=False), reads=[cm, B.rh], writes=[sl[1].b], inc=False)
                    kb.op(PE, lambda: nc.tensor.matmul(sl[1].ap, lhsT=ident_b, rhs=mIT_b, start=False, stop=True), reads=[cmb], writes=[sl[1].b], inc=False)
                    kb.op(PE, lambda: nc.tensor.transpose(sl[3].apb, vT.t[:, hi, cs], ident_b), reads=[vT, cmb], writes=[sl[3].b])
                    kb.op(ACT, lambda: nc.scalar.activation(out=B.Ds.t[:], in_=sl[0].ap, func=AF.Exp, scale=-1.0), reads=[sl[0].b], writes=[B.Ds])
                    kb.op(ACT, lambda: nc.scalar.activation(out=B.DT.t[:], in_=sl[1].ap, func=AF.Exp, scale=-1.0), reads=[sl[1].b], writes=[B.DT])
                    kb.op(DVE, lambda: nc.vector.tensor_scalar(out=bv.t[:], in0=sl[3].apb, scalar1=sc("beta"), scalar2=None, op0=ALU.mult), reads=[sl[3].b, tm["beta"]], writes=[bv])
                    yield
                    kb.op(DVE, lambda: nc.vector.tensor_scalar(out=B.T1.t[:], in0=B.Ds.t[:], scalar1=sc("beta"), scalar2=None, op0=ALU.mult), reads=[B.Ds, tm["beta"]], writes=[B.T1])
                    kb.op(DVE, lambda: nc.vector.tensor_tensor(out=B.Abd.t[:], in0=B.T1.t[:], in1=KKbd[hq].t[:], op=ALU.mult), reads=[B.T1, KKbd[hq]], writes=[B.Abd])
                    kb.op(DVE, lambda: nc.vector.tensor_tensor(out=B.Aoff.t[:], in0=B.T1.t[:], in1=KKoff[hq].t[:], op=ALU.mult), reads=[B.T1, KKoff[hq]], writes=[B.Aoff])
                    kb.op(DVE, lambda: nc.vector.tensor_tensor(out=QKd.t[:], in0=QKT[hq].t[:], in1=B.DT.t[:], op=ALU.mult), reads=[QKT[hq], B.DT], writes=[QKd])
                    yield
                    bankA = banks[2 * hi]
                    kb.op(PE, lambda: nc.tensor.transpose(sl[2].apb, B.Abd.t[:], ident_b), reads=[B.Abd, cmb], writes=[sl[2].b])
                    kb.op(ACT, lambda: nc.scalar.copy(out=B.QM[0].t[:, 0:128], in_=sl[2].apb), reads=[sl[2].b], writes=[B.QM[0]])
                    kb.op(DVE, lambda: nc.vector.tensor_tensor(out=B.QM[0].t[:, 128:256], in0=ident_f, in1=sl[2].apb, op=ALU.subtract), reads=[cm, sl[2].b], writes=[B.QM[0]])
                    kb.op(DVE, lambda: nc.vector.tensor_tensor(out=B.QM[1].t[:, 128:256], in0=ident_f, in1=sl[2].apb, op=ALU.subtract), reads=[cm, sl[2].b], writes=[B.QM[1]])
                    yield
                    P = B.Abd
                    for lv in range(1, 6):
                        cur, nxt, Pn = B.QM[(lv - 1) % 2], B.QM[lv % 2], B.P[lv % 2]
                        lo, hi_ = (0, 128) if lv == 1 else ((0, 256) if lv < 5 else (128, 256))
                        kb.op(PE, lambda: nc.tensor.matmul(sl[2].ap, lhsT=cur.t[:, 0:128], rhs=P.t[:], start=True, stop=True), reads=[P, cur], writes=[sl[2].b], inc=False)
                        kb.op(PE, lambda: nc.tensor.matmul(bankA.t[:, lo:hi_], lhsT=P.t[:], rhs=cur.t[:, lo:hi_], start=True, stop=True), reads=[P, cur], writes=[bankA])
                        kb.op(ACT, lambda: nc.scalar.copy(out=Pn.t[:], in_=sl[2].ap), reads=[bankA], writes=[Pn])
                        if lv < 5:
                            kb.op(ACT, lambda: nc.scalar.copy(out=nxt.t[:, 0:128], in_=bankA.t[:, 0:128]), reads=[bankA], writes=[nxt])
                        if lv >= 2:
                            kb.op(DVE, lambda: nc.vector.tensor_tensor(out=nxt.t[:, 128:256], in0=cur.t[:, 128:256], in1=bankA.t[:, 128:256], op=ALU.add), reads=[cur, bankA], writes=[nxt])
                        yield
                        P = Pn
                    M4 = B.QM[1]
                    kb.op(PE, lambda: nc.tensor.matmul(sl[0].ap, lhsT=P.t[:], rhs=M4.t[:, 128:256], start=True, stop=True), reads=[P, M4], writes=[bankA])
                    kb.op(DVE, lambda: nc.vector.tensor_tensor(out=B.M0.t[:], in0=M4.t[:, 128:256], in1=sl[0].ap, op=ALU.add), reads=[M4, bankA], writes=[B.M0])
                    yield
                    M = B.M0
                    kb.op(PE, lambda: nc.tensor.matmul(sl[2].ap, lhsT=B.Aoff.t[:], rhs=M.t[:], start=True, stop=True), reads=[B.Aoff, M], writes=[bankA], inc=False)
                    kb.op(PE, lambda: nc.tensor.transpose(sl[3].apb, M.t[:], ident_b), reads=[M, cmb], writes=[bankA])
                    kb.op(ACT, lambda: nc.scalar.copy(out=B.Y1.t[:], in_=sl[2].ap), reads=[bankA], writes=[B.Y1])
                    kb.op(DVE, lambda: nc.vector.tensor_copy(out=B.Tb.t[:], in_=sl[3].apb), reads=[bankA], writes=[B.Tb])
                    yield
                    kb.op(PE, lambda: nc.tensor.matmul(sl[0].ap, lhsT=B.Tb.t[:], rhs=B.Y1.t[:], start=True, stop=True), reads=[B.Tb, B.Y1], writes=[bankA])
                    kb.op(DVE, lambda: nc.vector.tensor_tensor(out=M128.t[:], in0=M.t[:], in1=sl[0].ap, op=ALU.subtract), reads=[M, bankA], writes=[M128])
                    yield
                    kb.op(PE, lambda: nc.tensor.matmul(sl[1].ap, lhsT=B.bek.t[:], rhs=M128.t[:], start=True, stop=True), reads=[B.bek, M128], writes=[sl[1].b])
                    kb.op(ACT, lambda: nc.scalar.activation(out=nwT.t[:], in_=sl[1].ap, func=AF.Identity, scale=-1.0), reads=[sl[1].b], writes=[nwT])
                    yield

                def post_seq(hi, n):
                    t, blk = n // 4, n % 4
                    cs = slice(blk * 128, (blk + 1) * 128)
                    qT = qT2[t % 2]
                    pr = n % 2
                    hh_ = hv0 + hi
                    hq = hi // 2
                    B, sl = pb[hi], hsB[hi]
                    sc = lambda nm: tm[nm].t[:, n, hh_:hh_ + 1]
                    QKd, M128, bv, kdec, nwT = B.QKd[pr], B.M128[pr], B.bv[pr], B.kdec[pr], B.nwT[pr]
                    kb.op(PE, lambda: nc.tensor.matmul(sl[0].ap, lhsT=M128.t[:], rhs=bv.t[:], start=True, stop=False), reads=[M128, bv], writes=[sl[0].b], inc=False)
                    kb.op(PE, lambda: nc.tensor.matmul(sl[0].ap, lhsT=nwT.t[:], rhs=Sb[hi].t[:], start=False, stop=True), reads=[nwT, Sb[hi]], writes=[sl[0].b], inc=False)
                    kb.op(PE, lambda: nc.tensor.matmul(sl[1].ap, lhsT=qT.t[:, hq, cs], rhs=Sb[hi].t[:], start=True, stop=True), reads=[qT, Sb[hi]], writes=[sl[1].b])
                    kb.op(ACT, lambda: nc.scalar.copy(out=B.vn.t[:], in_=sl[0].ap), reads=[sl[0].b], writes=[B.vn])
                    kb.op(ACT, lambda: nc.scalar.activation(out=B.o1.t[:], in_=sl[1].ap, func=AF.Identity, scale=sc("eG")), reads=[sl[1].b, tm["eG"]], writes=[B.o1])
                    yield
                    kb.op(PE, lambda: nc.tensor.matmul(sl[2].ap, lhsT=kdec.t[:], rhs=B.vn.t[:], start=True, stop=True), reads=[kdec, B.vn], writes=[sl[2].b], inc=False)
                    kb.op(PE, lambda: nc.tensor.matmul(sl[3].ap, lhsT=QKd.t[:], rhs=B.vn.t[:], start=True, stop=True), reads=[QKd, B.vn], writes=[sl[3].b])
                    kb.op(DVE, lambda: nc.vector.scalar_tensor_tensor(out=Sf[hi].t[:], in0=Sf[hi].t[:], scalar=sc("gl"), in1=sl[2].ap, op0=ALU.mult, op1=ALU.add),
                          reads=[Sf[hi], tm["gl"], sl[2].b], writes=[Sf[hi]])
                    kb.op(DVE, lambda: nc.vector.tensor_tensor(out=ob2[pr].t[:, hi, :], in0=B.o1.t[:], in1=sl[3].ap, op=ALU.add), reads=[B.o1, sl[3].b], writes=[ob2[pr]])
                    yield
                    kb.op(ACT, lambda: nc.scalar.copy(out=Sb[hi].t[:], in_=Sf[hi].t[:]), reads=[Sf[hi]], writes=[Sb[hi]])
                    yield

                def norm_seq(n):
                    t, blk = n // 4, n % 4
                    cs = slice(blk * 128, (blk + 1) * 128)
                    zg, dn_o = zg2[t % 2], dno[t % 2]
                    ob, ss, rstd = ob2[n % 2], ss2[n % 2], rstd2[n % 2]
                    for hi in range(HVG):
                        kb.op(ACT, lambda: nc.scalar.activation(out=junk.t[:], in_=ob.t[:, hi, :], func=AF.Square, accum_out=ss.t[:, hi:hi + 1]), reads=[ob], writes=[junk, ss])
                    yield
                    kb.op(ACT, lambda: nc.scalar.activation(out=rstd.t[:], in_=ss.t[:], func=AF.Ln, scale=1.0 / 128.0, bias=eps6), reads=[ss, kb.eps_ln], writes=[rstd])
                    kb.op(ACT, lambda: nc.scalar.activation(out=rstd.t[:], in_=rstd.t[:], func=AF.Exp, scale=-0.5), reads=[rstd], writes=[rstd])
                    yield
                    for hi in range(HVG):
                        B = pb[hi]
                        kb.op(DVE, lambda: nc.vector.scalar_tensor_tensor(out=B.dntm.t[:], in0=ob.t[:, hi, :], scalar=rstd.t[:, hi:hi + 1], in1=zg.t[:, blk, hi * 128:(hi + 1) * 128],
                                                                          op0=ALU.mult, op1=ALU.mult), reads=[ob, rstd, zg], writes=[B.dntm])
                    yield
                    for hi in range(HVG):
                        B, sl = pb[hi], hsB[hi]
                        kb.op(PE, lambda: nc.tensor.transpose(sl[0].apb, B.dntm.t[:], ident_b), reads=[B.dntm, cmb], writes=[sl[0].b])
                        kb.op(ACT, lambda: nc.scalar.copy(out=dn_o.t[:, hi, cs], in_=sl[0].apb), reads=[sl[0].b], writes=[dn_o])
                    if blk == 3:
                        kb.dma(dnT_d[hv0:hv0 + HVG, :, t * 512:(t + 1) * 512].rearrange("c p t -> p c t"), dn_o.t[:], reads=[dn_o])
                    yield

                def run(gens, delays=None):
                    gens = list(gens)
                    delays = list(delays) if delays is not None else [0] * len(gens)
                    rnd = 0
                    while gens:
                        for i in range(len(gens) - 1, -1, -1):
                            pass
                        keep_g, keep_d = [], []
                        for g, d in zip(gens, delays):
                            if rnd < d:
                                keep_g.append(g); keep_d.append(d)
                                continue
                            try:
                                next(g)
                                keep_g.append(g); keep_d.append(d)
                            except StopIteration:
                                pass
                        gens, delays = keep_g, keep_d
                        rnd += 1

                STAG = getattr(build_program, "stagger", 0)
                load_h(0)
                if NT > 1:
                    load_h(1)
                stageA(0)
                hq_ops(0)
                run([pre_seq(hi, 0) for hi in range(HVG)])
                pend_norm = []
                for n in range(NCH):
                    gens = []
                    if n + 1 < NCH:
                        if (n + 1) % 4 == 0:
                            t1 = (n + 1) // 4
                            stageA(t1)
                            if t1 + 1 < NT:
                                load_h(t1 + 1)
                        hq_ops(n + 1)
                        gens = [pre_seq(hi, n + 1) for hi in range(HVG)]
                    posts = [post_seq(hi, n) for hi in range(HVG)]
                    run(posts + gens + pend_norm)
                    pend_norm = [norm_seq(n)]
                run(pend_norm)
                kb.barrier()


def _consts():
    i = np.arange(128)
    ident = np.eye(128, dtype=np.float32)
    L = (i[:, None] <= i[None, :]).astype(np.float32)
    U = (i[:, None] > i[None, :]).astype(np.float32)
    mS = np.where(i[:, None] > i[None, :], 0.0, BIG).astype(np.float32)
    mIT = np.where(i[None, :] >= i[:, None], 0.0, BIG).astype(np.float32)
    blk = i // 64
    BD = (blk[:, None] == blk[None, :]).astype(np.float32)
    OFF = ((blk[:, None] == 1) & (blk[None, :] == 0)).astype(np.float32)
    ones = np.ones((128, 128), np.float32)
    return np.concatenate([ident, L, U, mS, mIT, BD, OFF, ones], axis=1)


def _fm(v, n):
    return np.ascontiguousarray(np.asarray(v, np.float32).reshape(n, 128).T)


def _band(w):
    full = np.zeros((DRNN, DRNN), np.float32)
    for b in range(16):
        full[b * 80:(b + 1) * 80, b * 80:(b + 1) * 80] = w[b]
    out = np.zeros((128, 10, 3, 128), np.float32)
    for m in range(10):
        for d in range(3):
            k = m + d - 1
            if 0 <= k < 10:
                out[:, m, d, :] = full[k * 128:(k + 1) * 128, m * 128:(m + 1) * 128]
    return out


def make_in_maps(inp, S, batches):
    l = 0
    f = lambda a: np.ascontiguousarray(np.asarray(a, np.float32))
    NCH = S // 128
    rows = np.zeros((128, NR), np.float32)
    b_ada = f(inp["b_ada"][l])
    def setr(name, v):
        o, w = _R[name]
        rows[:, o:o + w] = np.asarray(v, np.float32)[None, :]
    setr("b_gt1", b_ada[2048:3072]); setr("b_gt2", b_ada[5120:6144])
    setr("ln1_g", inp["ln1_g"][l]); setr("ln1_b", inp["ln1_b"][l]); setr("ln2_g", inp["ln2_g"][l]); setr("ln2_b", inp["ln2_b"][l])
    setr("nw", np.tile(f(inp["dn_norm_w"][l]), 16))
    hrep = np.zeros((128, 2, NCH * 16), np.float32)
    hrep[:, 0, :] = np.tile(f(inp["dn_dt_bias"][l]), NCH)[None, :]
    hrep[:, 1, :] = np.tile(f(inp["dn_a_log"][l]), NCH)[None, :]
    cmat = _consts()
    shared = dict(
        w_ada=f(inp["w_ada"][l]), w_in=f(inp["w_in"][l]), gate_a=_band(f(inp["rg_w_a"][l])), gate_x=_band(f(inp["rg_w_x"][l])),
        w_proj_a=f(inp["w_proj_a"][l]), w_proj_b=f(inp["w_proj_b"][l]), w_out=f(inp["w_out"][l]),
        ffn_w_gate=f(inp["ffn_w_gate"][l]), ffn_w_up=f(inp["ffn_w_up"][l]), ffn_w_down=f(inp["ffn_w_down"][l]),
        rows=rows, cmat=cmat, hrep=hrep)
    maps = []
    for b in batches:
        params = np.zeros((128, NP), np.float32)
        def setp(name, v):
            o, w = _P[name]
            params[:, o:o + w] = v
        setp("b_ada", _fm(b_ada, 48))
        setp("c", _fm(inp["c"][b], 8))
        setp("rg_cw", np.stack([_fm(inp["rg_conv_w"][l][k], 10) for k in range(4)], axis=2).reshape(128, 40))
        setp("rg_cb", _fm(inp["rg_conv_b"][l], 10)); setp("rg_ba", _fm(inp["rg_b_a"][l], 10)); setp("rg_bx", _fm(inp["rg_b_x"][l], 10))
        setp("rg_lam", _fm(inp["rg_lambda"][l], 10))
        setp("dn_cw", np.stack([_fm(inp["dn_conv_w"][l][k], 32) for k in range(4)], axis=2).reshape(128, 128))
        setp("ffn_cw", np.stack([_fm(inp["ffn_conv_w"][l][k], 22) for k in range(3)], axis=2).reshape(128, 66))
        setp("ffn_cb", _fm(inp["ffn_conv_b"][l], 22))
        xb = f(inp["x"][b])
        m = dict(shared)
        m["x"] = xb
        m["xT"] = np.ascontiguousarray(xb.T.reshape(8, 128, S))
        m["params"] = params
        maps.append(m)
    return maps


_NC_CACHE = {}


def kernel(**inputs):
    S = inputs["x"].shape[1]
    B = inputs["x"].shape[0]
    if S not in _NC_CACHE:
        _NC_CACHE[S] = build_program(S)
    nc = _NC_CACHE[S]
    maps = make_in_maps(inputs, S, list(range(B)))
    res = run_bass_kernel_spmd(nc, maps, core_ids=list(range(B)))
    return np.stack([np.asarray(r["out"], np.float32) for r in res.results], axis=0)
```
